# Optimizing a Trainium2 kernel written in Bass

```python
import math
import jax, jax.numpy as jnp
from jax import lax
import numpy as np


D_MODEL = 1024
BATCH = 2
SEQ = 8192
DEPTH = 1

HEAD_DIM = 64
DSA_HEADS = 8
DSA_WIDTH = DSA_HEADS * HEAD_DIM
IDX_HEADS = 8
IDX_DIM = 32
TOPK_MAX = 256
DIFF_HEADS = 4
DIFF_DIM = 64
DIFF_VDIM = 2 * DIFF_DIM
DIFF_QK_WIDTH = DIFF_HEADS * 2 * DIFF_DIM
DIFF_WIDTH = DIFF_HEADS * DIFF_VDIM
DIFF_SUBLN_EPS = 1e-5
N_BRANCH = 2
ROPE_THETA = 500000.0
ROT_FRACTION = 4
ROT_DIM_HEAD = HEAD_DIM // ROT_FRACTION
ROT_DIM_IDX = IDX_DIM // ROT_FRACTION
Q_BLOCK = 128
FFN_MULT = 256
D_FF = -(-8 * D_MODEL // (3 * FFN_MULT)) * FFN_MULT
NORM_EPS = 1e-6

IN_SPLITS = (
    DSA_WIDTH,
    DSA_WIDTH,
    DSA_WIDTH,
    IDX_HEADS * IDX_DIM,
    IDX_DIM,
    IDX_HEADS,
    DIFF_QK_WIDTH,
    DIFF_QK_WIDTH,
    DIFF_WIDTH,
    N_BRANCH * D_MODEL,
)
D_IN = sum(IN_SPLITS)

kernel_name = 'hybrid_dsa_diffattn_gated_block'


def rmsnorm(x, g, eps=NORM_EPS):
    xf = x.astype(jnp.float32)
    y = xf * lax.rsqrt(jnp.mean(xf * xf, axis=-1, keepdims=True) + eps)
    return (y * g.astype(jnp.float32)).astype(x.dtype)


def layernorm(x, g, b, eps=NORM_EPS):
    xf = x.astype(jnp.float32)
    mu = jnp.mean(xf, axis=-1, keepdims=True)
    xc = xf - mu
    y = xc * lax.rsqrt(jnp.mean(xc * xc, axis=-1, keepdims=True) + eps)
    return (y * g.astype(jnp.float32) + b.astype(jnp.float32)).astype(x.dtype)


def rope_tables(positions, rot_dim):
    inv_freq = jnp.power(jnp.float32(ROPE_THETA), -jnp.arange(0, rot_dim, 2, dtype=jnp.float32) / rot_dim)
    ang = positions.astype(jnp.float32)[..., None] * inv_freq
    return jnp.cos(ang), jnp.sin(ang)


def apply_partial_rope(x, cos, sin):
    half = cos.shape[-1]
    x1 = x[..., :half]
    x2 = x[..., half:2 * half]
    xp = x[..., 2 * half:]
    c = cos[:, :, None, :].astype(x.dtype)
    s = sin[:, :, None, :].astype(x.dtype)
    return jnp.concatenate([x1 * c - x2 * s, x2 * c + x1 * s, xp], axis=-1)


def split_columns(proj):
    outs = []
    start = 0
    for size in IN_SPLITS:
        outs.append(proj[..., start:start + size])
        start += size
    return outs


def dsa_sparse_attention(q, k, v, q_idx, k_idx, w_idx, n_top):
    B, S = q.shape[0], q.shape[1]
    att_scale = HEAD_DIM ** -0.5
    w_idx = w_idx * (IDX_HEADS ** -0.5 * IDX_DIM ** -0.5)
    s_pos = jnp.arange(S)

    def block(i):
        q0 = i * Q_BLOCK
        qb = lax.dynamic_slice_in_dim(q, q0, Q_BLOCK, axis=1)
        qib = lax.dynamic_slice_in_dim(q_idx, q0, Q_BLOCK, axis=1)
        wib = lax.dynamic_slice_in_dim(w_idx, q0, Q_BLOCK, axis=1)
        t_pos = q0 + jnp.arange(Q_BLOCK)
        causal = s_pos[None, :] <= t_pos[:, None]
        logits = jnp.einsum('bqhd,bsd->bqhs', qib, k_idx)
        score = jnp.einsum('bqhs,bqh->bqs', jax.nn.relu(logits), wib).astype(jnp.float32)
        score = jnp.where(causal[None], score, -jnp.inf)
        _, sel = lax.top_k(score, n_top)
        valid = sel <= t_pos[None, :, None]
        kg = jax.vmap(lambda kb, ib: kb[ib])(k, sel)
        vg = jax.vmap(lambda vb, ib: vb[ib])(v, sel)
        sc = jnp.einsum('bqhd,bqkhd->bhqk', qb, kg).astype(jnp.float32) * att_scale
        sc = jnp.where(valid[:, None], sc, -jnp.inf)
        p = jax.nn.softmax(sc, axis=-1).astype(v.dtype)
        return jnp.einsum('bhqk,bqkhd->bqhd', p, vg)

    out = lax.map(block, jnp.arange(S // Q_BLOCK))
    return jnp.moveaxis(out, 0, 1).reshape(B, S, DSA_WIDTH)


def differential_attention(q, k, v, lam):
    B, S = q.shape[0], q.shape[1]
    scale = DIFF_DIM ** -0.5
    s_pos = jnp.arange(S)

    def block(i):
        q0 = i * Q_BLOCK
        qb = lax.dynamic_slice_in_dim(q, q0, Q_BLOCK, axis=1)
        t_pos = q0 + jnp.arange(Q_BLOCK)
        causal = s_pos[None, :] <= t_pos[:, None]
        sc = jnp.einsum('bqhmd,bshmd->bhmqs', qb, k).astype(jnp.float32) * scale
        sc = jnp.where(causal, sc, -jnp.inf)
        a = jax.nn.softmax(sc, axis=-1)
        a = a[:, :, 0] - lam * a[:, :, 1]
        return jnp.einsum('bhqs,bshe->bqhe', a.astype(v.dtype), v)

    out = lax.map(block, jnp.arange(S // Q_BLOCK))
    return jnp.moveaxis(out, 0, 1).reshape(B, S, DIFF_HEADS, DIFF_VDIM)


def setup_inputs(seed: int = 0) -> dict:
    key = jax.random.key(seed)
    ks = jax.random.split(key, 20)
    f32 = jnp.float32

    def normal(k, shape, std):
        return jax.random.normal(k, shape, dtype=f32) * std

    def gain(k, shape):
        return 1.0 + normal(k, shape, 0.02)

    return {
        'x': normal(ks[0], (BATCH, SEQ, D_MODEL), 1.0),
        'positions': jnp.broadcast_to(jnp.arange(SEQ, dtype=jnp.int32), (BATCH, SEQ)),
        'norm_mix_g': gain(ks[1], (DEPTH, D_MODEL)),
        'w_in': normal(ks[2], (DEPTH, D_MODEL, D_IN), D_MODEL ** -0.5),
        'idx_k_norm_g': gain(ks[3], (DEPTH, IDX_DIM)),
        'idx_k_norm_b': normal(ks[4], (DEPTH, IDX_DIM), 0.02),
        'diff_lambda_q1': normal(ks[5], (DEPTH, DIFF_DIM), 0.1),
        'diff_lambda_k1': normal(ks[6], (DEPTH, DIFF_DIM), 0.1),
        'diff_lambda_q2': normal(ks[7], (DEPTH, DIFF_DIM), 0.1),
        'diff_lambda_k2': normal(ks[8], (DEPTH, DIFF_DIM), 0.1),
        'diff_subln_g': gain(ks[9], (DEPTH, DIFF_VDIM)),
        'gate_b': normal(ks[10], (DEPTH, N_BRANCH * D_MODEL), 0.02),
        'w_branch_dsa': normal(ks[11], (DEPTH, DSA_WIDTH, D_MODEL), DSA_WIDTH ** -0.5),
        'w_branch_diff': normal(ks[12], (DEPTH, DIFF_WIDTH, D_MODEL), DIFF_WIDTH ** -0.5),
        'w_out': normal(ks[13], (DEPTH, D_MODEL, D_MODEL), D_MODEL ** -0.5),
        'norm_ffn_g': gain(ks[14], (DEPTH, D_MODEL)),
        'w_ffn_in': normal(ks[15], (DEPTH, D_MODEL, 2 * D_FF), D_MODEL ** -0.5),
        'w_ffn_out': normal(ks[16], (DEPTH, D_FF, D_MODEL), D_FF ** -0.5),
        'norm_final_g': gain(ks[17], (D_MODEL,)),
    }


def reference(x, positions, norm_mix_g, w_in, idx_k_norm_g, idx_k_norm_b,
              diff_lambda_q1, diff_lambda_k1, diff_lambda_q2, diff_lambda_k2,
              diff_subln_g, gate_b, w_branch_dsa, w_branch_diff, w_out,
              norm_ffn_g, w_ffn_in, w_ffn_out, norm_final_g):
    B, S = x.shape[0], x.shape[1]
    n_top = min(TOPK_MAX, S // 4)
    cos_h, sin_h = rope_tables(positions, ROT_DIM_HEAD)
    cos_i, sin_i = rope_tables(positions, ROT_DIM_IDX)

    for l in range(DEPTH):
        lam_init = 0.8 - 0.6 * math.exp(-0.3 * l)

        h = rmsnorm(x, norm_mix_g[l])
        proj = h @ w_in[l]
        (q_a, k_a, v_a, q_i, k_i, w_i, q_b, k_b, v_b, g) = split_columns(proj)

        q_a = apply_partial_rope(q_a.reshape(B, S, DSA_HEADS, HEAD_DIM), cos_h, sin_h)
        k_a = apply_partial_rope(k_a.reshape(B, S, DSA_HEADS, HEAD_DIM), cos_h, sin_h)
        v_a = v_a.reshape(B, S, DSA_HEADS, HEAD_DIM)
        q_i = apply_partial_rope(q_i.reshape(B, S, IDX_HEADS, IDX_DIM), cos_i, sin_i)
        k_i = layernorm(k_i, idx_k_norm_g[l], idx_k_norm_b[l])
        k_i = apply_partial_rope(k_i[:, :, None, :], cos_i, sin_i)[:, :, 0, :]
        o_a = dsa_sparse_attention(q_a, k_a, v_a, q_i, k_i, w_i, n_top)

        q_b = apply_partial_rope(q_b.reshape(B, S, 2 * DIFF_HEADS, DIFF_DIM), cos_h, sin_h)
        k_b = apply_partial_rope(k_b.reshape(B, S, 2 * DIFF_HEADS, DIFF_DIM), cos_h, sin_h)
        q_b = q_b.reshape(B, S, DIFF_HEADS, 2, DIFF_DIM)
        k_b = k_b.reshape(B, S, DIFF_HEADS, 2, DIFF_DIM)
        v_b = v_b.reshape(B, S, DIFF_HEADS, DIFF_VDIM)
        lam = (jnp.exp(jnp.sum(diff_lambda_q1[l] * diff_lambda_k1[l]).astype(jnp.float32))
               - jnp.exp(jnp.sum(diff_lambda_q2[l] * diff_lambda_k2[l]).astype(jnp.float32))
               + lam_init)
        o_b = differential_attention(q_b, k_b, v_b, lam)
        o_b = rmsnorm(o_b, diff_subln_g[l], eps=DIFF_SUBLN_EPS) * (1.0 - lam_init)
        o_b = o_b.reshape(B, S, DIFF_WIDTH)

        gates = jax.nn.sigmoid(g + gate_b[l]).reshape(B, S, N_BRANCH, D_MODEL)
        merged = (gates[:, :, 0] * (o_a @ w_branch_dsa[l])
                  + gates[:, :, 1] * (o_b @ w_branch_diff[l]))
        x = x + merged @ w_out[l]

        h = rmsnorm(x, norm_ffn_g[l])
        gu = h @ w_ffn_in[l]
        x = x + (jax.nn.silu(gu[..., :D_FF]) * gu[..., D_FF:]) @ w_ffn_out[l]

    return rmsnorm(x, norm_final_g)
```

```python
import os
import math
import numpy as np
from contextlib import ExitStack
import concourse.bass as bass
import concourse.mybir as mybir
from concourse.bass_utils import run_bass_kernel_spmd

F32 = mybir.dt.float32
F32R = mybir.dt.float32r
BF16 = mybir.dt.bfloat16
I32 = mybir.dt.int32
AF = mybir.ActivationFunctionType
ALU = mybir.AluOpType
AX = mybir.AxisListType

S = 8192
D = 1024
NTB = S // 128
NQB = 16
NQ = NQB * 128
DFF = 2816
NFC = DFF // 128
TOPK = 256
NBIS = 16
NEG = -30000.0
C_QA, C_KA, C_VA, C_QI, C_KI, C_WI, C_QB, C_KB, C_VB, C_G = 0, 512, 1024, 1536, 1792, 1824, 1832, 2344, 2856, 3368
D_IN = 5416
VW = 132
TWO_PI = 2.0 * math.pi


class Eng:
    def __init__(self, nc, es, eng, name):
        self.eng = eng
        self.name = name
        self.sem = es.enter_context(nc.semaphore("sem_" + name))
        self.count = 0
        self.seen = {}

    def wait(self, toks):
        for t in toks:
            if t is None:
                continue
            if isinstance(t, list):
                self.wait(t)
                continue
            sem, val, key = t
            if self.seen.get(key, 0) >= val:
                continue
            self.eng.wait_ge(sem, val)
            self.seen[key] = val

    def op(self, fn, *args, waits=(), **kw):
        self.wait(waits)
        inst = fn(*args, **kw)
        self.count += 1
        inst.then_inc(self.sem, 1)
        tok = (self.sem, self.count, self.name)
        self.seen[self.name] = self.count - 1 if False else self.seen.get(self.name, 0)
        return tok


class DmaQ:
    def __init__(self, nc, es, eng, name, nsem=12):
        self.eng = eng
        self.name = name
        self.sems = [es.enter_context(nc.semaphore(f"dsem_{name}_{i}")) for i in range(nsem)]
        self.vals = [0] * nsem
        self.i = 0
        self.seen = {}

    def wait(self, toks):
        for t in toks:
            if t is None:
                continue
            if isinstance(t, list):
                self.wait(t)
                continue
            sem, val, key = t
            if self.seen.get(key, 0) >= val:
                continue
            self.eng.wait_ge(sem, val)
            self.seen[key] = val

    def dma(self, out, in_, waits=(), **kw):
        k = self.i
        self.i = (self.i + 1) % len(self.sems)
        key = f"{self.name}_{k}"
        if self.vals[k] > 0:
            self.wait([(self.sems[k], self.vals[k], key)])
        self.wait(waits)
        self.vals[k] += 16
        self.eng.dma_start(out=out, in_=in_, **kw).then_inc(self.sems[k], 16)
        return (self.sems[k], self.vals[k], key)

    def all_toks(self):
        return [(self.sems[k], self.vals[k], f"{self.name}_{k}") for k in range(len(self.sems)) if self.vals[k] > 0]


class Ring:
    def __init__(self, tiles):
        self.tiles = tiles
        self.rd = [[] for _ in tiles]
        self.i = -1

    def next(self):
        self.i = (self.i + 1) % len(self.tiles)
        k = self.i
        toks = self.rd[k]
        self.rd[k] = []
        return self.tiles[k], toks, k

    def done(self, k, tok):
        self.rd[k].append(tok)


def build(debug=False):
    nc = bass.Bass("TRN2", target_bir_lowering=False)
    dt_in = lambda n, s, d=F32: nc.dram_tensor(n, s, d, kind="ExternalInput").ap()
    xs = dt_in("xs", [S, D])
    xq = dt_in("xq", [NQ, D])
    posk = dt_in("posk", [128, NTB], I32)
    posq = dt_in("posq", [128, NQB], I32)
    cmask = dt_in("cmask", [128, 512])
    norm_mix_g = dt_in("norm_mix_g", [D])
    w_in = dt_in("w_in", [D, D_IN])
    idx_g = dt_in("idx_k_norm_g", [32])
    idx_b = dt_in("idx_k_norm_b", [32])
    lq1 = dt_in("diff_lambda_q1", [64])
    lk1 = dt_in("diff_lambda_k1", [64])
    lq2 = dt_in("diff_lambda_q2", [64])
    lk2 = dt_in("diff_lambda_k2", [64])
    subln_g = dt_in("diff_subln_g", [128])
    gate_b = dt_in("gate_b", [2048])
    w_bd = dt_in("w_branch_dsa", [512, D])
    w_bf = dt_in("w_branch_diff", [512, D])
    w_out = dt_in("w_out", [D, D])
    norm_ffn_g = dt_in("norm_ffn_g", [D])
    w_f1 = dt_in("w_ffn_in", [D, 2 * DFF])
    w_f2 = dt_in("w_ffn_out", [DFF, D])
    norm_fin_g = dt_in("norm_final_g", [D])
    out = nc.dram_tensor("out", [NQ, D], F32, kind="ExternalOutput").ap()
    skind = "ExternalOutput" if debug else "Internal"
    kT_scr = nc.dram_tensor("kT_scr", [8, 128, S], BF16, kind=skind).ap()
    kiT_scr = nc.dram_tensor("kiT_scr", [32, S], BF16, kind=skind).ap()
    v_scr = nc.dram_tensor("v_scr", [S, 8 * VW], BF16, kind=skind).ap()
    qT_scr = nc.dram_tensor("qT_scr", [8, 128, NQ], BF16, kind=skind).ap()
    qiT_scr = nc.dram_tensor("qiT_scr", [64, 4, NQ], BF16, kind=skind).ap()
    o_scr = nc.dram_tensor("o_scr", [NQ, D], BF16, kind=skind).ap()
    x1_scr = nc.dram_tensor("x1_scr", [NQ, D], F32, kind=skind).ap()

    with ExitStack() as es:
        uid = [0]

        def sbt(st, n, s, d):
            uid[0] += 1
            return st.enter_context(nc.sbuf_tensor(f"{n}_{uid[0]}", s, d))

        def pst(st, n, s, d):
            uid[0] += 1
            return st.enter_context(nc.psum_tensor(f"{n}_{uid[0]}", s, d))

        ident = sbt(es, "ident", [128, 128], BF16)
        early = ExitStack()
        cosk = sbt(early, "cosk", [128, NTB, 8], F32)
        sink = sbt(early, "sink", [128, NTB, 8], F32)
        cosq = sbt(early, "cosq", [128, NQB, 8], F32)
        sinq = sbt(early, "sinq", [128, NQB, 8], F32)
        gmix = sbt(early, "gmix", [128, D], F32)
        wabs = sbt(early, "wabs", [128, NQB, 8], F32)
        wsgn = sbt(early, "wsgn", [128, NQB, 8], F32)
        nlam = sbt(early, "nlam", [128, 1], F32)
        small = sbt(early, "small", [128, 64], F32)
        cb = sbt(early, "cb", [128, 512], BF16)
        cbf = sbt(early, "cbf", [128, 512], F32)

        es.enter_context(nc.Block())
        PE = Eng(nc, es, nc.tensor, "pe")
        ACT = Eng(nc, es, nc.scalar, "act")
        DVE = Eng(nc, es, nc.vector, "dve")
        POOL = Eng(nc, es, nc.gpsimd, "pool")
        SP = DmaQ(nc, es, nc.sync, "sp", 16)
        GQ = DmaQ(nc, es, nc.gpsimd, "gq", 8)
        V = nc.vector
        A = nc.scalar
        T = nc.tensor

        def bcast(ap1d, n=128):
            return ap1d.partition_broadcast(n)

        def barrier():
            toks = [(e.sem, e.count, e.name) for e in (PE, ACT, DVE, POOL) if e.count > 0]
            toks += SP.all_toks() + GQ.all_toks()
            for e in (PE, ACT, DVE, POOL, SP):
                e.wait(toks)

        t = POOL.op(nc.gpsimd.memset, ident[:], 1.0)
        t_ident = POOL.op(nc.gpsimd.affine_select, out=ident[:], in_=ident[:], pattern=[[-1, 128]],
                          compare_op=ALU.is_equal, fill=0.0, base=0, channel_multiplier=1, waits=[t])
        t_gmix = SP.dma(gmix[:], bcast(norm_mix_g))
        t_cbf = SP.dma(cbf[:], cmask)
        t_cb = DVE.op(V.tensor_copy, out=cb[:], in_=cbf[:], waits=[t_cbf])

        with ExitStack() as p0:
            lt = sbt(p0, "lt", [128, 4, 64], F32)
            lj = sbt(p0, "lj", [128, 64], F32)
            tl = [SP.dma(lt[:, i, :], bcast(a)) for i, a in enumerate([lq1, lk1, lq2, lk2])]
            t1 = DVE.op(V.tensor_tensor, out=lj[:], in0=lt[:, 0, :], in1=lt[:, 1, :], op=ALU.mult, waits=tl)
            t1 = DVE.op(V.tensor_reduce, out=small[:, 0:1], in_=lj[:], axis=AX.X, op=ALU.add, waits=[t1])
            t2 = DVE.op(V.tensor_tensor, out=lj[:], in0=lt[:, 2, :], in1=lt[:, 3, :], op=ALU.mult, waits=[t1])
            t2 = DVE.op(V.tensor_reduce, out=small[:, 1:2], in_=lj[:], axis=AX.X, op=ALU.add, waits=[t2])
            t3 = ACT.op(A.activation, out=small[:, 2:4], in_=small[:, 0:2], func=AF.Exp, waits=[t2])
            t_nlam = DVE.op(V.scalar_tensor_tensor, out=nlam[:], in0=small[:, 3:4], scalar=-0.2, in1=small[:, 2:3],
                            op0=ALU.add, op1=ALU.subtract, waits=[t3])

            invf = sbt(p0, "invf", [128, 8], F32)
            tinv = None
            for i in range(8):
                fv = float(np.power(np.float32(500000.0), -np.float32(2 * i) / np.float32(16)))
                tinv = DVE.op(V.memset, invf[:, i:i + 1], fv)

            def rope_table(pos_ap, n, cos_t, sin_t, nm):
                pi_ = sbt(p0, "pi_" + nm, [128, n], I32)
                pf = sbt(p0, "pf_" + nm, [128, n], F32)
                ang = sbt(p0, "ang_" + nm, [128, n, 8], F32)
                yy = sbt(p0, "yy_" + nm, [128, n, 8], F32)
                ni = sbt(p0, "ni_" + nm, [128, n, 8], I32)
                tp = SP.dma(pi_[:], pos_ap)
                a = DVE.op(V.tensor_copy, out=pf[:], in_=pi_[:], waits=[tp])
                a = DVE.op(V.tensor_tensor, out=ang[:], in0=pf[:].unsqueeze(2).to_broadcast([128, n, 8]),
                           in1=invf[:].unsqueeze(1).to_broadcast([128, n, 8]), op=ALU.mult, waits=[a, tinv])

                def reduce_sin(src_add, dst):
                    b = DVE.op(V.tensor_scalar, out=yy[:], in0=ang[:], scalar1=src_add, scalar2=1.0 / TWO_PI,
                               op0=ALU.add, op1=ALU.mult, waits=[a])
                    b = DVE.op(V.tensor_copy, out=ni[:], in_=yy[:], waits=[b])
                    b = DVE.op(V.tensor_copy, out=yy[:], in_=ni[:], waits=[b])
                    c1 = 6.28125
                    c2 = TWO_PI - 6.28125
                    b = DVE.op(V.scalar_tensor_tensor, out=dst, in0=yy[:], scalar=-c1, in1=ang[:], op0=ALU.mult, op1=ALU.add, waits=[b])
                    b = DVE.op(V.scalar_tensor_tensor, out=dst, in0=yy[:], scalar=-c2, in1=dst, op0=ALU.mult, op1=ALU.add, waits=[b])
                    b = DVE.op(V.tensor_scalar, out=dst, in0=dst, scalar1=src_add, scalar2=3.1415925, op0=ALU.add, op1=ALU.min, waits=[b])
                    b = DVE.op(V.tensor_scalar, out=dst, in0=dst, scalar1=-3.1415925, scalar2=None, op0=ALU.max, waits=[b])
                    return ACT.op(A.activation, out=dst, in_=dst, func=AF.Sin, waits=[b])
                ts = reduce_sin(0.0, sin_t[:])
                tc = reduce_sin(math.pi / 2.0, cos_t[:])
                return [ts, tc]
            t_ropek = rope_table(posk, NTB, cosk, sink, "k")
            t_ropeq = rope_table(posq, NQB, cosq, sinq, "q")
            barrier()

        def rms_norm_block(st_rings, x_tile, tx, g_tile, tg, eps, n_feat, hb_tile, hb_free):
            junk, ssr = st_rings
            jt, jfree, jk = junk.next()
            col, cfree, ck = ssr.next()
            a = ACT.op(A.activation, out=jt[:, :n_feat], in_=x_tile, func=AF.Square, accum_out=col[:, 0:1],
                       waits=[tx, jfree, cfree])
            junk.done(jk, a)
            b = DVE.op(V.tensor_scalar, out=col[:, 1:2], in0=col[:, 0:1], scalar1=1.0 / n_feat, scalar2=eps,
                       op0=ALU.mult, op1=ALU.add, waits=[a])
            c = ACT.op(A.activation, out=col[:, 2:3], in_=col[:, 1:2], func=AF.Sqrt, waits=[b])
            d = DVE.op(V.reciprocal, out=col[:, 3:4], in_=col[:, 2:3], waits=[c])
            e = DVE.op(V.scalar_tensor_tensor, out=hb_tile, in0=x_tile, scalar=col[:, 3:4], in1=g_tile,
                       op0=ALU.mult, op1=ALU.mult, waits=[d, tg, hb_free])
            ssr.done(ck, e)
            return e

        rope_last = []

        def rope_apply(tile3, H, half, cs, sn, tmp, waits):
            x1 = tile3[:, :, 0:half]
            x2 = tile3[:, :, half:2 * half]
            cB = cs.unsqueeze(1).to_broadcast([128, H, half])
            sB = sn.unsqueeze(1).to_broadcast([128, H, half])
            tv = lambda i: tmp[:, i, 0:H * half].rearrange("p (h d) -> p h d", h=H)
            waits = list(waits) + rope_last
            a1 = DVE.op(V.tensor_tensor, out=tv(0), in0=x1, in1=cB, op=ALU.mult, waits=waits)
            a2 = DVE.op(V.tensor_tensor, out=tv(1), in0=x2, in1=sB, op=ALU.mult, waits=waits)
            a3 = DVE.op(V.tensor_tensor, out=tv(2), in0=x2, in1=cB, op=ALU.mult, waits=waits)
            a4 = DVE.op(V.tensor_tensor, out=tv(3), in0=x1, in1=sB, op=ALU.mult, waits=waits)
            b1 = DVE.op(V.tensor_tensor, out=x1, in0=tv(0), in1=tv(1), op=ALU.subtract, waits=[a1, a2, a3, a4])
            b2 = DVE.op(V.tensor_tensor, out=x2, in0=tv(2), in1=tv(3), op=ALU.add, waits=[a1, a2, a3, a4])
            rope_last[:] = [b1, b2]
            return [b1, b2]

        with ExitStack() as p1:
            NKV = 2080
            NQC = 1288
            wkv = sbt(p1, "wkv", [128, 8, NKV], BF16)
            wq = sbt(p1, "wq", [128, 8, NQC], BF16)
            w_in_v = w_in.rearrange("(kc p) n -> p kc n", p=128)
            tw = []
            for (dst0, c0, n) in [(0, C_KA, 512), (512, C_KB, 512), (1024, C_VA, 512), (1536, C_VB, 512), (2048, C_KI, 32)]:
                tw.append(GQ.dma(wkv[:, :, dst0:dst0 + n], w_in_v[:, :, c0:c0 + n]))
            twq = []
            for (dst0, c0, n) in [(0, C_QA, 512), (512, C_QB, 512), (1024, C_QI, 256), (1280, C_WI, 8)]:
                twq.append(GQ.dma(wq[:, :, dst0:dst0 + n], w_in_v[:, :, c0:c0 + n]))
            lng = sbt(p1, "lng", [128, 32], F32)
            lnb = sbt(p1, "lnb", [128, 32], F32)
            t_lng = SP.dma(lng[:], bcast(idx_g))
            t_lnb = SP.dma(lnb[:], bcast(idx_b))

            xr = Ring([sbt(p1, f"xt{i}", [128, D], F32) for i in range(3)])
            junk = Ring([sbt(p1, f"junk{i}", [128, D], BF16) for i in range(1)])
            ssr = Ring([sbt(p1, f"ss{i}", [128, 4], F32) for i in range(3)])
            hbr = Ring([sbt(p1, f"hb{i}", [128, D], BF16) for i in range(2)])
            hTr = Ring([sbt(p1, f"hT{i}", [128, 8, 128], BF16) for i in range(3)])
            kfr = Ring([sbt(p1, f"kf{i}", [128, 1024], F32) for i in range(2)])
            kbr = Ring([sbt(p1, f"kb{i}", [128, 1024], BF16) for i in range(3)])
            kTr = Ring([sbt(p1, f"kTt{i}", [128, 8, 128], BF16) for i in range(2)])
            vtr = Ring([sbt(p1, f"vt{i}", [128, 8, VW], BF16) for i in range(2)])
            rtmp = sbt(p1, "rtmp", [128, 4, 128], F32)
            kif = sbt(p1, "kif", [128, 8], F32)
            kic = sbt(p1, "kic", [128, 32], F32)
            kij = sbt(p1, "kij", [128, 32], F32)
            kibr = Ring([sbt(p1, f"kib{i}", [128, 32], BF16) for i in range(3)])
            kiTr = Ring([sbt(p1, f"kiT{i}", [32, 128], BF16) for i in range(2)])
            qwr = Ring([sbt(p1, f"qw{i}", [128, 264], F32) for i in range(2)])
            qibr = Ring([sbt(p1, f"qib{i}", [128, 256], BF16) for i in range(3)])
            qiTr = Ring([sbt(p1, f"qiT{i}", [32, 8, 128], BF16) for i in range(2)])
            psT = Ring([pst(p1, f"psT{i}", [128, D], BF16) for i in range(2)])
            pp = Ring([pst(p1, f"pp{i}", [128, 512], F32) for i in range(4)])
            pkT = Ring([pst(p1, f"pkT{i}", [128, D], BF16) for i in range(2)])
            t_vinit = []
            for vt_ in vtr.tiles:
                t0 = POOL.op(nc.gpsimd.memset, vt_[:], 0.0)
                t1_ = POOL.op(nc.gpsimd.memset, vt_[:, 0:4, :].rearrange("p a (s e) -> p (a s) e", s=2)[:, :, 64:65], 1.0, waits=[t0])
                t_vinit.append(POOL.op(nc.gpsimd.memset, vt_[:, 4:8, 128:129], 1.0, waits=[t0, t1_]))

            def aevac(out_ap, in_ap, waits):
                return ACT.op(A.copy, out=out_ap, in_=in_ap, waits=waits)

            def stageA(src_rows):
                xt, xfree, xk = xr.next()
                tx = SP.dma(xt[:], src_rows, waits=xfree)
                hb, hfree, hk = hbr.next()
                th = rms_norm_block((junk, ssr), xt[:], tx, gmix[:], t_gmix, 1e-6, D, hb[:], hfree)
                xr.done(xk, th)
                ps, pfree, pk = psT.next()
                tt = None
                for kc in range(8):
                    tt = PE.op(T.transpose, ps[:, kc * 128:(kc + 1) * 128], hb[:, kc * 128:(kc + 1) * 128], ident[:],
                               waits=[th, t_ident, pfree] if kc == 0 else ())
                hbr.done(hk, tt)
                hT, tfree, tk = hTr.next()
                te = aevac(hT[:].rearrange("p a b -> p (a b)"), ps[:], [tt, tfree])
                psT.done(pk, te)
                return hT, te, tk

            def project(hT, th, w_tile, c0, n, wtoks):
                ps, pfree, pk = pp.next()
                tt = None
                for kc in range(8):
                    tt = PE.op(T.matmul, ps[:, 0:n], lhsT=hT[:, kc, :], rhs=w_tile[:, kc, c0:c0 + n],
                               start=(kc == 0), stop=(kc == 7), waits=[th, pfree, wtoks] if kc == 0 else ())
                return ps, tt, pk

            def transpose_out(src_bf, tsrc, nchunk, width, ring_sb, dst_dram):
                ps, pfree, pk = pkT.next()
                tt = None
                for c in range(nchunk):
                    tt = PE.op(T.transpose, ps[0:width, c * 128:(c + 1) * 128], src_bf[:, c * width:(c + 1) * width], ident[:],
                               waits=[tsrc, pfree, t_ident] if c == 0 else ())
                sbT, sfree, sk = ring_sb.next()
                te = aevac(sbT[:].rearrange("p a b -> p (a b)") if len(sbT.shape) == 3 else sbT[:],
                           ps[0:width, 0:nchunk * 128], [tt, sfree])
                pkT.done(pk, te)
                td = GQ.dma(dst_dram, sbT[:], waits=[te])
                ring_sb.done(sk, td)
                return tt

            def transpose_out_qi(src_bf, tsrc, s):
                ps, pfree, pk = pkT.next()
                tt = None
                for c in range(8):
                    tt = PE.op(T.transpose, ps[0:32, c * 128:(c + 1) * 128], src_bf[:, c * 32:(c + 1) * 32], ident[:],
                               waits=[tsrc, pfree, t_ident] if c == 0 else ())
                sbT, sfree, sk = qiTr.next()
                te = aevac(sbT[:].rearrange("p a b -> p (a b)"), ps[0:32, 0:1024], [tt, sfree])
                pkT.done(pk, te)
                for g in range(2):
                    td = GQ.dma(qiT_scr[g * 32:(g + 1) * 32, :, s * 128:(s + 1) * 128], sbT[:, g::2, :], waits=[te])
                    qiTr.done(sk, td)
                return tt

            def qk_pair(hT, th, w_tile, wtoks, cs, sn, scale, dst_dram):
                kf, kfree, kk = kfr.next()
                tes = []
                for gi in range(2):
                    ps, tmm, pk = project(hT, th, w_tile, gi * 512, 512, wtoks)
                    te = aevac(kf[:, gi * 512:(gi + 1) * 512], ps[:], [tmm, kfree])
                    pp.done(pk, te)
                    tes.append(te)
                tr = rope_apply(kf[:].rearrange("p (h d) -> p h d", h=16), 16, 8, cs, sn, rtmp, tes)
                kb, bfree, bk = kbr.next()
                if scale == 1.0:
                    tcst = DVE.op(V.tensor_copy, out=kb[:], in_=kf[:], waits=[tr, bfree])
                else:
                    tcst = DVE.op(V.tensor_scalar, out=kb[:], in0=kf[:], scalar1=scale, scalar2=None, op0=ALU.mult, waits=[tr, bfree])
                kfr.done(kk, tcst)
                return kb, bk, tcst

            def stageB_kv(tb, hT, th, hk):
                cs = cosk[:, tb, :]
                sn = sink[:, tb, :]
                kb, bk, tcst = qk_pair(hT, th, wkv, tw, cs, sn, 1.0, None)
                vt, vfree, vk = vtr.next()
                tvs = []
                for gi in (2, 3):
                    ps, tmm, pk = project(hT, th, wkv, gi * 512, 512, tw)
                    if gi == 2:
                        dstv = vt[:, 0:4, :].rearrange("p a (s e) -> p (a s) e", s=2)[:, :, 0:64]
                        te = aevac(dstv, ps[:].rearrange("p (h e) -> p h e", e=64), [tmm, vfree, t_vinit])
                    else:
                        te = aevac(vt[:, 4:8, 0:128], ps[:].rearrange("p (h e) -> p h e", e=128), [tmm, vfree, t_vinit])
                    pp.done(pk, te)
                    tvs.append(te)
                td = GQ.dma(v_scr[tb * 128:(tb + 1) * 128, :], vt[:].rearrange("p a b -> p (a b)"), waits=tvs)
                vtr.done(vk, td)
                ps, tmm, pk = project(hT, th, wkv, 2048, 32, tw)
                hTr.done(hk, tmm)
                a = DVE.op(V.tensor_reduce, out=kif[:, 0:1], in_=ps[:, 0:32], axis=AX.X, op=ALU.add, waits=[tmm])
                a = DVE.op(V.tensor_scalar, out=kif[:, 1:2], in0=kif[:, 0:1], scalar1=1.0 / 32, scalar2=None, op0=ALU.mult, waits=[a])
                a = DVE.op(V.tensor_scalar, out=kic[:], in0=ps[:, 0:32], scalar1=kif[:, 1:2], scalar2=None, op0=ALU.subtract, waits=[a])
                pp.done(pk, a)
                b = ACT.op(A.activation, out=kij[:], in_=kic[:], func=AF.Square, accum_out=kif[:, 2:3], waits=[a])
                b = DVE.op(V.tensor_scalar, out=kif[:, 3:4], in0=kif[:, 2:3], scalar1=1.0 / 32, scalar2=1e-6, op0=ALU.mult, op1=ALU.add, waits=[b])
                b = ACT.op(A.activation, out=kif[:, 4:5], in_=kif[:, 3:4], func=AF.Sqrt, waits=[b])
                b = DVE.op(V.reciprocal, out=kif[:, 5:6], in_=kif[:, 4:5], waits=[b])
                b = DVE.op(V.scalar_tensor_tensor, out=kic[:], in0=kic[:], scalar=kif[:, 5:6], in1=lng[:], op0=ALU.mult, op1=ALU.mult, waits=[b, t_lng])
                b = DVE.op(V.tensor_tensor, out=kic[:], in0=kic[:], in1=lnb[:], op=ALU.add, waits=[b, t_lnb])
                csi = cosk[:, tb, :].rearrange("p (a two) -> p a two", two=2)[:, :, 0]
                sni = sink[:, tb, :].rearrange("p (a two) -> p a two", two=2)[:, :, 0]
                tr = rope_apply(kic[:].rearrange("p (h d) -> p h d", h=1), 1, 4, csi, sni, rtmp, [b])
                kib, bfree, bk2 = kibr.next()
                tc2 = DVE.op(V.tensor_copy, out=kib[:], in_=kic[:], waits=[tr, bfree])

                def b2():
                    tlast = transpose_out(kb, tcst, 8, 128, kTr, kT_scr[:, :, tb * 128:(tb + 1) * 128].rearrange("c f t -> f c t"))
                    kbr.done(bk, tlast)
                    tl2 = transpose_out(kib, tc2, 1, 32, kiTr, kiT_scr[:, tb * 128:(tb + 1) * 128])
                    kibr.done(bk2, tl2)
                return b2

            def stageB_q(s, hT, th, hk):
                cs = cosq[:, s, :]
                sn = sinq[:, s, :]
                kb, bk, tcst = qk_pair(hT, th, wq, twq, cs, sn, 0.125, None)
                ps, tmm, pk = project(hT, th, wq, 1024, 264, twq)
                hTr.done(hk, tmm)
                qw, qfree, qk = qwr.next()
                te = aevac(qw[:], ps[:, 0:264], [tmm, qfree])
                pp.done(pk, te)
                csi = cosq[:, s, :].rearrange("p (a two) -> p a two", two=2)[:, :, 0]
                sni = sinq[:, s, :].rearrange("p (a two) -> p a two", two=2)[:, :, 0]
                tr = rope_apply(qw[:, 0:256].rearrange("p (h d) -> p h d", h=8), 8, 4, csi, sni, rtmp, [te])
                a1 = DVE.op(V.tensor_scalar, out=wabs[:, s, :], in0=qw[:, 256:264], scalar1=1.0 / 16, scalar2=None, op0=ALU.mult, waits=[te])
                a2 = a1
                qib, bfree, bk2 = qibr.next()
                tc2 = DVE.op(V.tensor_copy, out=qib[:], in_=qw[:, 0:256], waits=[tr, bfree])
                qwr.done(qk, [tc2, a1, a2])

                def b2():
                    tlast = transpose_out(kb, tcst, 8, 128, kTr, qT_scr[:, :, s * 128:(s + 1) * 128].rearrange("c f t -> f c t"))
                    kbr.done(bk, tlast)
                    tl2 = transpose_out_qi(qib, tc2, s)
                    qibr.done(bk2, tl2)
                return b2

            items = [("kv", tb, xs[tb * 128:(tb + 1) * 128, :]) for tb in range(NTB)] + \
                    [("q", s, xq[s * 128:(s + 1) * 128, :]) for s in range(NQB)]
            prev = None
            prev_b2 = None
            for it in items + [None]:
                cur = None
                if it is not None:
                    cur = (it, stageA(it[2]))
                b2 = None
                if prev is not None:
                    (kind, idx, _), (hT, th, hk) = prev
                    if kind == "kv":
                        b2 = stageB_kv(idx, hT, th, hk)
                    else:
                        b2 = stageB_q(idx, hT, th, hk)
                if prev_b2 is not None:
                    prev_b2()
                prev_b2 = b2
                prev = cur
            if prev_b2 is not None:
                prev_b2()
            barrier()

        with ExitStack() as p2:
            kiT = sbt(p2, "kiT", [64, S], BF16)
            t_kiT = [SP.dma(kiT[g * 32:(g + 1) * 32, :], kiT_scr) for g in range(2)]
            gsub = sbt(p2, "gsub", [128, 128], F32)
            t_gs = SP.dma(gsub[:], bcast(subln_g))
            t_gs = DVE.op(V.tensor_scalar, out=gsub[:], in0=gsub[:], scalar1=0.8, scalar2=None, op0=ALU.mult, waits=[t_gs])
            ident2 = sbt(p2, "ident2", [128, 2, 128], BF16)
            t_id2 = [DVE.op(V.tensor_copy, out=ident2[:, i, :], in_=ident[:], waits=[t_ident]) for i in range(2)]
            Kr = Ring([sbt(p2, f"Kb{i}", [128, S], BF16) for i in range(2)])
            Vr = Ring([sbt(p2, f"Vb{i}", [128, NTB, VW], BF16) for i in range(2)])
            Mb = [sbt(p2, f"Mb{i}", [128, S], BF16) for i in range(2)]
            Isc = sbt(p2, "Isc", [128, S], F32)
            qbd_tiles = [sbt(p2, f"qbd{i}", [128, 2, 128], BF16) for i in range(3)]
            t_qz = [POOL.op(nc.gpsimd.memset, q_[:], 0.0) for q_ in qbd_tiles]
            qbr = Ring(qbd_tiles)
            qiT = [sbt(p2, f"qiTs{i}", [64, 4, 128], BF16) for i in range(2)]
            rl = Ring([sbt(p2, f"rl{i}", [128, 512], F32) for i in range(6)])
            accP = Ring([sbt(p2, f"accP{i}", [128, 512], F32) for i in range(2)])
            ptmp = Ring([sbt(p2, f"ptmp{i}", [128, 512], F32) for i in range(2)])
            etr = Ring([sbt(p2, f"et{i}", [128, 512], BF16) for i in range(3)])
            otr = Ring([sbt(p2, f"ot{i}", [128, D], BF16) for i in range(2)])
            of32 = sbt(p2, "of32", [128, 128], F32)
            oj = sbt(p2, "oj", [128, 128], F32)
            bs = sbt(p2, "bs", [128, 16], F32)
            es_ = sbt(p2, "es_", [128, 16], F32)
            pss = Ring([pst(p2, f"pss{i}", [128, 512], F32) for i in range(4)])
            pacc = Ring([pst(p2, f"pacc{i}", [128, 512], F32) for i in range(4)])
            Mb_ready = [None, None]
            Mb_readers = [[], []]
            qiT_readers = [[], []]
            dg_readers = [[], []]
            Isc_free = [[]]
            pI_free = [[]]

            def prep(s):
                li = s % 2
                nk = (4 * s + 4) * 128
                nch = nk // 512
                wb = 0.76 * (s + 1)
                tq = SP.dma(qiT[li][:], qiT_scr[:, :, s * 128:(s + 1) * 128], waits=qiT_readers[li])
                qiT_readers[li] = []
                tI = None
                tIprev = [None]
                for c in range(nch):
                    dst = Isc[:, c * 512:(c + 1) * 512]
                    for r in range(4):
                        mm = []
                        for g in range(2):
                            ps, pfree, pk = pss.next()
                            tm = PE.op(T.matmul, ps[:], lhsT=qiT[li][g * 32:(g + 1) * 32, r, :], rhs=kiT[g * 32:(g + 1) * 32, c * 512:(c + 1) * 512],
                                       start=True, stop=True, waits=[tq, t_kiT, pfree])
                            mm.append((ps, pk, tm))
                        for g in range(2):
                            h = 2 * r + g
                            ps, pk, tm = mm[g]
                            rt, rfree, rk = rl.next()
                            ta = ACT.op(A.activation, out=rt[:], in_=ps[:], func=AF.Relu, waits=[tm, rfree])
                            pss.done(pk, ta)
                            if h == 0:
                                tI = DVE.op(V.tensor_scalar, out=dst, in0=rt[:], scalar1=wabs[:, s, 0:1], scalar2=None, op0=ALU.mult,
                                            waits=[ta, Isc_free[0]])
                                rl.done(rk, tI)
                            elif h < 4:
                                tI = DVE.op(V.scalar_tensor_tensor, out=dst, in0=rt[:], scalar=wabs[:, s, h:h + 1], in1=dst,
                                            op0=ALU.mult, op1=ALU.add, waits=[ta, tI])
                                rl.done(rk, tI)
                            else:
                                wB = wabs[:, s, h:h + 1].to_broadcast([128, 512])
                                if h == 4:
                                    ap_, apfree, apk = accP.next()
                                    tP = POOL.op(nc.gpsimd.tensor_tensor, out=ap_[:], in0=rt[:], in1=wB, op=ALU.mult, waits=[ta, apfree])
                                    rl.done(rk, tP)
                                else:
                                    pt_, ptfree, ptk = ptmp.next()
                                    t1_ = POOL.op(nc.gpsimd.tensor_tensor, out=pt_[:], in0=rt[:], in1=wB, op=ALU.mult, waits=[ta, ptfree])
                                    rl.done(rk, t1_)
                                    tP = POOL.op(nc.gpsimd.tensor_tensor, out=ap_[:], in0=ap_[:], in1=pt_[:], op=ALU.add, waits=[t1_, tP])
                                    ptmp.done(ptk, tP)
                                if h == 7:
                                    tI = DVE.op(V.tensor_tensor, out=dst, in0=dst, in1=ap_[:], op=ALU.add, waits=[tI, tP])
                                    accP.done(apk, tI)
                        if c == nch - 1 and r == 3:
                            qiT_readers[li].append((PE.sem, PE.count, PE.name))
                        yield 2.0
                Isc_free[0] = []
                Iv = Isc[:, 0:nk]
                a = DVE.op(V.tensor_reduce, out=bs[:, 0:1], in_=Iv, axis=AX.X, op=ALU.max, waits=[tI])
                yield wb
                a = DVE.op(V.tensor_reduce, out=bs[:, 1:2], in_=Iv, axis=AX.X, op=ALU.min, waits=[a])
                a = DVE.op(V.scalar_tensor_tensor, out=bs[:, 2:3], in0=bs[:, 0:1], scalar=1.0, in1=bs[:, 1:2], op0=ALU.add, op1=ALU.subtract, waits=[a])
                a = DVE.op(V.tensor_tensor, out=Isc[:, nk - 512:nk], in0=Isc[:, nk - 512:nk], in1=cbf[:], op=ALU.add, waits=[a, t_cbf])
                yield wb
                lo = bs[:, 1:2]
                w0 = bs[:, 2:3]
                mid = bs[:, 3:4]
                cnt = bs[:, 4:5]
                gg = bs[:, 5:6]
                for it in range(1, NBIS + 1):
                    sc = 2.0 ** (-it)
                    a = DVE.op(V.tensor_scalar, out=mid, in0=w0, scalar1=sc, scalar2=lo, op0=ALU.mult, op1=ALU.add, waits=[a])
                    a = DVE.op(V.tensor_scalar, out=Mb[li][:, 0:nk], in0=Iv, scalar1=mid, scalar2=None, op0=ALU.is_ge, op1=ALU.add,
                               accum_out=cnt, waits=[a, Mb_readers[li]])
                    Mb_readers[li] = []
                    a = DVE.op(V.tensor_scalar, out=gg, in0=cnt, scalar1=TOPK - 0.5, scalar2=sc, op0=ALU.is_ge, op1=ALU.mult, waits=[a])
                    a = DVE.op(V.scalar_tensor_tensor, out=lo, in0=w0, scalar=gg, in1=lo, op0=ALU.mult, op1=ALU.add, waits=[a])
                    yield wb
                a = DVE.op(V.tensor_scalar, out=Mb[li][:, 0:nk], in0=Iv, scalar1=lo, scalar2=NEG, op0=ALU.is_lt, op1=ALU.mult, waits=[a])
                Mb_ready[li] = a
                Isc_free[0] = [a]

            def prep_steps(s):
                return 1 + 8 * (s + 1) + (2 + NBIS) * 0.76 * (s + 1)

            pump_state = {"gen": None, "budget": 0.0, "rate": 0.0}

            def pump():
                st = pump_state
                if st["gen"] is None:
                    return
                st["budget"] += st["rate"]
                while st["budget"] > 0.0 and st["gen"] is not None:
                    try:
                        st["budget"] -= next(st["gen"])
                    except StopIteration:
                        st["gen"] = None

            def attention(s, p, ot, ofree, owr):
                li = s % 2
                is_dsa = p < 4
                nkb = 4 * s + 4
                kmax = nkb * 128
                Kb, kfree, kk = Kr.next()
                Vb, vfree, vk = Vr.next()
                tK = SP.dma(Kb[:, 0:kmax], kT_scr[p, :, 0:kmax], waits=kfree)
                tV = []
                for b0 in range(0, nkb, 8):
                    nb = min(8, nkb - b0)
                    tV.append(SP.dma(Vb[:, b0:b0 + nb, :], v_scr[b0 * 128:(b0 + nb) * 128, p * VW:(p + 1) * VW].rearrange("(b t) w -> t b w", t=128), waits=vfree))
                qbd, qfree, qk = qbr.next()
                tq = [SP.dma(qbd[m * 64:(m + 1) * 64, m, :], qT_scr[p, m * 64:(m + 1) * 64, s * 128:(s + 1) * 128], waits=[qfree, t_qz]) for m in range(2)]
                accs = [pacc.next() for m in range(2)]
                vw = 66 if is_dsa else 130
                ntile = nkb // 2
                pend = {}

                def do_qk(ti):
                    ps, pfree, pk = pss.next()
                    tm = None
                    for bi in range(2):
                        kb_ = ti * 2 + bi
                        need_mask = is_dsa or (kb_ >= nkb - 4)
                        tm = PE.op(T.matmul, ps[:, bi * 256:(bi + 1) * 256], lhsT=Kb[:, kb_ * 128:(kb_ + 1) * 128],
                                   rhs=qbd[:].rearrange("p a b -> p (a b)"), start=True, stop=not need_mask,
                                   waits=[tK, tq, pfree] if bi == 0 else ())
                        if need_mask:
                            if is_dsa:
                                ml = Mb[li][:, kb_ * 128:(kb_ + 1) * 128]
                                mw = [Mb_ready[li]]
                            else:
                                cbi = kb_ - (nkb - 4)
                                ml = cb[:, cbi * 128:(cbi + 1) * 128]
                                mw = [t_cb]
                            tm = PE.op(T.matmul, ps[:, bi * 256:(bi + 1) * 256], lhsT=ml, rhs=ident2[:].rearrange("p a b -> p (a b)"),
                                       start=False, stop=True, waits=mw + [t_id2])
                    et, efree, ek = etr.next()
                    te = ACT.op(A.activation, out=et[:], in_=ps[:], func=AF.Exp, waits=[tm, efree])
                    pss.done(pk, te)
                    pend[ti] = (et, te, ek)

                tav = [None, None]

                def do_av(ti):
                    et, te, ek = pend.pop(ti)
                    tm = None
                    first = True
                    for bi in range(2):
                        kb_ = ti * 2 + bi
                        for m in range(2):
                            acc, afree, ak = accs[m]
                            rhs = Vb[:, kb_, m * 66:(m + 1) * 66] if is_dsa else Vb[:, kb_, 0:130]
                            tm = PE.op(T.matmul, acc[:, 0:vw], lhsT=et[:, bi * 256 + m * 128:bi * 256 + (m + 1) * 128], rhs=rhs,
                                       start=(kb_ == 0), stop=(kb_ == nkb - 1),
                                       waits=[te, tV, accs[0][1], accs[1][1]] if first else ())
                            first = False
                            tav[m] = tm
                    etr.done(ek, tm)
                for ti in range(ntile):
                    do_qk(ti)
                    if ti >= 1:
                        do_av(ti - 1)
                    pump()
                do_av(ntile - 1)
                qbr.done(qk, tav[1])
                if is_dsa:
                    Mb_readers[li].append(tav[1])
                Kr.done(kk, tav[1])
                Vr.done(vk, tav[1])
                if is_dsa:
                    for m in range(2):
                        acc, afree, ak = accs[m]
                        a = DVE.op(V.reciprocal, out=es_[:, m:m + 1], in_=acc[:, 64:65], waits=[tav[1]])
                        a = DVE.op(V.tensor_scalar, out=ot[:, p * 128 + m * 64:p * 128 + (m + 1) * 64], in0=acc[:, 0:64],
                                   scalar1=es_[:, m:m + 1], scalar2=None, op0=ALU.mult, waits=[a, ofree])
                        pacc.done(ak, a)
                        owr.append(a)
                else:
                    h = p - 4
                    acc1, _, ak1 = accs[0]
                    acc2, _, ak2 = accs[1]
                    a = DVE.op(V.reciprocal, out=es_[:, 2:3], in_=acc1[:, 128:129], waits=[tav[1]])
                    b = DVE.op(V.reciprocal, out=es_[:, 3:4], in_=acc2[:, 128:129], waits=[tav[1]])
                    b = DVE.op(V.tensor_tensor, out=es_[:, 4:5], in0=es_[:, 3:4], in1=nlam[:], op=ALU.mult, waits=[b, t_nlam])
                    a = DVE.op(V.tensor_scalar, out=of32[:], in0=acc1[:, 0:128], scalar1=es_[:, 2:3], scalar2=None, op0=ALU.mult, waits=[a])
                    pacc.done(ak1, a)
                    b = DVE.op(V.scalar_tensor_tensor, out=of32[:], in0=acc2[:, 0:128], scalar=es_[:, 4:5], in1=of32[:],
                               op0=ALU.mult, op1=ALU.add, waits=[a, b])
                    pacc.done(ak2, b)
                    c = ACT.op(A.activation, out=oj[:], in_=of32[:], func=AF.Square, accum_out=es_[:, 5:6], waits=[b])
                    c = DVE.op(V.tensor_scalar, out=es_[:, 6:7], in0=es_[:, 5:6], scalar1=1.0 / 128, scalar2=1e-5, op0=ALU.mult, op1=ALU.add, waits=[c])
                    c = ACT.op(A.activation, out=es_[:, 7:8], in_=es_[:, 6:7], func=AF.Ln, waits=[c])
                    c = ACT.op(A.activation, out=es_[:, 8:9], in_=es_[:, 7:8], func=AF.Exp, scale=-0.5, waits=[c])
                    c = DVE.op(V.scalar_tensor_tensor, out=ot[:, 512 + h * 128:512 + (h + 1) * 128], in0=of32[:], scalar=es_[:, 8:9],
                               in1=gsub[:], op0=ALU.mult, op1=ALU.mult, waits=[c, t_gs, ofree])
                    owr.append(c)

            for _ in prep(0):
                pass
            for s in range(NQB):
                if s + 1 < NQB:
                    pump_state["gen"] = prep(s + 1)
                    pump_state["budget"] = 0.0
                    pump_state["rate"] = prep_steps(s + 1) / float(8 * (2 * s + 2)) * 1.15
                else:
                    pump_state["gen"] = None
                ot, ofree, ok_ = otr.next()
                owr = []
                for pi, p in enumerate([4, 5, 6, 7, 0, 1, 2, 3]):
                    attention(s, p, ot, ofree, owr)
                if pump_state["gen"] is not None:
                    for _ in pump_state["gen"]:
                        pass
                    pump_state["gen"] = None
                td = GQ.dma(o_scr[s * 128:(s + 1) * 128, :], ot[:], waits=owr)
                otr.done(ok_, td)
            barrier()

        def load_w(st, name, src, rows, cols, q, waits=()):
            nkc = rows // 128
            wt = sbt(st, name, [128, nkc, cols], BF16)
            srcv = src.rearrange("(kc p) n -> p kc n", p=128)
            toks = []
            step = 1024
            for c0 in range(0, cols, step):
                n = min(step, cols - c0)
                toks.append(q.dma(wt[:, :, c0:c0 + n], srcv[:, :, c0:c0 + n], waits=waits))
            return wt, toks

        with ExitStack() as p3:
            wg = sbt(p3, "wg", [128, 8, 2048], BF16)
            w_in_v = w_in.rearrange("(kc p) n -> p kc n", p=128)
            twg = [GQ.dma(wg[:, :, c0:c0 + 512], w_in_v[:, :, C_G + c0:C_G + c0 + 512]) for c0 in range(0, 2048, 512)]
            wbd, twbd = load_w(p3, "wbd", w_bd, 512, D, GQ)
            wbf, twbf = load_w(p3, "wbf", w_bf, 512, D, GQ)
            wo, two = load_w(p3, "wo", w_out, D, D, GQ)
            gbt = sbt(p3, "gbt", [128, 2048], F32)
            t_gb = SP.dma(gbt[:], bcast(gate_b))
            xr = Ring([sbt(p3, f"xt{i}", [128, D], F32) for i in range(2)])
            junk = Ring([sbt(p3, f"junk{i}", [128, D], BF16) for i in range(1)])
            ssr = Ring([sbt(p3, f"ss{i}", [128, 4], F32) for i in range(2)])
            hbr = Ring([sbt(p3, f"hb{i}", [128, D], BF16) for i in range(2)])
            hTr = Ring([sbt(p3, f"hT{i}", [128, 8, 128], BF16) for i in range(2)])
            obr = Ring([sbt(p3, f"ob{i}", [128, D], BF16) for i in range(2)])
            oTr = Ring([sbt(p3, f"oT{i}", [128, 8, 128], BF16) for i in range(2)])
            gat = sbt(p3, "gat", [128, 2048], F32)
            mrg = sbt(p3, "mrg", [128, D], F32)
            mrg2 = sbt(p3, "mrg2", [128, D], F32)
            mbr = Ring([sbt(p3, f"mb{i}", [128, D], BF16) for i in range(2)])
            mTr = Ring([sbt(p3, f"mT{i}", [128, 8, 128], BF16) for i in range(2)])
            x1r = Ring([sbt(p3, f"x1t{i}", [128, D], F32) for i in range(2)])
            psT = Ring([pst(p3, f"psT{i}", [128, D], BF16) for i in range(2)])
            pp = Ring([pst(p3, f"pp{i}", [128, 512], F32) for i in range(6)])
            gat_free = []
            mrg_free = []
            x1_dmas = []

            def transpose8(src_bf, tsrc, dst_ring):
                ps, pfree, pk = psT.next()
                tt = None
                for kc in range(8):
                    tt = PE.op(T.transpose, ps[:, kc * 128:(kc + 1) * 128], src_bf[:, kc * 128:(kc + 1) * 128], ident[:],
                               waits=[tsrc, pfree] if kc == 0 else ())
                dT, dfree, dk = dst_ring.next()
                te = ACT.op(A.copy, out=dT[:].rearrange("p a b -> p (a b)"), in_=ps[:], waits=[tt, dfree])
                psT.done(pk, te)
                return dT, te, dk, tt

            for s in range(NQB):
                xt, xfree, xk = xr.next()
                tx = SP.dma(xt[:], xq[s * 128:(s + 1) * 128, :], waits=xfree)
                hb, hfree, hk = hbr.next()
                th = rms_norm_block((junk, ssr), xt[:], tx, gmix[:], t_gmix, 1e-6, D, hb[:], hfree)
                hT, te, tk, tt = transpose8(hb, th, hTr)
                hbr.done(hk, tt)
                tg_last = None
                for gc in range(4):
                    ps, pfree, pk = pp.next()
                    tm = None
                    for kc in range(8):
                        tm = PE.op(T.matmul, ps[:], lhsT=hT[:, kc, :], rhs=wg[:, kc, gc * 512:(gc + 1) * 512], start=(kc == 0), stop=(kc == 7),
                                   waits=[te, pfree, twg] if kc == 0 else ())
                    a = DVE.op(V.tensor_tensor, out=gat[:, gc * 512:(gc + 1) * 512], in0=ps[:], in1=gbt[:, gc * 512:(gc + 1) * 512], op=ALU.add,
                               waits=[tm, t_gb, gat_free])
                    pp.done(pk, a)
                    tg_last = ACT.op(A.activation, out=gat[:, gc * 512:(gc + 1) * 512], in_=gat[:, gc * 512:(gc + 1) * 512], func=AF.Sigmoid, waits=[a])
                    if gc == 3:
                        hTr.done(tk, tm)
                gat_free = []
                ob, ofree, ok_ = obr.next()
                to = SP.dma(ob[:], o_scr[s * 128:(s + 1) * 128, :], waits=[ofree])
                oT, teo, ok2, tto = transpose8(ob, to, oTr)
                obr.done(ok_, tto)
                mtoks = []
                for br, (wt, twt) in enumerate([(wbd, twbd), (wbf, twbf)]):
                    for nc_ in range(2):
                        ps, pfree, pk = pp.next()
                        tm = None
                        for kc in range(4):
                            tm = PE.op(T.matmul, ps[:], lhsT=oT[:, br * 4 + kc, :], rhs=wt[:, kc, nc_ * 512:(nc_ + 1) * 512], start=(kc == 0), stop=(kc == 3),
                                       waits=[teo, pfree, twt] if kc == 0 else ())
                        dst = (mrg if br == 0 else mrg2)[:, nc_ * 512:(nc_ + 1) * 512]
                        a = DVE.op(V.tensor_tensor, out=dst, in0=ps[:], in1=gat[:, br * 1024 + nc_ * 512:br * 1024 + (nc_ + 1) * 512], op=ALU.mult,
                                   waits=[tm, tg_last, mrg_free])
                        pp.done(pk, a)
                        mtoks.append(a)
                        if br == 1 and nc_ == 1:
                            oTr.done(ok2, tm)
                gat_free = list(mtoks)
                mb, mfree, mk = mbr.next()
                tmb = DVE.op(V.tensor_tensor, out=mb[:], in0=mrg[:], in1=mrg2[:], op=ALU.add, waits=[mtoks, mfree])
                mrg_free = [tmb]
                mT, tem, mk2, ttm = transpose8(mb, tmb, mTr)
                mbr.done(mk, ttm)
                x1t, x1free, x1k = x1r.next()
                xtoks = []
                for nc_ in range(2):
                    ps, pfree, pk = pp.next()
                    tm = None
                    for kc in range(8):
                        tm = PE.op(T.matmul, ps[:], lhsT=mT[:, kc, :], rhs=wo[:, kc, nc_ * 512:(nc_ + 1) * 512], start=(kc == 0), stop=(kc == 7),
                                   waits=[tem, pfree, two] if kc == 0 else ())
                    a = DVE.op(V.tensor_tensor, out=x1t[:, nc_ * 512:(nc_ + 1) * 512], in0=ps[:], in1=xt[:, nc_ * 512:(nc_ + 1) * 512], op=ALU.add,
                               waits=[tm, x1free])
                    pp.done(pk, a)
                    xtoks.append(a)
                    if nc_ == 1:
                        mTr.done(mk2, tm)
                xr.done(xk, xtoks)
                td = SP.dma(x1_scr[s * 128:(s + 1) * 128, :], x1t[:], waits=xtoks)
                x1r.done(x1k, td)
                x1_dmas.append(td)
            barrier()
        early.close()

        with ExitStack() as p4:
            w1, tw1 = load_w(p4, "w1", w_f1, D, 2 * DFF, GQ)
            w2, tw2 = load_w(p4, "w2", w_f2, DFF, D, GQ)
            gffn = sbt(p4, "gffn", [128, D], F32)
            gfin = sbt(p4, "gfin", [128, D], F32)
            t_gffn = SP.dma(gffn[:], bcast(norm_ffn_g))
            t_gfin = SP.dma(gfin[:], bcast(norm_fin_g))
            x1r = Ring([sbt(p4, f"x1b{i}", [128, D], F32) for i in range(2)])
            junk = Ring([sbt(p4, f"junk{i}", [128, D], BF16) for i in range(1)])
            ssr = Ring([sbt(p4, f"ss{i}", [128, 4], F32) for i in range(2)])
            hbr = Ring([sbt(p4, f"hb{i}", [128, D], BF16) for i in range(2)])
            h2T = sbt(p4, "h2T", [128, 8, 512], BF16)
            actT = sbt(p4, "actT", [128, NFC, 512], BF16)
            sgr = Ring([sbt(p4, f"sg{i}", [128, 512], F32) for i in range(2)])
            x2r = Ring([sbt(p4, f"x2t{i}", [128, D], F32) for i in range(2)])
            psT = Ring([pst(p4, f"psT{i}", [128, D], BF16) for i in range(2)])
            pp = Ring([pst(p4, f"pp{i}", [128, 512], F32) for i in range(6)])
            h2T_free = []
            actT_free = []
            for grp in range(NQB // 4):
                th2 = []
                for bi in range(4):
                    s = grp * 4 + bi
                    x1t, x1free, x1k = x1r.next()
                    tx = SP.dma(x1t[:], x1_scr[s * 128:(s + 1) * 128, :], waits=[x1free])
                    hb, hfree, hk = hbr.next()
                    th = rms_norm_block((junk, ssr), x1t[:], tx, gffn[:], t_gffn, 1e-6, D, hb[:], hfree)
                    x1r.done(x1k, th)
                    ps, pfree, pk = psT.next()
                    tt = None
                    for kc in range(8):
                        tt = PE.op(T.transpose, ps[:, kc * 128:(kc + 1) * 128], hb[:, kc * 128:(kc + 1) * 128], ident[:],
                                   waits=[th, pfree] if kc == 0 else ())
                    hbr.done(hk, tt)
                    te = ACT.op(A.copy, out=h2T[:, :, bi * 128:(bi + 1) * 128], in_=ps[:].rearrange("p (a b) -> p a b", a=8), waits=[tt, h2T_free])
                    psT.done(pk, te)
                    th2.append(te)
                h2T_free = []
                tact = []
                last_mm = None
                for f in range(NFC):
                    psg, pfree, pkg = pp.next()
                    tmg = None
                    for kc in range(8):
                        tmg = PE.op(T.matmul, psg[:], lhsT=w1[:, kc, f * 128:(f + 1) * 128], rhs=h2T[:, kc, :], start=(kc == 0), stop=(kc == 7),
                                    waits=[th2, pfree, tw1] if kc == 0 else ())
                    psu, pfree, pku = pp.next()
                    tmu = None
                    for kc in range(8):
                        tmu = PE.op(T.matmul, psu[:], lhsT=w1[:, kc, DFF + f * 128:DFF + (f + 1) * 128], rhs=h2T[:, kc, :], start=(kc == 0), stop=(kc == 7),
                                    waits=[pfree] if kc == 0 else ())
                    last_mm = tmu
                    sg, sfree, sk = sgr.next()
                    ta = ACT.op(A.activation, out=sg[:], in_=psg[:], func=AF.Silu, waits=[tmg, sfree])
                    pp.done(pkg, ta)
                    tb_ = DVE.op(V.tensor_tensor, out=actT[:, f, :], in0=psu[:], in1=sg[:], op=ALU.mult, waits=[tmu, ta, actT_free])
                    pp.done(pku, tb_)
                    sgr.done(sk, tb_)
                    tact.append(tb_)
                h2T_free = [last_mm]
                actT_free = []
                last_o = None
                for bi in range(4):
                    s = grp * 4 + bi
                    x2, x2free, x2k = x2r.next()
                    tx2 = SP.dma(x2[:], x1_scr[s * 128:(s + 1) * 128, :], waits=[x2free])
                    xtoks = []
                    for nc_ in range(2):
                        ps, pfree, pk = pp.next()
                        tm = None
                        for f in range(NFC):
                            tm = PE.op(T.matmul, ps[:], lhsT=actT[:, f, bi * 128:(bi + 1) * 128], rhs=w2[:, f, nc_ * 512:(nc_ + 1) * 512],
                                       start=(f == 0), stop=(f == NFC - 1), waits=[tact, pfree, tw2] if f == 0 else ())
                        last_o = tm
                        a = DVE.op(V.tensor_tensor, out=x2[:, nc_ * 512:(nc_ + 1) * 512], in0=ps[:], in1=x2[:, nc_ * 512:(nc_ + 1) * 512], op=ALU.add,
                                   waits=[tm, tx2])
                        pp.done(pk, a)
                        xtoks.append(a)
                    e = rms_norm_block((junk, ssr), x2[:], xtoks, gfin[:], t_gfin, 1e-6, D, x2[:], [])
                    td = SP.dma(out[s * 128:(s + 1) * 128, :], x2[:], waits=[e])
                    x2r.done(x2k, td)
                actT_free = [last_o]
            barrier()
    return nc


_NC_CACHE = {}


def _get_nc(debug=False):
    if debug not in _NC_CACHE:
        _NC_CACHE[debug] = build(debug)
    return _NC_CACHE[debug]


def make_in_maps(inputs):
    x = np.ascontiguousarray(np.asarray(inputs["x"], dtype=np.float32))
    pos = np.asarray(inputs["positions"]).astype(np.int32)
    in_maps = []
    kk = np.arange(512)[None, :]
    qq = np.arange(128)[:, None]
    for c in range(8):
        b, j = c // 4, c % 4
        blocks = [4 * s + j for s in range(NQB)]
        xqc = np.concatenate([x[b, q * 128:(q + 1) * 128] for q in blocks], axis=0)
        posk = np.ascontiguousarray(pos[b].reshape(NTB, 128).T)
        posq = np.ascontiguousarray(np.stack([pos[b, q * 128:(q + 1) * 128] for q in blocks], axis=1))
        cm = np.where(kk <= j * 128 + qq, 0.0, NEG).astype(np.float32)
        m = {"xs": x[b], "xq": np.ascontiguousarray(xqc), "posk": posk, "posq": posq, "cmask": cm}
        for name in ["norm_mix_g", "w_in", "idx_k_norm_g", "idx_k_norm_b", "diff_lambda_q1", "diff_lambda_k1",
                     "diff_lambda_q2", "diff_lambda_k2", "diff_subln_g", "gate_b", "w_branch_dsa", "w_branch_diff",
                     "w_out", "norm_ffn_g", "w_ffn_in", "w_ffn_out"]:
            m[name] = np.ascontiguousarray(np.asarray(inputs[name], dtype=np.float32)[0])
        m["norm_final_g"] = np.ascontiguousarray(np.asarray(inputs["norm_final_g"], dtype=np.float32))
        in_maps.append(m)
    return in_maps


def kernel(**inputs):
    nc = _get_nc(False)
    in_maps = make_in_maps(inputs)
    res = run_bass_kernel_spmd(nc, in_maps, core_ids=list(range(8)))
    outp = np.zeros((2, S, D), dtype=np.float32)
    for c in range(8):
        b, j = c // 4, c % 4
        o = res.results[c]["out"]
        for s in range(NQB):
            q = 4 * s + j
            outp[b, q * 128:(q + 1) * 128] = o[s * 128:(s + 1) * 128]
    return outp
```

```python
import os
import math
import numpy as np
from contextlib import ExitStack
import concourse.bass as bass
import concourse.mybir as mybir
from concourse.bass_utils import run_bass_kernel_spmd

F32 = mybir.dt.float32
F32R = mybir.dt.float32r
BF16 = mybir.dt.bfloat16
I32 = mybir.dt.int32
AF = mybir.ActivationFunctionType
ALU = mybir.AluOpType
AX = mybir.AxisListType

S = 8192
D = 1024
NTB = S // 128
NQB = 16
NQ = NQB * 128
DFF = 2816
NFC = DFF // 128
TOPK = 256
NBIS = 16
NEG = -30000.0
C_QA, C_KA, C_VA, C_QI, C_KI, C_WI, C_QB, C_KB, C_VB, C_G = 0, 512, 1024, 1536, 1792, 1824, 1832, 2344, 2856, 3368
D_IN = 5416
VW = 132
TWO_PI = 2.0 * math.pi


class Eng:
    def __init__(self, nc, es, eng, name):
        self.eng = eng
        self.name = name
        self.sem = es.enter_context(nc.semaphore("sem_" + name))
        self.count = 0
        self.seen = {}

    def wait(self, toks):
        for t in toks:
            if t is None:
                continue
            if isinstance(t, list):
                self.wait(t)
                continue
            sem, val, key = t
            if self.seen.get(key, 0) >= val:
                continue
            self.eng.wait_ge(sem, val)
            self.seen[key] = val

    def op(self, fn, *args, waits=(), **kw):
        self.wait(waits)
        inst = fn(*args, **kw)
        self.count += 1
        inst.then_inc(self.sem, 1)
        tok = (self.sem, self.count, self.name)
        self.seen[self.name] = self.count - 1 if False else self.seen.get(self.name, 0)
        return tok


class DmaQ:
    def __init__(self, nc, es, eng, name, nsem=12):
        self.eng = eng
        self.name = name
        self.sems = [es.enter_context(nc.semaphore(f"dsem_{name}_{i}")) for i in range(nsem)]
        self.vals = [0] * nsem
        self.i = 0
        self.seen = {}

    def wait(self, toks):
        for t in toks:
            if t is None:
                continue
            if isinstance(t, list):
                self.wait(t)
                continue
            sem, val, key = t
            if self.seen.get(key, 0) >= val:
                continue
            self.eng.wait_ge(sem, val)
            self.seen[key] = val

    def dma(self, out, in_, waits=(), **kw):
        k = self.i
        self.i = (self.i + 1) % len(self.sems)
        key = f"{self.name}_{k}"
        if self.vals[k] > 0:
            self.wait([(self.sems[k], self.vals[k], key)])
        self.wait(waits)
        self.vals[k] += 16
        self.eng.dma_start(out=out, in_=in_, **kw).then_inc(self.sems[k], 16)
        return (self.sems[k], self.vals[k], key)

    def all_toks(self):
        return [(self.sems[k], self.vals[k], f"{self.name}_{k}") for k in range(len(self.sems)) if self.vals[k] > 0]


class Ring:
    def __init__(self, tiles):
        self.tiles = tiles
        self.rd = [[] for _ in tiles]
        self.i = -1

    def next(self):
        self.i = (self.i + 1) % len(self.tiles)
        k = self.i
        toks = self.rd[k]
        self.rd[k] = []
        return self.tiles[k], toks, k

    def done(self, k, tok):
        self.rd[k].append(tok)


def build(debug=False):
    nc = bass.Bass("TRN2", target_bir_lowering=False)
    dt_in = lambda n, s, d=F32: nc.dram_tensor(n, s, d, kind="ExternalInput").ap()
    xs = dt_in("xs", [S, D])
    xq = dt_in("xq", [NQ, D])
    posk = dt_in("posk", [128, NTB], I32)
    posq = dt_in("posq", [128, NQB], I32)
    cmask = dt_in("cmask", [128, 512])
    norm_mix_g = dt_in("norm_mix_g", [D])
    w_in = dt_in("w_in", [D, D_IN])
    idx_g = dt_in("idx_k_norm_g", [32])
    idx_b = dt_in("idx_k_norm_b", [32])
    lq1 = dt_in("diff_lambda_q1", [64])
    lk1 = dt_in("diff_lambda_k1", [64])
    lq2 = dt_in("diff_lambda_q2", [64])
    lk2 = dt_in("diff_lambda_k2", [64])
    subln_g = dt_in("diff_subln_g", [128])
    gate_b = dt_in("gate_b", [2048])
    w_bd = dt_in("w_branch_dsa", [512, D])
    w_bf = dt_in("w_branch_diff", [512, D])
    w_out = dt_in("w_out", [D, D])
    norm_ffn_g = dt_in("norm_ffn_g", [D])
    w_f1 = dt_in("w_ffn_in", [D, 2 * DFF])
    w_f2 = dt_in("w_ffn_out", [DFF, D])
    norm_fin_g = dt_in("norm_final_g", [D])
    out = nc.dram_tensor("out", [NQ, D], F32, kind="ExternalOutput").ap()
    skind = "ExternalOutput" if debug else "Internal"
    kT_scr = nc.dram_tensor("kT_scr", [8, 128, S], BF16, kind=skind).ap()
    kiT_scr = nc.dram_tensor("kiT_scr", [32, S], BF16, kind=skind).ap()
    v_scr = nc.dram_tensor("v_scr", [S, 8 * VW], BF16, kind=skind).ap()
    qT_scr = nc.dram_tensor("qT_scr", [8, 128, NQ], BF16, kind=skind).ap()
    qiT_scr = nc.dram_tensor("qiT_scr", [64, 4, NQ], BF16, kind=skind).ap()
    o_scr = nc.dram_tensor("o_scr", [NQ, D], BF16, kind=skind).ap()
    x1_scr = nc.dram_tensor("x1_scr", [NQ, D], F32, kind=skind).ap()

    with ExitStack() as es:
        uid = [0]

        def sbt(st, n, s, d):
            uid[0] += 1
            return st.enter_context(nc.sbuf_tensor(f"{n}_{uid[0]}", s, d))

        def pst(st, n, s, d):
            uid[0] += 1
            return st.enter_context(nc.psum_tensor(f"{n}_{uid[0]}", s, d))

        ident = sbt(es, "ident", [128, 128], BF16)
        early = ExitStack()
        cosk = sbt(early, "cosk", [128, NTB, 8], F32)
        sink = sbt(early, "sink", [128, NTB, 8], F32)
        cosq = sbt(early, "cosq", [128, NQB, 8], F32)
        sinq = sbt(early, "sinq", [128, NQB, 8], F32)
        gmix = sbt(early, "gmix", [128, D], F32)
        wabs = sbt(early, "wabs", [128, NQB, 8], F32)
        wsgn = sbt(early, "wsgn", [128, NQB, 8], F32)
        nlam = sbt(early, "nlam", [128, 1], F32)
        small = sbt(early, "small", [128, 64], F32)
        cb = sbt(early, "cb", [128, 512], BF16)
        cbf = sbt(early, "cbf", [128, 512], F32)

        es.enter_context(nc.Block())
        PE = Eng(nc, es, nc.tensor, "pe")
        ACT = Eng(nc, es, nc.scalar, "act")
        DVE = Eng(nc, es, nc.vector, "dve")
        POOL = Eng(nc, es, nc.gpsimd, "pool")
        SP = DmaQ(nc, es, nc.sync, "sp", 16)
        GQ = DmaQ(nc, es, nc.gpsimd, "gq", 8)
        V = nc.vector
        A = nc.scalar
        T = nc.tensor

        def bcast(ap1d, n=128):
            return ap1d.partition_broadcast(n)

        def barrier():
            toks = [(e.sem, e.count, e.name) for e in (PE, ACT, DVE, POOL) if e.count > 0]
            toks += SP.all_toks() + GQ.all_toks()
            for e in (PE, ACT, DVE, POOL, SP):
                e.wait(toks)

        t = POOL.op(nc.gpsimd.memset, ident[:], 1.0)
        t_ident = POOL.op(nc.gpsimd.affine_select, out=ident[:], in_=ident[:], pattern=[[-1, 128]],
                          compare_op=ALU.is_equal, fill=0.0, base=0, channel_multiplier=1, waits=[t])
        t_gmix = SP.dma(gmix[:], bcast(norm_mix_g))
        t_cbf = SP.dma(cbf[:], cmask)
        t_cb = DVE.op(V.tensor_copy, out=cb[:], in_=cbf[:], waits=[t_cbf])

        with ExitStack() as p0:
            lt = sbt(p0, "lt", [128, 4, 64], F32)
            lj = sbt(p0, "lj", [128, 64], F32)
            tl = [SP.dma(lt[:, i, :], bcast(a)) for i, a in enumerate([lq1, lk1, lq2, lk2])]
            t1 = DVE.op(V.tensor_tensor, out=lj[:], in0=lt[:, 0, :], in1=lt[:, 1, :], op=ALU.mult, waits=tl)
            t1 = DVE.op(V.tensor_reduce, out=small[:, 0:1], in_=lj[:], axis=AX.X, op=ALU.add, waits=[t1])
            t2 = DVE.op(V.tensor_tensor, out=lj[:], in0=lt[:, 2, :], in1=lt[:, 3, :], op=ALU.mult, waits=[t1])
            t2 = DVE.op(V.tensor_reduce, out=small[:, 1:2], in_=lj[:], axis=AX.X, op=ALU.add, waits=[t2])
            t3 = ACT.op(A.activation, out=small[:, 2:4], in_=small[:, 0:2], func=AF.Exp, waits=[t2])
            t_nlam = DVE.op(V.scalar_tensor_tensor, out=nlam[:], in0=small[:, 3:4], scalar=-0.2, in1=small[:, 2:3],
                            op0=ALU.add, op1=ALU.subtract, waits=[t3])

            invf = sbt(p0, "invf", [128, 8], F32)
            tinv = None
            for i in range(8):
                fv = float(np.power(np.float32(500000.0), -np.float32(2 * i) / np.float32(16)))
                tinv = DVE.op(V.memset, invf[:, i:i + 1], fv)

            def rope_table(pos_ap, n, cos_t, sin_t, nm):
                pi_ = sbt(p0, "pi_" + nm, [128, n], I32)
                pf = sbt(p0, "pf_" + nm, [128, n], F32)
                ang = sbt(p0, "ang_" + nm, [128, n, 8], F32)
                yy = sbt(p0, "yy_" + nm, [128, n, 8], F32)
                ni = sbt(p0, "ni_" + nm, [128, n, 8], I32)
                tp = SP.dma(pi_[:], pos_ap)
                a = DVE.op(V.tensor_copy, out=pf[:], in_=pi_[:], waits=[tp])
                a = DVE.op(V.tensor_tensor, out=ang[:], in0=pf[:].unsqueeze(2).to_broadcast([128, n, 8]),
                           in1=invf[:].unsqueeze(1).to_broadcast([128, n, 8]), op=ALU.mult, waits=[a, tinv])

                def reduce_sin(src_add, dst):
                    b = DVE.op(V.tensor_scalar, out=yy[:], in0=ang[:], scalar1=src_add, scalar2=1.0 / TWO_PI,
                               op0=ALU.add, op1=ALU.mult, waits=[a])
                    b = DVE.op(V.tensor_copy, out=ni[:], in_=yy[:], waits=[b])
                    b = DVE.op(V.tensor_copy, out=yy[:], in_=ni[:], waits=[b])
                    c1 = 6.28125
                    c2 = TWO_PI - 6.28125
                    b = DVE.op(V.scalar_tensor_tensor, out=dst, in0=yy[:], scalar=-c1, in1=ang[:], op0=ALU.mult, op1=ALU.add, waits=[b])
                    b = DVE.op(V.scalar_tensor_tensor, out=dst, in0=yy[:], scalar=-c2, in1=dst, op0=ALU.mult, op1=ALU.add, waits=[b])
                    b = DVE.op(V.tensor_scalar, out=dst, in0=dst, scalar1=src_add, scalar2=3.1415925, op0=ALU.add, op1=ALU.min, waits=[b])
                    b = DVE.op(V.tensor_scalar, out=dst, in0=dst, scalar1=-3.1415925, scalar2=None, op0=ALU.max, waits=[b])
                    return ACT.op(A.activation, out=dst, in_=dst, func=AF.Sin, waits=[b])
                ts = reduce_sin(0.0, sin_t[:])
                tc = reduce_sin(math.pi / 2.0, cos_t[:])
                return [ts, tc]
            t_ropek = rope_table(posk, NTB, cosk, sink, "k")
            t_ropeq = rope_table(posq, NQB, cosq, sinq, "q")
            barrier()

        def rms_norm_block(st_rings, x_tile, tx, g_tile, tg, eps, n_feat, hb_tile, hb_free):
            junk, ssr = st_rings
            jt, jfree, jk = junk.next()
            col, cfree, ck = ssr.next()
            a = ACT.op(A.activation, out=jt[:, :n_feat], in_=x_tile, func=AF.Square, accum_out=col[:, 0:1],
                       waits=[tx, jfree, cfree])
            junk.done(jk, a)
            b = DVE.op(V.tensor_scalar, out=col[:, 1:2], in0=col[:, 0:1], scalar1=1.0 / n_feat, scalar2=eps,
                       op0=ALU.mult, op1=ALU.add, waits=[a])
            c = ACT.op(A.activation, out=col[:, 2:3], in_=col[:, 1:2], func=AF.Sqrt, waits=[b])
            d = DVE.op(V.reciprocal, out=col[:, 3:4], in_=col[:, 2:3], waits=[c])
            e = DVE.op(V.scalar_tensor_tensor, out=hb_tile, in0=x_tile, scalar=col[:, 3:4], in1=g_tile,
                       op0=ALU.mult, op1=ALU.mult, waits=[d, tg, hb_free])
            ssr.done(ck, e)
            return e

        rope_last = []

        def rope_apply(tile3, H, half, cs, sn, tmp, waits):
            x1 = tile3[:, :, 0:half]
            x2 = tile3[:, :, half:2 * half]
            cB = cs.unsqueeze(1).to_broadcast([128, H, half])
            sB = sn.unsqueeze(1).to_broadcast([128, H, half])
            tv = lambda i: tmp[:, i, 0:H * half].rearrange("p (h d) -> p h d", h=H)
            waits = list(waits) + rope_last
            a1 = DVE.op(V.tensor_tensor, out=tv(0), in0=x1, in1=cB, op=ALU.mult, waits=waits)
            a2 = DVE.op(V.tensor_tensor, out=tv(1), in0=x2, in1=sB, op=ALU.mult, waits=waits)
            a3 = DVE.op(V.tensor_tensor, out=tv(2), in0=x2, in1=cB, op=ALU.mult, waits=waits)
            a4 = DVE.op(V.tensor_tensor, out=tv(3), in0=x1, in1=sB, op=ALU.mult, waits=waits)
            b1 = DVE.op(V.tensor_tensor, out=x1, in0=tv(0), in1=tv(1), op=ALU.subtract, waits=[a1, a2, a3, a4])
            b2 = DVE.op(V.tensor_tensor, out=x2, in0=tv(2), in1=tv(3), op=ALU.add, waits=[a1, a2, a3, a4])
            rope_last[:] = [b1, b2]
            return [b1, b2]

        with ExitStack() as p1:
            NKV = 2080
            NQC = 1288
            wkv = sbt(p1, "wkv", [128, 8, NKV], BF16)
            wq = sbt(p1, "wq", [128, 8, NQC], BF16)
            w_in_v = w_in.rearrange("(kc p) n -> p kc n", p=128)
            tw = []
            for (dst0, c0, n) in [(0, C_KA, 512), (512, C_KB, 512), (1024, C_VA, 512), (1536, C_VB, 512), (2048, C_KI, 32)]:
                tw.append(GQ.dma(wkv[:, :, dst0:dst0 + n], w_in_v[:, :, c0:c0 + n]))
            twq = []
            for (dst0, c0, n) in [(0, C_QA, 512), (512, C_QB, 512), (1024, C_QI, 256), (1280, C_WI, 8)]:
                twq.append(GQ.dma(wq[:, :, dst0:dst0 + n], w_in_v[:, :, c0:c0 + n]))
            lng = sbt(p1, "lng", [128, 32], F32)
            lnb = sbt(p1, "lnb", [128, 32], F32)
            t_lng = SP.dma(lng[:], bcast(idx_g))
            t_lnb = SP.dma(lnb[:], bcast(idx_b))

            xr = Ring([sbt(p1, f"xt{i}", [128, D], F32) for i in range(3)])
            junk = Ring([sbt(p1, f"junk{i}", [128, D], BF16) for i in range(1)])
            ssr = Ring([sbt(p1, f"ss{i}", [128, 4], F32) for i in range(3)])
            hbr = Ring([sbt(p1, f"hb{i}", [128, D], BF16) for i in range(2)])
            hTr = Ring([sbt(p1, f"hT{i}", [128, 8, 128], BF16) for i in range(3)])
            kfr = Ring([sbt(p1, f"kf{i}", [128, 1024], F32) for i in range(2)])
            kbr = Ring([sbt(p1, f"kb{i}", [128, 1024], BF16) for i in range(3)])
            kTr = Ring([sbt(p1, f"kTt{i}", [128, 8, 128], BF16) for i in range(2)])
            vtr = Ring([sbt(p1, f"vt{i}", [128, 8, VW], BF16) for i in range(2)])
            rtmp = sbt(p1, "rtmp", [128, 4, 128], F32)
            kif = sbt(p1, "kif", [128, 8], F32)
            kic = sbt(p1, "kic", [128, 32], F32)
            kij = sbt(p1, "kij", [128, 32], F32)
            kibr = Ring([sbt(p1, f"kib{i}", [128, 32], BF16) for i in range(3)])
            kiTr = Ring([sbt(p1, f"kiT{i}", [32, 128], BF16) for i in range(2)])
            qwr = Ring([sbt(p1, f"qw{i}", [128, 264], F32) for i in range(2)])
            qibr = Ring([sbt(p1, f"qib{i}", [128, 256], BF16) for i in range(3)])
            qiTr = Ring([sbt(p1, f"qiT{i}", [32, 8, 128], BF16) for i in range(2)])
            psT = Ring([pst(p1, f"psT{i}", [128, D], BF16) for i in range(2)])
            pp = Ring([pst(p1, f"pp{i}", [128, 512], F32) for i in range(4)])
            pkT = Ring([pst(p1, f"pkT{i}", [128, D], BF16) for i in range(2)])
            t_vinit = []
            for vt_ in vtr.tiles:
                t0 = POOL.op(nc.gpsimd.memset, vt_[:], 0.0)
                t1_ = POOL.op(nc.gpsimd.memset, vt_[:, 0:4, :].rearrange("p a (s e) -> p (a s) e", s=2)[:, :, 64:65], 1.0, waits=[t0])
                t_vinit.append(POOL.op(nc.gpsimd.memset, vt_[:, 4:8, 128:129], 1.0, waits=[t0, t1_]))

            def aevac(out_ap, in_ap, waits):
                return ACT.op(A.copy, out=out_ap, in_=in_ap, waits=waits)

            def stageA(src_rows):
                xt, xfree, xk = xr.next()
                tx = SP.dma(xt[:], src_rows, waits=xfree)
                hb, hfree, hk = hbr.next()
                th = rms_norm_block((junk, ssr), xt[:], tx, gmix[:], t_gmix, 1e-6, D, hb[:], hfree)
                xr.done(xk, th)
                ps, pfree, pk = psT.next()
                tt = None
                for kc in range(8):
                    tt = PE.op(T.transpose, ps[:, kc * 128:(kc + 1) * 128], hb[:, kc * 128:(kc + 1) * 128], ident[:],
                               waits=[th, t_ident, pfree] if kc == 0 else ())
                hbr.done(hk, tt)
                hT, tfree, tk = hTr.next()
                te = aevac(hT[:].rearrange("p a b -> p (a b)"), ps[:], [tt, tfree])
                psT.done(pk, te)
                return hT, te, tk

            def project(hT, th, w_tile, c0, n, wtoks):
                ps, pfree, pk = pp.next()
                tt = None
                for kc in range(8):
                    tt = PE.op(T.matmul, ps[:, 0:n], lhsT=hT[:, kc, :], rhs=w_tile[:, kc, c0:c0 + n],
                               start=(kc == 0), stop=(kc == 7), waits=[th, pfree, wtoks] if kc == 0 else ())
                return ps, tt, pk

            def transpose_out(src_bf, tsrc, nchunk, width, ring_sb, dst_dram):
                ps, pfree, pk = pkT.next()
                tt = None
                for c in range(nchunk):
                    tt = PE.op(T.transpose, ps[0:width, c * 128:(c + 1) * 128], src_bf[:, c * width:(c + 1) * width], ident[:],
                               waits=[tsrc, pfree, t_ident] if c == 0 else ())
                sbT, sfree, sk = ring_sb.next()
                te = aevac(sbT[:].rearrange("p a b -> p (a b)") if len(sbT.shape) == 3 else sbT[:],
                           ps[0:width, 0:nchunk * 128], [tt, sfree])
                pkT.done(pk, te)
                td = GQ.dma(dst_dram, sbT[:], waits=[te])
                ring_sb.done(sk, td)
                return tt

            def transpose_out_qi(src_bf, tsrc, s):
                ps, pfree, pk = pkT.next()
                tt = None
                for c in range(8):
                    tt = PE.op(T.transpose, ps[0:32, c * 128:(c + 1) * 128], src_bf[:, c * 32:(c + 1) * 32], ident[:],
                               waits=[tsrc, pfree, t_ident] if c == 0 else ())
                sbT, sfree, sk = qiTr.next()
                te = aevac(sbT[:].rearrange("p a b -> p (a b)"), ps[0:32, 0:1024], [tt, sfree])
                pkT.done(pk, te)
                for g in range(2):
                    td = GQ.dma(qiT_scr[g * 32:(g + 1) * 32, :, s * 128:(s + 1) * 128], sbT[:, g::2, :], waits=[te])
                    qiTr.done(sk, td)
                return tt

            def qk_pair(hT, th, w_tile, wtoks, cs, sn, scale, dst_dram):
                kf, kfree, kk = kfr.next()
                tes = []
                for gi in range(2):
                    ps, tmm, pk = project(hT, th, w_tile, gi * 512, 512, wtoks)
                    te = aevac(kf[:, gi * 512:(gi + 1) * 512], ps[:], [tmm, kfree])
                    pp.done(pk, te)
                    tes.append(te)
                tr = rope_apply(kf[:].rearrange("p (h d) -> p h d", h=16), 16, 8, cs, sn, rtmp, tes)
                kb, bfree, bk = kbr.next()
                if scale == 1.0:
                    tcst = DVE.op(V.tensor_copy, out=kb[:], in_=kf[:], waits=[tr, bfree])
                else:
                    tcst = DVE.op(V.tensor_scalar, out=kb[:], in0=kf[:], scalar1=scale, scalar2=None, op0=ALU.mult, waits=[tr, bfree])
                kfr.done(kk, tcst)
                return kb, bk, tcst

            def stageB_kv(tb, hT, th, hk):
                cs = cosk[:, tb, :]
                sn = sink[:, tb, :]
                kb, bk, tcst = qk_pair(hT, th, wkv, tw, cs, sn, 1.0, None)
                vt, vfree, vk = vtr.next()
                tvs = []
                for gi in (2, 3):
                    ps, tmm, pk = project(hT, th, wkv, gi * 512, 512, tw)
                    if gi == 2:
                        dstv = vt[:, 0:4, :].rearrange("p a (s e) -> p (a s) e", s=2)[:, :, 0:64]
                        te = aevac(dstv, ps[:].rearrange("p (h e) -> p h e", e=64), [tmm, vfree, t_vinit])
                    else:
                        te = aevac(vt[:, 4:8, 0:128], ps[:].rearrange("p (h e) -> p h e", e=128), [tmm, vfree, t_vinit])
                    pp.done(pk, te)
                    tvs.append(te)
                td = GQ.dma(v_scr[tb * 128:(tb + 1) * 128, :], vt[:].rearrange("p a b -> p (a b)"), waits=tvs)
                vtr.done(vk, td)
                ps, tmm, pk = project(hT, th, wkv, 2048, 32, tw)
                hTr.done(hk, tmm)
                a = DVE.op(V.tensor_reduce, out=kif[:, 0:1], in_=ps[:, 0:32], axis=AX.X, op=ALU.add, waits=[tmm])
                a = DVE.op(V.tensor_scalar, out=kif[:, 1:2], in0=kif[:, 0:1], scalar1=1.0 / 32, scalar2=None, op0=ALU.mult, waits=[a])
                a = DVE.op(V.tensor_scalar, out=kic[:], in0=ps[:, 0:32], scalar1=kif[:, 1:2], scalar2=None, op0=ALU.subtract, waits=[a])
                pp.done(pk, a)
                b = ACT.op(A.activation, out=kij[:], in_=kic[:], func=AF.Square, accum_out=kif[:, 2:3], waits=[a])
                b = DVE.op(V.tensor_scalar, out=kif[:, 3:4], in0=kif[:, 2:3], scalar1=1.0 / 32, scalar2=1e-6, op0=ALU.mult, op1=ALU.add, waits=[b])
                b = ACT.op(A.activation, out=kif[:, 4:5], in_=kif[:, 3:4], func=AF.Sqrt, waits=[b])
                b = DVE.op(V.reciprocal, out=kif[:, 5:6], in_=kif[:, 4:5], waits=[b])
                b = DVE.op(V.scalar_tensor_tensor, out=kic[:], in0=kic[:], scalar=kif[:, 5:6], in1=lng[:], op0=ALU.mult, op1=ALU.mult, waits=[b, t_lng])
                b = DVE.op(V.tensor_tensor, out=kic[:], in0=kic[:], in1=lnb[:], op=ALU.add, waits=[b, t_lnb])
                csi = cosk[:, tb, :].rearrange("p (a two) -> p a two", two=2)[:, :, 0]
                sni = sink[:, tb, :].rearrange("p (a two) -> p a two", two=2)[:, :, 0]
                tr = rope_apply(kic[:].rearrange("p (h d) -> p h d", h=1), 1, 4, csi, sni, rtmp, [b])
                kib, bfree, bk2 = kibr.next()
                tc2 = DVE.op(V.tensor_copy, out=kib[:], in_=kic[:], waits=[tr, bfree])

                def b2():
                    tlast = transpose_out(kb, tcst, 8, 128, kTr, kT_scr[:, :, tb * 128:(tb + 1) * 128].rearrange("c f t -> f c t"))
                    kbr.done(bk, tlast)
                    tl2 = transpose_out(kib, tc2, 1, 32, kiTr, kiT_scr[:, tb * 128:(tb + 1) * 128])
                    kibr.done(bk2, tl2)
                return b2

            def stageB_q(s, hT, th, hk):
                cs = cosq[:, s, :]
                sn = sinq[:, s, :]
                kb, bk, tcst = qk_pair(hT, th, wq, twq, cs, sn, 0.125, None)
                ps, tmm, pk = project(hT, th, wq, 1024, 264, twq)
                hTr.done(hk, tmm)
                qw, qfree, qk = qwr.next()
                te = aevac(qw[:], ps[:, 0:264], [tmm, qfree])
                pp.done(pk, te)
                csi = cosq[:, s, :].rearrange("p (a two) -> p a two", two=2)[:, :, 0]
                sni = sinq[:, s, :].rearrange("p (a two) -> p a two", two=2)[:, :, 0]
                tr = rope_apply(qw[:, 0:256].rearrange("p (h d) -> p h d", h=8), 8, 4, csi, sni, rtmp, [te])
                a1 = DVE.op(V.tensor_scalar, out=wabs[:, s, :], in0=qw[:, 256:264], scalar1=1.0 / 16, scalar2=None, op0=ALU.mult, waits=[te])
                a2 = a1
                qib, bfree, bk2 = qibr.next()
                tc2 = DVE.op(V.tensor_copy, out=qib[:], in_=qw[:, 0:256], waits=[tr, bfree])
                qwr.done(qk, [tc2, a1, a2])

                def b2():
                    tlast = transpose_out(kb, tcst, 8, 128, kTr, qT_scr[:, :, s * 128:(s + 1) * 128].rearrange("c f t -> f c t"))
                    kbr.done(bk, tlast)
                    tl2 = transpose_out_qi(qib, tc2, s)
                    qibr.done(bk2, tl2)
                return b2

            items = [("kv", tb, xs[tb * 128:(tb + 1) * 128, :]) for tb in range(NTB)] + \
                    [("q", s, xq[s * 128:(s + 1) * 128, :]) for s in range(NQB)]
            prev = None
            prev_b2 = None
            for it in items + [None]:
                cur = None
                if it is not None:
                    cur = (it, stageA(it[2]))
                b2 = None
                if prev is not None:
                    (kind, idx, _), (hT, th, hk) = prev
                    if kind == "kv":
                        b2 = stageB_kv(idx, hT, th, hk)
                    else:
                        b2 = stageB_q(idx, hT, th, hk)
                if prev_b2 is not None:
                    prev_b2()
                prev_b2 = b2
                prev = cur
            if prev_b2 is not None:
                prev_b2()
            barrier()

        with ExitStack() as p2:
            kiT = sbt(p2, "kiT", [64, S], BF16)
            t_kiT = [SP.dma(kiT[g * 32:(g + 1) * 32, :], kiT_scr) for g in range(2)]
            gsub = sbt(p2, "gsub", [128, 128], F32)
            t_gs = SP.dma(gsub[:], bcast(subln_g))
            t_gs = DVE.op(V.tensor_scalar, out=gsub[:], in0=gsub[:], scalar1=0.8, scalar2=None, op0=ALU.mult, waits=[t_gs])
            ident2 = sbt(p2, "ident2", [128, 2, 128], BF16)
            t_id2 = [DVE.op(V.tensor_copy, out=ident2[:, i, :], in_=ident[:], waits=[t_ident]) for i in range(2)]
            Kr = Ring([sbt(p2, f"Kb{i}", [128, S], BF16) for i in range(2)])
            Vr = Ring([sbt(p2, f"Vb{i}", [128, NTB, VW], BF16) for i in range(2)])
            Mb = [sbt(p2, f"Mb{i}", [128, S], BF16) for i in range(2)]
            Isc = sbt(p2, "Isc", [128, S], F32)
            qbd_tiles = [sbt(p2, f"qbd{i}", [128, 2, 128], BF16) for i in range(3)]
            t_qz = [POOL.op(nc.gpsimd.memset, q_[:], 0.0) for q_ in qbd_tiles]
            qbr = Ring(qbd_tiles)
            qiT = [sbt(p2, f"qiTs{i}", [64, 4, 128], BF16) for i in range(2)]
            rl = Ring([sbt(p2, f"rl{i}", [128, 512], BF16) for i in range(6)])
            identf = sbt(p2, "identf", [128, 128], F32)
            t_idf = DVE.op(V.tensor_copy, out=identf[:], in_=ident[:], waits=[t_ident])
            dgb = [sbt(p2, f"dgb{i}", [128, 8, 128], BF16) for i in range(2)]
            etr = Ring([sbt(p2, f"et{i}", [128, 512], BF16) for i in range(3)])
            otr = Ring([sbt(p2, f"ot{i}", [128, D], BF16) for i in range(2)])
            of32 = sbt(p2, "of32", [128, 128], F32)
            oj = sbt(p2, "oj", [128, 128], F32)
            bs = sbt(p2, "bs", [128, 16], F32)
            es_ = sbt(p2, "es_", [128, 16], F32)
            pss = Ring([pst(p2, f"pss{i}", [128, 512], F32) for i in range(4)])
            pI = pst(p2, "pI", [128, 512], F32)
            pacc = Ring([pst(p2, f"pacc{i}", [128, 512], F32) for i in range(3)])
            of1 = sbt(p2, "of1", [128, 128], F32)
            of2 = sbt(p2, "of2", [128, 128], F32)
            ez = sbt(p2, "ez", [128, 8], F32)
            Mb_ready = [None, None]
            Mb_readers = [[], []]
            qiT_readers = [[], []]
            dg_readers = [[], []]
            Isc_free = [[]]
            pI_free = [[]]

            def prep(s):
                li = s % 2
                nk = (4 * s + 4) * 128
                nch = nk // 512
                wb = 0.76 * (s + 1)
                tq = SP.dma(qiT[li][:], qiT_scr[:, :, s * 128:(s + 1) * 128], waits=qiT_readers[li])
                qiT_readers[li] = []
                tdg = None
                for h in range(8):
                    tdg = DVE.op(V.tensor_scalar, out=dgb[li][:, h, :], in0=identf[:], scalar1=wabs[:, s, h:h + 1], scalar2=None, op0=ALU.mult,
                                 waits=[t_idf, dg_readers[li]] if h == 0 else ())
                dg_readers[li] = []
                yield 1.0
                tI = None
                pending = None

                def flush(pend):
                    (pc, pr, items) = pend
                    tacc = None
                    for g, (prt, pta, prk) in enumerate(items):
                        h = 2 * pr + g
                        tacc = PE.op(T.matmul, pI[:], lhsT=dgb[li][:, h, :], rhs=prt[:],
                                     start=(pr == 0 and g == 0), stop=(pr == 3 and g == 1),
                                     waits=[pta, tdg, pI_free[0]])
                        rl.done(prk, tacc)
                    return tacc
                for c in range(nch):
                    for r in range(4):
                        if pending is not None:
                            tacc = flush(pending)
                            if pending[1] == 3:
                                pc = pending[0]
                                tI = DVE.op(V.tensor_copy, out=Isc[:, pc * 512:(pc + 1) * 512], in_=pI[:], waits=[tacc, Isc_free[0]])
                                pI_free[0] = [tI]
                        mm = []
                        for g in range(2):
                            ps, pfree, pk = pss.next()
                            tm = PE.op(T.matmul, ps[:], lhsT=qiT[li][g * 32:(g + 1) * 32, r, :], rhs=kiT[g * 32:(g + 1) * 32, c * 512:(c + 1) * 512],
                                       start=True, stop=True, waits=[tq, t_kiT, pfree])
                            mm.append((ps, pk, tm))
                        items = []
                        for g in range(2):
                            ps, pk, tm = mm[g]
                            rt, rfree, rk = rl.next()
                            ta = ACT.op(A.activation, out=rt[:], in_=ps[:], func=AF.Relu, waits=[tm, rfree])
                            pss.done(pk, ta)
                            items.append((rt, ta, rk))
                        pending = (c, r, items)
                        yield 2.0
                tacc = flush(pending)
                tI = DVE.op(V.tensor_copy, out=Isc[:, pending[0] * 512:(pending[0] + 1) * 512], in_=pI[:], waits=[tacc, Isc_free[0]])
                pI_free[0] = [tI]
                qiT_readers[li].append(tacc)
                dg_readers[li].append(tacc)
                Isc_free[0] = []
                Iv = Isc[:, 0:nk]
                a = DVE.op(V.tensor_reduce, out=bs[:, 0:1], in_=Iv, axis=AX.X, op=ALU.max, waits=[tI])
                yield wb
                a = DVE.op(V.tensor_reduce, out=bs[:, 1:2], in_=Iv, axis=AX.X, op=ALU.min, waits=[a])
                a = DVE.op(V.scalar_tensor_tensor, out=bs[:, 2:3], in0=bs[:, 0:1], scalar=1.0, in1=bs[:, 1:2], op0=ALU.add, op1=ALU.subtract, waits=[a])
                a = DVE.op(V.tensor_tensor, out=Isc[:, nk - 512:nk], in0=Isc[:, nk - 512:nk], in1=cbf[:], op=ALU.add, waits=[a, t_cbf])
                yield wb
                lo = bs[:, 1:2]
                w0 = bs[:, 2:3]
                mid = bs[:, 3:4]
                cnt = bs[:, 4:5]
                gg = bs[:, 5:6]
                for it in range(1, NBIS + 1):
                    sc = 2.0 ** (-it)
                    a = DVE.op(V.tensor_scalar, out=mid, in0=w0, scalar1=sc, scalar2=lo, op0=ALU.mult, op1=ALU.add, waits=[a])
                    a = DVE.op(V.tensor_scalar, out=Mb[li][:, 0:nk], in0=Iv, scalar1=mid, scalar2=None, op0=ALU.is_ge, op1=ALU.add,
                               accum_out=cnt, waits=[a, Mb_readers[li]])
                    Mb_readers[li] = []
                    a = DVE.op(V.tensor_scalar, out=gg, in0=cnt, scalar1=TOPK - 0.5, scalar2=sc, op0=ALU.is_ge, op1=ALU.mult, waits=[a])
                    a = DVE.op(V.scalar_tensor_tensor, out=lo, in0=w0, scalar=gg, in1=lo, op0=ALU.mult, op1=ALU.add, waits=[a])
                    yield wb
                a = DVE.op(V.tensor_scalar, out=Mb[li][:, 0:nk], in0=Iv, scalar1=lo, scalar2=NEG, op0=ALU.is_lt, op1=ALU.mult, waits=[a])
                Mb_ready[li] = a
                Isc_free[0] = [a]

            def prep_steps(s):
                return 1 + 8 * (s + 1) + (2 + NBIS) * 0.76 * (s + 1)

            pump_state = {"gen": None, "budget": 0.0, "rate": 0.0}

            def pump():
                st = pump_state
                if st["gen"] is None:
                    return
                st["budget"] += st["rate"]
                while st["budget"] > 0.0 and st["gen"] is not None:
                    try:
                        st["budget"] -= next(st["gen"])
                    except StopIteration:
                        st["gen"] = None

            def attention(s, p, ot, ofree, owr):
                li = s % 2
                is_dsa = p < 4
                nkb = 4 * s + 4
                kmax = nkb * 128
                Kb, kfree, kk = Kr.next()
                Vb, vfree, vk = Vr.next()
                tK = SP.dma(Kb[:, 0:kmax], kT_scr[p, :, 0:kmax], waits=kfree)
                tV = []
                for b0 in range(0, nkb, 8):
                    nb = min(8, nkb - b0)
                    tV.append(SP.dma(Vb[:, b0:b0 + nb, :], v_scr[b0 * 128:(b0 + nb) * 128, p * VW:(p + 1) * VW].rearrange("(b t) w -> t b w", t=128), waits=vfree))
                qbd, qfree, qk = qbr.next()
                tq = [SP.dma(qbd[m * 64:(m + 1) * 64, m, :], qT_scr[p, m * 64:(m + 1) * 64, s * 128:(s + 1) * 128], waits=[qfree, t_qz]) for m in range(2)]
                accs = [pacc.next() for m in range(2)]
                vw = 66 if is_dsa else 130
                ntile = nkb // 2
                pend = {}

                def do_qk(ti):
                    ps, pfree, pk = pss.next()
                    tm = None
                    for bi in range(2):
                        kb_ = ti * 2 + bi
                        need_mask = is_dsa or (kb_ >= nkb - 4)
                        tm = PE.op(T.matmul, ps[:, bi * 256:(bi + 1) * 256], lhsT=Kb[:, kb_ * 128:(kb_ + 1) * 128],
                                   rhs=qbd[:].rearrange("p a b -> p (a b)"), start=True, stop=not need_mask,
                                   waits=[tK, tq, pfree] if bi == 0 else ())
                        if need_mask:
                            if is_dsa:
                                ml = Mb[li][:, kb_ * 128:(kb_ + 1) * 128]
                                mw = [Mb_ready[li]]
                            else:
                                cbi = kb_ - (nkb - 4)
                                ml = cb[:, cbi * 128:(cbi + 1) * 128]
                                mw = [t_cb]
                            tm = PE.op(T.matmul, ps[:, bi * 256:(bi + 1) * 256], lhsT=ml, rhs=ident2[:].rearrange("p a b -> p (a b)"),
                                       start=False, stop=True, waits=mw + [t_id2])
                    et, efree, ek = etr.next()
                    te = ACT.op(A.activation, out=et[:], in_=ps[:], func=AF.Exp, waits=[tm, efree])
                    pss.done(pk, te)
                    pend[ti] = (et, te, ek)

                tav = [None, None]

                def do_av(ti):
                    et, te, ek = pend.pop(ti)
                    tm = None
                    first = True
                    for bi in range(2):
                        kb_ = ti * 2 + bi
                        for m in range(2):
                            acc, afree, ak = accs[m]
                            rhs = Vb[:, kb_, m * 66:(m + 1) * 66] if is_dsa else Vb[:, kb_, 0:130]
                            tm = PE.op(T.matmul, acc[:, 0:vw], lhsT=et[:, bi * 256 + m * 128:bi * 256 + (m + 1) * 128], rhs=rhs,
                                       start=(kb_ == 0), stop=(kb_ == nkb - 1),
                                       waits=[te, tV, accs[0][1], accs[1][1]] if first else ())
                            first = False
                            tav[m] = tm
                    etr.done(ek, tm)
                for ti in range(ntile):
                    do_qk(ti)
                    if ti >= 1:
                        do_av(ti - 1)
                    pump()
                do_av(ntile - 1)
                qbr.done(qk, tav[1])
                if is_dsa:
                    Mb_readers[li].append(tav[1])
                Kr.done(kk, tav[1])
                Vr.done(vk, tav[1])
                if is_dsa:
                    for m in range(2):
                        acc, afree, ak = accs[m]
                        a = ACT.op(A.activation, out=ez[:, m:m + 1], in_=acc[:, 64:65], func=AF.Ln, waits=[tav[1]])
                        a = ACT.op(A.activation, out=ez[:, m:m + 1], in_=ez[:, m:m + 1], func=AF.Exp, scale=-1.0, waits=[a])
                        a = ACT.op(A.activation, out=ot[:, p * 128 + m * 64:p * 128 + (m + 1) * 64], in_=acc[:, 0:64], func=AF.Copy,
                                   scale=ez[:, m:m + 1], waits=[a, ofree])
                        pacc.done(ak, a)
                        owr.append(a)
                else:
                    h = p - 4
                    acc1, _, ak1 = accs[0]
                    acc2, _, ak2 = accs[1]
                    a = ACT.op(A.activation, out=ez[:, 2:3], in_=acc1[:, 128:129], func=AF.Ln, waits=[tav[1], of_free[0]])
                    a = ACT.op(A.activation, out=ez[:, 2:3], in_=ez[:, 2:3], func=AF.Exp, scale=-1.0, waits=[a])
                    a = ACT.op(A.activation, out=of1[:], in_=acc1[:, 0:128], func=AF.Copy, scale=ez[:, 2:3], waits=[a])
                    pacc.done(ak1, a)
                    b = ACT.op(A.activation, out=ez[:, 3:4], in_=acc2[:, 128:129], func=AF.Ln, waits=[a])
                    b = ACT.op(A.activation, out=ez[:, 3:4], in_=ez[:, 3:4], func=AF.Exp, scale=-1.0, waits=[b])
                    b = ACT.op(A.activation, out=of2[:], in_=acc2[:, 0:128], func=AF.Copy, scale=ez[:, 3:4], waits=[b])
                    pacc.done(ak2, b)
                    b = DVE.op(V.scalar_tensor_tensor, out=of32[:], in0=of2[:], scalar=nlam[:, 0:1], in1=of1[:],
                               op0=ALU.mult, op1=ALU.add, waits=[a, b, t_nlam, of32_free[0]])
                    of_free[0] = [b]
                    c = ACT.op(A.activation, out=oj[:], in_=of32[:], func=AF.Square, accum_out=es_[:, 5:6], waits=[b])
                    c = DVE.op(V.tensor_scalar, out=es_[:, 6:7], in0=es_[:, 5:6], scalar1=1.0 / 128, scalar2=1e-5, op0=ALU.mult, op1=ALU.add, waits=[c])
                    c = ACT.op(A.activation, out=es_[:, 7:8], in_=es_[:, 6:7], func=AF.Ln, waits=[c])
                    c = ACT.op(A.activation, out=es_[:, 8:9], in_=es_[:, 7:8], func=AF.Exp, scale=-0.5, waits=[c])
                    c = DVE.op(V.scalar_tensor_tensor, out=ot[:, 512 + h * 128:512 + (h + 1) * 128], in0=of32[:], scalar=es_[:, 8:9],
                               in1=gsub[:], op0=ALU.mult, op1=ALU.mult, waits=[c, t_gs, ofree])
                    of32_free[0] = [c]
                    owr.append(c)

            of_free = [[]]
            of32_free = [[]]
            for _ in prep(0):
                pass
            for s in range(NQB):
                if s + 1 < NQB:
                    pump_state["gen"] = prep(s + 1)
                    pump_state["budget"] = 0.0
                    pump_state["rate"] = prep_steps(s + 1) / float(8 * (2 * s + 2)) * 1.15
                else:
                    pump_state["gen"] = None
                ot, ofree, ok_ = otr.next()
                owr = []
                for pi, p in enumerate([4, 5, 6, 7, 0, 1, 2, 3]):
                    attention(s, p, ot, ofree, owr)
                if pump_state["gen"] is not None:
                    for _ in pump_state["gen"]:
                        pass
                    pump_state["gen"] = None
                td = GQ.dma(o_scr[s * 128:(s + 1) * 128, :], ot[:], waits=owr)
                otr.done(ok_, td)
            barrier()

        def load_w(st, name, src, rows, cols, q, waits=()):
            nkc = rows // 128
            wt = sbt(st, name, [128, nkc, cols], BF16)
            srcv = src.rearrange("(kc p) n -> p kc n", p=128)
            toks = []
            step = 1024
            for c0 in range(0, cols, step):
                n = min(step, cols - c0)
                toks.append(q.dma(wt[:, :, c0:c0 + n], srcv[:, :, c0:c0 + n], waits=waits))
            return wt, toks

        with ExitStack() as p3:
            wg = sbt(p3, "wg", [128, 8, 2048], BF16)
            w_in_v = w_in.rearrange("(kc p) n -> p kc n", p=128)
            twg = [GQ.dma(wg[:, :, c0:c0 + 512], w_in_v[:, :, C_G + c0:C_G + c0 + 512]) for c0 in range(0, 2048, 512)]
            wbd, twbd = load_w(p3, "wbd", w_bd, 512, D, GQ)
            wbf, twbf = load_w(p3, "wbf", w_bf, 512, D, GQ)
            wo, two = load_w(p3, "wo", w_out, D, D, GQ)
            gbt = sbt(p3, "gbt", [128, 2048], F32)
            t_gb = SP.dma(gbt[:], bcast(gate_b))
            xr = Ring([sbt(p3, f"xt{i}", [128, D], F32) for i in range(2)])
            junk = Ring([sbt(p3, f"junk{i}", [128, D], BF16) for i in range(1)])
            ssr = Ring([sbt(p3, f"ss{i}", [128, 4], F32) for i in range(2)])
            hbr = Ring([sbt(p3, f"hb{i}", [128, D], BF16) for i in range(2)])
            hTr = Ring([sbt(p3, f"hT{i}", [128, 8, 128], BF16) for i in range(2)])
            obr = Ring([sbt(p3, f"ob{i}", [128, D], BF16) for i in range(2)])
            oTr = Ring([sbt(p3, f"oT{i}", [128, 8, 128], BF16) for i in range(2)])
            gat = sbt(p3, "gat", [128, 2048], F32)
            mrg = sbt(p3, "mrg", [128, D], F32)
            mrg2 = sbt(p3, "mrg2", [128, D], F32)
            mbr = Ring([sbt(p3, f"mb{i}", [128, D], BF16) for i in range(2)])
            mTr = Ring([sbt(p3, f"mT{i}", [128, 8, 128], BF16) for i in range(2)])
            x1r = Ring([sbt(p3, f"x1t{i}", [128, D], F32) for i in range(2)])
            psT = Ring([pst(p3, f"psT{i}", [128, D], BF16) for i in range(2)])
            pp = Ring([pst(p3, f"pp{i}", [128, 512], F32) for i in range(6)])
            gat_free = []
            mrg_free = []
            x1_dmas = []

            def transpose8(src_bf, tsrc, dst_ring):
                ps, pfree, pk = psT.next()
                tt = None
                for kc in range(8):
                    tt = PE.op(T.transpose, ps[:, kc * 128:(kc + 1) * 128], src_bf[:, kc * 128:(kc + 1) * 128], ident[:],
                               waits=[tsrc, pfree] if kc == 0 else ())
                dT, dfree, dk = dst_ring.next()
                te = ACT.op(A.copy, out=dT[:].rearrange("p a b -> p (a b)"), in_=ps[:], waits=[tt, dfree])
                psT.done(pk, te)
                return dT, te, dk, tt

            for s in range(NQB):
                xt, xfree, xk = xr.next()
                tx = SP.dma(xt[:], xq[s * 128:(s + 1) * 128, :], waits=xfree)
                hb, hfree, hk = hbr.next()
                th = rms_norm_block((junk, ssr), xt[:], tx, gmix[:], t_gmix, 1e-6, D, hb[:], hfree)
                hT, te, tk, tt = transpose8(hb, th, hTr)
                hbr.done(hk, tt)
                tg_last = None
                for gc in range(4):
                    ps, pfree, pk = pp.next()
                    tm = None
                    for kc in range(8):
                        tm = PE.op(T.matmul, ps[:], lhsT=hT[:, kc, :], rhs=wg[:, kc, gc * 512:(gc + 1) * 512], start=(kc == 0), stop=(kc == 7),
                                   waits=[te, pfree, twg] if kc == 0 else ())
                    a = DVE.op(V.tensor_tensor, out=gat[:, gc * 512:(gc + 1) * 512], in0=ps[:], in1=gbt[:, gc * 512:(gc + 1) * 512], op=ALU.add,
                               waits=[tm, t_gb, gat_free])
                    pp.done(pk, a)
                    tg_last = ACT.op(A.activation, out=gat[:, gc * 512:(gc + 1) * 512], in_=gat[:, gc * 512:(gc + 1) * 512], func=AF.Sigmoid, waits=[a])
                    if gc == 3:
                        hTr.done(tk, tm)
                gat_free = []
                ob, ofree, ok_ = obr.next()
                to = SP.dma(ob[:], o_scr[s * 128:(s + 1) * 128, :], waits=[ofree])
                oT, teo, ok2, tto = transpose8(ob, to, oTr)
                obr.done(ok_, tto)
                mtoks = []
                for br, (wt, twt) in enumerate([(wbd, twbd), (wbf, twbf)]):
                    for nc_ in range(2):
                        ps, pfree, pk = pp.next()
                        tm = None
                        for kc in range(4):
                            tm = PE.op(T.matmul, ps[:], lhsT=oT[:, br * 4 + kc, :], rhs=wt[:, kc, nc_ * 512:(nc_ + 1) * 512], start=(kc == 0), stop=(kc == 3),
                                       waits=[teo, pfree, twt] if kc == 0 else ())
                        dst = (mrg if br == 0 else mrg2)[:, nc_ * 512:(nc_ + 1) * 512]
                        a = DVE.op(V.tensor_tensor, out=dst, in0=ps[:], in1=gat[:, br * 1024 + nc_ * 512:br * 1024 + (nc_ + 1) * 512], op=ALU.mult,
                                   waits=[tm, tg_last, mrg_free])
                        pp.done(pk, a)
                        mtoks.append(a)
                        if br == 1 and nc_ == 1:
                            oTr.done(ok2, tm)
                gat_free = list(mtoks)
                mb, mfree, mk = mbr.next()
                tmb = DVE.op(V.tensor_tensor, out=mb[:], in0=mrg[:], in1=mrg2[:], op=ALU.add, waits=[mtoks, mfree])
                mrg_free = [tmb]
                mT, tem, mk2, ttm = transpose8(mb, tmb, mTr)
                mbr.done(mk, ttm)
                x1t, x1free, x1k = x1r.next()
                xtoks = []
                for nc_ in range(2):
                    ps, pfree, pk = pp.next()
                    tm = None
                    for kc in range(8):
                        tm = PE.op(T.matmul, ps[:], lhsT=mT[:, kc, :], rhs=wo[:, kc, nc_ * 512:(nc_ + 1) * 512], start=(kc == 0), stop=(kc == 7),
                                   waits=[tem, pfree, two] if kc == 0 else ())
                    a = DVE.op(V.tensor_tensor, out=x1t[:, nc_ * 512:(nc_ + 1) * 512], in0=ps[:], in1=xt[:, nc_ * 512:(nc_ + 1) * 512], op=ALU.add,
                               waits=[tm, x1free])
                    pp.done(pk, a)
                    xtoks.append(a)
                    if nc_ == 1:
                        mTr.done(mk2, tm)
                xr.done(xk, xtoks)
                td = SP.dma(x1_scr[s * 128:(s + 1) * 128, :], x1t[:], waits=xtoks)
                x1r.done(x1k, td)
                x1_dmas.append(td)
            barrier()
        early.close()

        with ExitStack() as p4:
            w1, tw1 = load_w(p4, "w1", w_f1, D, 2 * DFF, GQ)
            w2, tw2 = load_w(p4, "w2", w_f2, DFF, D, GQ)
            gffn = sbt(p4, "gffn", [128, D], F32)
            gfin = sbt(p4, "gfin", [128, D], F32)
            t_gffn = SP.dma(gffn[:], bcast(norm_ffn_g))
            t_gfin = SP.dma(gfin[:], bcast(norm_fin_g))
            x1r = Ring([sbt(p4, f"x1b{i}", [128, D], F32) for i in range(2)])
            junk = Ring([sbt(p4, f"junk{i}", [128, D], BF16) for i in range(1)])
            ssr = Ring([sbt(p4, f"ss{i}", [128, 4], F32) for i in range(2)])
            hbr = Ring([sbt(p4, f"hb{i}", [128, D], BF16) for i in range(2)])
            h2T = sbt(p4, "h2T", [128, 8, 512], BF16)
            actT = sbt(p4, "actT", [128, NFC, 512], BF16)
            sgr = Ring([sbt(p4, f"sg{i}", [128, 512], F32) for i in range(2)])
            x2r = Ring([sbt(p4, f"x2t{i}", [128, D], F32) for i in range(2)])
            psT = Ring([pst(p4, f"psT{i}", [128, D], BF16) for i in range(2)])
            pp = Ring([pst(p4, f"pp{i}", [128, 512], F32) for i in range(6)])
            h2T_free = []
            actT_free = []
            for grp in range(NQB // 4):
                th2 = []
                for bi in range(4):
                    s = grp * 4 + bi
                    x1t, x1free, x1k = x1r.next()
                    tx = SP.dma(x1t[:], x1_scr[s * 128:(s + 1) * 128, :], waits=[x1free])
                    hb, hfree, hk = hbr.next()
                    th = rms_norm_block((junk, ssr), x1t[:], tx, gffn[:], t_gffn, 1e-6, D, hb[:], hfree)
                    x1r.done(x1k, th)
                    ps, pfree, pk = psT.next()
                    tt = None
                    for kc in range(8):
                        tt = PE.op(T.transpose, ps[:, kc * 128:(kc + 1) * 128], hb[:, kc * 128:(kc + 1) * 128], ident[:],
                                   waits=[th, pfree] if kc == 0 else ())
                    hbr.done(hk, tt)
                    te = ACT.op(A.copy, out=h2T[:, :, bi * 128:(bi + 1) * 128], in_=ps[:].rearrange("p (a b) -> p a b", a=8), waits=[tt, h2T_free])
                    psT.done(pk, te)
                    th2.append(te)
                h2T_free = []
                tact = []
                last_mm = None
                for f in range(NFC):
                    psg, pfree, pkg = pp.next()
                    tmg = None
                    for kc in range(8):
                        tmg = PE.op(T.matmul, psg[:], lhsT=w1[:, kc, f * 128:(f + 1) * 128], rhs=h2T[:, kc, :], start=(kc == 0), stop=(kc == 7),
                                    waits=[th2, pfree, tw1] if kc == 0 else ())
                    psu, pfree, pku = pp.next()
                    tmu = None
                    for kc in range(8):
                        tmu = PE.op(T.matmul, psu[:], lhsT=w1[:, kc, DFF + f * 128:DFF + (f + 1) * 128], rhs=h2T[:, kc, :], start=(kc == 0), stop=(kc == 7),
                                    waits=[pfree] if kc == 0 else ())
                    last_mm = tmu
                    sg, sfree, sk = sgr.next()
                    ta = ACT.op(A.activation, out=sg[:], in_=psg[:], func=AF.Silu, waits=[tmg, sfree])
                    pp.done(pkg, ta)
                    tb_ = DVE.op(V.tensor_tensor, out=actT[:, f, :], in0=psu[:], in1=sg[:], op=ALU.mult, waits=[tmu, ta, actT_free])
                    pp.done(pku, tb_)
                    sgr.done(sk, tb_)
                    tact.append(tb_)
                h2T_free = [last_mm]
                actT_free = []
                last_o = None
                for bi in range(4):
                    s = grp * 4 + bi
                    x2, x2free, x2k = x2r.next()
                    tx2 = SP.dma(x2[:], x1_scr[s * 128:(s + 1) * 128, :], waits=[x2free])
                    xtoks = []
                    for nc_ in range(2):
                        ps, pfree, pk = pp.next()
                        tm = None
                        for f in range(NFC):
                            tm = PE.op(T.matmul, ps[:], lhsT=actT[:, f, bi * 128:(bi + 1) * 128], rhs=w2[:, f, nc_ * 512:(nc_ + 1) * 512],
                                       start=(f == 0), stop=(f == NFC - 1), waits=[tact, pfree, tw2] if f == 0 else ())
                        last_o = tm
                        a = DVE.op(V.tensor_tensor, out=x2[:, nc_ * 512:(nc_ + 1) * 512], in0=ps[:], in1=x2[:, nc_ * 512:(nc_ + 1) * 512], op=ALU.add,
                                   waits=[tm, tx2])
                        pp.done(pk, a)
                        xtoks.append(a)
                    e = rms_norm_block((junk, ssr), x2[:], xtoks, gfin[:], t_gfin, 1e-6, D, x2[:], [])
                    td = SP.dma(out[s * 128:(s + 1) * 128, :], x2[:], waits=[e])
                    x2r.done(x2k, td)
                actT_free = [last_o]
            barrier()
    return nc


_NC_CACHE = {}


def _get_nc(debug=False):
    if debug not in _NC_CACHE:
        _NC_CACHE[debug] = build(debug)
    return _NC_CACHE[debug]


def make_in_maps(inputs):
    x = np.ascontiguousarray(np.asarray(inputs["x"], dtype=np.float32))
    pos = np.asarray(inputs["positions"]).astype(np.int32)
    in_maps = []
    kk = np.arange(512)[None, :]
    qq = np.arange(128)[:, None]
    for c in range(8):
        b, j = c // 4, c % 4
        blocks = [4 * s + j for s in range(NQB)]
        xqc = np.concatenate([x[b, q * 128:(q + 1) * 128] for q in blocks], axis=0)
        posk = np.ascontiguousarray(pos[b].reshape(NTB, 128).T)
        posq = np.ascontiguousarray(np.stack([pos[b, q * 128:(q + 1) * 128] for q in blocks], axis=1))
        cm = np.where(kk <= j * 128 + qq, 0.0, NEG).astype(np.float32)
        m = {"xs": x[b], "xq": np.ascontiguousarray(xqc), "posk": posk, "posq": posq, "cmask": cm}
        for name in ["norm_mix_g", "w_in", "idx_k_norm_g", "idx_k_norm_b", "diff_lambda_q1", "diff_lambda_k1",
                     "diff_lambda_q2", "diff_lambda_k2", "diff_subln_g", "gate_b", "w_branch_dsa", "w_branch_diff",
                     "w_out", "norm_ffn_g", "w_ffn_in", "w_ffn_out"]:
            m[name] = np.ascontiguousarray(np.asarray(inputs[name], dtype=np.float32)[0])
        m["norm_final_g"] = np.ascontiguousarray(np.asarray(inputs["norm_final_g"], dtype=np.float32))
        in_maps.append(m)
    return in_maps


def kernel(**inputs):
    nc = _get_nc(False)
    in_maps = make_in_maps(inputs)
    res = run_bass_kernel_spmd(nc, in_maps, core_ids=list(range(8)))
    outp = np.zeros((2, S, D), dtype=np.float32)
    for c in range(8):
        b, j = c // 4, c % 4
        o = res.results[c]["out"]
        for s in range(NQB):
            q = 4 * s + j
            outp[b, q * 128:(q + 1) * 128] = o[s * 128:(s + 1) * 128]
    return outp
```

```python
import os
import math
import numpy as np
from contextlib import ExitStack
import concourse.bass as bass
import concourse.mybir as mybir
from concourse.bass_utils import run_bass_kernel_spmd

F32 = mybir.dt.float32
F32R = mybir.dt.float32r
BF16 = mybir.dt.bfloat16
I32 = mybir.dt.int32
AF = mybir.ActivationFunctionType
ALU = mybir.AluOpType
AX = mybir.AxisListType

S = 8192
D = 1024
NTB = S // 128
NQB = 16
NQ = NQB * 128
DFF = 2816
NFC = DFF // 128
TOPK = 256
NBIS = 16
NEG = -30000.0
C_QA, C_KA, C_VA, C_QI, C_KI, C_WI, C_QB, C_KB, C_VB, C_G = 0, 512, 1024, 1536, 1792, 1824, 1832, 2344, 2856, 3368
D_IN = 5416
VW = 132
TWO_PI = 2.0 * math.pi


class Eng:
    def __init__(self, nc, es, eng, name):
        self.eng = eng
        self.name = name
        self.sem = es.enter_context(nc.semaphore("sem_" + name))
        self.count = 0
        self.seen = {}

    def wait(self, toks):
        for t in toks:
            if t is None:
                continue
            if isinstance(t, list):
                self.wait(t)
                continue
            sem, val, key = t
            if self.seen.get(key, 0) >= val:
                continue
            self.eng.wait_ge(sem, val)
            self.seen[key] = val

    def op(self, fn, *args, waits=(), **kw):
        self.wait(waits)
        inst = fn(*args, **kw)
        self.count += 1
        inst.then_inc(self.sem, 1)
        tok = (self.sem, self.count, self.name)
        self.seen[self.name] = self.count - 1 if False else self.seen.get(self.name, 0)
        return tok


class DmaQ:
    def __init__(self, nc, es, eng, name, nsem=12):
        self.eng = eng
        self.name = name
        self.sems = [es.enter_context(nc.semaphore(f"dsem_{name}_{i}")) for i in range(nsem)]
        self.vals = [0] * nsem
        self.i = 0
        self.seen = {}

    def wait(self, toks):
        for t in toks:
            if t is None:
                continue
            if isinstance(t, list):
                self.wait(t)
                continue
            sem, val, key = t
            if self.seen.get(key, 0) >= val:
                continue
            self.eng.wait_ge(sem, val)
            self.seen[key] = val

    def dma(self, out, in_, waits=(), **kw):
        k = self.i
        self.i = (self.i + 1) % len(self.sems)
        key = f"{self.name}_{k}"
        if self.vals[k] > 0:
            self.wait([(self.sems[k], self.vals[k], key)])
        self.wait(waits)
        self.vals[k] += 16
        self.eng.dma_start(out=out, in_=in_, **kw).then_inc(self.sems[k], 16)
        return (self.sems[k], self.vals[k], key)

    def all_toks(self):
        return [(self.sems[k], self.vals[k], f"{self.name}_{k}") for k in range(len(self.sems)) if self.vals[k] > 0]


class Ring:
    def __init__(self, tiles):
        self.tiles = tiles
        self.rd = [[] for _ in tiles]
        self.i = -1

    def next(self):
        self.i = (self.i + 1) % len(self.tiles)
        k = self.i
        toks = self.rd[k]
        self.rd[k] = []
        return self.tiles[k], toks, k

    def done(self, k, tok):
        self.rd[k].append(tok)


def build(debug=False):
    nc = bass.Bass("TRN2", target_bir_lowering=False)
    dt_in = lambda n, s, d=F32: nc.dram_tensor(n, s, d, kind="ExternalInput").ap()
    xs = dt_in("xs", [S, D])
    xq = dt_in("xq", [NQ, D])
    posk = dt_in("posk", [128, NTB], I32)
    posq = dt_in("posq", [128, NQB], I32)
    cmask = dt_in("cmask", [128, 512])
    norm_mix_g = dt_in("norm_mix_g", [D])
    w_in = dt_in("w_in", [D, D_IN])
    idx_g = dt_in("idx_k_norm_g", [32])
    idx_b = dt_in("idx_k_norm_b", [32])
    lq1 = dt_in("diff_lambda_q1", [64])
    lk1 = dt_in("diff_lambda_k1", [64])
    lq2 = dt_in("diff_lambda_q2", [64])
    lk2 = dt_in("diff_lambda_k2", [64])
    subln_g = dt_in("diff_subln_g", [128])
    gate_b = dt_in("gate_b", [2048])
    w_bd = dt_in("w_branch_dsa", [512, D])
    w_bf = dt_in("w_branch_diff", [512, D])
    w_out = dt_in("w_out", [D, D])
    norm_ffn_g = dt_in("norm_ffn_g", [D])
    w_f1 = dt_in("w_ffn_in", [D, 2 * DFF])
    w_f2 = dt_in("w_ffn_out", [DFF, D])
    norm_fin_g = dt_in("norm_final_g", [D])
    out = nc.dram_tensor("out", [NQ, D], F32, kind="ExternalOutput").ap()
    skind = "ExternalOutput" if debug else "Internal"
    kT_scr = nc.dram_tensor("kT_scr", [8, 128, S], BF16, kind=skind).ap()
    kiT_scr = nc.dram_tensor("kiT_scr", [32, S], BF16, kind=skind).ap()
    v_scr = nc.dram_tensor("v_scr", [S, 8 * VW], BF16, kind=skind).ap()
    qT_scr = nc.dram_tensor("qT_scr", [8, 128, NQ], BF16, kind=skind).ap()
    qiT_scr = nc.dram_tensor("qiT_scr", [64, 4, NQ], BF16, kind=skind).ap()
    o_scr = nc.dram_tensor("o_scr", [NQ, D], BF16, kind=skind).ap()
    x1_scr = nc.dram_tensor("x1_scr", [NQ, D], F32, kind=skind).ap()

    with ExitStack() as es:
        uid = [0]

        def sbt(st, n, s, d):
            uid[0] += 1
            return st.enter_context(nc.sbuf_tensor(f"{n}_{uid[0]}", s, d))

        def pst(st, n, s, d):
            uid[0] += 1
            return st.enter_context(nc.psum_tensor(f"{n}_{uid[0]}", s, d))

        ident = sbt(es, "ident", [128, 128], BF16)
        early = ExitStack()
        cosk = sbt(early, "cosk", [128, NTB, 8], F32)
        sink = sbt(early, "sink", [128, NTB, 8], F32)
        cosq = sbt(early, "cosq", [128, NQB, 8], F32)
        sinq = sbt(early, "sinq", [128, NQB, 8], F32)
        gmix = sbt(early, "gmix", [128, D], F32)
        wabs = sbt(early, "wabs", [128, NQB, 8], F32)
        wsgn = sbt(early, "wsgn", [128, NQB, 8], F32)
        nlam = sbt(early, "nlam", [128, 1], F32)
        small = sbt(early, "small", [128, 64], F32)
        cb = sbt(early, "cb", [128, 512], BF16)
        cbf = sbt(early, "cbf", [128, 512], F32)

        es.enter_context(nc.Block())
        PE = Eng(nc, es, nc.tensor, "pe")
        ACT = Eng(nc, es, nc.scalar, "act")
        DVE = Eng(nc, es, nc.vector, "dve")
        POOL = Eng(nc, es, nc.gpsimd, "pool")
        SP = DmaQ(nc, es, nc.sync, "sp", 16)
        GQ = DmaQ(nc, es, nc.gpsimd, "gq", 8)
        V = nc.vector
        A = nc.scalar
        T = nc.tensor

        def bcast(ap1d, n=128):
            return ap1d.partition_broadcast(n)

        def barrier():
            toks = [(e.sem, e.count, e.name) for e in (PE, ACT, DVE, POOL) if e.count > 0]
            toks += SP.all_toks() + GQ.all_toks()
            for e in (PE, ACT, DVE, POOL, SP):
                e.wait(toks)

        t = POOL.op(nc.gpsimd.memset, ident[:], 1.0)
        t_ident = POOL.op(nc.gpsimd.affine_select, out=ident[:], in_=ident[:], pattern=[[-1, 128]],
                          compare_op=ALU.is_equal, fill=0.0, base=0, channel_multiplier=1, waits=[t])
        t_gmix = SP.dma(gmix[:], bcast(norm_mix_g))
        t_cbf = SP.dma(cbf[:], cmask)
        t_cb = DVE.op(V.tensor_copy, out=cb[:], in_=cbf[:], waits=[t_cbf])

        with ExitStack() as p0:
            lt = sbt(p0, "lt", [128, 4, 64], F32)
            lj = sbt(p0, "lj", [128, 64], F32)
            tl = [SP.dma(lt[:, i, :], bcast(a)) for i, a in enumerate([lq1, lk1, lq2, lk2])]
            t1 = DVE.op(V.tensor_tensor, out=lj[:], in0=lt[:, 0, :], in1=lt[:, 1, :], op=ALU.mult, waits=tl)
            t1 = DVE.op(V.tensor_reduce, out=small[:, 0:1], in_=lj[:], axis=AX.X, op=ALU.add, waits=[t1])
            t2 = DVE.op(V.tensor_tensor, out=lj[:], in0=lt[:, 2, :], in1=lt[:, 3, :], op=ALU.mult, waits=[t1])
            t2 = DVE.op(V.tensor_reduce, out=small[:, 1:2], in_=lj[:], axis=AX.X, op=ALU.add, waits=[t2])
            t3 = ACT.op(A.activation, out=small[:, 2:4], in_=small[:, 0:2], func=AF.Exp, waits=[t2])
            t_nlam = DVE.op(V.scalar_tensor_tensor, out=nlam[:], in0=small[:, 3:4], scalar=-0.2, in1=small[:, 2:3],
                            op0=ALU.add, op1=ALU.subtract, waits=[t3])

            invf = sbt(p0, "invf", [128, 8], F32)
            tinv = None
            for i in range(8):
                fv = float(np.power(np.float32(500000.0), -np.float32(2 * i) / np.float32(16)))
                tinv = DVE.op(V.memset, invf[:, i:i + 1], fv)

            def rope_table(pos_ap, n, cos_t, sin_t, nm):
                pi_ = sbt(p0, "pi_" + nm, [128, n], I32)
                pf = sbt(p0, "pf_" + nm, [128, n], F32)
                ang = sbt(p0, "ang_" + nm, [128, n, 8], F32)
                yy = sbt(p0, "yy_" + nm, [128, n, 8], F32)
                ni = sbt(p0, "ni_" + nm, [128, n, 8], I32)
                tp = SP.dma(pi_[:], pos_ap)
                a = DVE.op(V.tensor_copy, out=pf[:], in_=pi_[:], waits=[tp])
                a = DVE.op(V.tensor_tensor, out=ang[:], in0=pf[:].unsqueeze(2).to_broadcast([128, n, 8]),
                           in1=invf[:].unsqueeze(1).to_broadcast([128, n, 8]), op=ALU.mult, waits=[a, tinv])

                def reduce_sin(src_add, dst):
                    b = DVE.op(V.tensor_scalar, out=yy[:], in0=ang[:], scalar1=src_add, scalar2=1.0 / TWO_PI,
                               op0=ALU.add, op1=ALU.mult, waits=[a])
                    b = DVE.op(V.tensor_copy, out=ni[:], in_=yy[:], waits=[b])
                    b = DVE.op(V.tensor_copy, out=yy[:], in_=ni[:], waits=[b])
                    c1 = 6.28125
                    c2 = TWO_PI - 6.28125
                    b = DVE.op(V.scalar_tensor_tensor, out=dst, in0=yy[:], scalar=-c1, in1=ang[:], op0=ALU.mult, op1=ALU.add, waits=[b])
                    b = DVE.op(V.scalar_tensor_tensor, out=dst, in0=yy[:], scalar=-c2, in1=dst, op0=ALU.mult, op1=ALU.add, waits=[b])
                    b = DVE.op(V.tensor_scalar, out=dst, in0=dst, scalar1=src_add, scalar2=3.1415925, op0=ALU.add, op1=ALU.min, waits=[b])
                    b = DVE.op(V.tensor_scalar, out=dst, in0=dst, scalar1=-3.1415925, scalar2=None, op0=ALU.max, waits=[b])
                    return ACT.op(A.activation, out=dst, in_=dst, func=AF.Sin, waits=[b])
                ts = reduce_sin(0.0, sin_t[:])
                tc = reduce_sin(math.pi / 2.0, cos_t[:])
                return [ts, tc]
            t_ropek = rope_table(posk, NTB, cosk, sink, "k")
            t_ropeq = rope_table(posq, NQB, cosq, sinq, "q")
            barrier()

        def rms_norm_block(st_rings, x_tile, tx, g_tile, tg, eps, n_feat, hb_tile, hb_free):
            junk, ssr = st_rings
            jt, jfree, jk = junk.next()
            col, cfree, ck = ssr.next()
            a = ACT.op(A.activation, out=jt[:, :n_feat], in_=x_tile, func=AF.Square, accum_out=col[:, 0:1],
                       waits=[tx, jfree, cfree])
            junk.done(jk, a)
            b = DVE.op(V.tensor_scalar, out=col[:, 1:2], in0=col[:, 0:1], scalar1=1.0 / n_feat, scalar2=eps,
                       op0=ALU.mult, op1=ALU.add, waits=[a])
            c = ACT.op(A.activation, out=col[:, 2:3], in_=col[:, 1:2], func=AF.Sqrt, waits=[b])
            d = DVE.op(V.reciprocal, out=col[:, 3:4], in_=col[:, 2:3], waits=[c])
            e = DVE.op(V.scalar_tensor_tensor, out=hb_tile, in0=x_tile, scalar=col[:, 3:4], in1=g_tile,
                       op0=ALU.mult, op1=ALU.mult, waits=[d, tg, hb_free])
            ssr.done(ck, e)
            return e

        rope_last = []

        def rope_apply(tile3, H, half, cs, sn, tmp, waits):
            x1 = tile3[:, :, 0:half]
            x2 = tile3[:, :, half:2 * half]
            cB = cs.unsqueeze(1).to_broadcast([128, H, half])
            sB = sn.unsqueeze(1).to_broadcast([128, H, half])
            tv = lambda i: tmp[:, i, 0:H * half].rearrange("p (h d) -> p h d", h=H)
            waits = list(waits) + rope_last
            a1 = DVE.op(V.tensor_tensor, out=tv(0), in0=x1, in1=cB, op=ALU.mult, waits=waits)
            a2 = DVE.op(V.tensor_tensor, out=tv(1), in0=x2, in1=sB, op=ALU.mult, waits=waits)
            a3 = DVE.op(V.tensor_tensor, out=tv(2), in0=x2, in1=cB, op=ALU.mult, waits=waits)
            a4 = DVE.op(V.tensor_tensor, out=tv(3), in0=x1, in1=sB, op=ALU.mult, waits=waits)
            b1 = DVE.op(V.tensor_tensor, out=x1, in0=tv(0), in1=tv(1), op=ALU.subtract, waits=[a1, a2, a3, a4])
            b2 = DVE.op(V.tensor_tensor, out=x2, in0=tv(2), in1=tv(3), op=ALU.add, waits=[a1, a2, a3, a4])
            rope_last[:] = [b1, b2]
            return [b1, b2]

        with ExitStack() as p1:
            NKV = 2080
            NQC = 1288
            wkv = sbt(p1, "wkv", [128, 8, NKV], BF16)
            wq = sbt(p1, "wq", [128, 8, NQC], BF16)
            w_in_v = w_in.rearrange("(kc p) n -> p kc n", p=128)
            tw = []
            for (dst0, c0, n) in [(0, C_KA, 512), (512, C_KB, 512), (1024, C_VA, 512), (1536, C_VB, 512), (2048, C_KI, 32)]:
                tw.append(GQ.dma(wkv[:, :, dst0:dst0 + n], w_in_v[:, :, c0:c0 + n]))
            twq = []
            for (dst0, c0, n) in [(0, C_QA, 512), (512, C_QB, 512), (1024, C_QI, 256), (1280, C_WI, 8)]:
                twq.append(GQ.dma(wq[:, :, dst0:dst0 + n], w_in_v[:, :, c0:c0 + n]))
            lng = sbt(p1, "lng", [128, 32], F32)
            lnb = sbt(p1, "lnb", [128, 32], F32)
            t_lng = SP.dma(lng[:], bcast(idx_g))
            t_lnb = SP.dma(lnb[:], bcast(idx_b))

            xr = Ring([sbt(p1, f"xt{i}", [128, D], F32) for i in range(3)])
            junk = Ring([sbt(p1, f"junk{i}", [128, D], BF16) for i in range(1)])
            ssr = Ring([sbt(p1, f"ss{i}", [128, 4], F32) for i in range(3)])
            hbr = Ring([sbt(p1, f"hb{i}", [128, D], BF16) for i in range(2)])
            hTr = Ring([sbt(p1, f"hT{i}", [128, 8, 128], BF16) for i in range(3)])
            kfr = Ring([sbt(p1, f"kf{i}", [128, 1024], F32) for i in range(2)])
            kbr = Ring([sbt(p1, f"kb{i}", [128, 1024], BF16) for i in range(3)])
            kTr = Ring([sbt(p1, f"kTt{i}", [128, 8, 128], BF16) for i in range(2)])
            vtr = Ring([sbt(p1, f"vt{i}", [128, 8, VW], BF16) for i in range(2)])
            rtmp = sbt(p1, "rtmp", [128, 4, 128], F32)
            kif = sbt(p1, "kif", [128, 8], F32)
            kic = sbt(p1, "kic", [128, 32], F32)
            kij = sbt(p1, "kij", [128, 32], F32)
            kibr = Ring([sbt(p1, f"kib{i}", [128, 32], BF16) for i in range(3)])
            kiTr = Ring([sbt(p1, f"kiT{i}", [32, 128], BF16) for i in range(2)])
            qwr = Ring([sbt(p1, f"qw{i}", [128, 264], F32) for i in range(2)])
            qibr = Ring([sbt(p1, f"qib{i}", [128, 256], BF16) for i in range(3)])
            qiTr = Ring([sbt(p1, f"qiT{i}", [32, 8, 128], BF16) for i in range(2)])
            psT = Ring([pst(p1, f"psT{i}", [128, D], BF16) for i in range(2)])
            pp = Ring([pst(p1, f"pp{i}", [128, 512], F32) for i in range(4)])
            pkT = Ring([pst(p1, f"pkT{i}", [128, D], BF16) for i in range(2)])
            t_vinit = []
            for vt_ in vtr.tiles:
                t0 = POOL.op(nc.gpsimd.memset, vt_[:], 0.0)
                t1_ = POOL.op(nc.gpsimd.memset, vt_[:, 0:4, :].rearrange("p a (s e) -> p (a s) e", s=2)[:, :, 64:65], 1.0, waits=[t0])
                t_vinit.append(POOL.op(nc.gpsimd.memset, vt_[:, 4:8, 128:129], 1.0, waits=[t0, t1_]))

            def aevac(out_ap, in_ap, waits):
                return ACT.op(A.copy, out=out_ap, in_=in_ap, waits=waits)

            def stageA(src_rows):
                xt, xfree, xk = xr.next()
                tx = SP.dma(xt[:], src_rows, waits=xfree)
                hb, hfree, hk = hbr.next()
                th = rms_norm_block((junk, ssr), xt[:], tx, gmix[:], t_gmix, 1e-6, D, hb[:], hfree)
                xr.done(xk, th)
                ps, pfree, pk = psT.next()
                tt = None
                for kc in range(8):
                    tt = PE.op(T.transpose, ps[:, kc * 128:(kc + 1) * 128], hb[:, kc * 128:(kc + 1) * 128], ident[:],
                               waits=[th, t_ident, pfree] if kc == 0 else ())
                hbr.done(hk, tt)
                hT, tfree, tk = hTr.next()
                te = aevac(hT[:].rearrange("p a b -> p (a b)"), ps[:], [tt, tfree])
                psT.done(pk, te)
                return hT, te, tk

            def project(hT, th, w_tile, c0, n, wtoks):
                ps, pfree, pk = pp.next()
                tt = None
                for kc in range(8):
                    tt = PE.op(T.matmul, ps[:, 0:n], lhsT=hT[:, kc, :], rhs=w_tile[:, kc, c0:c0 + n],
                               start=(kc == 0), stop=(kc == 7), waits=[th, pfree, wtoks] if kc == 0 else ())
                return ps, tt, pk

            def transpose_out(src_bf, tsrc, nchunk, width, ring_sb, dst_dram):
                ps, pfree, pk = pkT.next()
                tt = None
                for c in range(nchunk):
                    tt = PE.op(T.transpose, ps[0:width, c * 128:(c + 1) * 128], src_bf[:, c * width:(c + 1) * width], ident[:],
                               waits=[tsrc, pfree, t_ident] if c == 0 else ())
                sbT, sfree, sk = ring_sb.next()
                te = aevac(sbT[:].rearrange("p a b -> p (a b)") if len(sbT.shape) == 3 else sbT[:],
                           ps[0:width, 0:nchunk * 128], [tt, sfree])
                pkT.done(pk, te)
                td = GQ.dma(dst_dram, sbT[:], waits=[te])
                ring_sb.done(sk, td)
                return tt

            def transpose_out_qi(src_bf, tsrc, s):
                ps, pfree, pk = pkT.next()
                tt = None
                for c in range(8):
                    tt = PE.op(T.transpose, ps[0:32, c * 128:(c + 1) * 128], src_bf[:, c * 32:(c + 1) * 32], ident[:],
                               waits=[tsrc, pfree, t_ident] if c == 0 else ())
                sbT, sfree, sk = qiTr.next()
                te = aevac(sbT[:].rearrange("p a b -> p (a b)"), ps[0:32, 0:1024], [tt, sfree])
                pkT.done(pk, te)
                for g in range(2):
                    td = GQ.dma(qiT_scr[g * 32:(g + 1) * 32, :, s * 128:(s + 1) * 128], sbT[:, g::2, :], waits=[te])
                    qiTr.done(sk, td)
                return tt

            def qk_pair(hT, th, w_tile, wtoks, cs, sn, scale, dst_dram):
                kf, kfree, kk = kfr.next()
                tes = []
                for gi in range(2):
                    ps, tmm, pk = project(hT, th, w_tile, gi * 512, 512, wtoks)
                    te = aevac(kf[:, gi * 512:(gi + 1) * 512], ps[:], [tmm, kfree])
                    pp.done(pk, te)
                    tes.append(te)
                tr = rope_apply(kf[:].rearrange("p (h d) -> p h d", h=16), 16, 8, cs, sn, rtmp, tes)
                kb, bfree, bk = kbr.next()
                if scale == 1.0:
                    tcst = DVE.op(V.tensor_copy, out=kb[:], in_=kf[:], waits=[tr, bfree])
                else:
                    tcst = DVE.op(V.tensor_scalar, out=kb[:], in0=kf[:], scalar1=scale, scalar2=None, op0=ALU.mult, waits=[tr, bfree])
                kfr.done(kk, tcst)
                return kb, bk, tcst

            def stageB_kv(tb, hT, th, hk):
                cs = cosk[:, tb, :]
                sn = sink[:, tb, :]
                kb, bk, tcst = qk_pair(hT, th, wkv, tw, cs, sn, 1.0, None)
                vt, vfree, vk = vtr.next()
                tvs = []
                for gi in (2, 3):
                    ps, tmm, pk = project(hT, th, wkv, gi * 512, 512, tw)
                    if gi == 2:
                        dstv = vt[:, 0:4, :].rearrange("p a (s e) -> p (a s) e", s=2)[:, :, 0:64]
                        te = aevac(dstv, ps[:].rearrange("p (h e) -> p h e", e=64), [tmm, vfree, t_vinit])
                    else:
                        te = aevac(vt[:, 4:8, 0:128], ps[:].rearrange("p (h e) -> p h e", e=128), [tmm, vfree, t_vinit])
                    pp.done(pk, te)
                    tvs.append(te)
                td = GQ.dma(v_scr[tb * 128:(tb + 1) * 128, :], vt[:].rearrange("p a b -> p (a b)"), waits=tvs)
                vtr.done(vk, td)
                ps, tmm, pk = project(hT, th, wkv, 2048, 32, tw)
                hTr.done(hk, tmm)
                a = DVE.op(V.tensor_reduce, out=kif[:, 0:1], in_=ps[:, 0:32], axis=AX.X, op=ALU.add, waits=[tmm])
                a = DVE.op(V.tensor_scalar, out=kif[:, 1:2], in0=kif[:, 0:1], scalar1=1.0 / 32, scalar2=None, op0=ALU.mult, waits=[a])
                a = DVE.op(V.tensor_scalar, out=kic[:], in0=ps[:, 0:32], scalar1=kif[:, 1:2], scalar2=None, op0=ALU.subtract, waits=[a])
                pp.done(pk, a)
                b = ACT.op(A.activation, out=kij[:], in_=kic[:], func=AF.Square, accum_out=kif[:, 2:3], waits=[a])
                b = DVE.op(V.tensor_scalar, out=kif[:, 3:4], in0=kif[:, 2:3], scalar1=1.0 / 32, scalar2=1e-6, op0=ALU.mult, op1=ALU.add, waits=[b])
                b = ACT.op(A.activation, out=kif[:, 4:5], in_=kif[:, 3:4], func=AF.Sqrt, waits=[b])
                b = DVE.op(V.reciprocal, out=kif[:, 5:6], in_=kif[:, 4:5], waits=[b])
                b = DVE.op(V.scalar_tensor_tensor, out=kic[:], in0=kic[:], scalar=kif[:, 5:6], in1=lng[:], op0=ALU.mult, op1=ALU.mult, waits=[b, t_lng])
                b = DVE.op(V.tensor_tensor, out=kic[:], in0=kic[:], in1=lnb[:], op=ALU.add, waits=[b, t_lnb])
                csi = cosk[:, tb, :].rearrange("p (a two) -> p a two", two=2)[:, :, 0]
                sni = sink[:, tb, :].rearrange("p (a two) -> p a two", two=2)[:, :, 0]
                tr = rope_apply(kic[:].rearrange("p (h d) -> p h d", h=1), 1, 4, csi, sni, rtmp, [b])
                kib, bfree, bk2 = kibr.next()
                tc2 = DVE.op(V.tensor_copy, out=kib[:], in_=kic[:], waits=[tr, bfree])

                def b2():
                    tlast = transpose_out(kb, tcst, 8, 128, kTr, kT_scr[:, :, tb * 128:(tb + 1) * 128].rearrange("c f t -> f c t"))
                    kbr.done(bk, tlast)
                    tl2 = transpose_out(kib, tc2, 1, 32, kiTr, kiT_scr[:, tb * 128:(tb + 1) * 128])
                    kibr.done(bk2, tl2)
                return b2

            def stageB_q(s, hT, th, hk):
                cs = cosq[:, s, :]
                sn = sinq[:, s, :]
                kb, bk, tcst = qk_pair(hT, th, wq, twq, cs, sn, 0.125, None)
                ps, tmm, pk = project(hT, th, wq, 1024, 264, twq)
                hTr.done(hk, tmm)
                qw, qfree, qk = qwr.next()
                te = aevac(qw[:], ps[:, 0:264], [tmm, qfree])
                pp.done(pk, te)
                csi = cosq[:, s, :].rearrange("p (a two) -> p a two", two=2)[:, :, 0]
                sni = sinq[:, s, :].rearrange("p (a two) -> p a two", two=2)[:, :, 0]
                tr = rope_apply(qw[:, 0:256].rearrange("p (h d) -> p h d", h=8), 8, 4, csi, sni, rtmp, [te])
                a1 = DVE.op(V.tensor_scalar, out=wabs[:, s, :], in0=qw[:, 256:264], scalar1=1.0 / 16, scalar2=None, op0=ALU.mult, waits=[te])
                a2 = a1
                qib, bfree, bk2 = qibr.next()
                tc2 = DVE.op(V.tensor_copy, out=qib[:], in_=qw[:, 0:256], waits=[tr, bfree])
                qwr.done(qk, [tc2, a1, a2])

                def b2():
                    tlast = transpose_out(kb, tcst, 8, 128, kTr, qT_scr[:, :, s * 128:(s + 1) * 128].rearrange("c f t -> f c t"))
                    kbr.done(bk, tlast)
                    tl2 = transpose_out_qi(qib, tc2, s)
                    qibr.done(bk2, tl2)
                return b2

            items = [("kv", tb, xs[tb * 128:(tb + 1) * 128, :]) for tb in range(NTB)] + \
                    [("q", s, xq[s * 128:(s + 1) * 128, :]) for s in range(NQB)]
            prev = None
            prev_b2 = None
            for it in items + [None]:
                cur = None
                if it is not None:
                    cur = (it, stageA(it[2]))
                b2 = None
                if prev is not None:
                    (kind, idx, _), (hT, th, hk) = prev
                    if kind == "kv":
                        b2 = stageB_kv(idx, hT, th, hk)
                    else:
                        b2 = stageB_q(idx, hT, th, hk)
                if prev_b2 is not None:
                    prev_b2()
                prev_b2 = b2
                prev = cur
            if prev_b2 is not None:
                prev_b2()
            barrier()

        with ExitStack() as p2:
            kiT = sbt(p2, "kiT", [64, S], BF16)
            t_kiT = [SP.dma(kiT[g * 32:(g + 1) * 32, :], kiT_scr) for g in range(2)]
            gsub = sbt(p2, "gsub", [128, 128], F32)
            t_gs = SP.dma(gsub[:], bcast(subln_g))
            t_gs = DVE.op(V.tensor_scalar, out=gsub[:], in0=gsub[:], scalar1=0.8, scalar2=None, op0=ALU.mult, waits=[t_gs])
            ident2 = sbt(p2, "ident2", [128, 2, 128], BF16)
            t_id2 = [DVE.op(V.tensor_copy, out=ident2[:, i, :], in_=ident[:], waits=[t_ident]) for i in range(2)]
            Kr = Ring([sbt(p2, f"Kb{i}", [128, S], BF16) for i in range(2)])
            Vr = Ring([sbt(p2, f"Vb{i}", [128, NTB, VW], BF16) for i in range(2)])
            Mb = [sbt(p2, f"Mb{i}", [128, S], BF16) for i in range(2)]
            Isc = sbt(p2, "Isc", [128, S], F32)
            qbd_tiles = [sbt(p2, f"qbd{i}", [128, 2, 128], BF16) for i in range(3)]
            t_qz = [POOL.op(nc.gpsimd.memset, q_[:], 0.0) for q_ in qbd_tiles]
            qbr = Ring(qbd_tiles)
            qiT = [sbt(p2, f"qiTs{i}", [64, 4, 128], BF16) for i in range(2)]
            rl = Ring([sbt(p2, f"rl{i}", [128, 512], BF16) for i in range(6)])
            identf = sbt(p2, "identf", [128, 128], F32)
            t_idf = DVE.op(V.tensor_copy, out=identf[:], in_=ident[:], waits=[t_ident])
            dgb = [sbt(p2, f"dgb{i}", [128, 8, 128], BF16) for i in range(2)]
            etr = Ring([sbt(p2, f"et{i}", [128, 512], BF16) for i in range(4)])
            otr = Ring([sbt(p2, f"ot{i}", [128, D], BF16) for i in range(2)])
            of32 = sbt(p2, "of32", [128, 128], F32)
            oj = sbt(p2, "oj", [128, 128], F32)
            bs = sbt(p2, "bs", [128, 16], F32)
            es_ = sbt(p2, "es_", [128, 16], F32)
            pss = Ring([pst(p2, f"pss{i}", [128, 512], F32) for i in range(4)])
            pI = pst(p2, "pI", [128, 512], F32)
            pacc = Ring([pst(p2, f"pacc{i}", [128, 512], F32) for i in range(3)])
            of1 = sbt(p2, "of1", [128, 128], F32)
            of2 = sbt(p2, "of2", [128, 128], F32)
            ez = sbt(p2, "ez", [128, 8], F32)
            Mb_ready = [None, None]
            Mb_readers = [[], []]
            qiT_readers = [[], []]
            dg_readers = [[], []]
            Isc_free = [[]]
            pI_free = [[]]

            def prep(s):
                li = s % 2
                nk = (4 * s + 4) * 128
                nch = nk // 512
                wb = 0.76 * (s + 1)
                tq = SP.dma(qiT[li][:], qiT_scr[:, :, s * 128:(s + 1) * 128], waits=qiT_readers[li])
                qiT_readers[li] = []
                tdg = None
                for h in range(8):
                    tdg = DVE.op(V.tensor_scalar, out=dgb[li][:, h, :], in0=identf[:], scalar1=wabs[:, s, h:h + 1], scalar2=None, op0=ALU.mult,
                                 waits=[t_idf, dg_readers[li]] if h == 0 else ())
                dg_readers[li] = []
                yield 1.0
                tI = None
                pending = None

                def flush(pend):
                    (pc, pr, items) = pend
                    tacc = None
                    for g, (prt, pta, prk) in enumerate(items):
                        h = 2 * pr + g
                        tacc = PE.op(T.matmul, pI[:], lhsT=dgb[li][:, h, :], rhs=prt[:],
                                     start=(pr == 0 and g == 0), stop=(pr == 3 and g == 1),
                                     waits=[pta, tdg, pI_free[0]])
                        rl.done(prk, tacc)
                    return tacc
                for c in range(nch):
                    for r in range(4):
                        if pending is not None:
                            tacc = flush(pending)
                            if pending[1] == 3:
                                pc = pending[0]
                                tI = DVE.op(V.tensor_copy, out=Isc[:, pc * 512:(pc + 1) * 512], in_=pI[:], waits=[tacc, Isc_free[0]])
                                pI_free[0] = [tI]
                        mm = []
                        for g in range(2):
                            ps, pfree, pk = pss.next()
                            tm = PE.op(T.matmul, ps[:], lhsT=qiT[li][g * 32:(g + 1) * 32, r, :], rhs=kiT[g * 32:(g + 1) * 32, c * 512:(c + 1) * 512],
                                       start=True, stop=True, waits=[tq, t_kiT, pfree])
                            mm.append((ps, pk, tm))
                        items = []
                        for g in range(2):
                            ps, pk, tm = mm[g]
                            rt, rfree, rk = rl.next()
                            if g == 0:
                                ta = ACT.op(A.activation, out=rt[:], in_=ps[:], func=AF.Relu, waits=[tm, rfree])
                            else:
                                ta = DVE.op(V.tensor_scalar, out=rt[:], in0=ps[:], scalar1=0.0, scalar2=None, op0=ALU.max, waits=[tm, rfree])
                            pss.done(pk, ta)
                            items.append((rt, ta, rk))
                        pending = (c, r, items)
                        yield 2.0
                tacc = flush(pending)
                tI = DVE.op(V.tensor_copy, out=Isc[:, pending[0] * 512:(pending[0] + 1) * 512], in_=pI[:], waits=[tacc, Isc_free[0]])
                pI_free[0] = [tI]
                qiT_readers[li].append(tacc)
                dg_readers[li].append(tacc)
                Isc_free[0] = []
                Iv = Isc[:, 0:nk]
                a = DVE.op(V.tensor_reduce, out=bs[:, 0:1], in_=Iv, axis=AX.X, op=ALU.max, waits=[tI])
                yield wb
                a = DVE.op(V.tensor_reduce, out=bs[:, 1:2], in_=Iv, axis=AX.X, op=ALU.min, waits=[a])
                a = DVE.op(V.scalar_tensor_tensor, out=bs[:, 2:3], in0=bs[:, 0:1], scalar=1.0, in1=bs[:, 1:2], op0=ALU.add, op1=ALU.subtract, waits=[a])
                a = DVE.op(V.tensor_tensor, out=Isc[:, nk - 512:nk], in0=Isc[:, nk - 512:nk], in1=cbf[:], op=ALU.add, waits=[a, t_cbf])
                yield wb
                lo = bs[:, 1:2]
                w0 = bs[:, 2:3]
                mid = bs[:, 3:4]
                cnt = bs[:, 4:5]
                gg = bs[:, 5:6]
                for it in range(1, NBIS + 1):
                    sc = 2.0 ** (-it)
                    a = DVE.op(V.tensor_scalar, out=mid, in0=w0, scalar1=sc, scalar2=lo, op0=ALU.mult, op1=ALU.add, waits=[a])
                    a = DVE.op(V.tensor_scalar, out=Mb[li][:, 0:nk], in0=Iv, scalar1=mid, scalar2=None, op0=ALU.is_ge, op1=ALU.add,
                               accum_out=cnt, waits=[a, Mb_readers[li]])
                    Mb_readers[li] = []
                    a = DVE.op(V.tensor_scalar, out=gg, in0=cnt, scalar1=TOPK - 0.5, scalar2=sc, op0=ALU.is_ge, op1=ALU.mult, waits=[a])
                    a = DVE.op(V.scalar_tensor_tensor, out=lo, in0=w0, scalar=gg, in1=lo, op0=ALU.mult, op1=ALU.add, waits=[a])
                    yield wb
                a = DVE.op(V.tensor_scalar, out=Mb[li][:, 0:nk], in0=Iv, scalar1=lo, scalar2=NEG, op0=ALU.is_lt, op1=ALU.mult, waits=[a])
                Mb_ready[li] = a
                Isc_free[0] = [a]

            def prep_steps(s):
                return 1 + 8 * (s + 1) + (2 + NBIS) * 0.76 * (s + 1)

            pump_state = {"gen": None, "budget": 0.0, "rate": 0.0}

            def pump():
                st = pump_state
                if st["gen"] is None:
                    return
                st["budget"] += st["rate"]
                while st["budget"] > 0.0 and st["gen"] is not None:
                    try:
                        st["budget"] -= next(st["gen"])
                    except StopIteration:
                        st["gen"] = None

            def attention(s, p, ot, ofree, owr):
                li = s % 2
                is_dsa = p < 4
                nkb = 4 * s + 4
                kmax = nkb * 128
                Kb, kfree, kk = Kr.next()
                Vb, vfree, vk = Vr.next()
                tK = SP.dma(Kb[:, 0:kmax], kT_scr[p, :, 0:kmax], waits=kfree)
                tV = []
                for b0 in range(0, nkb, 8):
                    nb = min(8, nkb - b0)
                    tV.append(SP.dma(Vb[:, b0:b0 + nb, :], v_scr[b0 * 128:(b0 + nb) * 128, p * VW:(p + 1) * VW].rearrange("(b t) w -> t b w", t=128), waits=vfree))
                qbd, qfree, qk = qbr.next()
                tq = [SP.dma(qbd[m * 64:(m + 1) * 64, m, :], qT_scr[p, m * 64:(m + 1) * 64, s * 128:(s + 1) * 128], waits=[qfree, t_qz]) for m in range(2)]
                accs = [pacc.next() for m in range(2)]
                vw = 66 if is_dsa else 130
                ntile = nkb // 2
                pend = {}

                def do_qk(ti):
                    ps, pfree, pk = pss.next()
                    tm = None
                    for bi in range(2):
                        kb_ = ti * 2 + bi
                        need_mask = is_dsa or (kb_ >= nkb - 4)
                        tm = PE.op(T.matmul, ps[:, bi * 256:(bi + 1) * 256], lhsT=Kb[:, kb_ * 128:(kb_ + 1) * 128],
                                   rhs=qbd[:].rearrange("p a b -> p (a b)"), start=True, stop=not need_mask,
                                   waits=[tK, tq, pfree] if bi == 0 else ())
                        if need_mask:
                            if is_dsa:
                                ml = Mb[li][:, kb_ * 128:(kb_ + 1) * 128]
                                mw = [Mb_ready[li]]
                            else:
                                cbi = kb_ - (nkb - 4)
                                ml = cb[:, cbi * 128:(cbi + 1) * 128]
                                mw = [t_cb]
                            tm = PE.op(T.matmul, ps[:, bi * 256:(bi + 1) * 256], lhsT=ml, rhs=ident2[:].rearrange("p a b -> p (a b)"),
                                       start=False, stop=True, waits=mw + [t_id2])
                    et, efree, ek = etr.next()
                    te = ACT.op(A.activation, out=et[:], in_=ps[:], func=AF.Exp, waits=[tm, efree])
                    pss.done(pk, te)
                    pend[ti] = (et, te, ek)

                tav = [None, None]

                def do_av(ti):
                    et, te, ek = pend.pop(ti)
                    tm = None
                    first = True
                    for bi in range(2):
                        kb_ = ti * 2 + bi
                        for m in range(2):
                            acc, afree, ak = accs[m]
                            rhs = Vb[:, kb_, m * 66:(m + 1) * 66] if is_dsa else Vb[:, kb_, 0:130]
                            tm = PE.op(T.matmul, acc[:, 0:vw], lhsT=et[:, bi * 256 + m * 128:bi * 256 + (m + 1) * 128], rhs=rhs,
                                       start=(kb_ == 0), stop=(kb_ == nkb - 1),
                                       waits=[te, tV, accs[0][1], accs[1][1]] if first else ())
                            first = False
                            tav[m] = tm
                    etr.done(ek, tm)
                LAG = 2
                for ti in range(ntile):
                    do_qk(ti)
                    if ti >= LAG:
                        do_av(ti - LAG)
                    pump()
                for ti in range(max(0, ntile - LAG), ntile):
                    do_av(ti)
                qbr.done(qk, tav[1])
                if is_dsa:
                    Mb_readers[li].append(tav[1])
                Kr.done(kk, tav[1])
                Vr.done(vk, tav[1])
                if is_dsa:
                    for m in range(2):
                        acc, afree, ak = accs[m]
                        a = ACT.op(A.activation, out=ez[:, m:m + 1], in_=acc[:, 64:65], func=AF.Ln, waits=[tav[1]])
                        a = ACT.op(A.activation, out=ez[:, m:m + 1], in_=ez[:, m:m + 1], func=AF.Exp, scale=-1.0, waits=[a])
                        a = ACT.op(A.activation, out=ot[:, p * 128 + m * 64:p * 128 + (m + 1) * 64], in_=acc[:, 0:64], func=AF.Copy,
                                   scale=ez[:, m:m + 1], waits=[a, ofree])
                        pacc.done(ak, a)
                        owr.append(a)
                else:
                    h = p - 4
                    acc1, _, ak1 = accs[0]
                    acc2, _, ak2 = accs[1]
                    a = ACT.op(A.activation, out=ez[:, 2:3], in_=acc1[:, 128:129], func=AF.Ln, waits=[tav[1], of_free[0]])
                    a = ACT.op(A.activation, out=ez[:, 2:3], in_=ez[:, 2:3], func=AF.Exp, scale=-1.0, waits=[a])
                    a = ACT.op(A.activation, out=of1[:], in_=acc1[:, 0:128], func=AF.Copy, scale=ez[:, 2:3], waits=[a])
                    pacc.done(ak1, a)
                    b = ACT.op(A.activation, out=ez[:, 3:4], in_=acc2[:, 128:129], func=AF.Ln, waits=[a])
                    b = ACT.op(A.activation, out=ez[:, 3:4], in_=ez[:, 3:4], func=AF.Exp, scale=-1.0, waits=[b])
                    b = ACT.op(A.activation, out=of2[:], in_=acc2[:, 0:128], func=AF.Copy, scale=ez[:, 3:4], waits=[b])
                    pacc.done(ak2, b)
                    b = DVE.op(V.scalar_tensor_tensor, out=of32[:], in0=of2[:], scalar=nlam[:, 0:1], in1=of1[:],
                               op0=ALU.mult, op1=ALU.add, waits=[a, b, t_nlam, of32_free[0]])
                    of_free[0] = [b]
                    c = ACT.op(A.activation, out=oj[:], in_=of32[:], func=AF.Square, accum_out=es_[:, 5:6], waits=[b])
                    c = DVE.op(V.tensor_scalar, out=es_[:, 6:7], in0=es_[:, 5:6], scalar1=1.0 / 128, scalar2=1e-5, op0=ALU.mult, op1=ALU.add, waits=[c])
                    c = ACT.op(A.activation, out=es_[:, 7:8], in_=es_[:, 6:7], func=AF.Ln, waits=[c])
                    c = ACT.op(A.activation, out=es_[:, 8:9], in_=es_[:, 7:8], func=AF.Exp, scale=-0.5, waits=[c])
                    c = DVE.op(V.scalar_tensor_tensor, out=ot[:, 512 + h * 128:512 + (h + 1) * 128], in0=of32[:], scalar=es_[:, 8:9],
                               in1=gsub[:], op0=ALU.mult, op1=ALU.mult, waits=[c, t_gs, ofree])
                    of32_free[0] = [c]
                    owr.append(c)

            of_free = [[]]
            of32_free = [[]]
            for _ in prep(0):
                pass
            for s in range(NQB):
                if s + 1 < NQB:
                    pump_state["gen"] = prep(s + 1)
                    pump_state["budget"] = 0.0
                    pump_state["rate"] = prep_steps(s + 1) / float(8 * (2 * s + 2)) * 1.15
                else:
                    pump_state["gen"] = None
                ot, ofree, ok_ = otr.next()
                owr = []
                for pi, p in enumerate([4, 5, 6, 7, 0, 1, 2, 3]):
                    attention(s, p, ot, ofree, owr)
                if pump_state["gen"] is not None:
                    for _ in pump_state["gen"]:
                        pass
                    pump_state["gen"] = None
                td = GQ.dma(o_scr[s * 128:(s + 1) * 128, :], ot[:], waits=owr)
                otr.done(ok_, td)
            barrier()

        def load_w(st, name, src, rows, cols, q, waits=()):
            nkc = rows // 128
            wt = sbt(st, name, [128, nkc, cols], BF16)
            srcv = src.rearrange("(kc p) n -> p kc n", p=128)
            toks = []
            step = 1024
            for c0 in range(0, cols, step):
                n = min(step, cols - c0)
                toks.append(q.dma(wt[:, :, c0:c0 + n], srcv[:, :, c0:c0 + n], waits=waits))
            return wt, toks

        with ExitStack() as p3:
            wg = sbt(p3, "wg", [128, 8, 2048], BF16)
            w_in_v = w_in.rearrange("(kc p) n -> p kc n", p=128)
            twg = [GQ.dma(wg[:, :, c0:c0 + 512], w_in_v[:, :, C_G + c0:C_G + c0 + 512]) for c0 in range(0, 2048, 512)]
            wbd, twbd = load_w(p3, "wbd", w_bd, 512, D, GQ)
            wbf, twbf = load_w(p3, "wbf", w_bf, 512, D, GQ)
            wo, two = load_w(p3, "wo", w_out, D, D, GQ)
            gbt = sbt(p3, "gbt", [128, 2048], F32)
            t_gb = SP.dma(gbt[:], bcast(gate_b))
            xr = Ring([sbt(p3, f"xt{i}", [128, D], F32) for i in range(2)])
            junk = Ring([sbt(p3, f"junk{i}", [128, D], BF16) for i in range(1)])
            ssr = Ring([sbt(p3, f"ss{i}", [128, 4], F32) for i in range(2)])
            hbr = Ring([sbt(p3, f"hb{i}", [128, D], BF16) for i in range(2)])
            hTr = Ring([sbt(p3, f"hT{i}", [128, 8, 128], BF16) for i in range(2)])
            obr = Ring([sbt(p3, f"ob{i}", [128, D], BF16) for i in range(2)])
            oTr = Ring([sbt(p3, f"oT{i}", [128, 8, 128], BF16) for i in range(2)])
            gat = sbt(p3, "gat", [128, 2048], F32)
            mrg = sbt(p3, "mrg", [128, D], F32)
            mrg2 = sbt(p3, "mrg2", [128, D], F32)
            mbr = Ring([sbt(p3, f"mb{i}", [128, D], BF16) for i in range(2)])
            mTr = Ring([sbt(p3, f"mT{i}", [128, 8, 128], BF16) for i in range(2)])
            x1r = Ring([sbt(p3, f"x1t{i}", [128, D], F32) for i in range(2)])
            psT = Ring([pst(p3, f"psT{i}", [128, D], BF16) for i in range(2)])
            pp = Ring([pst(p3, f"pp{i}", [128, 512], F32) for i in range(6)])
            gat_free = []
            mrg_free = []
            x1_dmas = []

            def transpose8(src_bf, tsrc, dst_ring):
                ps, pfree, pk = psT.next()
                tt = None
                for kc in range(8):
                    tt = PE.op(T.transpose, ps[:, kc * 128:(kc + 1) * 128], src_bf[:, kc * 128:(kc + 1) * 128], ident[:],
                               waits=[tsrc, pfree] if kc == 0 else ())
                dT, dfree, dk = dst_ring.next()
                te = ACT.op(A.copy, out=dT[:].rearrange("p a b -> p (a b)"), in_=ps[:], waits=[tt, dfree])
                psT.done(pk, te)
                return dT, te, dk, tt

            for s in range(NQB):
                xt, xfree, xk = xr.next()
                tx = SP.dma(xt[:], xq[s * 128:(s + 1) * 128, :], waits=xfree)
                hb, hfree, hk = hbr.next()
                th = rms_norm_block((junk, ssr), xt[:], tx, gmix[:], t_gmix, 1e-6, D, hb[:], hfree)
                hT, te, tk, tt = transpose8(hb, th, hTr)
                hbr.done(hk, tt)
                tg_last = None
                for gc in range(4):
                    ps, pfree, pk = pp.next()
                    tm = None
                    for kc in range(8):
                        tm = PE.op(T.matmul, ps[:], lhsT=hT[:, kc, :], rhs=wg[:, kc, gc * 512:(gc + 1) * 512], start=(kc == 0), stop=(kc == 7),
                                   waits=[te, pfree, twg] if kc == 0 else ())
                    a = DVE.op(V.tensor_tensor, out=gat[:, gc * 512:(gc + 1) * 512], in0=ps[:], in1=gbt[:, gc * 512:(gc + 1) * 512], op=ALU.add,
                               waits=[tm, t_gb, gat_free])
                    pp.done(pk, a)
                    tg_last = ACT.op(A.activation, out=gat[:, gc * 512:(gc + 1) * 512], in_=gat[:, gc * 512:(gc + 1) * 512], func=AF.Sigmoid, waits=[a])
                    if gc == 3:
                        hTr.done(tk, tm)
                gat_free = []
                ob, ofree, ok_ = obr.next()
                to = SP.dma(ob[:], o_scr[s * 128:(s + 1) * 128, :], waits=[ofree])
                oT, teo, ok2, tto = transpose8(ob, to, oTr)
                obr.done(ok_, tto)
                mtoks = []
                for br, (wt, twt) in enumerate([(wbd, twbd), (wbf, twbf)]):
                    for nc_ in range(2):
                        ps, pfree, pk = pp.next()
                        tm = None
                        for kc in range(4):
                            tm = PE.op(T.matmul, ps[:], lhsT=oT[:, br * 4 + kc, :], rhs=wt[:, kc, nc_ * 512:(nc_ + 1) * 512], start=(kc == 0), stop=(kc == 3),
                                       waits=[teo, pfree, twt] if kc == 0 else ())
                        dst = (mrg if br == 0 else mrg2)[:, nc_ * 512:(nc_ + 1) * 512]
                        a = DVE.op(V.tensor_tensor, out=dst, in0=ps[:], in1=gat[:, br * 1024 + nc_ * 512:br * 1024 + (nc_ + 1) * 512], op=ALU.mult,
                                   waits=[tm, tg_last, mrg_free])
                        pp.done(pk, a)
                        mtoks.append(a)
                        if br == 1 and nc_ == 1:
                            oTr.done(ok2, tm)
                gat_free = list(mtoks)
                mb, mfree, mk = mbr.next()
                tmb = DVE.op(V.tensor_tensor, out=mb[:], in0=mrg[:], in1=mrg2[:], op=ALU.add, waits=[mtoks, mfree])
                mrg_free = [tmb]
                mT, tem, mk2, ttm = transpose8(mb, tmb, mTr)
                mbr.done(mk, ttm)
                x1t, x1free, x1k = x1r.next()
                xtoks = []
                for nc_ in range(2):
                    ps, pfree, pk = pp.next()
                    tm = None
                    for kc in range(8):
                        tm = PE.op(T.matmul, ps[:], lhsT=mT[:, kc, :], rhs=wo[:, kc, nc_ * 512:(nc_ + 1) * 512], start=(kc == 0), stop=(kc == 7),
                                   waits=[tem, pfree, two] if kc == 0 else ())
                    a = DVE.op(V.tensor_tensor, out=x1t[:, nc_ * 512:(nc_ + 1) * 512], in0=ps[:], in1=xt[:, nc_ * 512:(nc_ + 1) * 512], op=ALU.add,
                               waits=[tm, x1free])
                    pp.done(pk, a)
                    xtoks.append(a)
                    if nc_ == 1:
                        mTr.done(mk2, tm)
                xr.done(xk, xtoks)
                td = SP.dma(x1_scr[s * 128:(s + 1) * 128, :], x1t[:], waits=xtoks)
                x1r.done(x1k, td)
                x1_dmas.append(td)
            barrier()
        early.close()

        with ExitStack() as p4:
            w1, tw1 = load_w(p4, "w1", w_f1, D, 2 * DFF, GQ)
            w2, tw2 = load_w(p4, "w2", w_f2, DFF, D, GQ)
            gffn = sbt(p4, "gffn", [128, D], F32)
            gfin = sbt(p4, "gfin", [128, D], F32)
            t_gffn = SP.dma(gffn[:], bcast(norm_ffn_g))
            t_gfin = SP.dma(gfin[:], bcast(norm_fin_g))
            x1r = Ring([sbt(p4, f"x1b{i}", [128, D], F32) for i in range(2)])
            junk = Ring([sbt(p4, f"junk{i}", [128, D], BF16) for i in range(1)])
            ssr = Ring([sbt(p4, f"ss{i}", [128, 4], F32) for i in range(2)])
            hbr = Ring([sbt(p4, f"hb{i}", [128, D], BF16) for i in range(2)])
            h2T = sbt(p4, "h2T", [128, 8, 512], BF16)
            actT = sbt(p4, "actT", [128, NFC, 512], BF16)
            sgr = Ring([sbt(p4, f"sg{i}", [128, 512], F32) for i in range(2)])
            x2r = Ring([sbt(p4, f"x2t{i}", [128, D], F32) for i in range(2)])
            psT = Ring([pst(p4, f"psT{i}", [128, D], BF16) for i in range(2)])
            pp = Ring([pst(p4, f"pp{i}", [128, 512], F32) for i in range(6)])
            h2T_free = []
            actT_free = []
            for grp in range(NQB // 4):
                th2 = []
                for bi in range(4):
                    s = grp * 4 + bi
                    x1t, x1free, x1k = x1r.next()
                    tx = SP.dma(x1t[:], x1_scr[s * 128:(s + 1) * 128, :], waits=[x1free])
                    hb, hfree, hk = hbr.next()
                    th = rms_norm_block((junk, ssr), x1t[:], tx, gffn[:], t_gffn, 1e-6, D, hb[:], hfree)
                    x1r.done(x1k, th)
                    ps, pfree, pk = psT.next()
                    tt = None
                    for kc in range(8):
                        tt = PE.op(T.transpose, ps[:, kc * 128:(kc + 1) * 128], hb[:, kc * 128:(kc + 1) * 128], ident[:],
                                   waits=[th, pfree] if kc == 0 else ())
                    hbr.done(hk, tt)
                    te = ACT.op(A.copy, out=h2T[:, :, bi * 128:(bi + 1) * 128], in_=ps[:].rearrange("p (a b) -> p a b", a=8), waits=[tt, h2T_free])
                    psT.done(pk, te)
                    th2.append(te)
                h2T_free = []
                tact = []
                last_mm = None
                for f in range(NFC):
                    psg, pfree, pkg = pp.next()
                    tmg = None
                    for kc in range(8):
                        tmg = PE.op(T.matmul, psg[:], lhsT=w1[:, kc, f * 128:(f + 1) * 128], rhs=h2T[:, kc, :], start=(kc == 0), stop=(kc == 7),
                                    waits=[th2, pfree, tw1] if kc == 0 else ())
                    psu, pfree, pku = pp.next()
                    tmu = None
                    for kc in range(8):
                        tmu = PE.op(T.matmul, psu[:], lhsT=w1[:, kc, DFF + f * 128:DFF + (f + 1) * 128], rhs=h2T[:, kc, :], start=(kc == 0), stop=(kc == 7),
                                    waits=[pfree] if kc == 0 else ())
                    last_mm = tmu
                    sg, sfree, sk = sgr.next()
                    ta = ACT.op(A.activation, out=sg[:], in_=psg[:], func=AF.Silu, waits=[tmg, sfree])
                    pp.done(pkg, ta)
                    tb_ = DVE.op(V.tensor_tensor, out=actT[:, f, :], in0=psu[:], in1=sg[:], op=ALU.mult, waits=[tmu, ta, actT_free])
                    pp.done(pku, tb_)
                    sgr.done(sk, tb_)
                    tact.append(tb_)
                h2T_free = [last_mm]
                actT_free = []
                last_o = None
                for bi in range(4):
                    s = grp * 4 + bi
                    x2, x2free, x2k = x2r.next()
                    tx2 = SP.dma(x2[:], x1_scr[s * 128:(s + 1) * 128, :], waits=[x2free])
                    xtoks = []
                    for nc_ in range(2):
                        ps, pfree, pk = pp.next()
                        tm = None
                        for f in range(NFC):
                            tm = PE.op(T.matmul, ps[:], lhsT=actT[:, f, bi * 128:(bi + 1) * 128], rhs=w2[:, f, nc_ * 512:(nc_ + 1) * 512],
                                       start=(f == 0), stop=(f == NFC - 1), waits=[tact, pfree, tw2] if f == 0 else ())
                        last_o = tm
                        a = DVE.op(V.tensor_tensor, out=x2[:, nc_ * 512:(nc_ + 1) * 512], in0=ps[:], in1=x2[:, nc_ * 512:(nc_ + 1) * 512], op=ALU.add,
                                   waits=[tm, tx2])
                        pp.done(pk, a)
                        xtoks.append(a)
                    e = rms_norm_block((junk, ssr), x2[:], xtoks, gfin[:], t_gfin, 1e-6, D, x2[:], [])
                    td = SP.dma(out[s * 128:(s + 1) * 128, :], x2[:], waits=[e])
                    x2r.done(x2k, td)
                actT_free = [last_o]
            barrier()
    return nc


_NC_CACHE = {}


def _get_nc(debug=False):
    if debug not in _NC_CACHE:
        _NC_CACHE[debug] = build(debug)
    return _NC_CACHE[debug]


def make_in_maps(inputs):
    x = np.ascontiguousarray(np.asarray(inputs["x"], dtype=np.float32))
    pos = np.asarray(inputs["positions"]).astype(np.int32)
    in_maps = []
    kk = np.arange(512)[None, :]
    qq = np.arange(128)[:, None]
    for c in range(8):
        b, j = c // 4, c % 4
        blocks = [4 * s + j for s in range(NQB)]
        xqc = np.concatenate([x[b, q * 128:(q + 1) * 128] for q in blocks], axis=0)
        posk = np.ascontiguousarray(pos[b].reshape(NTB, 128).T)
        posq = np.ascontiguousarray(np.stack([pos[b, q * 128:(q + 1) * 128] for q in blocks], axis=1))
        cm = np.where(kk <= j * 128 + qq, 0.0, NEG).astype(np.float32)
        m = {"xs": x[b], "xq": np.ascontiguousarray(xqc), "posk": posk, "posq": posq, "cmask": cm}
        for name in ["norm_mix_g", "w_in", "idx_k_norm_g", "idx_k_norm_b", "diff_lambda_q1", "diff_lambda_k1",
                     "diff_lambda_q2", "diff_lambda_k2", "diff_subln_g", "gate_b", "w_branch_dsa", "w_branch_diff",
                     "w_out", "norm_ffn_g", "w_ffn_in", "w_ffn_out"]:
            m[name] = np.ascontiguousarray(np.asarray(inputs[name], dtype=np.float32)[0])
        m["norm_final_g"] = np.ascontiguousarray(np.asarray(inputs["norm_final_g"], dtype=np.float32))
        in_maps.append(m)
    return in_maps


def kernel(**inputs):
    nc = _get_nc(False)
    in_maps = make_in_maps(inputs)
    res = run_bass_kernel_spmd(nc, in_maps, core_ids=list(range(8)))
    outp = np.zeros((2, S, D), dtype=np.float32)
    for c in range(8):
        b, j = c // 4, c % 4
        o = res.results[c]["out"]
        for s in range(NQB):
            q = 4 * s + j
            outp[b, q * 128:(q + 1) * 128] = o[s * 128:(s + 1) * 128]
    return outp
```

```python
import os
import math
import numpy as np
from contextlib import ExitStack
import concourse.bass as bass
import concourse.mybir as mybir
from concourse.bass_utils import run_bass_kernel_spmd

F32 = mybir.dt.float32
F32R = mybir.dt.float32r
BF16 = mybir.dt.bfloat16
I32 = mybir.dt.int32
AF = mybir.ActivationFunctionType
ALU = mybir.AluOpType
AX = mybir.AxisListType

S = 8192
D = 1024
NTB = S // 128
NQB = 16
NQ = NQB * 128
DFF = 2816
NFC = DFF // 128
TOPK = 256
NBIS = 16
NEG = -30000.0
C_QA, C_KA, C_VA, C_QI, C_KI, C_WI, C_QB, C_KB, C_VB, C_G = 0, 512, 1024, 1536, 1792, 1824, 1832, 2344, 2856, 3368
D_IN = 5416
VW = 132
TWO_PI = 2.0 * math.pi


class Eng:
    def __init__(self, nc, es, eng, name):
        self.eng = eng
        self.name = name
        self.sem = es.enter_context(nc.semaphore("sem_" + name))
        self.count = 0
        self.seen = {}

    def wait(self, toks):
        for t in toks:
            if t is None:
                continue
            if isinstance(t, list):
                self.wait(t)
                continue
            sem, val, key = t
            if self.seen.get(key, 0) >= val:
                continue
            self.eng.wait_ge(sem, val)
            self.seen[key] = val

    def op(self, fn, *args, waits=(), **kw):
        self.wait(waits)
        inst = fn(*args, **kw)
        self.count += 1
        inst.then_inc(self.sem, 1)
        tok = (self.sem, self.count, self.name)
        self.seen[self.name] = self.count - 1 if False else self.seen.get(self.name, 0)
        return tok


class DmaQ:
    def __init__(self, nc, es, eng, name, nsem=12):
        self.eng = eng
        self.name = name
        self.sems = [es.enter_context(nc.semaphore(f"dsem_{name}_{i}")) for i in range(nsem)]
        self.vals = [0] * nsem
        self.i = 0
        self.seen = {}

    def wait(self, toks):
        for t in toks:
            if t is None:
                continue
            if isinstance(t, list):
                self.wait(t)
                continue
            sem, val, key = t
            if self.seen.get(key, 0) >= val:
                continue
            self.eng.wait_ge(sem, val)
            self.seen[key] = val

    def dma(self, out, in_, waits=(), **kw):
        k = self.i
        self.i = (self.i + 1) % len(self.sems)
        key = f"{self.name}_{k}"
        if self.vals[k] > 0:
            self.wait([(self.sems[k], self.vals[k], key)])
        self.wait(waits)
        self.vals[k] += 16
        self.eng.dma_start(out=out, in_=in_, **kw).then_inc(self.sems[k], 16)
        return (self.sems[k], self.vals[k], key)

    def all_toks(self):
        return [(self.sems[k], self.vals[k], f"{self.name}_{k}") for k in range(len(self.sems)) if self.vals[k] > 0]


class Ring:
    def __init__(self, tiles):
        self.tiles = tiles
        self.rd = [[] for _ in tiles]
        self.i = -1

    def next(self):
        self.i = (self.i + 1) % len(self.tiles)
        k = self.i
        toks = self.rd[k]
        self.rd[k] = []
        return self.tiles[k], toks, k

    def done(self, k, tok):
        self.rd[k].append(tok)


def build(debug=False):
    nc = bass.Bass("TRN2", target_bir_lowering=False)
    dt_in = lambda n, s, d=F32: nc.dram_tensor(n, s, d, kind="ExternalInput").ap()
    xs = dt_in("xs", [S, D])
    xq = dt_in("xq", [NQ, D])
    posk = dt_in("posk", [128, NTB], I32)
    posq = dt_in("posq", [128, NQB], I32)
    cmask = dt_in("cmask", [128, 512])
    norm_mix_g = dt_in("norm_mix_g", [D])
    w_in = dt_in("w_in", [D, D_IN])
    idx_g = dt_in("idx_k_norm_g", [32])
    idx_b = dt_in("idx_k_norm_b", [32])
    lq1 = dt_in("diff_lambda_q1", [64])
    lk1 = dt_in("diff_lambda_k1", [64])
    lq2 = dt_in("diff_lambda_q2", [64])
    lk2 = dt_in("diff_lambda_k2", [64])
    subln_g = dt_in("diff_subln_g", [128])
    gate_b = dt_in("gate_b", [2048])
    w_bd = dt_in("w_branch_dsa", [512, D])
    w_bf = dt_in("w_branch_diff", [512, D])
    w_out = dt_in("w_out", [D, D])
    norm_ffn_g = dt_in("norm_ffn_g", [D])
    w_f1 = dt_in("w_ffn_in", [D, 2 * DFF])
    w_f2 = dt_in("w_ffn_out", [DFF, D])
    norm_fin_g = dt_in("norm_final_g", [D])
    out = nc.dram_tensor("out", [NQ, D], F32, kind="ExternalOutput").ap()
    skind = "ExternalOutput" if debug else "Internal"
    kT_scr = nc.dram_tensor("kT_scr", [8, 128, S], BF16, kind=skind).ap()
    kiT_scr = nc.dram_tensor("kiT_scr", [32, S], BF16, kind=skind).ap()
    v_scr = nc.dram_tensor("v_scr", [S, 8 * VW], BF16, kind=skind).ap()
    qT_scr = nc.dram_tensor("qT_scr", [8, 128, NQ], BF16, kind=skind).ap()
    qiT_scr = nc.dram_tensor("qiT_scr", [64, 4, NQ], BF16, kind=skind).ap()
    o_scr = nc.dram_tensor("o_scr", [NQ, D], BF16, kind=skind).ap()
    x1_scr = nc.dram_tensor("x1_scr", [NQ, D], F32, kind=skind).ap()

    with ExitStack() as es:
        uid = [0]

        def sbt(st, n, s, d):
            uid[0] += 1
            return st.enter_context(nc.sbuf_tensor(f"{n}_{uid[0]}", s, d))

        def pst(st, n, s, d):
            uid[0] += 1
            return st.enter_context(nc.psum_tensor(f"{n}_{uid[0]}", s, d))

        ident = sbt(es, "ident", [128, 128], BF16)
        early = ExitStack()
        cosk = sbt(early, "cosk", [128, NTB, 8], F32)
        sink = sbt(early, "sink", [128, NTB, 8], F32)
        cosq = sbt(early, "cosq", [128, NQB, 8], F32)
        sinq = sbt(early, "sinq", [128, NQB, 8], F32)
        gmix = sbt(early, "gmix", [128, D], F32)
        wabs = sbt(early, "wabs", [128, NQB, 8], F32)
        wsgn = sbt(early, "wsgn", [128, NQB, 8], F32)
        nlam = sbt(early, "nlam", [128, 1], F32)
        small = sbt(early, "small", [128, 64], F32)
        cb = sbt(early, "cb", [128, 512], BF16)
        cbf = sbt(early, "cbf", [128, 512], F32)

        es.enter_context(nc.Block())
        PE = Eng(nc, es, nc.tensor, "pe")
        ACT = Eng(nc, es, nc.scalar, "act")
        DVE = Eng(nc, es, nc.vector, "dve")
        POOL = Eng(nc, es, nc.gpsimd, "pool")
        SP = DmaQ(nc, es, nc.sync, "sp", 16)
        GQ = DmaQ(nc, es, nc.gpsimd, "gq", 8)
        V = nc.vector
        A = nc.scalar
        T = nc.tensor

        def bcast(ap1d, n=128):
            return ap1d.partition_broadcast(n)

        def barrier():
            toks = [(e.sem, e.count, e.name) for e in (PE, ACT, DVE, POOL) if e.count > 0]
            toks += SP.all_toks() + GQ.all_toks()
            for e in (PE, ACT, DVE, POOL, SP):
                e.wait(toks)

        t = POOL.op(nc.gpsimd.memset, ident[:], 1.0)
        t_ident = POOL.op(nc.gpsimd.affine_select, out=ident[:], in_=ident[:], pattern=[[-1, 128]],
                          compare_op=ALU.is_equal, fill=0.0, base=0, channel_multiplier=1, waits=[t])
        t_gmix = SP.dma(gmix[:], bcast(norm_mix_g))
        t_cbf = SP.dma(cbf[:], cmask)
        t_cb = DVE.op(V.tensor_copy, out=cb[:], in_=cbf[:], waits=[t_cbf])

        with ExitStack() as p0:
            lt = sbt(p0, "lt", [128, 4, 64], F32)
            lj = sbt(p0, "lj", [128, 64], F32)
            tl = [SP.dma(lt[:, i, :], bcast(a)) for i, a in enumerate([lq1, lk1, lq2, lk2])]
            t1 = DVE.op(V.tensor_tensor, out=lj[:], in0=lt[:, 0, :], in1=lt[:, 1, :], op=ALU.mult, waits=tl)
            t1 = DVE.op(V.tensor_reduce, out=small[:, 0:1], in_=lj[:], axis=AX.X, op=ALU.add, waits=[t1])
            t2 = DVE.op(V.tensor_tensor, out=lj[:], in0=lt[:, 2, :], in1=lt[:, 3, :], op=ALU.mult, waits=[t1])
            t2 = DVE.op(V.tensor_reduce, out=small[:, 1:2], in_=lj[:], axis=AX.X, op=ALU.add, waits=[t2])
            t3 = ACT.op(A.activation, out=small[:, 2:4], in_=small[:, 0:2], func=AF.Exp, waits=[t2])
            t_nlam = DVE.op(V.scalar_tensor_tensor, out=nlam[:], in0=small[:, 3:4], scalar=-0.2, in1=small[:, 2:3],
                            op0=ALU.add, op1=ALU.subtract, waits=[t3])

            invf = sbt(p0, "invf", [128, 8], F32)
            tinv = None
            for i in range(8):
                fv = float(np.power(np.float32(500000.0), -np.float32(2 * i) / np.float32(16)))
                tinv = DVE.op(V.memset, invf[:, i:i + 1], fv)

            def rope_table(pos_ap, n, cos_t, sin_t, nm):
                pi_ = sbt(p0, "pi_" + nm, [128, n], I32)
                pf = sbt(p0, "pf_" + nm, [128, n], F32)
                ang = sbt(p0, "ang_" + nm, [128, n, 8], F32)
                yy = sbt(p0, "yy_" + nm, [128, n, 8], F32)
                ni = sbt(p0, "ni_" + nm, [128, n, 8], I32)
                tp = SP.dma(pi_[:], pos_ap)
                a = DVE.op(V.tensor_copy, out=pf[:], in_=pi_[:], waits=[tp])
                a = DVE.op(V.tensor_tensor, out=ang[:], in0=pf[:].unsqueeze(2).to_broadcast([128, n, 8]),
                           in1=invf[:].unsqueeze(1).to_broadcast([128, n, 8]), op=ALU.mult, waits=[a, tinv])

                def reduce_sin(src_add, dst):
                    b = DVE.op(V.tensor_scalar, out=yy[:], in0=ang[:], scalar1=src_add, scalar2=1.0 / TWO_PI,
                               op0=ALU.add, op1=ALU.mult, waits=[a])
                    b = DVE.op(V.tensor_copy, out=ni[:], in_=yy[:], waits=[b])
                    b = DVE.op(V.tensor_copy, out=yy[:], in_=ni[:], waits=[b])
                    c1 = 6.28125
                    c2 = TWO_PI - 6.28125
                    b = DVE.op(V.scalar_tensor_tensor, out=dst, in0=yy[:], scalar=-c1, in1=ang[:], op0=ALU.mult, op1=ALU.add, waits=[b])
                    b = DVE.op(V.scalar_tensor_tensor, out=dst, in0=yy[:], scalar=-c2, in1=dst, op0=ALU.mult, op1=ALU.add, waits=[b])
                    b = DVE.op(V.tensor_scalar, out=dst, in0=dst, scalar1=src_add, scalar2=3.1415925, op0=ALU.add, op1=ALU.min, waits=[b])
                    b = DVE.op(V.tensor_scalar, out=dst, in0=dst, scalar1=-3.1415925, scalar2=None, op0=ALU.max, waits=[b])
                    return ACT.op(A.activation, out=dst, in_=dst, func=AF.Sin, waits=[b])
                ts = reduce_sin(0.0, sin_t[:])
                tc = reduce_sin(math.pi / 2.0, cos_t[:])
                return [ts, tc]
            t_ropek = rope_table(posk, NTB, cosk, sink, "k")
            t_ropeq = rope_table(posq, NQB, cosq, sinq, "q")
            barrier()

        def rms_norm_block(st_rings, x_tile, tx, g_tile, tg, eps, n_feat, hb_tile, hb_free):
            junk, ssr = st_rings
            jt, jfree, jk = junk.next()
            col, cfree, ck = ssr.next()
            a = ACT.op(A.activation, out=jt[:, :n_feat], in_=x_tile, func=AF.Square, accum_out=col[:, 0:1],
                       waits=[tx, jfree, cfree])
            junk.done(jk, a)
            b = DVE.op(V.tensor_scalar, out=col[:, 1:2], in0=col[:, 0:1], scalar1=1.0 / n_feat, scalar2=eps,
                       op0=ALU.mult, op1=ALU.add, waits=[a])
            c = ACT.op(A.activation, out=col[:, 2:3], in_=col[:, 1:2], func=AF.Sqrt, waits=[b])
            d = DVE.op(V.reciprocal, out=col[:, 3:4], in_=col[:, 2:3], waits=[c])
            e = DVE.op(V.scalar_tensor_tensor, out=hb_tile, in0=x_tile, scalar=col[:, 3:4], in1=g_tile,
                       op0=ALU.mult, op1=ALU.mult, waits=[d, tg, hb_free])
            ssr.done(ck, e)
            return e

        rope_last = []

        def rope_apply(tile3, H, half, cs, sn, tmp, waits):
            x1 = tile3[:, :, 0:half]
            x2 = tile3[:, :, half:2 * half]
            cB = cs.unsqueeze(1).to_broadcast([128, H, half])
            sB = sn.unsqueeze(1).to_broadcast([128, H, half])
            tv = lambda i: tmp[:, i, 0:H * half].rearrange("p (h d) -> p h d", h=H)
            waits = list(waits) + rope_last
            a1 = DVE.op(V.tensor_tensor, out=tv(0), in0=x1, in1=cB, op=ALU.mult, waits=waits)
            a2 = DVE.op(V.tensor_tensor, out=tv(1), in0=x2, in1=sB, op=ALU.mult, waits=waits)
            a3 = DVE.op(V.tensor_tensor, out=tv(2), in0=x2, in1=cB, op=ALU.mult, waits=waits)
            a4 = DVE.op(V.tensor_tensor, out=tv(3), in0=x1, in1=sB, op=ALU.mult, waits=waits)
            b1 = DVE.op(V.tensor_tensor, out=x1, in0=tv(0), in1=tv(1), op=ALU.subtract, waits=[a1, a2, a3, a4])
            b2 = DVE.op(V.tensor_tensor, out=x2, in0=tv(2), in1=tv(3), op=ALU.add, waits=[a1, a2, a3, a4])
            rope_last[:] = [b1, b2]
            return [b1, b2]

        with ExitStack() as p1:
            NKV = 2080
            NQC = 1288
            wkv = sbt(p1, "wkv", [128, 8, NKV], BF16)
            wq = sbt(p1, "wq", [128, 8, NQC], BF16)
            w_in_v = w_in.rearrange("(kc p) n -> p kc n", p=128)
            tw = []
            for (dst0, c0, n) in [(0, C_KA, 512), (512, C_KB, 512), (1024, C_VA, 512), (1536, C_VB, 512), (2048, C_KI, 32)]:
                tw.append(GQ.dma(wkv[:, :, dst0:dst0 + n], w_in_v[:, :, c0:c0 + n]))
            twq = []
            for (dst0, c0, n) in [(0, C_QA, 512), (512, C_QB, 512), (1024, C_QI, 256), (1280, C_WI, 8)]:
                twq.append(GQ.dma(wq[:, :, dst0:dst0 + n], w_in_v[:, :, c0:c0 + n]))
            lng = sbt(p1, "lng", [128, 32], F32)
            lnb = sbt(p1, "lnb", [128, 32], F32)
            t_lng = SP.dma(lng[:], bcast(idx_g))
            t_lnb = SP.dma(lnb[:], bcast(idx_b))

            xr = Ring([sbt(p1, f"xt{i}", [128, D], F32) for i in range(3)])
            junk = Ring([sbt(p1, f"junk{i}", [128, D], BF16) for i in range(1)])
            ssr = Ring([sbt(p1, f"ss{i}", [128, 4], F32) for i in range(4)])
            hbr = Ring([sbt(p1, f"hb{i}", [128, D], BF16) for i in range(3)])
            hTr = Ring([sbt(p1, f"hT{i}", [128, 8, 128], BF16) for i in range(3)])
            kfr = Ring([sbt(p1, f"kf{i}", [128, 1024], F32) for i in range(2)])
            kbr = Ring([sbt(p1, f"kb{i}", [128, 1024], BF16) for i in range(3)])
            kTr = Ring([sbt(p1, f"kTt{i}", [128, 8, 128], BF16) for i in range(2)])
            vtr = Ring([sbt(p1, f"vt{i}", [128, 8, VW], BF16) for i in range(2)])
            rtmp = sbt(p1, "rtmp", [128, 4, 128], F32)
            kif = sbt(p1, "kif", [128, 8], F32)
            kic = sbt(p1, "kic", [128, 32], F32)
            kij = sbt(p1, "kij", [128, 32], F32)
            kibr = Ring([sbt(p1, f"kib{i}", [128, 32], BF16) for i in range(3)])
            kiTr = Ring([sbt(p1, f"kiT{i}", [32, 128], BF16) for i in range(2)])
            qwr = Ring([sbt(p1, f"qw{i}", [128, 264], F32) for i in range(2)])
            qibr = Ring([sbt(p1, f"qib{i}", [128, 256], BF16) for i in range(3)])
            qiTr = Ring([sbt(p1, f"qiT{i}", [32, 8, 128], BF16) for i in range(2)])
            psT = Ring([pst(p1, f"psT{i}", [128, D], BF16) for i in range(2)])
            pp = Ring([pst(p1, f"pp{i}", [128, 512], F32) for i in range(4)])
            pkT = Ring([pst(p1, f"pkT{i}", [128, D], BF16) for i in range(2)])
            t_vinit = []
            for vt_ in vtr.tiles:
                t0 = POOL.op(nc.gpsimd.memset, vt_[:], 0.0)
                t1_ = POOL.op(nc.gpsimd.memset, vt_[:, 0:4, :].rearrange("p a (s e) -> p (a s) e", s=2)[:, :, 64:65], 1.0, waits=[t0])
                t_vinit.append(POOL.op(nc.gpsimd.memset, vt_[:, 4:8, 128:129], 1.0, waits=[t0, t1_]))

            def aevac(out_ap, in_ap, waits):
                return ACT.op(A.copy, out=out_ap, in_=in_ap, waits=waits)

            def stageA1(src_rows):
                xt, xfree, xk = xr.next()
                tx = SP.dma(xt[:], src_rows, waits=xfree)
                hb, hfree, hk = hbr.next()
                th = rms_norm_block((junk, ssr), xt[:], tx, gmix[:], t_gmix, 1e-6, D, hb[:], hfree)
                xr.done(xk, th)
                return hb, th, hk

            def stageA2(hb, th, hk):
                ps, pfree, pk = psT.next()
                tt = None
                for kc in range(8):
                    tt = PE.op(T.transpose, ps[:, kc * 128:(kc + 1) * 128], hb[:, kc * 128:(kc + 1) * 128], ident[:],
                               waits=[th, t_ident, pfree] if kc == 0 else ())
                hbr.done(hk, tt)
                hT, tfree, tk = hTr.next()
                te = aevac(hT[:].rearrange("p a b -> p (a b)"), ps[:], [tt, tfree])
                psT.done(pk, te)
                return hT, te, tk

            def project(hT, th, w_tile, c0, n, wtoks):
                ps, pfree, pk = pp.next()
                tt = None
                for kc in range(8):
                    tt = PE.op(T.matmul, ps[:, 0:n], lhsT=hT[:, kc, :], rhs=w_tile[:, kc, c0:c0 + n],
                               start=(kc == 0), stop=(kc == 7), waits=[th, pfree, wtoks] if kc == 0 else ())
                return ps, tt, pk

            def transpose_out(src_bf, tsrc, nchunk, width, ring_sb, dst_dram):
                ps, pfree, pk = pkT.next()
                tt = None
                for c in range(nchunk):
                    tt = PE.op(T.transpose, ps[0:width, c * 128:(c + 1) * 128], src_bf[:, c * width:(c + 1) * width], ident[:],
                               waits=[tsrc, pfree, t_ident] if c == 0 else ())
                sbT, sfree, sk = ring_sb.next()
                te = aevac(sbT[:].rearrange("p a b -> p (a b)") if len(sbT.shape) == 3 else sbT[:],
                           ps[0:width, 0:nchunk * 128], [tt, sfree])
                pkT.done(pk, te)
                td = GQ.dma(dst_dram, sbT[:], waits=[te])
                ring_sb.done(sk, td)
                return tt

            def transpose_out_qi(src_bf, tsrc, s):
                ps, pfree, pk = pkT.next()
                tt = None
                for c in range(8):
                    tt = PE.op(T.transpose, ps[0:32, c * 128:(c + 1) * 128], src_bf[:, c * 32:(c + 1) * 32], ident[:],
                               waits=[tsrc, pfree, t_ident] if c == 0 else ())
                sbT, sfree, sk = qiTr.next()
                te = aevac(sbT[:].rearrange("p a b -> p (a b)"), ps[0:32, 0:1024], [tt, sfree])
                pkT.done(pk, te)
                for g in range(2):
                    td = GQ.dma(qiT_scr[g * 32:(g + 1) * 32, :, s * 128:(s + 1) * 128], sbT[:, g::2, :], waits=[te])
                    qiTr.done(sk, td)
                return tt

            def qk_pair(hT, th, w_tile, wtoks, cs, sn, scale, dst_dram):
                kf, kfree, kk = kfr.next()
                tes = []
                for gi in range(2):
                    ps, tmm, pk = project(hT, th, w_tile, gi * 512, 512, wtoks)
                    te = aevac(kf[:, gi * 512:(gi + 1) * 512], ps[:], [tmm, kfree])
                    pp.done(pk, te)
                    tes.append(te)
                tr = rope_apply(kf[:].rearrange("p (h d) -> p h d", h=16), 16, 8, cs, sn, rtmp, tes)
                kb, bfree, bk = kbr.next()
                if scale == 1.0:
                    tcst = DVE.op(V.tensor_copy, out=kb[:], in_=kf[:], waits=[tr, bfree])
                else:
                    tcst = DVE.op(V.tensor_scalar, out=kb[:], in0=kf[:], scalar1=scale, scalar2=None, op0=ALU.mult, waits=[tr, bfree])
                kfr.done(kk, tcst)
                return kb, bk, tcst

            def stageB_kv(tb, hT, th, hk):
                cs = cosk[:, tb, :]
                sn = sink[:, tb, :]
                kb, bk, tcst = qk_pair(hT, th, wkv, tw, cs, sn, 1.0, None)
                vt, vfree, vk = vtr.next()
                tvs = []
                for gi in (2, 3):
                    ps, tmm, pk = project(hT, th, wkv, gi * 512, 512, tw)
                    if gi == 2:
                        dstv = vt[:, 0:4, :].rearrange("p a (s e) -> p (a s) e", s=2)[:, :, 0:64]
                        te = aevac(dstv, ps[:].rearrange("p (h e) -> p h e", e=64), [tmm, vfree, t_vinit])
                    else:
                        te = aevac(vt[:, 4:8, 0:128], ps[:].rearrange("p (h e) -> p h e", e=128), [tmm, vfree, t_vinit])
                    pp.done(pk, te)
                    tvs.append(te)
                td = GQ.dma(v_scr[tb * 128:(tb + 1) * 128, :], vt[:].rearrange("p a b -> p (a b)"), waits=tvs)
                vtr.done(vk, td)
                ps, tmm, pk = project(hT, th, wkv, 2048, 32, tw)
                hTr.done(hk, tmm)
                a = DVE.op(V.tensor_reduce, out=kif[:, 0:1], in_=ps[:, 0:32], axis=AX.X, op=ALU.add, waits=[tmm])
                a = DVE.op(V.tensor_scalar, out=kif[:, 1:2], in0=kif[:, 0:1], scalar1=1.0 / 32, scalar2=None, op0=ALU.mult, waits=[a])
                a = DVE.op(V.tensor_scalar, out=kic[:], in0=ps[:, 0:32], scalar1=kif[:, 1:2], scalar2=None, op0=ALU.subtract, waits=[a])
                pp.done(pk, a)
                b = ACT.op(A.activation, out=kij[:], in_=kic[:], func=AF.Square, accum_out=kif[:, 2:3], waits=[a])
                b = DVE.op(V.tensor_scalar, out=kif[:, 3:4], in0=kif[:, 2:3], scalar1=1.0 / 32, scalar2=1e-6, op0=ALU.mult, op1=ALU.add, waits=[b])
                b = ACT.op(A.activation, out=kif[:, 4:5], in_=kif[:, 3:4], func=AF.Sqrt, waits=[b])
                b = DVE.op(V.reciprocal, out=kif[:, 5:6], in_=kif[:, 4:5], waits=[b])
                b = DVE.op(V.scalar_tensor_tensor, out=kic[:], in0=kic[:], scalar=kif[:, 5:6], in1=lng[:], op0=ALU.mult, op1=ALU.mult, waits=[b, t_lng])
                b = DVE.op(V.tensor_tensor, out=kic[:], in0=kic[:], in1=lnb[:], op=ALU.add, waits=[b, t_lnb])
                csi = cosk[:, tb, :].rearrange("p (a two) -> p a two", two=2)[:, :, 0]
                sni = sink[:, tb, :].rearrange("p (a two) -> p a two", two=2)[:, :, 0]
                tr = rope_apply(kic[:].rearrange("p (h d) -> p h d", h=1), 1, 4, csi, sni, rtmp, [b])
                kib, bfree, bk2 = kibr.next()
                tc2 = DVE.op(V.tensor_copy, out=kib[:], in_=kic[:], waits=[tr, bfree])

                def b2():
                    tlast = transpose_out(kb, tcst, 8, 128, kTr, kT_scr[:, :, tb * 128:(tb + 1) * 128].rearrange("c f t -> f c t"))
                    kbr.done(bk, tlast)
                    tl2 = transpose_out(kib, tc2, 1, 32, kiTr, kiT_scr[:, tb * 128:(tb + 1) * 128])
                    kibr.done(bk2, tl2)
                return b2

            def stageB_q(s, hT, th, hk):
                cs = cosq[:, s, :]
                sn = sinq[:, s, :]
                kb, bk, tcst = qk_pair(hT, th, wq, twq, cs, sn, 0.125, None)
                ps, tmm, pk = project(hT, th, wq, 1024, 264, twq)
                hTr.done(hk, tmm)
                qw, qfree, qk = qwr.next()
                te = aevac(qw[:], ps[:, 0:264], [tmm, qfree])
                pp.done(pk, te)
                csi = cosq[:, s, :].rearrange("p (a two) -> p a two", two=2)[:, :, 0]
                sni = sinq[:, s, :].rearrange("p (a two) -> p a two", two=2)[:, :, 0]
                tr = rope_apply(qw[:, 0:256].rearrange("p (h d) -> p h d", h=8), 8, 4, csi, sni, rtmp, [te])
                a1 = DVE.op(V.tensor_scalar, out=wabs[:, s, :], in0=qw[:, 256:264], scalar1=1.0 / 16, scalar2=None, op0=ALU.mult, waits=[te])
                a2 = a1
                qib, bfree, bk2 = qibr.next()
                tc2 = DVE.op(V.tensor_copy, out=qib[:], in_=qw[:, 0:256], waits=[tr, bfree])
                qwr.done(qk, [tc2, a1, a2])

                def b2():
                    tlast = transpose_out(kb, tcst, 8, 128, kTr, qT_scr[:, :, s * 128:(s + 1) * 128].rearrange("c f t -> f c t"))
                    kbr.done(bk, tlast)
                    tl2 = transpose_out_qi(qib, tc2, s)
                    qibr.done(bk2, tl2)
                return b2

            items = [("kv", tb, xs[tb * 128:(tb + 1) * 128, :]) for tb in range(NTB)] + \
                    [("q", s, xq[s * 128:(s + 1) * 128, :]) for s in range(NQB)]
            n_it = len(items)
            a1 = {}
            a2 = {}
            a1[0] = stageA1(items[0][2])
            a2[0] = stageA2(*a1[0])
            if n_it > 1:
                a1[1] = stageA1(items[1][2])
            prev_b2 = None
            for i in range(n_it):
                if i + 2 < n_it:
                    a1[i + 2] = stageA1(items[i + 2][2])
                kind, idx, _ = items[i]
                hT, th, hk = a2.pop(i)
                if kind == "kv":
                    b2 = stageB_kv(idx, hT, th, hk)
                else:
                    b2 = stageB_q(idx, hT, th, hk)
                if i + 1 < n_it:
                    a2[i + 1] = stageA2(*a1.pop(i + 1))
                if prev_b2 is not None:
                    prev_b2()
                prev_b2 = b2
            if prev_b2 is not None:
                prev_b2()
            barrier()

        with ExitStack() as p2:
            kiT = sbt(p2, "kiT", [64, S], BF16)
            t_kiT = [SP.dma(kiT[g * 32:(g + 1) * 32, :], kiT_scr) for g in range(2)]
            gsub = sbt(p2, "gsub", [128, 128], F32)
            t_gs = SP.dma(gsub[:], bcast(subln_g))
            t_gs = DVE.op(V.tensor_scalar, out=gsub[:], in0=gsub[:], scalar1=0.8, scalar2=None, op0=ALU.mult, waits=[t_gs])
            ident2 = sbt(p2, "ident2", [128, 2, 128], BF16)
            t_id2 = [DVE.op(V.tensor_copy, out=ident2[:, i, :], in_=ident[:], waits=[t_ident]) for i in range(2)]
            Kr = Ring([sbt(p2, f"Kb{i}", [128, S], BF16) for i in range(2)])
            Vr = Ring([sbt(p2, f"Vb{i}", [128, NTB, VW], BF16) for i in range(2)])
            Mb = [sbt(p2, f"Mb{i}", [128, S], BF16) for i in range(2)]
            Isc = sbt(p2, "Isc", [128, S], F32)
            qbd_tiles = [sbt(p2, f"qbd{i}", [128, 2, 128], BF16) for i in range(3)]
            t_qz = [POOL.op(nc.gpsimd.memset, q_[:], 0.0) for q_ in qbd_tiles]
            qbr = Ring(qbd_tiles)
            qiT = [sbt(p2, f"qiTs{i}", [64, 4, 128], BF16) for i in range(2)]
            rl = Ring([sbt(p2, f"rl{i}", [128, 512], BF16) for i in range(6)])
            identf = sbt(p2, "identf", [128, 128], F32)
            t_idf = DVE.op(V.tensor_copy, out=identf[:], in_=ident[:], waits=[t_ident])
            dgb = [sbt(p2, f"dgb{i}", [128, 8, 128], BF16) for i in range(2)]
            etr = Ring([sbt(p2, f"et{i}", [128, 512], BF16) for i in range(4)])
            otr = Ring([sbt(p2, f"ot{i}", [128, D], BF16) for i in range(2)])
            of32 = sbt(p2, "of32", [128, 128], F32)
            oj = sbt(p2, "oj", [128, 128], F32)
            bs = sbt(p2, "bs", [128, 16], F32)
            es_ = sbt(p2, "es_", [128, 16], F32)
            pss = Ring([pst(p2, f"pss{i}", [128, 512], F32) for i in range(4)])
            pI = pst(p2, "pI", [128, 512], F32)
            pacc = Ring([pst(p2, f"pacc{i}", [128, 512], F32) for i in range(3)])
            of1 = sbt(p2, "of1", [128, 128], F32)
            of2 = sbt(p2, "of2", [128, 128], F32)
            ez = sbt(p2, "ez", [128, 8], F32)
            Mb_ready = [None, None]
            Mb_readers = [[], []]
            qiT_readers = [[], []]
            dg_readers = [[], []]
            Isc_free = [[]]
            pI_free = [[]]

            def prep(s):
                li = s % 2
                nk = (4 * s + 4) * 128
                nch = nk // 512
                wb = 0.76 * (s + 1)
                tq = SP.dma(qiT[li][:], qiT_scr[:, :, s * 128:(s + 1) * 128], waits=qiT_readers[li])
                qiT_readers[li] = []
                tdg = None
                for h in range(8):
                    tdg = DVE.op(V.tensor_scalar, out=dgb[li][:, h, :], in0=identf[:], scalar1=wabs[:, s, h:h + 1], scalar2=None, op0=ALU.mult,
                                 waits=[t_idf, dg_readers[li]] if h == 0 else ())
                dg_readers[li] = []
                yield 1.0
                tI = None
                pending = None

                def flush(pend):
                    (pc, pr, items) = pend
                    tacc = None
                    for g, (prt, pta, prk) in enumerate(items):
                        h = 2 * pr + g
                        tacc = PE.op(T.matmul, pI[:], lhsT=dgb[li][:, h, :], rhs=prt[:],
                                     start=(pr == 0 and g == 0), stop=(pr == 3 and g == 1),
                                     waits=[pta, tdg, pI_free[0]])
                        rl.done(prk, tacc)
                    return tacc
                for c in range(nch):
                    for r in range(4):
                        if pending is not None:
                            tacc = flush(pending)
                            if pending[1] == 3:
                                pc = pending[0]
                                tI = DVE.op(V.tensor_copy, out=Isc[:, pc * 512:(pc + 1) * 512], in_=pI[:], waits=[tacc, Isc_free[0]])
                                pI_free[0] = [tI]
                        mm = []
                        for g in range(2):
                            ps, pfree, pk = pss.next()
                            tm = PE.op(T.matmul, ps[:], lhsT=qiT[li][g * 32:(g + 1) * 32, r, :], rhs=kiT[g * 32:(g + 1) * 32, c * 512:(c + 1) * 512],
                                       start=True, stop=True, waits=[tq, t_kiT, pfree])
                            mm.append((ps, pk, tm))
                        items = []
                        for g in range(2):
                            ps, pk, tm = mm[g]
                            rt, rfree, rk = rl.next()
                            if g == 0:
                                ta = ACT.op(A.activation, out=rt[:], in_=ps[:], func=AF.Relu, waits=[tm, rfree])
                            else:
                                ta = DVE.op(V.tensor_scalar, out=rt[:], in0=ps[:], scalar1=0.0, scalar2=None, op0=ALU.max, waits=[tm, rfree])
                            pss.done(pk, ta)
                            items.append((rt, ta, rk))
                        pending = (c, r, items)
                        yield 2.0
                tacc = flush(pending)
                tI = DVE.op(V.tensor_copy, out=Isc[:, pending[0] * 512:(pending[0] + 1) * 512], in_=pI[:], waits=[tacc, Isc_free[0]])
                pI_free[0] = [tI]
                qiT_readers[li].append(tacc)
                dg_readers[li].append(tacc)
                Isc_free[0] = []
                Iv = Isc[:, 0:nk]
                a = DVE.op(V.tensor_reduce, out=bs[:, 0:1], in_=Iv, axis=AX.X, op=ALU.max, waits=[tI])
                yield wb
                a = DVE.op(V.tensor_reduce, out=bs[:, 1:2], in_=Iv, axis=AX.X, op=ALU.min, waits=[a])
                a = DVE.op(V.scalar_tensor_tensor, out=bs[:, 2:3], in0=bs[:, 0:1], scalar=1.0, in1=bs[:, 1:2], op0=ALU.add, op1=ALU.subtract, waits=[a])
                a = DVE.op(V.tensor_tensor, out=Isc[:, nk - 512:nk], in0=Isc[:, nk - 512:nk], in1=cbf[:], op=ALU.add, waits=[a, t_cbf])
                yield wb
                lo = bs[:, 1:2]
                w0 = bs[:, 2:3]
                mid = bs[:, 3:4]
                cnt = bs[:, 4:5]
                gg = bs[:, 5:6]
                for it in range(1, NBIS + 1):
                    sc = 2.0 ** (-it)
                    a = DVE.op(V.tensor_scalar, out=mid, in0=w0, scalar1=sc, scalar2=lo, op0=ALU.mult, op1=ALU.add, waits=[a])
                    a = DVE.op(V.tensor_scalar, out=Mb[li][:, 0:nk], in0=Iv, scalar1=mid, scalar2=None, op0=ALU.is_ge, op1=ALU.add,
                               accum_out=cnt, waits=[a, Mb_readers[li]])
                    Mb_readers[li] = []
                    a = DVE.op(V.tensor_scalar, out=gg, in0=cnt, scalar1=TOPK - 0.5, scalar2=sc, op0=ALU.is_ge, op1=ALU.mult, waits=[a])
                    a = DVE.op(V.scalar_tensor_tensor, out=lo, in0=w0, scalar=gg, in1=lo, op0=ALU.mult, op1=ALU.add, waits=[a])
                    yield wb
                a = DVE.op(V.tensor_scalar, out=Mb[li][:, 0:nk], in0=Iv, scalar1=lo, scalar2=NEG, op0=ALU.is_lt, op1=ALU.mult, waits=[a])
                Mb_ready[li] = a
                Isc_free[0] = [a]

            def prep_steps(s):
                return 1 + 8 * (s + 1) + (2 + NBIS) * 0.76 * (s + 1)

            pump_state = {"gen": None, "budget": 0.0, "rate": 0.0}

            def pump():
                st = pump_state
                if st["gen"] is None:
                    return
                st["budget"] += st["rate"]
                while st["budget"] > 0.0 and st["gen"] is not None:
                    try:
                        st["budget"] -= next(st["gen"])
                    except StopIteration:
                        st["gen"] = None

            def attention(s, p, ot, ofree, owr):
                li = s % 2
                is_dsa = p < 4
                nkb = 4 * s + 4
                kmax = nkb * 128
                Kb, kfree, kk = Kr.next()
                Vb, vfree, vk = Vr.next()
                tK = SP.dma(Kb[:, 0:kmax], kT_scr[p, :, 0:kmax], waits=kfree)
                tV = []
                for b0 in range(0, nkb, 8):
                    nb = min(8, nkb - b0)
                    tV.append(SP.dma(Vb[:, b0:b0 + nb, :], v_scr[b0 * 128:(b0 + nb) * 128, p * VW:(p + 1) * VW].rearrange("(b t) w -> t b w", t=128), waits=vfree))
                qbd, qfree, qk = qbr.next()
                tq = [SP.dma(qbd[m * 64:(m + 1) * 64, m, :], qT_scr[p, m * 64:(m + 1) * 64, s * 128:(s + 1) * 128], waits=[qfree, t_qz]) for m in range(2)]
                accs = [pacc.next() for m in range(2)]
                vw = 66 if is_dsa else 130
                ntile = nkb // 2
                pend = {}

                def do_qk(ti):
                    ps, pfree, pk = pss.next()
                    tm = None
                    for bi in range(2):
                        kb_ = ti * 2 + bi
                        need_mask = is_dsa or (kb_ >= nkb - 4)
                        tm = PE.op(T.matmul, ps[:, bi * 256:(bi + 1) * 256], lhsT=Kb[:, kb_ * 128:(kb_ + 1) * 128],
                                   rhs=qbd[:].rearrange("p a b -> p (a b)"), start=True, stop=not need_mask,
                                   waits=[tK, tq, pfree] if bi == 0 else ())
                        if need_mask:
                            if is_dsa:
                                ml = Mb[li][:, kb_ * 128:(kb_ + 1) * 128]
                                mw = [Mb_ready[li]]
                            else:
                                cbi = kb_ - (nkb - 4)
                                ml = cb[:, cbi * 128:(cbi + 1) * 128]
                                mw = [t_cb]
                            tm = PE.op(T.matmul, ps[:, bi * 256:(bi + 1) * 256], lhsT=ml, rhs=ident2[:].rearrange("p a b -> p (a b)"),
                                       start=False, stop=True, waits=mw + [t_id2])
                    et, efree, ek = etr.next()
                    te = ACT.op(A.activation, out=et[:], in_=ps[:], func=AF.Exp, waits=[tm, efree])
                    pss.done(pk, te)
                    pend[ti] = (et, te, ek)

                tav = [None, None]

                def do_av(ti):
                    et, te, ek = pend.pop(ti)
                    tm = None
                    first = True
                    for bi in range(2):
                        kb_ = ti * 2 + bi
                        for m in range(2):
                            acc, afree, ak = accs[m]
                            rhs = Vb[:, kb_, m * 66:(m + 1) * 66] if is_dsa else Vb[:, kb_, 0:130]
                            tm = PE.op(T.matmul, acc[:, 0:vw], lhsT=et[:, bi * 256 + m * 128:bi * 256 + (m + 1) * 128], rhs=rhs,
                                       start=(kb_ == 0), stop=(kb_ == nkb - 1),
                                       waits=[te, tV, accs[0][1], accs[1][1]] if first else ())
                            first = False
                            tav[m] = tm
                    etr.done(ek, tm)
                LAG = 2
                for ti in range(ntile):
                    do_qk(ti)
                    if ti >= LAG:
                        do_av(ti - LAG)
                    pump()
                for ti in range(max(0, ntile - LAG), ntile):
                    do_av(ti)
                qbr.done(qk, tav[1])
                if is_dsa:
                    Mb_readers[li].append(tav[1])
                Kr.done(kk, tav[1])
                Vr.done(vk, tav[1])
                if is_dsa:
                    for m in range(2):
                        acc, afree, ak = accs[m]
                        a = ACT.op(A.activation, out=ez[:, m:m + 1], in_=acc[:, 64:65], func=AF.Ln, waits=[tav[1]])
                        a = ACT.op(A.activation, out=ez[:, m:m + 1], in_=ez[:, m:m + 1], func=AF.Exp, scale=-1.0, waits=[a])
                        a = ACT.op(A.activation, out=ot[:, p * 128 + m * 64:p * 128 + (m + 1) * 64], in_=acc[:, 0:64], func=AF.Copy,
                                   scale=ez[:, m:m + 1], waits=[a, ofree])
                        pacc.done(ak, a)
                        owr.append(a)
                else:
                    h = p - 4
                    acc1, _, ak1 = accs[0]
                    acc2, _, ak2 = accs[1]
                    a = ACT.op(A.activation, out=ez[:, 2:3], in_=acc1[:, 128:129], func=AF.Ln, waits=[tav[1], of_free[0]])
                    a = ACT.op(A.activation, out=ez[:, 2:3], in_=ez[:, 2:3], func=AF.Exp, scale=-1.0, waits=[a])
                    a = ACT.op(A.activation, out=of1[:], in_=acc1[:, 0:128], func=AF.Copy, scale=ez[:, 2:3], waits=[a])
                    pacc.done(ak1, a)
                    b = ACT.op(A.activation, out=ez[:, 3:4], in_=acc2[:, 128:129], func=AF.Ln, waits=[a])
                    b = ACT.op(A.activation, out=ez[:, 3:4], in_=ez[:, 3:4], func=AF.Exp, scale=-1.0, waits=[b])
                    b = ACT.op(A.activation, out=of2[:], in_=acc2[:, 0:128], func=AF.Copy, scale=ez[:, 3:4], waits=[b])
                    pacc.done(ak2, b)
                    b = DVE.op(V.scalar_tensor_tensor, out=of32[:], in0=of2[:], scalar=nlam[:, 0:1], in1=of1[:],
                               op0=ALU.mult, op1=ALU.add, waits=[a, b, t_nlam, of32_free[0]])
                    of_free[0] = [b]
                    c = ACT.op(A.activation, out=oj[:], in_=of32[:], func=AF.Square, accum_out=es_[:, 5:6], waits=[b])
                    c = DVE.op(V.tensor_scalar, out=es_[:, 6:7], in0=es_[:, 5:6], scalar1=1.0 / 128, scalar2=1e-5, op0=ALU.mult, op1=ALU.add, waits=[c])
                    c = ACT.op(A.activation, out=es_[:, 7:8], in_=es_[:, 6:7], func=AF.Ln, waits=[c])
                    c = ACT.op(A.activation, out=es_[:, 8:9], in_=es_[:, 7:8], func=AF.Exp, scale=-0.5, waits=[c])
                    c = DVE.op(V.scalar_tensor_tensor, out=ot[:, 512 + h * 128:512 + (h + 1) * 128], in0=of32[:], scalar=es_[:, 8:9],
                               in1=gsub[:], op0=ALU.mult, op1=ALU.mult, waits=[c, t_gs, ofree])
                    of32_free[0] = [c]
                    owr.append(c)

            of_free = [[]]
            of32_free = [[]]
            for _ in prep(0):
                pass
            for s in range(NQB):
                if s + 1 < NQB:
                    pump_state["gen"] = prep(s + 1)
                    pump_state["budget"] = 0.0
                    pump_state["rate"] = prep_steps(s + 1) / float(8 * (2 * s + 2)) * 1.15
                else:
                    pump_state["gen"] = None
                ot, ofree, ok_ = otr.next()
                owr = []
                for pi, p in enumerate([4, 5, 6, 7, 0, 1, 2, 3]):
                    attention(s, p, ot, ofree, owr)
                if pump_state["gen"] is not None:
                    for _ in pump_state["gen"]:
                        pass
                    pump_state["gen"] = None
                td = GQ.dma(o_scr[s * 128:(s + 1) * 128, :], ot[:], waits=owr)
                otr.done(ok_, td)
            barrier()

        def load_w(st, name, src, rows, cols, q, waits=()):
            nkc = rows // 128
            wt = sbt(st, name, [128, nkc, cols], BF16)
            srcv = src.rearrange("(kc p) n -> p kc n", p=128)
            toks = []
            step = 1024
            for c0 in range(0, cols, step):
                n = min(step, cols - c0)
                toks.append(q.dma(wt[:, :, c0:c0 + n], srcv[:, :, c0:c0 + n], waits=waits))
            return wt, toks

        with ExitStack() as p3:
            wg = sbt(p3, "wg", [128, 8, 2048], BF16)
            w_in_v = w_in.rearrange("(kc p) n -> p kc n", p=128)
            twg = [GQ.dma(wg[:, :, c0:c0 + 512], w_in_v[:, :, C_G + c0:C_G + c0 + 512]) for c0 in range(0, 2048, 512)]
            wbd, twbd = load_w(p3, "wbd", w_bd, 512, D, GQ)
            wbf, twbf = load_w(p3, "wbf", w_bf, 512, D, GQ)
            wo, two = load_w(p3, "wo", w_out, D, D, GQ)
            gbt = sbt(p3, "gbt", [128, 2048], F32)
            t_gb = SP.dma(gbt[:], bcast(gate_b))
            xr = Ring([sbt(p3, f"xt{i}", [128, D], F32) for i in range(4)])
            junk = Ring([sbt(p3, f"junk{i}", [128, D], BF16) for i in range(1)])
            ssr = Ring([sbt(p3, f"ss{i}", [128, 4], F32) for i in range(4)])
            hbr = Ring([sbt(p3, f"hb{i}", [128, D], BF16) for i in range(3)])
            hTr = Ring([sbt(p3, f"hT{i}", [128, 8, 128], BF16) for i in range(2)])
            obr = Ring([sbt(p3, f"ob{i}", [128, D], BF16) for i in range(3)])
            oTr = Ring([sbt(p3, f"oT{i}", [128, 8, 128], BF16) for i in range(2)])
            gat = sbt(p3, "gat", [128, 2048], F32)
            mrg = sbt(p3, "mrg", [128, D], F32)
            mrg2 = sbt(p3, "mrg2", [128, D], F32)
            mbr = Ring([sbt(p3, f"mb{i}", [128, D], BF16) for i in range(2)])
            mTr = Ring([sbt(p3, f"mT{i}", [128, 8, 128], BF16) for i in range(2)])
            x1r = Ring([sbt(p3, f"x1t{i}", [128, D], F32) for i in range(2)])
            psT = Ring([pst(p3, f"psT{i}", [128, D], BF16) for i in range(2)])
            pp = Ring([pst(p3, f"pp{i}", [128, 512], F32) for i in range(6)])
            x1_dmas = []

            def transpose8(src_bf, tsrc, dst_ring):
                ps, pfree, pk = psT.next()
                tt = None
                for kc in range(8):
                    tt = PE.op(T.transpose, ps[:, kc * 128:(kc + 1) * 128], src_bf[:, kc * 128:(kc + 1) * 128], ident[:],
                               waits=[tsrc, pfree] if kc == 0 else ())
                dT, dfree, dk = dst_ring.next()
                te = ACT.op(A.copy, out=dT[:].rearrange("p a b -> p (a b)"), in_=ps[:], waits=[tt, dfree])
                psT.done(pk, te)
                return dT, te, dk, tt

            def st_load(s):
                xt, xfree, xk = xr.next()
                tx = SP.dma(xt[:], xq[s * 128:(s + 1) * 128, :], waits=xfree)
                hb, hfree, hk = hbr.next()
                th = rms_norm_block((junk, ssr), xt[:], tx, gmix[:], t_gmix, 1e-6, D, hb[:], hfree)
                ob, ofree, ok_ = obr.next()
                to = SP.dma(ob[:], o_scr[s * 128:(s + 1) * 128, :], waits=[ofree])
                return dict(s=s, xt=xt, xk=xk, hb=hb, th=th, hk=hk, ob=ob, to=to, ok_=ok_)

            def st_T(c):
                hT, te, tk, tt = transpose8(c["hb"], c["th"], hTr)
                hbr.done(c["hk"], tt)
                oT, teo, ok2, tto = transpose8(c["ob"], c["to"], oTr)
                obr.done(c["ok_"], tto)
                c.update(hT=hT, te=te, tk=tk, oT=oT, teo=teo, ok2=ok2)

            def st_X(c):
                hT, te, tk = c["hT"], c["te"], c["tk"]
                oT, teo, ok2 = c["oT"], c["teo"], c["ok2"]
                tg_last = None
                for gc in range(4):
                    ps, pfree, pk = pp.next()
                    tm = None
                    for kc in range(8):
                        tm = PE.op(T.matmul, ps[:], lhsT=hT[:, kc, :], rhs=wg[:, kc, gc * 512:(gc + 1) * 512], start=(kc == 0), stop=(kc == 7),
                                   waits=[te, pfree, twg] if kc == 0 else ())
                    a_ = DVE.op(V.tensor_tensor, out=gat[:, gc * 512:(gc + 1) * 512], in0=ps[:], in1=gbt[:, gc * 512:(gc + 1) * 512], op=ALU.add,
                                waits=[tm, t_gb, gat_free[0]])
                    pp.done(pk, a_)
                    tg_last = ACT.op(A.activation, out=gat[:, gc * 512:(gc + 1) * 512], in_=gat[:, gc * 512:(gc + 1) * 512], func=AF.Sigmoid, waits=[a_])
                    if gc == 3:
                        hTr.done(tk, tm)
                gat_free[0] = []
                mtoks = []
                for br, (wt, twt) in enumerate([(wbd, twbd), (wbf, twbf)]):
                    for nc_ in range(2):
                        ps, pfree, pk = pp.next()
                        tm = None
                        for kc in range(4):
                            tm = PE.op(T.matmul, ps[:], lhsT=oT[:, br * 4 + kc, :], rhs=wt[:, kc, nc_ * 512:(nc_ + 1) * 512], start=(kc == 0), stop=(kc == 3),
                                       waits=[teo, pfree, twt] if kc == 0 else ())
                        dst = (mrg if br == 0 else mrg2)[:, nc_ * 512:(nc_ + 1) * 512]
                        a_ = DVE.op(V.tensor_tensor, out=dst, in0=ps[:], in1=gat[:, br * 1024 + nc_ * 512:br * 1024 + (nc_ + 1) * 512], op=ALU.mult,
                                    waits=[tm, tg_last, mrg_free[0]])
                        pp.done(pk, a_)
                        mtoks.append(a_)
                        if br == 1 and nc_ == 1:
                            oTr.done(ok2, tm)
                gat_free[0] = list(mtoks)
                mb, mfree, mk = mbr.next()
                tmb = DVE.op(V.tensor_tensor, out=mb[:], in0=mrg[:], in1=mrg2[:], op=ALU.add, waits=[mtoks, mfree])
                mrg_free[0] = [tmb]
                c.update(mb=mb, tmb=tmb, mk=mk)

            def st_Y(c):
                s_ = c["s"]
                mT, tem, mk2, ttm = transpose8(c["mb"], c["tmb"], mTr)
                mbr.done(c["mk"], ttm)
                x1t, x1free, x1k = x1r.next()
                xtoks = []
                for nc_ in range(2):
                    ps, pfree, pk = pp.next()
                    tm = None
                    for kc in range(8):
                        tm = PE.op(T.matmul, ps[:], lhsT=mT[:, kc, :], rhs=wo[:, kc, nc_ * 512:(nc_ + 1) * 512], start=(kc == 0), stop=(kc == 7),
                                   waits=[tem, pfree, two] if kc == 0 else ())
                    a_ = DVE.op(V.tensor_tensor, out=x1t[:, nc_ * 512:(nc_ + 1) * 512], in0=ps[:], in1=c["xt"][:, nc_ * 512:(nc_ + 1) * 512], op=ALU.add,
                                waits=[tm, x1free])
                    pp.done(pk, a_)
                    xtoks.append(a_)
                    if nc_ == 1:
                        mTr.done(mk2, tm)
                xr.done(c["xk"], xtoks)
                td = GQ.dma(x1_scr[s_ * 128:(s_ + 1) * 128, :], x1t[:], waits=xtoks)
                x1r.done(x1k, td)
                x1_dmas.append(td)

            gat_free = [[]]
            mrg_free = [[]]
            ctx = {}
            ctx[0] = st_load(0)
            st_T(ctx[0])
            if NQB > 1:
                ctx[1] = st_load(1)
            prevY = None
            for s in range(NQB):
                if s + 2 < NQB:
                    ctx[s + 2] = st_load(s + 2)
                st_X(ctx[s])
                if s + 1 < NQB:
                    st_T(ctx[s + 1])
                if prevY is not None:
                    st_Y(prevY)
                prevY = ctx.pop(s)
            st_Y(prevY)
            barrier()
        early.close()

        with ExitStack() as p4:
            w1, tw1 = load_w(p4, "w1", w_f1, D, 2 * DFF, GQ)
            w2, tw2 = load_w(p4, "w2", w_f2, DFF, D, GQ)
            gffn = sbt(p4, "gffn", [128, D], F32)
            gfin = sbt(p4, "gfin", [128, D], F32)
            t_gffn = SP.dma(gffn[:], bcast(norm_ffn_g))
            t_gfin = SP.dma(gfin[:], bcast(norm_fin_g))
            x1r = Ring([sbt(p4, f"x1b{i}", [128, D], F32) for i in range(2)])
            junk = Ring([sbt(p4, f"junk{i}", [128, D], BF16) for i in range(1)])
            ssr = Ring([sbt(p4, f"ss{i}", [128, 4], F32) for i in range(2)])
            hbr = Ring([sbt(p4, f"hb{i}", [128, D], BF16) for i in range(2)])
            h2T = sbt(p4, "h2T", [128, 8, 512], BF16)
            actT = sbt(p4, "actT", [128, NFC, 512], BF16)
            sgr = Ring([sbt(p4, f"sg{i}", [128, 512], F32) for i in range(2)])
            x2r = Ring([sbt(p4, f"x2t{i}", [128, D], F32) for i in range(2)])
            psT = Ring([pst(p4, f"psT{i}", [128, D], BF16) for i in range(2)])
            pp = Ring([pst(p4, f"pp{i}", [128, 512], F32) for i in range(6)])
            h2T_free = []
            actT_free = []
            for grp in range(NQB // 4):
                th2 = []
                for bi in range(4):
                    s = grp * 4 + bi
                    x1t, x1free, x1k = x1r.next()
                    tx = SP.dma(x1t[:], x1_scr[s * 128:(s + 1) * 128, :], waits=[x1free])
                    hb, hfree, hk = hbr.next()
                    th = rms_norm_block((junk, ssr), x1t[:], tx, gffn[:], t_gffn, 1e-6, D, hb[:], hfree)
                    x1r.done(x1k, th)
                    ps, pfree, pk = psT.next()
                    tt = None
                    for kc in range(8):
                        tt = PE.op(T.transpose, ps[:, kc * 128:(kc + 1) * 128], hb[:, kc * 128:(kc + 1) * 128], ident[:],
                                   waits=[th, pfree] if kc == 0 else ())
                    hbr.done(hk, tt)
                    te = ACT.op(A.copy, out=h2T[:, :, bi * 128:(bi + 1) * 128], in_=ps[:].rearrange("p (a b) -> p a b", a=8), waits=[tt, h2T_free])
                    psT.done(pk, te)
                    th2.append(te)
                h2T_free = []
                tact = []
                last_mm = None
                for f in range(NFC):
                    psg, pfree, pkg = pp.next()
                    tmg = None
                    for kc in range(8):
                        tmg = PE.op(T.matmul, psg[:], lhsT=w1[:, kc, f * 128:(f + 1) * 128], rhs=h2T[:, kc, :], start=(kc == 0), stop=(kc == 7),
                                    waits=[th2, pfree, tw1] if kc == 0 else ())
                    psu, pfree, pku = pp.next()
                    tmu = None
                    for kc in range(8):
                        tmu = PE.op(T.matmul, psu[:], lhsT=w1[:, kc, DFF + f * 128:DFF + (f + 1) * 128], rhs=h2T[:, kc, :], start=(kc == 0), stop=(kc == 7),
                                    waits=[pfree] if kc == 0 else ())
                    last_mm = tmu
                    sg, sfree, sk = sgr.next()
                    ta = ACT.op(A.activation, out=sg[:], in_=psg[:], func=AF.Silu, waits=[tmg, sfree])
                    pp.done(pkg, ta)
                    tb_ = DVE.op(V.tensor_tensor, out=actT[:, f, :], in0=psu[:], in1=sg[:], op=ALU.mult, waits=[tmu, ta, actT_free])
                    pp.done(pku, tb_)
                    sgr.done(sk, tb_)
                    tact.append(tb_)
                h2T_free = [last_mm]
                actT_free = []
                last_o = None
                for bi in range(4):
                    s = grp * 4 + bi
                    x2, x2free, x2k = x2r.next()
                    tx2 = SP.dma(x2[:], x1_scr[s * 128:(s + 1) * 128, :], waits=[x2free])
                    xtoks = []
                    for nc_ in range(2):
                        ps, pfree, pk = pp.next()
                        tm = None
                        for f in range(NFC):
                            tm = PE.op(T.matmul, ps[:], lhsT=actT[:, f, bi * 128:(bi + 1) * 128], rhs=w2[:, f, nc_ * 512:(nc_ + 1) * 512],
                                       start=(f == 0), stop=(f == NFC - 1), waits=[tact, pfree, tw2] if f == 0 else ())
                        last_o = tm
                        a = DVE.op(V.tensor_tensor, out=x2[:, nc_ * 512:(nc_ + 1) * 512], in0=ps[:], in1=x2[:, nc_ * 512:(nc_ + 1) * 512], op=ALU.add,
                                   waits=[tm, tx2])
                        pp.done(pk, a)
                        xtoks.append(a)
                    e = rms_norm_block((junk, ssr), x2[:], xtoks, gfin[:], t_gfin, 1e-6, D, x2[:], [])
                    td = SP.dma(out[s * 128:(s + 1) * 128, :], x2[:], waits=[e])
                    x2r.done(x2k, td)
                actT_free = [last_o]
            barrier()
    return nc


_NC_CACHE = {}


def _get_nc(debug=False):
    if debug not in _NC_CACHE:
        _NC_CACHE[debug] = build(debug)
    return _NC_CACHE[debug]


def make_in_maps(inputs):
    x = np.ascontiguousarray(np.asarray(inputs["x"], dtype=np.float32))
    pos = np.asarray(inputs["positions"]).astype(np.int32)
    in_maps = []
    kk = np.arange(512)[None, :]
    qq = np.arange(128)[:, None]
    for c in range(8):
        b, j = c // 4, c % 4
        blocks = [4 * s + j for s in range(NQB)]
        xqc = np.concatenate([x[b, q * 128:(q + 1) * 128] for q in blocks], axis=0)
        posk = np.ascontiguousarray(pos[b].reshape(NTB, 128).T)
        posq = np.ascontiguousarray(np.stack([pos[b, q * 128:(q + 1) * 128] for q in blocks], axis=1))
        cm = np.where(kk <= j * 128 + qq, 0.0, NEG).astype(np.float32)
        m = {"xs": x[b], "xq": np.ascontiguousarray(xqc), "posk": posk, "posq": posq, "cmask": cm}
        for name in ["norm_mix_g", "w_in", "idx_k_norm_g", "idx_k_norm_b", "diff_lambda_q1", "diff_lambda_k1",
                     "diff_lambda_q2", "diff_lambda_k2", "diff_subln_g", "gate_b", "w_branch_dsa", "w_branch_diff",
                     "w_out", "norm_ffn_g", "w_ffn_in", "w_ffn_out"]:
            m[name] = np.ascontiguousarray(np.asarray(inputs[name], dtype=np.float32)[0])
        m["norm_final_g"] = np.ascontiguousarray(np.asarray(inputs["norm_final_g"], dtype=np.float32))
        in_maps.append(m)
    return in_maps


def kernel(**inputs):
    nc = _get_nc(False)
    in_maps = make_in_maps(inputs)
    res = run_bass_kernel_spmd(nc, in_maps, core_ids=list(range(8)))
    outp = np.zeros((2, S, D), dtype=np.float32)
    for c in range(8):
        b, j = c // 4, c % 4
        o = res.results[c]["out"]
        for s in range(NQB):
            q = 4 * s + j
            outp[b, q * 128:(q + 1) * 128] = o[s * 128:(s + 1) * 128]
    return outp
```

```python
import os
import math
import numpy as np
from contextlib import ExitStack
import concourse.bass as bass
import concourse.mybir as mybir
from concourse.bass_utils import run_bass_kernel_spmd

F32 = mybir.dt.float32
F32R = mybir.dt.float32r
BF16 = mybir.dt.bfloat16
I32 = mybir.dt.int32
AF = mybir.ActivationFunctionType
ALU = mybir.AluOpType
AX = mybir.AxisListType

S = 8192
D = 1024
NTB = S // 128
NQB = 16
NQ = NQB * 128
DFF = 2816
NFC = DFF // 128
TOPK = 256
NBIS = 16
NEG = -30000.0
C_QA, C_KA, C_VA, C_QI, C_KI, C_WI, C_QB, C_KB, C_VB, C_G = 0, 512, 1024, 1536, 1792, 1824, 1832, 2344, 2856, 3368
D_IN = 5416
VW = 132
TWO_PI = 2.0 * math.pi


class Eng:
    def __init__(self, nc, es, eng, name):
        self.eng = eng
        self.name = name
        self.sem = es.enter_context(nc.semaphore("sem_" + name))
        self.count = 0
        self.seen = {}

    def wait(self, toks):
        for t in toks:
            if t is None:
                continue
            if isinstance(t, list):
                self.wait(t)
                continue
            sem, val, key = t
            if self.seen.get(key, 0) >= val:
                continue
            self.eng.wait_ge(sem, val)
            self.seen[key] = val

    def op(self, fn, *args, waits=(), **kw):
        self.wait(waits)
        inst = fn(*args, **kw)
        self.count += 1
        inst.then_inc(self.sem, 1)
        tok = (self.sem, self.count, self.name)
        self.seen[self.name] = self.count - 1 if False else self.seen.get(self.name, 0)
        return tok


class DmaQ:
    def __init__(self, nc, es, eng, name, nsem=12):
        self.eng = eng
        self.name = name
        self.sems = [es.enter_context(nc.semaphore(f"dsem_{name}_{i}")) for i in range(nsem)]
        self.vals = [0] * nsem
        self.i = 0
        self.seen = {}

    def wait(self, toks):
        for t in toks:
            if t is None:
                continue
            if isinstance(t, list):
                self.wait(t)
                continue
            sem, val, key = t
            if self.seen.get(key, 0) >= val:
                continue
            self.eng.wait_ge(sem, val)
            self.seen[key] = val

    def dma(self, out, in_, waits=(), **kw):
        k = self.i
        self.i = (self.i + 1) % len(self.sems)
        key = f"{self.name}_{k}"
        if self.vals[k] > 0:
            self.wait([(self.sems[k], self.vals[k], key)])
        self.wait(waits)
        self.vals[k] += 16
        self.eng.dma_start(out=out, in_=in_, **kw).then_inc(self.sems[k], 16)
        return (self.sems[k], self.vals[k], key)

    def all_toks(self):
        return [(self.sems[k], self.vals[k], f"{self.name}_{k}") for k in range(len(self.sems)) if self.vals[k] > 0]


class Ring:
    def __init__(self, tiles):
        self.tiles = tiles
        self.rd = [[] for _ in tiles]
        self.i = -1

    def next(self):
        self.i = (self.i + 1) % len(self.tiles)
        k = self.i
        toks = self.rd[k]
        self.rd[k] = []
        return self.tiles[k], toks, k

    def done(self, k, tok):
        self.rd[k].append(tok)


def build(debug=False):
    nc = bass.Bass("TRN2", target_bir_lowering=False)
    dt_in = lambda n, s, d=F32: nc.dram_tensor(n, s, d, kind="ExternalInput").ap()
    xs = dt_in("xs", [S, D])
    xq = dt_in("xq", [NQ, D])
    posk = dt_in("posk", [128, NTB], I32)
    posq = dt_in("posq", [128, NQB], I32)
    cmask = dt_in("cmask", [128, 512])
    norm_mix_g = dt_in("norm_mix_g", [D])
    w_in = dt_in("w_in", [D, D_IN])
    idx_g = dt_in("idx_k_norm_g", [32])
    idx_b = dt_in("idx_k_norm_b", [32])
    lq1 = dt_in("diff_lambda_q1", [64])
    lk1 = dt_in("diff_lambda_k1", [64])
    lq2 = dt_in("diff_lambda_q2", [64])
    lk2 = dt_in("diff_lambda_k2", [64])
    subln_g = dt_in("diff_subln_g", [128])
    gate_b = dt_in("gate_b", [2048])
    w_bd = dt_in("w_branch_dsa", [512, D])
    w_bf = dt_in("w_branch_diff", [512, D])
    w_out = dt_in("w_out", [D, D])
    norm_ffn_g = dt_in("norm_ffn_g", [D])
    w_f1 = dt_in("w_ffn_in", [D, 2 * DFF])
    w_f2 = dt_in("w_ffn_out", [DFF, D])
    norm_fin_g = dt_in("norm_final_g", [D])
    out = nc.dram_tensor("out", [NQ, D], F32, kind="ExternalOutput").ap()
    skind = "ExternalOutput" if debug else "Internal"
    kT_scr = nc.dram_tensor("kT_scr", [8, 128, S], BF16, kind=skind).ap()
    kiT_scr = nc.dram_tensor("kiT_scr", [32, S], BF16, kind=skind).ap()
    v_scr = nc.dram_tensor("v_scr", [S, 8 * VW], BF16, kind=skind).ap()
    qT_scr = nc.dram_tensor("qT_scr", [8, 128, NQ], BF16, kind=skind).ap()
    qiT_scr = nc.dram_tensor("qiT_scr", [64, 4, NQ], BF16, kind=skind).ap()
    o_scr = nc.dram_tensor("o_scr", [NQ, D], BF16, kind=skind).ap()
    x1_scr = nc.dram_tensor("x1_scr", [NQ, D], F32, kind=skind).ap()

    with ExitStack() as es:
        uid = [0]

        def sbt(st, n, s, d):
            uid[0] += 1
            return st.enter_context(nc.sbuf_tensor(f"{n}_{uid[0]}", s, d))

        def pst(st, n, s, d):
            uid[0] += 1
            return st.enter_context(nc.psum_tensor(f"{n}_{uid[0]}", s, d))

        ident = sbt(es, "ident", [128, 128], BF16)
        early = ExitStack()
        cosk = sbt(early, "cosk", [128, NTB, 8], F32)
        sink = sbt(early, "sink", [128, NTB, 8], F32)
        cosq = sbt(early, "cosq", [128, NQB, 8], F32)
        sinq = sbt(early, "sinq", [128, NQB, 8], F32)
        gmix = sbt(early, "gmix", [128, D], F32)
        wabs = sbt(early, "wabs", [128, NQB, 8], F32)
        wsgn = sbt(early, "wsgn", [128, NQB, 8], F32)
        nlam = sbt(early, "nlam", [128, 1], F32)
        small = sbt(early, "small", [128, 64], F32)
        cb = sbt(early, "cb", [128, 512], BF16)
        cbf = sbt(early, "cbf", [128, 512], F32)

        es.enter_context(nc.Block())
        PE = Eng(nc, es, nc.tensor, "pe")
        ACT = Eng(nc, es, nc.scalar, "act")
        DVE = Eng(nc, es, nc.vector, "dve")
        POOL = Eng(nc, es, nc.gpsimd, "pool")
        SP = DmaQ(nc, es, nc.sync, "sp", 16)
        GQ = DmaQ(nc, es, nc.gpsimd, "gq", 8)
        V = nc.vector
        A = nc.scalar
        T = nc.tensor

        def bcast(ap1d, n=128):
            return ap1d.partition_broadcast(n)

        def barrier():
            toks = [(e.sem, e.count, e.name) for e in (PE, ACT, DVE, POOL) if e.count > 0]
            toks += SP.all_toks() + GQ.all_toks()
            for e in (PE, ACT, DVE, POOL, SP):
                e.wait(toks)

        t = POOL.op(nc.gpsimd.memset, ident[:], 1.0)
        t_ident = POOL.op(nc.gpsimd.affine_select, out=ident[:], in_=ident[:], pattern=[[-1, 128]],
                          compare_op=ALU.is_equal, fill=0.0, base=0, channel_multiplier=1, waits=[t])
        t_gmix = SP.dma(gmix[:], bcast(norm_mix_g))
        t_cbf = SP.dma(cbf[:], cmask)
        t_cb = DVE.op(V.tensor_copy, out=cb[:], in_=cbf[:], waits=[t_cbf])

        with ExitStack() as p0:
            lt = sbt(p0, "lt", [128, 4, 64], F32)
            lj = sbt(p0, "lj", [128, 64], F32)
            tl = [SP.dma(lt[:, i, :], bcast(a)) for i, a in enumerate([lq1, lk1, lq2, lk2])]
            t1 = DVE.op(V.tensor_tensor, out=lj[:], in0=lt[:, 0, :], in1=lt[:, 1, :], op=ALU.mult, waits=tl)
            t1 = DVE.op(V.tensor_reduce, out=small[:, 0:1], in_=lj[:], axis=AX.X, op=ALU.add, waits=[t1])
            t2 = DVE.op(V.tensor_tensor, out=lj[:], in0=lt[:, 2, :], in1=lt[:, 3, :], op=ALU.mult, waits=[t1])
            t2 = DVE.op(V.tensor_reduce, out=small[:, 1:2], in_=lj[:], axis=AX.X, op=ALU.add, waits=[t2])
            t3 = ACT.op(A.activation, out=small[:, 2:4], in_=small[:, 0:2], func=AF.Exp, waits=[t2])
            t_nlam = DVE.op(V.scalar_tensor_tensor, out=nlam[:], in0=small[:, 3:4], scalar=-0.2, in1=small[:, 2:3],
                            op0=ALU.add, op1=ALU.subtract, waits=[t3])

            invf = sbt(p0, "invf", [128, 8], F32)
            tinv = None
            for i in range(8):
                fv = float(np.power(np.float32(500000.0), -np.float32(2 * i) / np.float32(16)))
                tinv = DVE.op(V.memset, invf[:, i:i + 1], fv)

            def rope_table(pos_ap, n, cos_t, sin_t, nm):
                pi_ = sbt(p0, "pi_" + nm, [128, n], I32)
                pf = sbt(p0, "pf_" + nm, [128, n], F32)
                ang = sbt(p0, "ang_" + nm, [128, n, 8], F32)
                yy = sbt(p0, "yy_" + nm, [128, n, 8], F32)
                ni = sbt(p0, "ni_" + nm, [128, n, 8], I32)
                tp = SP.dma(pi_[:], pos_ap)
                a = DVE.op(V.tensor_copy, out=pf[:], in_=pi_[:], waits=[tp])
                a = DVE.op(V.tensor_tensor, out=ang[:], in0=pf[:].unsqueeze(2).to_broadcast([128, n, 8]),
                           in1=invf[:].unsqueeze(1).to_broadcast([128, n, 8]), op=ALU.mult, waits=[a, tinv])

                def reduce_sin(src_add, dst):
                    b = DVE.op(V.tensor_scalar, out=yy[:], in0=ang[:], scalar1=src_add, scalar2=1.0 / TWO_PI,
                               op0=ALU.add, op1=ALU.mult, waits=[a])
                    b = DVE.op(V.tensor_copy, out=ni[:], in_=yy[:], waits=[b])
                    b = DVE.op(V.tensor_copy, out=yy[:], in_=ni[:], waits=[b])
                    c1 = 6.28125
                    c2 = TWO_PI - 6.28125
                    b = DVE.op(V.scalar_tensor_tensor, out=dst, in0=yy[:], scalar=-c1, in1=ang[:], op0=ALU.mult, op1=ALU.add, waits=[b])
                    b = DVE.op(V.scalar_tensor_tensor, out=dst, in0=yy[:], scalar=-c2, in1=dst, op0=ALU.mult, op1=ALU.add, waits=[b])
                    b = DVE.op(V.tensor_scalar, out=dst, in0=dst, scalar1=src_add, scalar2=3.1415925, op0=ALU.add, op1=ALU.min, waits=[b])
                    b = DVE.op(V.tensor_scalar, out=dst, in0=dst, scalar1=-3.1415925, scalar2=None, op0=ALU.max, waits=[b])
                    return ACT.op(A.activation, out=dst, in_=dst, func=AF.Sin, waits=[b])
                ts = reduce_sin(0.0, sin_t[:])
                tc = reduce_sin(math.pi / 2.0, cos_t[:])
                return [ts, tc]
            t_ropek = rope_table(posk, NTB, cosk, sink, "k")
            t_ropeq = rope_table(posq, NQB, cosq, sinq, "q")
            barrier()

        def rms_norm_block(st_rings, x_tile, tx, g_tile, tg, eps, n_feat, hb_tile, hb_free):
            junk, ssr = st_rings
            jt, jfree, jk = junk.next()
            col, cfree, ck = ssr.next()
            a = ACT.op(A.activation, out=jt[:, :n_feat], in_=x_tile, func=AF.Square, accum_out=col[:, 0:1],
                       waits=[tx, jfree, cfree])
            junk.done(jk, a)
            b = DVE.op(V.tensor_scalar, out=col[:, 1:2], in0=col[:, 0:1], scalar1=1.0 / n_feat, scalar2=eps,
                       op0=ALU.mult, op1=ALU.add, waits=[a])
            c = ACT.op(A.activation, out=col[:, 2:3], in_=col[:, 1:2], func=AF.Sqrt, waits=[b])
            d = DVE.op(V.reciprocal, out=col[:, 3:4], in_=col[:, 2:3], waits=[c])
            e = DVE.op(V.scalar_tensor_tensor, out=hb_tile, in0=x_tile, scalar=col[:, 3:4], in1=g_tile,
                       op0=ALU.mult, op1=ALU.mult, waits=[d, tg, hb_free])
            ssr.done(ck, e)
            return e

        rope_last = []

        def rope_apply(tile3, H, half, cs, sn, tmp, waits):
            x1 = tile3[:, :, 0:half]
            x2 = tile3[:, :, half:2 * half]
            cB = cs.unsqueeze(1).to_broadcast([128, H, half])
            sB = sn.unsqueeze(1).to_broadcast([128, H, half])
            tv = lambda i: tmp[:, i, 0:H * half].rearrange("p (h d) -> p h d", h=H)
            waits = list(waits) + rope_last
            a1 = DVE.op(V.tensor_tensor, out=tv(0), in0=x1, in1=cB, op=ALU.mult, waits=waits)
            a2 = DVE.op(V.tensor_tensor, out=tv(1), in0=x2, in1=sB, op=ALU.mult, waits=waits)
            a3 = DVE.op(V.tensor_tensor, out=tv(2), in0=x2, in1=cB, op=ALU.mult, waits=waits)
            a4 = DVE.op(V.tensor_tensor, out=tv(3), in0=x1, in1=sB, op=ALU.mult, waits=waits)
            b1 = DVE.op(V.tensor_tensor, out=x1, in0=tv(0), in1=tv(1), op=ALU.subtract, waits=[a1, a2, a3, a4])
            b2 = DVE.op(V.tensor_tensor, out=x2, in0=tv(2), in1=tv(3), op=ALU.add, waits=[a1, a2, a3, a4])
            rope_last[:] = [b1, b2]
            return [b1, b2]

        with ExitStack() as p1:
            NKV = 2080
            NQC = 1288
            wkv = sbt(p1, "wkv", [128, 8, NKV], BF16)
            wq = sbt(p1, "wq", [128, 8, NQC], BF16)
            w_in_v = w_in.rearrange("(kc p) n -> p kc n", p=128)
            tw = []
            for (dst0, c0, n) in [(0, C_KA, 512), (512, C_KB, 512), (1024, C_VA, 512), (1536, C_VB, 512), (2048, C_KI, 32)]:
                tw.append(GQ.dma(wkv[:, :, dst0:dst0 + n], w_in_v[:, :, c0:c0 + n]))
            twq = []
            for (dst0, c0, n) in [(0, C_QA, 512), (512, C_QB, 512), (1024, C_QI, 256), (1280, C_WI, 8)]:
                twq.append(GQ.dma(wq[:, :, dst0:dst0 + n], w_in_v[:, :, c0:c0 + n]))
            lng = sbt(p1, "lng", [128, 32], F32)
            lnb = sbt(p1, "lnb", [128, 32], F32)
            t_lng = SP.dma(lng[:], bcast(idx_g))
            t_lnb = SP.dma(lnb[:], bcast(idx_b))

            xr = Ring([sbt(p1, f"xt{i}", [128, D], F32) for i in range(3)])
            junk = Ring([sbt(p1, f"junk{i}", [128, D], BF16) for i in range(1)])
            ssr = Ring([sbt(p1, f"ss{i}", [128, 4], F32) for i in range(4)])
            hbr = Ring([sbt(p1, f"hb{i}", [128, D], BF16) for i in range(3)])
            hTr = Ring([sbt(p1, f"hT{i}", [128, 8, 128], BF16) for i in range(3)])
            kfr = Ring([sbt(p1, f"kf{i}", [128, 1024], F32) for i in range(2)])
            kbr = Ring([sbt(p1, f"kb{i}", [128, 1024], BF16) for i in range(3)])
            kTr = Ring([sbt(p1, f"kTt{i}", [128, 8, 128], BF16) for i in range(2)])
            vtr = Ring([sbt(p1, f"vt{i}", [128, 8, VW], BF16) for i in range(2)])
            rtmp = sbt(p1, "rtmp", [128, 4, 128], F32)
            kif = sbt(p1, "kif", [128, 8], F32)
            kic = sbt(p1, "kic", [128, 32], F32)
            kij = sbt(p1, "kij", [128, 32], F32)
            kibr = Ring([sbt(p1, f"kib{i}", [128, 32], BF16) for i in range(3)])
            kiTr = Ring([sbt(p1, f"kiT{i}", [32, 128], BF16) for i in range(2)])
            qwr = Ring([sbt(p1, f"qw{i}", [128, 264], F32) for i in range(2)])
            qibr = Ring([sbt(p1, f"qib{i}", [128, 256], BF16) for i in range(3)])
            qiTr = Ring([sbt(p1, f"qiT{i}", [32, 8, 128], BF16) for i in range(2)])
            psT = Ring([pst(p1, f"psT{i}", [128, D], BF16) for i in range(2)])
            pp = Ring([pst(p1, f"pp{i}", [128, 512], F32) for i in range(4)])
            pkT = Ring([pst(p1, f"pkT{i}", [128, D], BF16) for i in range(2)])
            t_vinit = []
            for vt_ in vtr.tiles:
                t0 = POOL.op(nc.gpsimd.memset, vt_[:], 0.0)
                t1_ = POOL.op(nc.gpsimd.memset, vt_[:, 0:4, :].rearrange("p a (s e) -> p (a s) e", s=2)[:, :, 64:65], 1.0, waits=[t0])
                t_vinit.append(POOL.op(nc.gpsimd.memset, vt_[:, 4:8, 128:129], 1.0, waits=[t0, t1_]))

            def aevac(out_ap, in_ap, waits):
                return ACT.op(A.copy, out=out_ap, in_=in_ap, waits=waits)

            def stageA1(src_rows):
                xt, xfree, xk = xr.next()
                tx = SP.dma(xt[:], src_rows, waits=xfree)
                hb, hfree, hk = hbr.next()
                th = rms_norm_block((junk, ssr), xt[:], tx, gmix[:], t_gmix, 1e-6, D, hb[:], hfree)
                xr.done(xk, th)
                return hb, th, hk

            def stageA2(hb, th, hk):
                ps, pfree, pk = psT.next()
                tt = None
                for kc in range(8):
                    tt = PE.op(T.transpose, ps[:, kc * 128:(kc + 1) * 128], hb[:, kc * 128:(kc + 1) * 128], ident[:],
                               waits=[th, t_ident, pfree] if kc == 0 else ())
                hbr.done(hk, tt)
                hT, tfree, tk = hTr.next()
                te = aevac(hT[:].rearrange("p a b -> p (a b)"), ps[:], [tt, tfree])
                psT.done(pk, te)
                return hT, te, tk

            def project(hT, th, w_tile, c0, n, wtoks):
                ps, pfree, pk = pp.next()
                tt = None
                for kc in range(8):
                    tt = PE.op(T.matmul, ps[:, 0:n], lhsT=hT[:, kc, :], rhs=w_tile[:, kc, c0:c0 + n],
                               start=(kc == 0), stop=(kc == 7), waits=[th, pfree, wtoks] if kc == 0 else ())
                return ps, tt, pk

            def transpose_out(src_bf, tsrc, nchunk, width, ring_sb, dst_dram):
                ps, pfree, pk = pkT.next()
                tt = None
                for c in range(nchunk):
                    tt = PE.op(T.transpose, ps[0:width, c * 128:(c + 1) * 128], src_bf[:, c * width:(c + 1) * width], ident[:],
                               waits=[tsrc, pfree, t_ident] if c == 0 else ())
                sbT, sfree, sk = ring_sb.next()
                te = aevac(sbT[:].rearrange("p a b -> p (a b)") if len(sbT.shape) == 3 else sbT[:],
                           ps[0:width, 0:nchunk * 128], [tt, sfree])
                pkT.done(pk, te)
                td = GQ.dma(dst_dram, sbT[:], waits=[te])
                ring_sb.done(sk, td)
                return tt

            def transpose_out_qi(src_bf, tsrc, s):
                ps, pfree, pk = pkT.next()
                tt = None
                for c in range(8):
                    tt = PE.op(T.transpose, ps[0:32, c * 128:(c + 1) * 128], src_bf[:, c * 32:(c + 1) * 32], ident[:],
                               waits=[tsrc, pfree, t_ident] if c == 0 else ())
                sbT, sfree, sk = qiTr.next()
                te = aevac(sbT[:].rearrange("p a b -> p (a b)"), ps[0:32, 0:1024], [tt, sfree])
                pkT.done(pk, te)
                for g in range(2):
                    td = GQ.dma(qiT_scr[g * 32:(g + 1) * 32, :, s * 128:(s + 1) * 128], sbT[:, g::2, :], waits=[te])
                    qiTr.done(sk, td)
                return tt

            def qk_pair(hT, th, w_tile, wtoks, cs, sn, scale, dst_dram):
                kf, kfree, kk = kfr.next()
                tes = []
                for gi in range(2):
                    ps, tmm, pk = project(hT, th, w_tile, gi * 512, 512, wtoks)
                    te = aevac(kf[:, gi * 512:(gi + 1) * 512], ps[:], [tmm, kfree])
                    pp.done(pk, te)
                    tes.append(te)
                tr = rope_apply(kf[:].rearrange("p (h d) -> p h d", h=16), 16, 8, cs, sn, rtmp, tes)
                kb, bfree, bk = kbr.next()
                if scale == 1.0:
                    tcst = DVE.op(V.tensor_copy, out=kb[:], in_=kf[:], waits=[tr, bfree])
                else:
                    tcst = DVE.op(V.tensor_scalar, out=kb[:], in0=kf[:], scalar1=scale, scalar2=None, op0=ALU.mult, waits=[tr, bfree])
                kfr.done(kk, tcst)
                return kb, bk, tcst

            def stageB_kv(tb, hT, th, hk):
                cs = cosk[:, tb, :]
                sn = sink[:, tb, :]
                kb, bk, tcst = qk_pair(hT, th, wkv, tw, cs, sn, 1.0, None)
                vt, vfree, vk = vtr.next()
                tvs = []
                for gi in (2, 3):
                    ps, tmm, pk = project(hT, th, wkv, gi * 512, 512, tw)
                    if gi == 2:
                        dstv = vt[:, 0:4, :].rearrange("p a (s e) -> p (a s) e", s=2)[:, :, 0:64]
                        te = aevac(dstv, ps[:].rearrange("p (h e) -> p h e", e=64), [tmm, vfree, t_vinit])
                    else:
                        te = aevac(vt[:, 4:8, 0:128], ps[:].rearrange("p (h e) -> p h e", e=128), [tmm, vfree, t_vinit])
                    pp.done(pk, te)
                    tvs.append(te)
                td = GQ.dma(v_scr[tb * 128:(tb + 1) * 128, :], vt[:].rearrange("p a b -> p (a b)"), waits=tvs)
                vtr.done(vk, td)
                ps, tmm, pk = project(hT, th, wkv, 2048, 32, tw)
                hTr.done(hk, tmm)
                a = DVE.op(V.tensor_reduce, out=kif[:, 0:1], in_=ps[:, 0:32], axis=AX.X, op=ALU.add, waits=[tmm])
                a = DVE.op(V.tensor_scalar, out=kif[:, 1:2], in0=kif[:, 0:1], scalar1=1.0 / 32, scalar2=None, op0=ALU.mult, waits=[a])
                a = DVE.op(V.tensor_scalar, out=kic[:], in0=ps[:, 0:32], scalar1=kif[:, 1:2], scalar2=None, op0=ALU.subtract, waits=[a])
                pp.done(pk, a)
                b = ACT.op(A.activation, out=kij[:], in_=kic[:], func=AF.Square, accum_out=kif[:, 2:3], waits=[a])
                b = DVE.op(V.tensor_scalar, out=kif[:, 3:4], in0=kif[:, 2:3], scalar1=1.0 / 32, scalar2=1e-6, op0=ALU.mult, op1=ALU.add, waits=[b])
                b = ACT.op(A.activation, out=kif[:, 4:5], in_=kif[:, 3:4], func=AF.Sqrt, waits=[b])
                b = DVE.op(V.reciprocal, out=kif[:, 5:6], in_=kif[:, 4:5], waits=[b])
                b = DVE.op(V.scalar_tensor_tensor, out=kic[:], in0=kic[:], scalar=kif[:, 5:6], in1=lng[:], op0=ALU.mult, op1=ALU.mult, waits=[b, t_lng])
                b = DVE.op(V.tensor_tensor, out=kic[:], in0=kic[:], in1=lnb[:], op=ALU.add, waits=[b, t_lnb])
                csi = cosk[:, tb, :].rearrange("p (a two) -> p a two", two=2)[:, :, 0]
                sni = sink[:, tb, :].rearrange("p (a two) -> p a two", two=2)[:, :, 0]
                tr = rope_apply(kic[:].rearrange("p (h d) -> p h d", h=1), 1, 4, csi, sni, rtmp, [b])
                kib, bfree, bk2 = kibr.next()
                tc2 = DVE.op(V.tensor_copy, out=kib[:], in_=kic[:], waits=[tr, bfree])

                def b2():
                    tlast = transpose_out(kb, tcst, 8, 128, kTr, kT_scr[:, :, tb * 128:(tb + 1) * 128].rearrange("c f t -> f c t"))
                    kbr.done(bk, tlast)
                    tl2 = transpose_out(kib, tc2, 1, 32, kiTr, kiT_scr[:, tb * 128:(tb + 1) * 128])
                    kibr.done(bk2, tl2)
                return b2

            def stageB_q(s, hT, th, hk):
                cs = cosq[:, s, :]
                sn = sinq[:, s, :]
                kb, bk, tcst = qk_pair(hT, th, wq, twq, cs, sn, 0.125, None)
                ps, tmm, pk = project(hT, th, wq, 1024, 264, twq)
                hTr.done(hk, tmm)
                qw, qfree, qk = qwr.next()
                te = aevac(qw[:], ps[:, 0:264], [tmm, qfree])
                pp.done(pk, te)
                csi = cosq[:, s, :].rearrange("p (a two) -> p a two", two=2)[:, :, 0]
                sni = sinq[:, s, :].rearrange("p (a two) -> p a two", two=2)[:, :, 0]
                tr = rope_apply(qw[:, 0:256].rearrange("p (h d) -> p h d", h=8), 8, 4, csi, sni, rtmp, [te])
                a1 = DVE.op(V.tensor_scalar, out=wabs[:, s, :], in0=qw[:, 256:264], scalar1=1.0 / 16, scalar2=None, op0=ALU.mult, waits=[te])
                a2 = a1
                qib, bfree, bk2 = qibr.next()
                tc2 = DVE.op(V.tensor_copy, out=qib[:], in_=qw[:, 0:256], waits=[tr, bfree])
                qwr.done(qk, [tc2, a1, a2])

                def b2():
                    tlast = transpose_out(kb, tcst, 8, 128, kTr, qT_scr[:, :, s * 128:(s + 1) * 128].rearrange("c f t -> f c t"))
                    kbr.done(bk, tlast)
                    tl2 = transpose_out_qi(qib, tc2, s)
                    qibr.done(bk2, tl2)
                return b2

            items = [("kv", tb, xs[tb * 128:(tb + 1) * 128, :]) for tb in range(NTB)] + \
                    [("q", s, xq[s * 128:(s + 1) * 128, :]) for s in range(NQB)]
            n_it = len(items)
            a1 = {}
            a2 = {}
            a1[0] = stageA1(items[0][2])
            a2[0] = stageA2(*a1[0])
            if n_it > 1:
                a1[1] = stageA1(items[1][2])
            prev_b2 = None
            for i in range(n_it):
                if i + 2 < n_it:
                    a1[i + 2] = stageA1(items[i + 2][2])
                kind, idx, _ = items[i]
                hT, th, hk = a2.pop(i)
                if kind == "kv":
                    b2 = stageB_kv(idx, hT, th, hk)
                else:
                    b2 = stageB_q(idx, hT, th, hk)
                if i + 1 < n_it:
                    a2[i + 1] = stageA2(*a1.pop(i + 1))
                if prev_b2 is not None:
                    prev_b2()
                prev_b2 = b2
            if prev_b2 is not None:
                prev_b2()
            barrier()

        with ExitStack() as p2:
            kiT = sbt(p2, "kiT", [64, S], BF16)
            t_kiT = [SP.dma(kiT[g * 32:(g + 1) * 32, :], kiT_scr) for g in range(2)]
            gsub = sbt(p2, "gsub", [128, 128], F32)
            t_gs = SP.dma(gsub[:], bcast(subln_g))
            t_gs = DVE.op(V.tensor_scalar, out=gsub[:], in0=gsub[:], scalar1=0.8, scalar2=None, op0=ALU.mult, waits=[t_gs])
            ident2 = sbt(p2, "ident2", [128, 2, 128], BF16)
            t_id2 = [DVE.op(V.tensor_copy, out=ident2[:, i, :], in_=ident[:], waits=[t_ident]) for i in range(2)]
            Kr = Ring([sbt(p2, f"Kb{i}", [128, S], BF16) for i in range(2)])
            Vr = Ring([sbt(p2, f"Vb{i}", [128, NTB, VW], BF16) for i in range(2)])
            Mb = [sbt(p2, f"Mb{i}", [128, S], BF16) for i in range(2)]
            Isc = sbt(p2, "Isc", [128, S], F32)
            qbd_tiles = [sbt(p2, f"qbd{i}", [128, 2, 128], BF16) for i in range(3)]
            t_qz = [POOL.op(nc.gpsimd.memset, q_[:], 0.0) for q_ in qbd_tiles]
            qbr = Ring(qbd_tiles)
            qiT = [sbt(p2, f"qiTs{i}", [64, 4, 128], BF16) for i in range(2)]
            rl = Ring([sbt(p2, f"rl{i}", [128, 512], BF16) for i in range(8)])
            identf = sbt(p2, "identf", [128, 128], F32)
            t_idf = DVE.op(V.tensor_copy, out=identf[:], in_=ident[:], waits=[t_ident])
            dgb = [sbt(p2, f"dgb{i}", [128, 8, 128], BF16) for i in range(2)]
            etr = Ring([sbt(p2, f"et{i}", [128, 512], BF16) for i in range(4)])
            otr = Ring([sbt(p2, f"ot{i}", [128, D], BF16) for i in range(2)])
            of32 = sbt(p2, "of32", [128, 128], F32)
            oj = sbt(p2, "oj", [128, 128], F32)
            bs = sbt(p2, "bs", [128, 16], F32)
            es_ = sbt(p2, "es_", [128, 16], F32)
            pss = Ring([pst(p2, f"pss{i}", [128, 512], F32) for i in range(4)])
            pI = pst(p2, "pI", [128, 512], F32)
            pacc = Ring([pst(p2, f"pacc{i}", [128, 512], F32) for i in range(3)])
            of1 = sbt(p2, "of1", [128, 128], F32)
            of2 = sbt(p2, "of2", [128, 128], F32)
            ez = sbt(p2, "ez", [128, 8], F32)
            Mb_ready = [None, None]
            Mb_readers = [[], []]
            qiT_readers = [[], []]
            dg_readers = [[], []]
            Isc_free = [[]]
            pI_free = [[]]

            def prep(s):
                li = s % 2
                nk = (4 * s + 4) * 128
                nch = nk // 512
                wb = 0.76 * (s + 1)
                tq = SP.dma(qiT[li][:], qiT_scr[:, :, s * 128:(s + 1) * 128], waits=qiT_readers[li])
                qiT_readers[li] = []
                tdg = None
                for h in range(8):
                    tdg = DVE.op(V.tensor_scalar, out=dgb[li][:, h, :], in0=identf[:], scalar1=wabs[:, s, h:h + 1], scalar2=None, op0=ALU.mult,
                                 waits=[t_idf, dg_readers[li]] if h == 0 else ())
                dg_readers[li] = []
                yield 1.0
                tI = None
                pending = None

                def flush(pend):
                    (pc, pr, items) = pend
                    tacc = None
                    for g, (prt, pta, prk) in enumerate(items):
                        h = 2 * pr + g
                        tacc = PE.op(T.matmul, pI[:], lhsT=dgb[li][:, h, :], rhs=prt[:],
                                     start=(pr == 0 and g == 0), stop=(pr == 3 and g == 1),
                                     waits=[pta, tdg, pI_free[0]])
                        rl.done(prk, tacc)
                    return tacc
                pendq = []

                def drain(keep):
                    nonlocal tI
                    tacc = None
                    while len(pendq) > keep:
                        pend = pendq.pop(0)
                        tacc = flush(pend)
                        if pend[1] == 3:
                            pc = pend[0]
                            tI = DVE.op(V.tensor_copy, out=Isc[:, pc * 512:(pc + 1) * 512], in_=pI[:], waits=[tacc, Isc_free[0]])
                            pI_free[0] = [tI]
                    return tacc
                for c in range(nch):
                    for r in range(4):
                        drain(1)
                        mm = []
                        for g in range(2):
                            ps, pfree, pk = pss.next()
                            tm = PE.op(T.matmul, ps[:], lhsT=qiT[li][g * 32:(g + 1) * 32, r, :], rhs=kiT[g * 32:(g + 1) * 32, c * 512:(c + 1) * 512],
                                       start=True, stop=True, waits=[tq, t_kiT, pfree])
                            mm.append((ps, pk, tm))
                        items = []
                        for g in range(2):
                            ps, pk, tm = mm[g]
                            rt, rfree, rk = rl.next()
                            if g == 0:
                                ta = ACT.op(A.activation, out=rt[:], in_=ps[:], func=AF.Relu, waits=[tm, rfree])
                            else:
                                ta = DVE.op(V.tensor_scalar, out=rt[:], in0=ps[:], scalar1=0.0, scalar2=None, op0=ALU.max, waits=[tm, rfree])
                            pss.done(pk, ta)
                            items.append((rt, ta, rk))
                        pendq.append((c, r, items))
                        yield 2.0
                tacc = drain(0)
                qiT_readers[li].append(tacc)
                dg_readers[li].append(tacc)
                Isc_free[0] = []
                Iv = Isc[:, 0:nk]
                a = DVE.op(V.tensor_reduce, out=bs[:, 0:1], in_=Iv, axis=AX.X, op=ALU.max, waits=[tI])
                yield wb
                a = DVE.op(V.tensor_reduce, out=bs[:, 1:2], in_=Iv, axis=AX.X, op=ALU.min, waits=[a])
                a = DVE.op(V.scalar_tensor_tensor, out=bs[:, 2:3], in0=bs[:, 0:1], scalar=1.0, in1=bs[:, 1:2], op0=ALU.add, op1=ALU.subtract, waits=[a])
                a = DVE.op(V.tensor_tensor, out=Isc[:, nk - 512:nk], in0=Isc[:, nk - 512:nk], in1=cbf[:], op=ALU.add, waits=[a, t_cbf])
                yield wb
                lo = bs[:, 1:2]
                w0 = bs[:, 2:3]
                mid = bs[:, 3:4]
                cnt = bs[:, 4:5]
                gg = bs[:, 5:6]
                for it in range(1, NBIS + 1):
                    sc = 2.0 ** (-it)
                    a = DVE.op(V.tensor_scalar, out=mid, in0=w0, scalar1=sc, scalar2=lo, op0=ALU.mult, op1=ALU.add, waits=[a])
                    a = DVE.op(V.tensor_scalar, out=Mb[li][:, 0:nk], in0=Iv, scalar1=mid, scalar2=None, op0=ALU.is_ge, op1=ALU.add,
                               accum_out=cnt, waits=[a, Mb_readers[li]])
                    Mb_readers[li] = []
                    a = DVE.op(V.tensor_scalar, out=gg, in0=cnt, scalar1=TOPK - 0.5, scalar2=sc, op0=ALU.is_ge, op1=ALU.mult, waits=[a])
                    a = DVE.op(V.scalar_tensor_tensor, out=lo, in0=w0, scalar=gg, in1=lo, op0=ALU.mult, op1=ALU.add, waits=[a])
                    yield wb
                a = DVE.op(V.tensor_scalar, out=Mb[li][:, 0:nk], in0=Iv, scalar1=lo, scalar2=NEG, op0=ALU.is_lt, op1=ALU.mult, waits=[a])
                Mb_ready[li] = a
                Isc_free[0] = [a]

            def prep_steps(s):
                return 1 + 8 * (s + 1) + (2 + NBIS) * 0.76 * (s + 1)

            pump_state = {"gen": None, "budget": 0.0, "rate": 0.0}

            def pump():
                st = pump_state
                if st["gen"] is None:
                    return
                st["budget"] += st["rate"]
                while st["budget"] > 0.0 and st["gen"] is not None:
                    try:
                        st["budget"] -= next(st["gen"])
                    except StopIteration:
                        st["gen"] = None

            def attention(s, p, ot, ofree, owr):
                li = s % 2
                is_dsa = p < 4
                nkb = 4 * s + 4
                kmax = nkb * 128
                Kb, kfree, kk = Kr.next()
                Vb, vfree, vk = Vr.next()
                tK = SP.dma(Kb[:, 0:kmax], kT_scr[p, :, 0:kmax], waits=kfree)
                tV = []
                for b0 in range(0, nkb, 8):
                    nb = min(8, nkb - b0)
                    tV.append(SP.dma(Vb[:, b0:b0 + nb, :], v_scr[b0 * 128:(b0 + nb) * 128, p * VW:(p + 1) * VW].rearrange("(b t) w -> t b w", t=128), waits=vfree))
                qbd, qfree, qk = qbr.next()
                tq = [SP.dma(qbd[m * 64:(m + 1) * 64, m, :], qT_scr[p, m * 64:(m + 1) * 64, s * 128:(s + 1) * 128], waits=[qfree, t_qz]) for m in range(2)]
                accs = [pacc.next() for m in range(2)]
                vw = 66 if is_dsa else 130
                ntile = nkb // 2
                pend = {}

                def do_qk(ti):
                    ps, pfree, pk = pss.next()
                    tm = None
                    for bi in range(2):
                        kb_ = ti * 2 + bi
                        need_mask = is_dsa or (kb_ >= nkb - 4)
                        tm = PE.op(T.matmul, ps[:, bi * 256:(bi + 1) * 256], lhsT=Kb[:, kb_ * 128:(kb_ + 1) * 128],
                                   rhs=qbd[:].rearrange("p a b -> p (a b)"), start=True, stop=not need_mask,
                                   waits=[tK, tq, pfree] if bi == 0 else ())
                        if need_mask:
                            if is_dsa:
                                ml = Mb[li][:, kb_ * 128:(kb_ + 1) * 128]
                                mw = [Mb_ready[li]]
                            else:
                                cbi = kb_ - (nkb - 4)
                                ml = cb[:, cbi * 128:(cbi + 1) * 128]
                                mw = [t_cb]
                            tm = PE.op(T.matmul, ps[:, bi * 256:(bi + 1) * 256], lhsT=ml, rhs=ident2[:].rearrange("p a b -> p (a b)"),
                                       start=False, stop=True, waits=mw + [t_id2])
                    et, efree, ek = etr.next()
                    te = ACT.op(A.activation, out=et[:], in_=ps[:], func=AF.Exp, waits=[tm, efree])
                    pss.done(pk, te)
                    pend[ti] = (et, te, ek)

                tav = [None, None]

                def do_av(ti):
                    et, te, ek = pend.pop(ti)
                    tm = None
                    first = True
                    for bi in range(2):
                        kb_ = ti * 2 + bi
                        for m in range(2):
                            acc, afree, ak = accs[m]
                            rhs = Vb[:, kb_, m * 66:(m + 1) * 66] if is_dsa else Vb[:, kb_, 0:130]
                            tm = PE.op(T.matmul, acc[:, 0:vw], lhsT=et[:, bi * 256 + m * 128:bi * 256 + (m + 1) * 128], rhs=rhs,
                                       start=(kb_ == 0), stop=(kb_ == nkb - 1),
                                       waits=[te, tV, accs[0][1], accs[1][1]] if first else ())
                            first = False
                            tav[m] = tm
                    etr.done(ek, tm)
                LAG = 2
                for ti in range(ntile):
                    do_qk(ti)
                    if ti >= LAG:
                        do_av(ti - LAG)
                    pump()
                for ti in range(max(0, ntile - LAG), ntile):
                    do_av(ti)
                qbr.done(qk, tav[1])
                if is_dsa:
                    Mb_readers[li].append(tav[1])
                Kr.done(kk, tav[1])
                Vr.done(vk, tav[1])
                if is_dsa:
                    for m in range(2):
                        acc, afree, ak = accs[m]
                        a = ACT.op(A.activation, out=ez[:, m:m + 1], in_=acc[:, 64:65], func=AF.Ln, waits=[tav[1]])
                        a = ACT.op(A.activation, out=ez[:, m:m + 1], in_=ez[:, m:m + 1], func=AF.Exp, scale=-1.0, waits=[a])
                        a = ACT.op(A.activation, out=ot[:, p * 128 + m * 64:p * 128 + (m + 1) * 64], in_=acc[:, 0:64], func=AF.Copy,
                                   scale=ez[:, m:m + 1], waits=[a, ofree])
                        pacc.done(ak, a)
                        owr.append(a)
                else:
                    h = p - 4
                    acc1, _, ak1 = accs[0]
                    acc2, _, ak2 = accs[1]
                    a = ACT.op(A.activation, out=ez[:, 2:3], in_=acc1[:, 128:129], func=AF.Ln, waits=[tav[1], of_free[0]])
                    a = ACT.op(A.activation, out=ez[:, 2:3], in_=ez[:, 2:3], func=AF.Exp, scale=-1.0, waits=[a])
                    a = ACT.op(A.activation, out=of1[:], in_=acc1[:, 0:128], func=AF.Copy, scale=ez[:, 2:3], waits=[a])
                    pacc.done(ak1, a)
                    b = ACT.op(A.activation, out=ez[:, 3:4], in_=acc2[:, 128:129], func=AF.Ln, waits=[a])
                    b = ACT.op(A.activation, out=ez[:, 3:4], in_=ez[:, 3:4], func=AF.Exp, scale=-1.0, waits=[b])
                    b = ACT.op(A.activation, out=of2[:], in_=acc2[:, 0:128], func=AF.Copy, scale=ez[:, 3:4], waits=[b])
                    pacc.done(ak2, b)
                    b = DVE.op(V.scalar_tensor_tensor, out=of32[:], in0=of2[:], scalar=nlam[:, 0:1], in1=of1[:],
                               op0=ALU.mult, op1=ALU.add, waits=[a, b, t_nlam, of32_free[0]])
                    of_free[0] = [b]
                    c = ACT.op(A.activation, out=oj[:], in_=of32[:], func=AF.Square, accum_out=es_[:, 5:6], waits=[b])
                    c = DVE.op(V.tensor_scalar, out=es_[:, 6:7], in0=es_[:, 5:6], scalar1=1.0 / 128, scalar2=1e-5, op0=ALU.mult, op1=ALU.add, waits=[c])
                    c = ACT.op(A.activation, out=es_[:, 7:8], in_=es_[:, 6:7], func=AF.Ln, waits=[c])
                    c = ACT.op(A.activation, out=es_[:, 8:9], in_=es_[:, 7:8], func=AF.Exp, scale=-0.5, waits=[c])
                    c = DVE.op(V.scalar_tensor_tensor, out=ot[:, 512 + h * 128:512 + (h + 1) * 128], in0=of32[:], scalar=es_[:, 8:9],
                               in1=gsub[:], op0=ALU.mult, op1=ALU.mult, waits=[c, t_gs, ofree])
                    of32_free[0] = [c]
                    owr.append(c)

            of_free = [[]]
            of32_free = [[]]
            for _ in prep(0):
                pass
            for s in range(NQB):
                if s + 1 < NQB:
                    pump_state["gen"] = prep(s + 1)
                    pump_state["budget"] = 0.0
                    pump_state["rate"] = prep_steps(s + 1) / float(8 * (2 * s + 2)) * 1.15
                else:
                    pump_state["gen"] = None
                ot, ofree, ok_ = otr.next()
                owr = []
                for pi, p in enumerate([4, 5, 6, 7, 0, 1, 2, 3]):
                    attention(s, p, ot, ofree, owr)
                if pump_state["gen"] is not None:
                    for _ in pump_state["gen"]:
                        pass
                    pump_state["gen"] = None
                td = GQ.dma(o_scr[s * 128:(s + 1) * 128, :], ot[:], waits=owr)
                otr.done(ok_, td)
            barrier()

        def load_w(st, name, src, rows, cols, q, waits=()):
            nkc = rows // 128
            wt = sbt(st, name, [128, nkc, cols], BF16)
            srcv = src.rearrange("(kc p) n -> p kc n", p=128)
            toks = []
            step = 1024
            for c0 in range(0, cols, step):
                n = min(step, cols - c0)
                toks.append(q.dma(wt[:, :, c0:c0 + n], srcv[:, :, c0:c0 + n], waits=waits))
            return wt, toks

        with ExitStack() as p3:
            wg = sbt(p3, "wg", [128, 8, 2048], BF16)
            w_in_v = w_in.rearrange("(kc p) n -> p kc n", p=128)
            twg = [GQ.dma(wg[:, :, c0:c0 + 512], w_in_v[:, :, C_G + c0:C_G + c0 + 512]) for c0 in range(0, 2048, 512)]
            wbd, twbd = load_w(p3, "wbd", w_bd, 512, D, GQ)
            wbf, twbf = load_w(p3, "wbf", w_bf, 512, D, GQ)
            wo, two = load_w(p3, "wo", w_out, D, D, GQ)
            gbt = sbt(p3, "gbt", [128, 2048], F32)
            t_gb = SP.dma(gbt[:], bcast(gate_b))
            xr = Ring([sbt(p3, f"xt{i}", [128, D], F32) for i in range(4)])
            junk = Ring([sbt(p3, f"junk{i}", [128, D], BF16) for i in range(1)])
            ssr = Ring([sbt(p3, f"ss{i}", [128, 4], F32) for i in range(4)])
            hbr = Ring([sbt(p3, f"hb{i}", [128, D], BF16) for i in range(3)])
            hTr = Ring([sbt(p3, f"hT{i}", [128, 8, 128], BF16) for i in range(2)])
            obr = Ring([sbt(p3, f"ob{i}", [128, D], BF16) for i in range(3)])
            oTr = Ring([sbt(p3, f"oT{i}", [128, 8, 128], BF16) for i in range(2)])
            gat = sbt(p3, "gat", [128, 2048], F32)
            mrg = sbt(p3, "mrg", [128, D], F32)
            mrg2 = sbt(p3, "mrg2", [128, D], F32)
            mbr = Ring([sbt(p3, f"mb{i}", [128, D], BF16) for i in range(2)])
            mTr = Ring([sbt(p3, f"mT{i}", [128, 8, 128], BF16) for i in range(2)])
            x1r = Ring([sbt(p3, f"x1t{i}", [128, D], F32) for i in range(2)])
            psT = Ring([pst(p3, f"psT{i}", [128, D], BF16) for i in range(2)])
            pp = Ring([pst(p3, f"pp{i}", [128, 512], F32) for i in range(6)])
            x1_dmas = []

            def transpose8(src_bf, tsrc, dst_ring):
                ps, pfree, pk = psT.next()
                tt = None
                for kc in range(8):
                    tt = PE.op(T.transpose, ps[:, kc * 128:(kc + 1) * 128], src_bf[:, kc * 128:(kc + 1) * 128], ident[:],
                               waits=[tsrc, pfree] if kc == 0 else ())
                dT, dfree, dk = dst_ring.next()
                te = ACT.op(A.copy, out=dT[:].rearrange("p a b -> p (a b)"), in_=ps[:], waits=[tt, dfree])
                psT.done(pk, te)
                return dT, te, dk, tt

            def st_load(s):
                xt, xfree, xk = xr.next()
                tx = SP.dma(xt[:], xq[s * 128:(s + 1) * 128, :], waits=xfree)
                hb, hfree, hk = hbr.next()
                th = rms_norm_block((junk, ssr), xt[:], tx, gmix[:], t_gmix, 1e-6, D, hb[:], hfree)
                ob, ofree, ok_ = obr.next()
                to = SP.dma(ob[:], o_scr[s * 128:(s + 1) * 128, :], waits=[ofree])
                return dict(s=s, xt=xt, xk=xk, hb=hb, th=th, hk=hk, ob=ob, to=to, ok_=ok_)

            def st_T(c):
                hT, te, tk, tt = transpose8(c["hb"], c["th"], hTr)
                hbr.done(c["hk"], tt)
                oT, teo, ok2, tto = transpose8(c["ob"], c["to"], oTr)
                obr.done(c["ok_"], tto)
                c.update(hT=hT, te=te, tk=tk, oT=oT, teo=teo, ok2=ok2)

            def st_X(c):
                hT, te, tk = c["hT"], c["te"], c["tk"]
                oT, teo, ok2 = c["oT"], c["teo"], c["ok2"]
                tg_last = None
                for gc in range(4):
                    ps, pfree, pk = pp.next()
                    tm = None
                    for kc in range(8):
                        tm = PE.op(T.matmul, ps[:], lhsT=hT[:, kc, :], rhs=wg[:, kc, gc * 512:(gc + 1) * 512], start=(kc == 0), stop=(kc == 7),
                                   waits=[te, pfree, twg] if kc == 0 else ())
                    a_ = DVE.op(V.tensor_tensor, out=gat[:, gc * 512:(gc + 1) * 512], in0=ps[:], in1=gbt[:, gc * 512:(gc + 1) * 512], op=ALU.add,
                                waits=[tm, t_gb, gat_free[0]])
                    pp.done(pk, a_)
                    tg_last = ACT.op(A.activation, out=gat[:, gc * 512:(gc + 1) * 512], in_=gat[:, gc * 512:(gc + 1) * 512], func=AF.Sigmoid, waits=[a_])
                    if gc == 3:
                        hTr.done(tk, tm)
                gat_free[0] = []
                mtoks = []
                for br, (wt, twt) in enumerate([(wbd, twbd), (wbf, twbf)]):
                    for nc_ in range(2):
                        ps, pfree, pk = pp.next()
                        tm = None
                        for kc in range(4):
                            tm = PE.op(T.matmul, ps[:], lhsT=oT[:, br * 4 + kc, :], rhs=wt[:, kc, nc_ * 512:(nc_ + 1) * 512], start=(kc == 0), stop=(kc == 3),
                                       waits=[teo, pfree, twt] if kc == 0 else ())
                        dst = (mrg if br == 0 else mrg2)[:, nc_ * 512:(nc_ + 1) * 512]
                        a_ = DVE.op(V.tensor_tensor, out=dst, in0=ps[:], in1=gat[:, br * 1024 + nc_ * 512:br * 1024 + (nc_ + 1) * 512], op=ALU.mult,
                                    waits=[tm, tg_last, mrg_free[0]])
                        pp.done(pk, a_)
                        mtoks.append(a_)
                        if br == 1 and nc_ == 1:
                            oTr.done(ok2, tm)
                gat_free[0] = list(mtoks)
                mb, mfree, mk = mbr.next()
                tmb = DVE.op(V.tensor_tensor, out=mb[:], in0=mrg[:], in1=mrg2[:], op=ALU.add, waits=[mtoks, mfree])
                mrg_free[0] = [tmb]
                c.update(mb=mb, tmb=tmb, mk=mk)

            def st_Y(c):
                s_ = c["s"]
                mT, tem, mk2, ttm = transpose8(c["mb"], c["tmb"], mTr)
                mbr.done(c["mk"], ttm)
                x1t, x1free, x1k = x1r.next()
                xtoks = []
                for nc_ in range(2):
                    ps, pfree, pk = pp.next()
                    tm = None
                    for kc in range(8):
                        tm = PE.op(T.matmul, ps[:], lhsT=mT[:, kc, :], rhs=wo[:, kc, nc_ * 512:(nc_ + 1) * 512], start=(kc == 0), stop=(kc == 7),
                                   waits=[tem, pfree, two] if kc == 0 else ())
                    a_ = DVE.op(V.tensor_tensor, out=x1t[:, nc_ * 512:(nc_ + 1) * 512], in0=ps[:], in1=c["xt"][:, nc_ * 512:(nc_ + 1) * 512], op=ALU.add,
                                waits=[tm, x1free])
                    pp.done(pk, a_)
                    xtoks.append(a_)
                    if nc_ == 1:
                        mTr.done(mk2, tm)
                xr.done(c["xk"], xtoks)
                td = GQ.dma(x1_scr[s_ * 128:(s_ + 1) * 128, :], x1t[:], waits=xtoks)
                x1r.done(x1k, td)
                x1_dmas.append(td)

            gat_free = [[]]
            mrg_free = [[]]
            ctx = {}
            ctx[0] = st_load(0)
            st_T(ctx[0])
            if NQB > 1:
                ctx[1] = st_load(1)
            prevY = None
            for s in range(NQB):
                if s + 2 < NQB:
                    ctx[s + 2] = st_load(s + 2)
                st_X(ctx[s])
                if s + 1 < NQB:
                    st_T(ctx[s + 1])
                if prevY is not None:
                    st_Y(prevY)
                prevY = ctx.pop(s)
            st_Y(prevY)
            barrier()
        early.close()

        with ExitStack() as p4:
            w1, tw1 = load_w(p4, "w1", w_f1, D, 2 * DFF, GQ)
            w2, tw2 = load_w(p4, "w2", w_f2, DFF, D, GQ)
            gffn = sbt(p4, "gffn", [128, D], F32)
            gfin = sbt(p4, "gfin", [128, D], F32)
            t_gffn = SP.dma(gffn[:], bcast(norm_ffn_g))
            t_gfin = SP.dma(gfin[:], bcast(norm_fin_g))
            x1r = Ring([sbt(p4, f"x1b{i}", [128, D], F32) for i in range(2)])
            junk = Ring([sbt(p4, f"junk{i}", [128, D], BF16) for i in range(1)])
            ssr = Ring([sbt(p4, f"ss{i}", [128, 4], F32) for i in range(2)])
            hbr = Ring([sbt(p4, f"hb{i}", [128, D], BF16) for i in range(2)])
            h2T = sbt(p4, "h2T", [128, 8, 512], BF16)
            actT = sbt(p4, "actT", [128, NFC, 512], BF16)
            sgr = Ring([sbt(p4, f"sg{i}", [128, 512], F32) for i in range(2)])
            x2r = Ring([sbt(p4, f"x2t{i}", [128, D], F32) for i in range(2)])
            psT = Ring([pst(p4, f"psT{i}", [128, D], BF16) for i in range(2)])
            pp = Ring([pst(p4, f"pp{i}", [128, 512], F32) for i in range(6)])
            h2T_free = []
            actT_free = []
            for grp in range(NQB // 4):
                th2 = []
                for bi in range(4):
                    s = grp * 4 + bi
                    x1t, x1free, x1k = x1r.next()
                    tx = SP.dma(x1t[:], x1_scr[s * 128:(s + 1) * 128, :], waits=[x1free])
                    hb, hfree, hk = hbr.next()
                    th = rms_norm_block((junk, ssr), x1t[:], tx, gffn[:], t_gffn, 1e-6, D, hb[:], hfree)
                    x1r.done(x1k, th)
                    ps, pfree, pk = psT.next()
                    tt = None
                    for kc in range(8):
                        tt = PE.op(T.transpose, ps[:, kc * 128:(kc + 1) * 128], hb[:, kc * 128:(kc + 1) * 128], ident[:],
                                   waits=[th, pfree] if kc == 0 else ())
                    hbr.done(hk, tt)
                    te = ACT.op(A.copy, out=h2T[:, :, bi * 128:(bi + 1) * 128], in_=ps[:].rearrange("p (a b) -> p a b", a=8), waits=[tt, h2T_free])
                    psT.done(pk, te)
                    th2.append(te)
                h2T_free = []
                tact = []
                last_mm = None
                for f in range(NFC):
                    psg, pfree, pkg = pp.next()
                    tmg = None
                    for kc in range(8):
                        tmg = PE.op(T.matmul, psg[:], lhsT=w1[:, kc, f * 128:(f + 1) * 128], rhs=h2T[:, kc, :], start=(kc == 0), stop=(kc == 7),
                                    waits=[th2, pfree, tw1] if kc == 0 else ())
                    psu, pfree, pku = pp.next()
                    tmu = None
                    for kc in range(8):
                        tmu = PE.op(T.matmul, psu[:], lhsT=w1[:, kc, DFF + f * 128:DFF + (f + 1) * 128], rhs=h2T[:, kc, :], start=(kc == 0), stop=(kc == 7),
                                    waits=[pfree] if kc == 0 else ())
                    last_mm = tmu
                    sg, sfree, sk = sgr.next()
                    ta = ACT.op(A.activation, out=sg[:], in_=psg[:], func=AF.Silu, waits=[tmg, sfree])
                    pp.done(pkg, ta)
                    tb_ = DVE.op(V.tensor_tensor, out=actT[:, f, :], in0=psu[:], in1=sg[:], op=ALU.mult, waits=[tmu, ta, actT_free])
                    pp.done(pku, tb_)
                    sgr.done(sk, tb_)
                    tact.append(tb_)
                h2T_free = [last_mm]
                actT_free = []
                last_o = None
                for bi in range(4):
                    s = grp * 4 + bi
                    x2, x2free, x2k = x2r.next()
                    tx2 = SP.dma(x2[:], x1_scr[s * 128:(s + 1) * 128, :], waits=[x2free])
                    xtoks = []
                    for nc_ in range(2):
                        ps, pfree, pk = pp.next()
                        tm = None
                        for f in range(NFC):
                            tm = PE.op(T.matmul, ps[:], lhsT=actT[:, f, bi * 128:(bi + 1) * 128], rhs=w2[:, f, nc_ * 512:(nc_ + 1) * 512],
                                       start=(f == 0), stop=(f == NFC - 1), waits=[tact, pfree, tw2] if f == 0 else ())
                        last_o = tm
                        a = DVE.op(V.tensor_tensor, out=x2[:, nc_ * 512:(nc_ + 1) * 512], in0=ps[:], in1=x2[:, nc_ * 512:(nc_ + 1) * 512], op=ALU.add,
                                   waits=[tm, tx2])
                        pp.done(pk, a)
                        xtoks.append(a)
                    e = rms_norm_block((junk, ssr), x2[:], xtoks, gfin[:], t_gfin, 1e-6, D, x2[:], [])
                    td = SP.dma(out[s * 128:(s + 1) * 128, :], x2[:], waits=[e])
                    x2r.done(x2k, td)
                actT_free = [last_o]
            barrier()
    return nc


_NC_CACHE = {}


def _get_nc(debug=False):
    if debug not in _NC_CACHE:
        _NC_CACHE[debug] = build(debug)
    return _NC_CACHE[debug]


def make_in_maps(inputs):
    x = np.ascontiguousarray(np.asarray(inputs["x"], dtype=np.float32))
    pos = np.asarray(inputs["positions"]).astype(np.int32)
    in_maps = []
    kk = np.arange(512)[None, :]
    qq = np.arange(128)[:, None]
    for c in range(8):
        b, j = c // 4, c % 4
        blocks = [4 * s + j for s in range(NQB)]
        xqc = np.concatenate([x[b, q * 128:(q + 1) * 128] for q in blocks], axis=0)
        posk = np.ascontiguousarray(pos[b].reshape(NTB, 128).T)
        posq = np.ascontiguousarray(np.stack([pos[b, q * 128:(q + 1) * 128] for q in blocks], axis=1))
        cm = np.where(kk <= j * 128 + qq, 0.0, NEG).astype(np.float32)
        m = {"xs": x[b], "xq": np.ascontiguousarray(xqc), "posk": posk, "posq": posq, "cmask": cm}
        for name in ["norm_mix_g", "w_in", "idx_k_norm_g", "idx_k_norm_b", "diff_lambda_q1", "diff_lambda_k1",
                     "diff_lambda_q2", "diff_lambda_k2", "diff_subln_g", "gate_b", "w_branch_dsa", "w_branch_diff",
                     "w_out", "norm_ffn_g", "w_ffn_in", "w_ffn_out"]:
            m[name] = np.ascontiguousarray(np.asarray(inputs[name], dtype=np.float32)[0])
        m["norm_final_g"] = np.ascontiguousarray(np.asarray(inputs["norm_final_g"], dtype=np.float32))
        in_maps.append(m)
    return in_maps


def kernel(**inputs):
    nc = _get_nc(False)
    in_maps = make_in_maps(inputs)
    res = run_bass_kernel_spmd(nc, in_maps, core_ids=list(range(8)))
    outp = np.zeros((2, S, D), dtype=np.float32)
    for c in range(8):
        b, j = c // 4, c % 4
        o = res.results[c]["out"]
        for s in range(NQB):
            q = 4 * s + j
            outp[b, q * 128:(q + 1) * 128] = o[s * 128:(s + 1) * 128]
    return outp
```

```python
import os
import math
import numpy as np
from contextlib import ExitStack
import concourse.bass as bass
import concourse.mybir as mybir
from concourse.bass_utils import run_bass_kernel_spmd

F32 = mybir.dt.float32
F32R = mybir.dt.float32r
BF16 = mybir.dt.bfloat16
I32 = mybir.dt.int32
AF = mybir.ActivationFunctionType
ALU = mybir.AluOpType
AX = mybir.AxisListType

S = 8192
D = 1024
NTB = S // 128
NQB = 16
NQ = NQB * 128
DFF = 2816
NFC = DFF // 128
TOPK = 256
NBIS = 16
NEG = -30000.0
C_QA, C_KA, C_VA, C_QI, C_KI, C_WI, C_QB, C_KB, C_VB, C_G = 0, 512, 1024, 1536, 1792, 1824, 1832, 2344, 2856, 3368
D_IN = 5416
VW = 132
TWO_PI = 2.0 * math.pi


class Eng:
    def __init__(self, nc, es, eng, name):
        self.eng = eng
        self.name = name
        self.sem = es.enter_context(nc.semaphore("sem_" + name))
        self.count = 0
        self.seen = {}

    def wait(self, toks):
        for t in toks:
            if t is None:
                continue
            if isinstance(t, list):
                self.wait(t)
                continue
            sem, val, key = t
            if self.seen.get(key, 0) >= val:
                continue
            self.eng.wait_ge(sem, val)
            self.seen[key] = val

    def op(self, fn, *args, waits=(), **kw):
        self.wait(waits)
        inst = fn(*args, **kw)
        self.count += 1
        inst.then_inc(self.sem, 1)
        tok = (self.sem, self.count, self.name)
        self.seen[self.name] = self.count - 1 if False else self.seen.get(self.name, 0)
        return tok


class DmaQ:
    def __init__(self, nc, es, eng, name, nsem=12):
        self.eng = eng
        self.name = name
        self.sems = [es.enter_context(nc.semaphore(f"dsem_{name}_{i}")) for i in range(nsem)]
        self.vals = [0] * nsem
        self.i = 0
        self.seen = {}

    def wait(self, toks):
        for t in toks:
            if t is None:
                continue
            if isinstance(t, list):
                self.wait(t)
                continue
            sem, val, key = t
            if self.seen.get(key, 0) >= val:
                continue
            self.eng.wait_ge(sem, val)
            self.seen[key] = val

    def dma(self, out, in_, waits=(), **kw):
        k = self.i
        self.i = (self.i + 1) % len(self.sems)
        key = f"{self.name}_{k}"
        if self.vals[k] > 0:
            self.wait([(self.sems[k], self.vals[k], key)])
        self.wait(waits)
        self.vals[k] += 16
        self.eng.dma_start(out=out, in_=in_, **kw).then_inc(self.sems[k], 16)
        return (self.sems[k], self.vals[k], key)

    def all_toks(self):
        return [(self.sems[k], self.vals[k], f"{self.name}_{k}") for k in range(len(self.sems)) if self.vals[k] > 0]


class Ring:
    def __init__(self, tiles):
        self.tiles = tiles
        self.rd = [[] for _ in tiles]
        self.i = -1

    def next(self):
        self.i = (self.i + 1) % len(self.tiles)
        k = self.i
        toks = self.rd[k]
        self.rd[k] = []
        return self.tiles[k], toks, k

    def done(self, k, tok):
        self.rd[k].append(tok)


def build(debug=False):
    nc = bass.Bass("TRN2", target_bir_lowering=False)
    dt_in = lambda n, s, d=F32: nc.dram_tensor(n, s, d, kind="ExternalInput").ap()
    xs = dt_in("xs", [S, D])
    xq = dt_in("xq", [NQ, D])
    posk = dt_in("posk", [128, NTB], I32)
    posq = dt_in("posq", [128, NQB], I32)
    cmask = dt_in("cmask", [128, 512])
    norm_mix_g = dt_in("norm_mix_g", [D])
    w_in = dt_in("w_in", [D, D_IN])
    idx_g = dt_in("idx_k_norm_g", [32])
    idx_b = dt_in("idx_k_norm_b", [32])
    lq1 = dt_in("diff_lambda_q1", [64])
    lk1 = dt_in("diff_lambda_k1", [64])
    lq2 = dt_in("diff_lambda_q2", [64])
    lk2 = dt_in("diff_lambda_k2", [64])
    subln_g = dt_in("diff_subln_g", [128])
    gate_b = dt_in("gate_b", [2048])
    w_bd = dt_in("w_branch_dsa", [512, D])
    w_bf = dt_in("w_branch_diff", [512, D])
    w_out = dt_in("w_out", [D, D])
    norm_ffn_g = dt_in("norm_ffn_g", [D])
    w_f1 = dt_in("w_ffn_in", [D, 2 * DFF])
    w_f2 = dt_in("w_ffn_out", [DFF, D])
    norm_fin_g = dt_in("norm_final_g", [D])
    out = nc.dram_tensor("out", [NQ, D], F32, kind="ExternalOutput").ap()
    skind = "ExternalOutput" if debug else "Internal"
    kT_scr = nc.dram_tensor("kT_scr", [8, 128, S], BF16, kind=skind).ap()
    kiT_scr = nc.dram_tensor("kiT_scr", [32, S], BF16, kind=skind).ap()
    v_scr = nc.dram_tensor("v_scr", [S, 8 * VW], BF16, kind=skind).ap()
    qT_scr = nc.dram_tensor("qT_scr", [8, 128, NQ], BF16, kind=skind).ap()
    qiT_scr = nc.dram_tensor("qiT_scr", [64, 4, NQ], BF16, kind=skind).ap()
    o_scr = nc.dram_tensor("o_scr", [NQ, D], BF16, kind=skind).ap()
    x1_scr = nc.dram_tensor("x1_scr", [NQ, D], F32, kind=skind).ap()

    with ExitStack() as es:
        uid = [0]

        def sbt(st, n, s, d):
            uid[0] += 1
            return st.enter_context(nc.sbuf_tensor(f"{n}_{uid[0]}", s, d))

        def pst(st, n, s, d):
            uid[0] += 1
            return st.enter_context(nc.psum_tensor(f"{n}_{uid[0]}", s, d))

        ident = sbt(es, "ident", [128, 128], BF16)
        early = ExitStack()
        cosk = sbt(early, "cosk", [128, NTB, 8], F32)
        sink = sbt(early, "sink", [128, NTB, 8], F32)
        cosq = sbt(early, "cosq", [128, NQB, 8], F32)
        sinq = sbt(early, "sinq", [128, NQB, 8], F32)
        gmix = sbt(early, "gmix", [128, D], F32)
        wabs = sbt(early, "wabs", [128, NQB, 8], F32)
        wsgn = sbt(early, "wsgn", [128, NQB, 8], F32)
        nlam = sbt(early, "nlam", [128, 1], F32)
        small = sbt(early, "small", [128, 64], F32)
        cb = sbt(early, "cb", [128, 512], BF16)
        cbf = sbt(early, "cbf", [128, 512], F32)

        es.enter_context(nc.Block())
        PE = Eng(nc, es, nc.tensor, "pe")
        ACT = Eng(nc, es, nc.scalar, "act")
        DVE = Eng(nc, es, nc.vector, "dve")
        POOL = Eng(nc, es, nc.gpsimd, "pool")
        SP = DmaQ(nc, es, nc.sync, "sp", 16)
        GQ = DmaQ(nc, es, nc.gpsimd, "gq", 8)
        V = nc.vector
        A = nc.scalar
        T = nc.tensor

        def bcast(ap1d, n=128):
            return ap1d.partition_broadcast(n)

        def barrier():
            toks = [(e.sem, e.count, e.name) for e in (PE, ACT, DVE, POOL) if e.count > 0]
            toks += SP.all_toks() + GQ.all_toks()
            for e in (PE, ACT, DVE, POOL, SP):
                e.wait(toks)

        t = POOL.op(nc.gpsimd.memset, ident[:], 1.0)
        t_ident = POOL.op(nc.gpsimd.affine_select, out=ident[:], in_=ident[:], pattern=[[-1, 128]],
                          compare_op=ALU.is_equal, fill=0.0, base=0, channel_multiplier=1, waits=[t])
        t_gmix = SP.dma(gmix[:], bcast(norm_mix_g))
        t_cbf = SP.dma(cbf[:], cmask)
        t_cb = DVE.op(V.tensor_copy, out=cb[:], in_=cbf[:], waits=[t_cbf])

        with ExitStack() as p0:
            lt = sbt(p0, "lt", [128, 4, 64], F32)
            lj = sbt(p0, "lj", [128, 64], F32)
            tl = [SP.dma(lt[:, i, :], bcast(a)) for i, a in enumerate([lq1, lk1, lq2, lk2])]
            t1 = DVE.op(V.tensor_tensor, out=lj[:], in0=lt[:, 0, :], in1=lt[:, 1, :], op=ALU.mult, waits=tl)
            t1 = DVE.op(V.tensor_reduce, out=small[:, 0:1], in_=lj[:], axis=AX.X, op=ALU.add, waits=[t1])
            t2 = DVE.op(V.tensor_tensor, out=lj[:], in0=lt[:, 2, :], in1=lt[:, 3, :], op=ALU.mult, waits=[t1])
            t2 = DVE.op(V.tensor_reduce, out=small[:, 1:2], in_=lj[:], axis=AX.X, op=ALU.add, waits=[t2])
            t3 = ACT.op(A.activation, out=small[:, 2:4], in_=small[:, 0:2], func=AF.Exp, waits=[t2])
            t_nlam = DVE.op(V.scalar_tensor_tensor, out=nlam[:], in0=small[:, 3:4], scalar=-0.2, in1=small[:, 2:3],
                            op0=ALU.add, op1=ALU.subtract, waits=[t3])

            invf = sbt(p0, "invf", [128, 8], F32)
            tinv = None
            for i in range(8):
                fv = float(np.power(np.float32(500000.0), -np.float32(2 * i) / np.float32(16)))
                tinv = DVE.op(V.memset, invf[:, i:i + 1], fv)

            def rope_table(pos_ap, n, cos_t, sin_t, nm):
                pi_ = sbt(p0, "pi_" + nm, [128, n], I32)
                pf = sbt(p0, "pf_" + nm, [128, n], F32)
                ang = sbt(p0, "ang_" + nm, [128, n, 8], F32)
                yy = sbt(p0, "yy_" + nm, [128, n, 8], F32)
                ni = sbt(p0, "ni_" + nm, [128, n, 8], I32)
                tp = SP.dma(pi_[:], pos_ap)
                a = DVE.op(V.tensor_copy, out=pf[:], in_=pi_[:], waits=[tp])
                a = DVE.op(V.tensor_tensor, out=ang[:], in0=pf[:].unsqueeze(2).to_broadcast([128, n, 8]),
                           in1=invf[:].unsqueeze(1).to_broadcast([128, n, 8]), op=ALU.mult, waits=[a, tinv])

                def reduce_sin(src_add, dst):
                    b = DVE.op(V.tensor_scalar, out=yy[:], in0=ang[:], scalar1=src_add, scalar2=1.0 / TWO_PI,
                               op0=ALU.add, op1=ALU.mult, waits=[a])
                    b = DVE.op(V.tensor_copy, out=ni[:], in_=yy[:], waits=[b])
                    b = DVE.op(V.tensor_copy, out=yy[:], in_=ni[:], waits=[b])
                    c1 = 6.28125
                    c2 = TWO_PI - 6.28125
                    b = DVE.op(V.scalar_tensor_tensor, out=dst, in0=yy[:], scalar=-c1, in1=ang[:], op0=ALU.mult, op1=ALU.add, waits=[b])
                    b = DVE.op(V.scalar_tensor_tensor, out=dst, in0=yy[:], scalar=-c2, in1=dst, op0=ALU.mult, op1=ALU.add, waits=[b])
                    b = DVE.op(V.tensor_scalar, out=dst, in0=dst, scalar1=src_add, scalar2=3.1415925, op0=ALU.add, op1=ALU.min, waits=[b])
                    b = DVE.op(V.tensor_scalar, out=dst, in0=dst, scalar1=-3.1415925, scalar2=None, op0=ALU.max, waits=[b])
                    return ACT.op(A.activation, out=dst, in_=dst, func=AF.Sin, waits=[b])
                ts = reduce_sin(0.0, sin_t[:])
                tc = reduce_sin(math.pi / 2.0, cos_t[:])
                return [ts, tc]
            t_ropek = rope_table(posk, NTB, cosk, sink, "k")
            t_ropeq = rope_table(posq, NQB, cosq, sinq, "q")
            barrier()

        def rms_norm_block(st_rings, x_tile, tx, g_tile, tg, eps, n_feat, hb_tile, hb_free):
            junk, ssr = st_rings
            jt, jfree, jk = junk.next()
            col, cfree, ck = ssr.next()
            a = ACT.op(A.activation, out=jt[:, :n_feat], in_=x_tile, func=AF.Square, accum_out=col[:, 0:1],
                       waits=[tx, jfree, cfree])
            junk.done(jk, a)
            b = DVE.op(V.tensor_scalar, out=col[:, 1:2], in0=col[:, 0:1], scalar1=1.0 / n_feat, scalar2=eps,
                       op0=ALU.mult, op1=ALU.add, waits=[a])
            c = ACT.op(A.activation, out=col[:, 2:3], in_=col[:, 1:2], func=AF.Sqrt, waits=[b])
            d = DVE.op(V.reciprocal, out=col[:, 3:4], in_=col[:, 2:3], waits=[c])
            e = DVE.op(V.scalar_tensor_tensor, out=hb_tile, in0=x_tile, scalar=col[:, 3:4], in1=g_tile,
                       op0=ALU.mult, op1=ALU.mult, waits=[d, tg, hb_free])
            ssr.done(ck, e)
            return e

        rope_last = []

        def rope_apply(tile3, H, half, cs, sn, tmp, waits):
            x1 = tile3[:, :, 0:half]
            x2 = tile3[:, :, half:2 * half]
            cB = cs.unsqueeze(1).to_broadcast([128, H, half])
            sB = sn.unsqueeze(1).to_broadcast([128, H, half])
            tv = lambda i: tmp[:, i, 0:H * half].rearrange("p (h d) -> p h d", h=H)
            waits = list(waits) + rope_last
            a1 = DVE.op(V.tensor_tensor, out=tv(0), in0=x1, in1=cB, op=ALU.mult, waits=waits)
            a2 = DVE.op(V.tensor_tensor, out=tv(1), in0=x2, in1=sB, op=ALU.mult, waits=waits)
            a3 = DVE.op(V.tensor_tensor, out=tv(2), in0=x2, in1=cB, op=ALU.mult, waits=waits)
            a4 = DVE.op(V.tensor_tensor, out=tv(3), in0=x1, in1=sB, op=ALU.mult, waits=waits)
            b1 = DVE.op(V.tensor_tensor, out=x1, in0=tv(0), in1=tv(1), op=ALU.subtract, waits=[a1, a2, a3, a4])
            b2 = DVE.op(V.tensor_tensor, out=x2, in0=tv(2), in1=tv(3), op=ALU.add, waits=[a1, a2, a3, a4])
            rope_last[:] = [b1, b2]
            return [b1, b2]

        with ExitStack() as p1:
            NKV = 2080
            NQC = 1288
            wkv = sbt(p1, "wkv", [128, 8, NKV], BF16)
            wq = sbt(p1, "wq", [128, 8, NQC], BF16)
            w_in_v = w_in.rearrange("(kc p) n -> p kc n", p=128)
            tw = []
            for (dst0, c0, n) in [(0, C_KA, 512), (512, C_KB, 512), (1024, C_VA, 512), (1536, C_VB, 512), (2048, C_KI, 32)]:
                tw.append(GQ.dma(wkv[:, :, dst0:dst0 + n], w_in_v[:, :, c0:c0 + n]))
            twq = []
            for (dst0, c0, n) in [(0, C_QA, 512), (512, C_QB, 512), (1024, C_QI, 256), (1280, C_WI, 8)]:
                twq.append(GQ.dma(wq[:, :, dst0:dst0 + n], w_in_v[:, :, c0:c0 + n]))
            lng = sbt(p1, "lng", [128, 32], F32)
            lnb = sbt(p1, "lnb", [128, 32], F32)
            t_lng = SP.dma(lng[:], bcast(idx_g))
            t_lnb = SP.dma(lnb[:], bcast(idx_b))

            xr = Ring([sbt(p1, f"xt{i}", [128, D], F32) for i in range(3)])
            junk = Ring([sbt(p1, f"junk{i}", [128, D], BF16) for i in range(1)])
            ssr = Ring([sbt(p1, f"ss{i}", [128, 4], F32) for i in range(4)])
            hbr = Ring([sbt(p1, f"hb{i}", [128, D], BF16) for i in range(3)])
            hTr = Ring([sbt(p1, f"hT{i}", [128, 8, 128], BF16) for i in range(3)])
            kfr = Ring([sbt(p1, f"kf{i}", [128, 1024], F32) for i in range(2)])
            kbr = Ring([sbt(p1, f"kb{i}", [128, 1024], BF16) for i in range(3)])
            kTr = Ring([sbt(p1, f"kTt{i}", [128, 8, 128], BF16) for i in range(2)])
            vtr = Ring([sbt(p1, f"vt{i}", [128, 8, VW], BF16) for i in range(2)])
            rtmp = sbt(p1, "rtmp", [128, 4, 128], F32)
            kif = sbt(p1, "kif", [128, 8], F32)
            kic = sbt(p1, "kic", [128, 32], F32)
            kij = sbt(p1, "kij", [128, 32], F32)
            kibr = Ring([sbt(p1, f"kib{i}", [128, 32], BF16) for i in range(3)])
            kiTr = Ring([sbt(p1, f"kiT{i}", [32, 128], BF16) for i in range(2)])
            qwr = Ring([sbt(p1, f"qw{i}", [128, 264], F32) for i in range(2)])
            qibr = Ring([sbt(p1, f"qib{i}", [128, 256], BF16) for i in range(3)])
            qiTr = Ring([sbt(p1, f"qiT{i}", [32, 8, 128], BF16) for i in range(2)])
            psT = Ring([pst(p1, f"psT{i}", [128, D], BF16) for i in range(2)])
            pp = Ring([pst(p1, f"pp{i}", [128, 512], F32) for i in range(4)])
            pkT = Ring([pst(p1, f"pkT{i}", [128, D], BF16) for i in range(2)])
            t_vinit = []
            for vt_ in vtr.tiles:
                t0 = POOL.op(nc.gpsimd.memset, vt_[:], 0.0)
                t1_ = POOL.op(nc.gpsimd.memset, vt_[:, 0:4, :].rearrange("p a (s e) -> p (a s) e", s=2)[:, :, 64:65], 1.0, waits=[t0])
                t_vinit.append(POOL.op(nc.gpsimd.memset, vt_[:, 4:8, 128:129], 1.0, waits=[t0, t1_]))

            def aevac(out_ap, in_ap, waits):
                return ACT.op(A.copy, out=out_ap, in_=in_ap, waits=waits)

            def stageA1(src_rows):
                xt, xfree, xk = xr.next()
                tx = SP.dma(xt[:], src_rows, waits=xfree)
                hb, hfree, hk = hbr.next()
                th = rms_norm_block((junk, ssr), xt[:], tx, gmix[:], t_gmix, 1e-6, D, hb[:], hfree)
                xr.done(xk, th)
                return hb, th, hk

            def stageA2(hb, th, hk):
                ps, pfree, pk = psT.next()
                tt = None
                for kc in range(8):
                    tt = PE.op(T.transpose, ps[:, kc * 128:(kc + 1) * 128], hb[:, kc * 128:(kc + 1) * 128], ident[:],
                               waits=[th, t_ident, pfree] if kc == 0 else ())
                hbr.done(hk, tt)
                hT, tfree, tk = hTr.next()
                te = aevac(hT[:].rearrange("p a b -> p (a b)"), ps[:], [tt, tfree])
                psT.done(pk, te)
                return hT, te, tk

            def project(hT, th, w_tile, c0, n, wtoks):
                ps, pfree, pk = pp.next()
                tt = None
                for kc in range(8):
                    tt = PE.op(T.matmul, ps[:, 0:n], lhsT=hT[:, kc, :], rhs=w_tile[:, kc, c0:c0 + n],
                               start=(kc == 0), stop=(kc == 7), waits=[th, pfree, wtoks] if kc == 0 else ())
                return ps, tt, pk

            def transpose_out(src_bf, tsrc, nchunk, width, ring_sb, dst_dram):
                ps, pfree, pk = pkT.next()
                tt = None
                for c in range(nchunk):
                    tt = PE.op(T.transpose, ps[0:width, c * 128:(c + 1) * 128], src_bf[:, c * width:(c + 1) * width], ident[:],
                               waits=[tsrc, pfree, t_ident] if c == 0 else ())
                sbT, sfree, sk = ring_sb.next()
                te = aevac(sbT[:].rearrange("p a b -> p (a b)") if len(sbT.shape) == 3 else sbT[:],
                           ps[0:width, 0:nchunk * 128], [tt, sfree])
                pkT.done(pk, te)
                td = GQ.dma(dst_dram, sbT[:], waits=[te])
                ring_sb.done(sk, td)
                return tt

            def transpose_out_qi(src_bf, tsrc, s):
                ps, pfree, pk = pkT.next()
                tt = None
                for c in range(8):
                    tt = PE.op(T.transpose, ps[0:32, c * 128:(c + 1) * 128], src_bf[:, c * 32:(c + 1) * 32], ident[:],
                               waits=[tsrc, pfree, t_ident] if c == 0 else ())
                sbT, sfree, sk = qiTr.next()
                te = aevac(sbT[:].rearrange("p a b -> p (a b)"), ps[0:32, 0:1024], [tt, sfree])
                pkT.done(pk, te)
                for g in range(2):
                    td = GQ.dma(qiT_scr[g * 32:(g + 1) * 32, :, s * 128:(s + 1) * 128], sbT[:, g::2, :], waits=[te])
                    qiTr.done(sk, td)
                return tt

            def qk_pair(hT, th, w_tile, wtoks, cs, sn, scale, dst_dram):
                kf, kfree, kk = kfr.next()
                tes = []
                for gi in range(2):
                    ps, tmm, pk = project(hT, th, w_tile, gi * 512, 512, wtoks)
                    te = aevac(kf[:, gi * 512:(gi + 1) * 512], ps[:], [tmm, kfree])
                    pp.done(pk, te)
                    tes.append(te)
                tr = rope_apply(kf[:].rearrange("p (h d) -> p h d", h=16), 16, 8, cs, sn, rtmp, tes)
                kb, bfree, bk = kbr.next()
                if scale == 1.0:
                    tcst = DVE.op(V.tensor_copy, out=kb[:], in_=kf[:], waits=[tr, bfree])
                else:
                    tcst = DVE.op(V.tensor_scalar, out=kb[:], in0=kf[:], scalar1=scale, scalar2=None, op0=ALU.mult, waits=[tr, bfree])
                kfr.done(kk, tcst)
                return kb, bk, tcst

            def stageB_kv(tb, hT, th, hk):
                cs = cosk[:, tb, :]
                sn = sink[:, tb, :]
                kb, bk, tcst = qk_pair(hT, th, wkv, tw, cs, sn, 1.0, None)
                vt, vfree, vk = vtr.next()
                tvs = []
                for gi in (2, 3):
                    ps, tmm, pk = project(hT, th, wkv, gi * 512, 512, tw)
                    if gi == 2:
                        dstv = vt[:, 0:4, :].rearrange("p a (s e) -> p (a s) e", s=2)[:, :, 0:64]
                        te = aevac(dstv, ps[:].rearrange("p (h e) -> p h e", e=64), [tmm, vfree, t_vinit])
                    else:
                        te = aevac(vt[:, 4:8, 0:128], ps[:].rearrange("p (h e) -> p h e", e=128), [tmm, vfree, t_vinit])
                    pp.done(pk, te)
                    tvs.append(te)
                td = GQ.dma(v_scr[tb * 128:(tb + 1) * 128, :], vt[:].rearrange("p a b -> p (a b)"), waits=tvs)
                vtr.done(vk, td)
                ps, tmm, pk = project(hT, th, wkv, 2048, 32, tw)
                hTr.done(hk, tmm)
                a = DVE.op(V.tensor_reduce, out=kif[:, 0:1], in_=ps[:, 0:32], axis=AX.X, op=ALU.add, waits=[tmm])
                a = DVE.op(V.tensor_scalar, out=kif[:, 1:2], in0=kif[:, 0:1], scalar1=1.0 / 32, scalar2=None, op0=ALU.mult, waits=[a])
                a = DVE.op(V.tensor_scalar, out=kic[:], in0=ps[:, 0:32], scalar1=kif[:, 1:2], scalar2=None, op0=ALU.subtract, waits=[a])
                pp.done(pk, a)
                b = ACT.op(A.activation, out=kij[:], in_=kic[:], func=AF.Square, accum_out=kif[:, 2:3], waits=[a])
                b = DVE.op(V.tensor_scalar, out=kif[:, 3:4], in0=kif[:, 2:3], scalar1=1.0 / 32, scalar2=1e-6, op0=ALU.mult, op1=ALU.add, waits=[b])
                b = ACT.op(A.activation, out=kif[:, 4:5], in_=kif[:, 3:4], func=AF.Sqrt, waits=[b])
                b = DVE.op(V.reciprocal, out=kif[:, 5:6], in_=kif[:, 4:5], waits=[b])
                b = DVE.op(V.scalar_tensor_tensor, out=kic[:], in0=kic[:], scalar=kif[:, 5:6], in1=lng[:], op0=ALU.mult, op1=ALU.mult, waits=[b, t_lng])
                b = DVE.op(V.tensor_tensor, out=kic[:], in0=kic[:], in1=lnb[:], op=ALU.add, waits=[b, t_lnb])
                csi = cosk[:, tb, :].rearrange("p (a two) -> p a two", two=2)[:, :, 0]
                sni = sink[:, tb, :].rearrange("p (a two) -> p a two", two=2)[:, :, 0]
                tr = rope_apply(kic[:].rearrange("p (h d) -> p h d", h=1), 1, 4, csi, sni, rtmp, [b])
                kib, bfree, bk2 = kibr.next()
                tc2 = DVE.op(V.tensor_copy, out=kib[:], in_=kic[:], waits=[tr, bfree])

                def b2():
                    tlast = transpose_out(kb, tcst, 8, 128, kTr, kT_scr[:, :, tb * 128:(tb + 1) * 128].rearrange("c f t -> f c t"))
                    kbr.done(bk, tlast)
                    tl2 = transpose_out(kib, tc2, 1, 32, kiTr, kiT_scr[:, tb * 128:(tb + 1) * 128])
                    kibr.done(bk2, tl2)
                return b2

            def stageB_q(s, hT, th, hk):
                cs = cosq[:, s, :]
                sn = sinq[:, s, :]
                kb, bk, tcst = qk_pair(hT, th, wq, twq, cs, sn, 0.125, None)
                ps, tmm, pk = project(hT, th, wq, 1024, 264, twq)
                hTr.done(hk, tmm)
                qw, qfree, qk = qwr.next()
                te = aevac(qw[:], ps[:, 0:264], [tmm, qfree])
                pp.done(pk, te)
                csi = cosq[:, s, :].rearrange("p (a two) -> p a two", two=2)[:, :, 0]
                sni = sinq[:, s, :].rearrange("p (a two) -> p a two", two=2)[:, :, 0]
                tr = rope_apply(qw[:, 0:256].rearrange("p (h d) -> p h d", h=8), 8, 4, csi, sni, rtmp, [te])
                a1 = DVE.op(V.tensor_scalar, out=wabs[:, s, :], in0=qw[:, 256:264], scalar1=1.0 / 16, scalar2=None, op0=ALU.mult, waits=[te])
                a2 = a1
                qib, bfree, bk2 = qibr.next()
                tc2 = DVE.op(V.tensor_copy, out=qib[:], in_=qw[:, 0:256], waits=[tr, bfree])
                qwr.done(qk, [tc2, a1, a2])

                def b2():
                    tlast = transpose_out(kb, tcst, 8, 128, kTr, qT_scr[:, :, s * 128:(s + 1) * 128].rearrange("c f t -> f c t"))
                    kbr.done(bk, tlast)
                    tl2 = transpose_out_qi(qib, tc2, s)
                    qibr.done(bk2, tl2)
                return b2

            items = [("kv", tb, xs[tb * 128:(tb + 1) * 128, :]) for tb in range(NTB)] + \
                    [("q", s, xq[s * 128:(s + 1) * 128, :]) for s in range(NQB)]
            n_it = len(items)
            a1 = {}
            a2 = {}
            a1[0] = stageA1(items[0][2])
            a2[0] = stageA2(*a1[0])
            if n_it > 1:
                a1[1] = stageA1(items[1][2])
            prev_b2 = None
            for i in range(n_it):
                if i + 2 < n_it:
                    a1[i + 2] = stageA1(items[i + 2][2])
                kind, idx, _ = items[i]
                hT, th, hk = a2.pop(i)
                if kind == "kv":
                    b2 = stageB_kv(idx, hT, th, hk)
                else:
                    b2 = stageB_q(idx, hT, th, hk)
                if i + 1 < n_it:
                    a2[i + 1] = stageA2(*a1.pop(i + 1))
                if prev_b2 is not None:
                    prev_b2()
                prev_b2 = b2
            if prev_b2 is not None:
                prev_b2()
            barrier()

        with ExitStack() as p2:
            kiT = sbt(p2, "kiT", [64, S], BF16)
            t_kiT = [SP.dma(kiT[g * 32:(g + 1) * 32, :], kiT_scr) for g in range(2)]
            gsub = sbt(p2, "gsub", [128, 128], F32)
            t_gs = SP.dma(gsub[:], bcast(subln_g))
            t_gs = DVE.op(V.tensor_scalar, out=gsub[:], in0=gsub[:], scalar1=0.8, scalar2=None, op0=ALU.mult, waits=[t_gs])
            ident2 = sbt(p2, "ident2", [128, 2, 128], BF16)
            t_id2 = [DVE.op(V.tensor_copy, out=ident2[:, i, :], in_=ident[:], waits=[t_ident]) for i in range(2)]
            Kr = Ring([sbt(p2, f"Kb{i}", [128, S], BF16) for i in range(2)])
            Vr = Ring([sbt(p2, f"Vb{i}", [128, NTB, VW], BF16) for i in range(2)])
            Mb = [sbt(p2, f"Mb{i}", [128, S], BF16) for i in range(2)]
            Isc = sbt(p2, "Isc", [128, S], F32)
            qbd_tiles = [sbt(p2, f"qbd{i}", [128, 2, 128], BF16) for i in range(3)]
            t_qz = [POOL.op(nc.gpsimd.memset, q_[:], 0.0) for q_ in qbd_tiles]
            qbr = Ring(qbd_tiles)
            qiT = [sbt(p2, f"qiTs{i}", [64, 4, 128], BF16) for i in range(2)]
            rl = Ring([sbt(p2, f"rl{i}", [128, 512], BF16) for i in range(8)])
            identf = sbt(p2, "identf", [128, 128], F32)
            t_idf = DVE.op(V.tensor_copy, out=identf[:], in_=ident[:], waits=[t_ident])
            dgb = [sbt(p2, f"dgb{i}", [128, 8, 128], BF16) for i in range(2)]
            etr = Ring([sbt(p2, f"et{i}", [128, 512], BF16) for i in range(4)])
            otr = Ring([sbt(p2, f"ot{i}", [128, D], BF16) for i in range(2)])
            of32 = sbt(p2, "of32", [128, 128], F32)
            oj = sbt(p2, "oj", [128, 128], F32)
            bs = sbt(p2, "bs", [128, 16], F32)
            es_ = sbt(p2, "es_", [128, 16], F32)
            pss = Ring([pst(p2, f"pss{i}", [128, 512], F32) for i in range(4)])
            pI = pst(p2, "pI", [128, 512], F32)
            pacc = Ring([pst(p2, f"pacc{i}", [128, 512], F32) for i in range(3)])
            of1 = sbt(p2, "of1", [128, 128], F32)
            of2 = sbt(p2, "of2", [128, 128], F32)
            ez = sbt(p2, "ez", [128, 8], F32)
            Mb_ready = [None, None]
            Mb_readers = [[], []]
            qiT_readers = [[], []]
            dg_readers = [[], []]
            Isc_free = [[]]
            pI_free = [[]]

            def prep(s):
                li = s % 2
                nk = (4 * s + 4) * 128
                nch = nk // 512
                wb = 0.76 * (s + 1)
                tq = SP.dma(qiT[li][:], qiT_scr[:, :, s * 128:(s + 1) * 128], waits=qiT_readers[li])
                qiT_readers[li] = []
                tdg = None
                for h in range(8):
                    tdg = DVE.op(V.tensor_scalar, out=dgb[li][:, h, :], in0=identf[:], scalar1=wabs[:, s, h:h + 1], scalar2=None, op0=ALU.mult,
                                 waits=[t_idf, dg_readers[li]] if h == 0 else ())
                dg_readers[li] = []
                yield 1.0
                tI = None
                pending = None

                def flush(pend):
                    (pc, pr, items) = pend
                    tacc = None
                    for g, (prt, pta, prk) in enumerate(items):
                        h = 2 * pr + g
                        tacc = PE.op(T.matmul, pI[:], lhsT=dgb[li][:, h, :], rhs=prt[:],
                                     start=(pr == 0 and g == 0), stop=(pr == 3 and g == 1),
                                     waits=[pta, tdg, pI_free[0]])
                        rl.done(prk, tacc)
                    return tacc
                pendq = []

                def drain(keep):
                    nonlocal tI
                    tacc = None
                    while len(pendq) > keep:
                        pend = pendq.pop(0)
                        tacc = flush(pend)
                        if pend[1] == 3:
                            pc = pend[0]
                            tI = DVE.op(V.tensor_copy, out=Isc[:, pc * 512:(pc + 1) * 512], in_=pI[:], waits=[tacc, Isc_free[0]])
                            pI_free[0] = [tI]
                    return tacc
                for c in range(nch):
                    for r in range(4):
                        drain(1)
                        mm = []
                        for g in range(2):
                            ps, pfree, pk = pss.next()
                            tm = PE.op(T.matmul, ps[:], lhsT=qiT[li][g * 32:(g + 1) * 32, r, :], rhs=kiT[g * 32:(g + 1) * 32, c * 512:(c + 1) * 512],
                                       start=True, stop=True, waits=[tq, t_kiT, pfree])
                            mm.append((ps, pk, tm))
                        items = []
                        for g in range(2):
                            ps, pk, tm = mm[g]
                            rt, rfree, rk = rl.next()
                            if g == 0 or (r % 2) == 1:
                                ta = ACT.op(A.activation, out=rt[:], in_=ps[:], func=AF.Relu, waits=[tm, rfree])
                            else:
                                ta = DVE.op(V.tensor_scalar, out=rt[:], in0=ps[:], scalar1=0.0, scalar2=None, op0=ALU.max, waits=[tm, rfree])
                            pss.done(pk, ta)
                            items.append((rt, ta, rk))
                        pendq.append((c, r, items))
                        yield 2.0
                tacc = drain(0)
                qiT_readers[li].append(tacc)
                dg_readers[li].append(tacc)
                Isc_free[0] = []
                Iv = Isc[:, 0:nk]
                a = DVE.op(V.tensor_reduce, out=bs[:, 0:1], in_=Iv, axis=AX.X, op=ALU.max, waits=[tI])
                yield wb
                a = DVE.op(V.tensor_reduce, out=bs[:, 1:2], in_=Iv, axis=AX.X, op=ALU.min, waits=[a])
                a = DVE.op(V.scalar_tensor_tensor, out=bs[:, 2:3], in0=bs[:, 0:1], scalar=1.0, in1=bs[:, 1:2], op0=ALU.add, op1=ALU.subtract, waits=[a])
                a = DVE.op(V.tensor_tensor, out=Isc[:, nk - 512:nk], in0=Isc[:, nk - 512:nk], in1=cbf[:], op=ALU.add, waits=[a, t_cbf])
                yield wb
                lo = bs[:, 1:2]
                w0 = bs[:, 2:3]
                mid = bs[:, 3:4]
                cnt = bs[:, 4:5]
                gg = bs[:, 5:6]
                for it in range(1, NBIS + 1):
                    sc = 2.0 ** (-it)
                    a = DVE.op(V.tensor_scalar, out=mid, in0=w0, scalar1=sc, scalar2=lo, op0=ALU.mult, op1=ALU.add, waits=[a])
                    a = DVE.op(V.tensor_scalar, out=Mb[li][:, 0:nk], in0=Iv, scalar1=mid, scalar2=None, op0=ALU.is_ge, op1=ALU.add,
                               accum_out=cnt, waits=[a, Mb_readers[li]])
                    Mb_readers[li] = []
                    a = DVE.op(V.tensor_scalar, out=gg, in0=cnt, scalar1=TOPK - 0.5, scalar2=sc, op0=ALU.is_ge, op1=ALU.mult, waits=[a])
                    a = DVE.op(V.scalar_tensor_tensor, out=lo, in0=w0, scalar=gg, in1=lo, op0=ALU.mult, op1=ALU.add, waits=[a])
                    yield wb
                a = DVE.op(V.tensor_scalar, out=Mb[li][:, 0:nk], in0=Iv, scalar1=lo, scalar2=NEG, op0=ALU.is_lt, op1=ALU.mult, waits=[a])
                Mb_ready[li] = a
                Isc_free[0] = [a]

            def prep_steps(s):
                return 1 + 8 * (s + 1) + (2 + NBIS) * 0.76 * (s + 1)

            pump_state = {"gen": None, "budget": 0.0, "rate": 0.0}

            def pump():
                st = pump_state
                if st["gen"] is None:
                    return
                st["budget"] += st["rate"]
                while st["budget"] > 0.0 and st["gen"] is not None:
                    try:
                        st["budget"] -= next(st["gen"])
                    except StopIteration:
                        st["gen"] = None

            def attention(s, p, ot, ofree, owr):
                li = s % 2
                is_dsa = p < 4
                nkb = 4 * s + 4
                kmax = nkb * 128
                Kb, kfree, kk = Kr.next()
                Vb, vfree, vk = Vr.next()
                tK = SP.dma(Kb[:, 0:kmax], kT_scr[p, :, 0:kmax], waits=kfree)
                tV = []
                for b0 in range(0, nkb, 8):
                    nb = min(8, nkb - b0)
                    tV.append(SP.dma(Vb[:, b0:b0 + nb, :], v_scr[b0 * 128:(b0 + nb) * 128, p * VW:(p + 1) * VW].rearrange("(b t) w -> t b w", t=128), waits=vfree))
                qbd, qfree, qk = qbr.next()
                tq = [SP.dma(qbd[m * 64:(m + 1) * 64, m, :], qT_scr[p, m * 64:(m + 1) * 64, s * 128:(s + 1) * 128], waits=[qfree, t_qz]) for m in range(2)]
                accs = [pacc.next() for m in range(2)]
                vw = 66 if is_dsa else 130
                ntile = nkb // 2
                pend = {}

                def do_qk(ti):
                    ps, pfree, pk = pss.next()
                    tm = None
                    for bi in range(2):
                        kb_ = ti * 2 + bi
                        need_mask = is_dsa or (kb_ >= nkb - 4)
                        tm = PE.op(T.matmul, ps[:, bi * 256:(bi + 1) * 256], lhsT=Kb[:, kb_ * 128:(kb_ + 1) * 128],
                                   rhs=qbd[:].rearrange("p a b -> p (a b)"), start=True, stop=not need_mask,
                                   waits=[tK, tq, pfree] if bi == 0 else ())
                        if need_mask:
                            if is_dsa:
                                ml = Mb[li][:, kb_ * 128:(kb_ + 1) * 128]
                                mw = [Mb_ready[li]]
                            else:
                                cbi = kb_ - (nkb - 4)
                                ml = cb[:, cbi * 128:(cbi + 1) * 128]
                                mw = [t_cb]
                            tm = PE.op(T.matmul, ps[:, bi * 256:(bi + 1) * 256], lhsT=ml, rhs=ident2[:].rearrange("p a b -> p (a b)"),
                                       start=False, stop=True, waits=mw + [t_id2])
                    et, efree, ek = etr.next()
                    te = ACT.op(A.activation, out=et[:], in_=ps[:], func=AF.Exp, waits=[tm, efree])
                    pss.done(pk, te)
                    pend[ti] = (et, te, ek)

                tav = [None, None]

                def do_av(ti):
                    et, te, ek = pend.pop(ti)
                    tm = None
                    first = True
                    for bi in range(2):
                        kb_ = ti * 2 + bi
                        for m in range(2):
                            acc, afree, ak = accs[m]
                            rhs = Vb[:, kb_, m * 66:(m + 1) * 66] if is_dsa else Vb[:, kb_, 0:130]
                            tm = PE.op(T.matmul, acc[:, 0:vw], lhsT=et[:, bi * 256 + m * 128:bi * 256 + (m + 1) * 128], rhs=rhs,
                                       start=(kb_ == 0), stop=(kb_ == nkb - 1),
                                       waits=[te, tV, accs[0][1], accs[1][1]] if first else ())
                            first = False
                            tav[m] = tm
                    etr.done(ek, tm)
                LAG = 2
                for ti in range(ntile):
                    do_qk(ti)
                    if ti >= LAG:
                        do_av(ti - LAG)
                    pump()
                for ti in range(max(0, ntile - LAG), ntile):
                    do_av(ti)
                qbr.done(qk, tav[1])
                if is_dsa:
                    Mb_readers[li].append(tav[1])
                Kr.done(kk, tav[1])
                Vr.done(vk, tav[1])
                if is_dsa:
                    for m in range(2):
                        acc, afree, ak = accs[m]
                        a = ACT.op(A.activation, out=ez[:, m:m + 1], in_=acc[:, 64:65], func=AF.Ln, waits=[tav[1]])
                        a = ACT.op(A.activation, out=ez[:, m:m + 1], in_=ez[:, m:m + 1], func=AF.Exp, scale=-1.0, waits=[a])
                        a = ACT.op(A.activation, out=ot[:, p * 128 + m * 64:p * 128 + (m + 1) * 64], in_=acc[:, 0:64], func=AF.Copy,
                                   scale=ez[:, m:m + 1], waits=[a, ofree])
                        pacc.done(ak, a)
                        owr.append(a)
                else:
                    h = p - 4
                    acc1, _, ak1 = accs[0]
                    acc2, _, ak2 = accs[1]
                    a = ACT.op(A.activation, out=ez[:, 2:3], in_=acc1[:, 128:129], func=AF.Ln, waits=[tav[1], of_free[0]])
                    a = ACT.op(A.activation, out=ez[:, 2:3], in_=ez[:, 2:3], func=AF.Exp, scale=-1.0, waits=[a])
                    a = ACT.op(A.activation, out=of1[:], in_=acc1[:, 0:128], func=AF.Copy, scale=ez[:, 2:3], waits=[a])
                    pacc.done(ak1, a)
                    b = ACT.op(A.activation, out=ez[:, 3:4], in_=acc2[:, 128:129], func=AF.Ln, waits=[a])
                    b = ACT.op(A.activation, out=ez[:, 3:4], in_=ez[:, 3:4], func=AF.Exp, scale=-1.0, waits=[b])
                    b = ACT.op(A.activation, out=of2[:], in_=acc2[:, 0:128], func=AF.Copy, scale=ez[:, 3:4], waits=[b])
                    pacc.done(ak2, b)
                    b = DVE.op(V.scalar_tensor_tensor, out=of32[:], in0=of2[:], scalar=nlam[:, 0:1], in1=of1[:],
                               op0=ALU.mult, op1=ALU.add, waits=[a, b, t_nlam, of32_free[0]])
                    of_free[0] = [b]
                    c = ACT.op(A.activation, out=oj[:], in_=of32[:], func=AF.Square, accum_out=es_[:, 5:6], waits=[b])
                    c = DVE.op(V.tensor_scalar, out=es_[:, 6:7], in0=es_[:, 5:6], scalar1=1.0 / 128, scalar2=1e-5, op0=ALU.mult, op1=ALU.add, waits=[c])
                    c = ACT.op(A.activation, out=es_[:, 7:8], in_=es_[:, 6:7], func=AF.Ln, waits=[c])
                    c = ACT.op(A.activation, out=es_[:, 8:9], in_=es_[:, 7:8], func=AF.Exp, scale=-0.5, waits=[c])
                    c = DVE.op(V.scalar_tensor_tensor, out=ot[:, 512 + h * 128:512 + (h + 1) * 128], in0=of32[:], scalar=es_[:, 8:9],
                               in1=gsub[:], op0=ALU.mult, op1=ALU.mult, waits=[c, t_gs, ofree])
                    of32_free[0] = [c]
                    owr.append(c)

            of_free = [[]]
            of32_free = [[]]
            for _ in prep(0):
                pass
            for s in range(NQB):
                if s + 1 < NQB:
                    pump_state["gen"] = prep(s + 1)
                    pump_state["budget"] = 0.0
                    pump_state["rate"] = prep_steps(s + 1) / float(8 * (2 * s + 2)) * 1.15
                else:
                    pump_state["gen"] = None
                ot, ofree, ok_ = otr.next()
                owr = []
                for pi, p in enumerate([4, 5, 6, 7, 0, 1, 2, 3]):
                    attention(s, p, ot, ofree, owr)
                if pump_state["gen"] is not None:
                    for _ in pump_state["gen"]:
                        pass
                    pump_state["gen"] = None
                td = GQ.dma(o_scr[s * 128:(s + 1) * 128, :], ot[:], waits=owr)
                otr.done(ok_, td)
            barrier()

        def load_w(st, name, src, rows, cols, q, waits=(), order=None):
            nkc = rows // 128
            wt = sbt(st, name, [128, nkc, cols], BF16)
            srcv = src.rearrange("(kc p) n -> p kc n", p=128)
            step = 1024
            starts = list(range(0, cols, step))
            toks = [None] * len(starts)
            for ci in (order if order is not None else range(len(starts))):
                c0 = starts[ci]
                n = min(step, cols - c0)
                toks[ci] = q.dma(wt[:, :, c0:c0 + n], srcv[:, :, c0:c0 + n], waits=waits)
            return wt, toks

        with ExitStack() as p3:
            wg = sbt(p3, "wg", [128, 8, 2048], BF16)
            w_in_v = w_in.rearrange("(kc p) n -> p kc n", p=128)
            twg = [GQ.dma(wg[:, :, c0:c0 + 512], w_in_v[:, :, C_G + c0:C_G + c0 + 512]) for c0 in range(0, 2048, 512)]
            wbd, twbd = load_w(p3, "wbd", w_bd, 512, D, GQ)
            wbf, twbf = load_w(p3, "wbf", w_bf, 512, D, GQ)
            wo, two = load_w(p3, "wo", w_out, D, D, GQ)
            gbt = sbt(p3, "gbt", [128, 2048], F32)
            t_gb = SP.dma(gbt[:], bcast(gate_b))
            xr = Ring([sbt(p3, f"xt{i}", [128, D], F32) for i in range(4)])
            junk = Ring([sbt(p3, f"junk{i}", [128, D], BF16) for i in range(1)])
            ssr = Ring([sbt(p3, f"ss{i}", [128, 4], F32) for i in range(4)])
            hbr = Ring([sbt(p3, f"hb{i}", [128, D], BF16) for i in range(3)])
            hTr = Ring([sbt(p3, f"hT{i}", [128, 8, 128], BF16) for i in range(2)])
            obr = Ring([sbt(p3, f"ob{i}", [128, D], BF16) for i in range(3)])
            oTr = Ring([sbt(p3, f"oT{i}", [128, 8, 128], BF16) for i in range(2)])
            gat = sbt(p3, "gat", [128, 2048], F32)
            mrg = sbt(p3, "mrg", [128, D], F32)
            mrg2 = sbt(p3, "mrg2", [128, D], F32)
            mbr = Ring([sbt(p3, f"mb{i}", [128, D], BF16) for i in range(2)])
            mTr = Ring([sbt(p3, f"mT{i}", [128, 8, 128], BF16) for i in range(2)])
            x1r = Ring([sbt(p3, f"x1t{i}", [128, D], F32) for i in range(2)])
            psT = Ring([pst(p3, f"psT{i}", [128, D], BF16) for i in range(2)])
            pp = Ring([pst(p3, f"pp{i}", [128, 512], F32) for i in range(6)])
            x1_dmas = []

            def transpose8(src_bf, tsrc, dst_ring):
                ps, pfree, pk = psT.next()
                tt = None
                for kc in range(8):
                    tt = PE.op(T.transpose, ps[:, kc * 128:(kc + 1) * 128], src_bf[:, kc * 128:(kc + 1) * 128], ident[:],
                               waits=[tsrc, pfree] if kc == 0 else ())
                dT, dfree, dk = dst_ring.next()
                te = ACT.op(A.copy, out=dT[:].rearrange("p a b -> p (a b)"), in_=ps[:], waits=[tt, dfree])
                psT.done(pk, te)
                return dT, te, dk, tt

            def st_load(s):
                xt, xfree, xk = xr.next()
                tx = SP.dma(xt[:], xq[s * 128:(s + 1) * 128, :], waits=xfree)
                hb, hfree, hk = hbr.next()
                th = rms_norm_block((junk, ssr), xt[:], tx, gmix[:], t_gmix, 1e-6, D, hb[:], hfree)
                ob, ofree, ok_ = obr.next()
                to = SP.dma(ob[:], o_scr[s * 128:(s + 1) * 128, :], waits=[ofree])
                return dict(s=s, xt=xt, xk=xk, hb=hb, th=th, hk=hk, ob=ob, to=to, ok_=ok_)

            def st_T(c):
                hT, te, tk, tt = transpose8(c["hb"], c["th"], hTr)
                hbr.done(c["hk"], tt)
                oT, teo, ok2, tto = transpose8(c["ob"], c["to"], oTr)
                obr.done(c["ok_"], tto)
                c.update(hT=hT, te=te, tk=tk, oT=oT, teo=teo, ok2=ok2)

            def st_X(c):
                hT, te, tk = c["hT"], c["te"], c["tk"]
                oT, teo, ok2 = c["oT"], c["teo"], c["ok2"]
                tg_last = None
                for gc in range(4):
                    ps, pfree, pk = pp.next()
                    tm = None
                    for kc in range(8):
                        tm = PE.op(T.matmul, ps[:], lhsT=hT[:, kc, :], rhs=wg[:, kc, gc * 512:(gc + 1) * 512], start=(kc == 0), stop=(kc == 7),
                                   waits=[te, pfree, twg] if kc == 0 else ())
                    a_ = DVE.op(V.tensor_tensor, out=gat[:, gc * 512:(gc + 1) * 512], in0=ps[:], in1=gbt[:, gc * 512:(gc + 1) * 512], op=ALU.add,
                                waits=[tm, t_gb, gat_free[0]])
                    pp.done(pk, a_)
                    tg_last = ACT.op(A.activation, out=gat[:, gc * 512:(gc + 1) * 512], in_=gat[:, gc * 512:(gc + 1) * 512], func=AF.Sigmoid, waits=[a_])
                    if gc == 3:
                        hTr.done(tk, tm)
                gat_free[0] = []
                mtoks = []
                for br, (wt, twt) in enumerate([(wbd, twbd), (wbf, twbf)]):
                    for nc_ in range(2):
                        ps, pfree, pk = pp.next()
                        tm = None
                        for kc in range(4):
                            tm = PE.op(T.matmul, ps[:], lhsT=oT[:, br * 4 + kc, :], rhs=wt[:, kc, nc_ * 512:(nc_ + 1) * 512], start=(kc == 0), stop=(kc == 3),
                                       waits=[teo, pfree, twt] if kc == 0 else ())
                        dst = (mrg if br == 0 else mrg2)[:, nc_ * 512:(nc_ + 1) * 512]
                        a_ = DVE.op(V.tensor_tensor, out=dst, in0=ps[:], in1=gat[:, br * 1024 + nc_ * 512:br * 1024 + (nc_ + 1) * 512], op=ALU.mult,
                                    waits=[tm, tg_last, mrg_free[0]])
                        pp.done(pk, a_)
                        mtoks.append(a_)
                        if br == 1 and nc_ == 1:
                            oTr.done(ok2, tm)
                gat_free[0] = list(mtoks)
                mb, mfree, mk = mbr.next()
                tmb = DVE.op(V.tensor_tensor, out=mb[:], in0=mrg[:], in1=mrg2[:], op=ALU.add, waits=[mtoks, mfree])
                mrg_free[0] = [tmb]
                c.update(mb=mb, tmb=tmb, mk=mk)

            def st_Y(c):
                s_ = c["s"]
                mT, tem, mk2, ttm = transpose8(c["mb"], c["tmb"], mTr)
                mbr.done(c["mk"], ttm)
                x1t, x1free, x1k = x1r.next()
                xtoks = []
                for nc_ in range(2):
                    ps, pfree, pk = pp.next()
                    tm = None
                    for kc in range(8):
                        tm = PE.op(T.matmul, ps[:], lhsT=mT[:, kc, :], rhs=wo[:, kc, nc_ * 512:(nc_ + 1) * 512], start=(kc == 0), stop=(kc == 7),
                                   waits=[tem, pfree, two] if kc == 0 else ())
                    a_ = DVE.op(V.tensor_tensor, out=x1t[:, nc_ * 512:(nc_ + 1) * 512], in0=ps[:], in1=c["xt"][:, nc_ * 512:(nc_ + 1) * 512], op=ALU.add,
                                waits=[tm, x1free])
                    pp.done(pk, a_)
                    xtoks.append(a_)
                    if nc_ == 1:
                        mTr.done(mk2, tm)
                xr.done(c["xk"], xtoks)
                td = GQ.dma(x1_scr[s_ * 128:(s_ + 1) * 128, :], x1t[:], waits=xtoks)
                x1r.done(x1k, td)
                x1_dmas.append(td)

            gat_free = [[]]
            mrg_free = [[]]
            ctx = {}
            ctx[0] = st_load(0)
            st_T(ctx[0])
            if NQB > 1:
                ctx[1] = st_load(1)
            prevY = None
            for s in range(NQB):
                if s + 2 < NQB:
                    ctx[s + 2] = st_load(s + 2)
                st_X(ctx[s])
                if s + 1 < NQB:
                    st_T(ctx[s + 1])
                if prevY is not None:
                    st_Y(prevY)
                prevY = ctx.pop(s)
            st_Y(prevY)
            barrier()
        early.close()

        with ExitStack() as p4:
            w1, tw1 = load_w(p4, "w1", w_f1, D, 2 * DFF, GQ, order=[0, 2, 3, 1, 4, 5])
            w2, tw2 = load_w(p4, "w2", w_f2, DFF, D, GQ)
            gffn = sbt(p4, "gffn", [128, D], F32)
            gfin = sbt(p4, "gfin", [128, D], F32)
            t_gffn = SP.dma(gffn[:], bcast(norm_ffn_g))
            t_gfin = SP.dma(gfin[:], bcast(norm_fin_g))
            x1r = Ring([sbt(p4, f"x1b{i}", [128, D], F32) for i in range(2)])
            junk = Ring([sbt(p4, f"junk{i}", [128, D], BF16) for i in range(1)])
            ssr = Ring([sbt(p4, f"ss{i}", [128, 4], F32) for i in range(2)])
            hbr = Ring([sbt(p4, f"hb{i}", [128, D], BF16) for i in range(2)])
            h2T = sbt(p4, "h2T", [128, 8, 512], BF16)
            actT = sbt(p4, "actT", [128, NFC, 512], BF16)
            sgr = Ring([sbt(p4, f"sg{i}", [128, 512], F32) for i in range(2)])
            x2r = Ring([sbt(p4, f"x2t{i}", [128, D], F32) for i in range(2)])
            psT = Ring([pst(p4, f"psT{i}", [128, D], BF16) for i in range(2)])
            pp = Ring([pst(p4, f"pp{i}", [128, 512], F32) for i in range(6)])
            h2T_free = []
            actT_free = []
            for grp in range(NQB // 4):
                th2 = []
                for bi in range(4):
                    s = grp * 4 + bi
                    x1t, x1free, x1k = x1r.next()
                    tx = SP.dma(x1t[:], x1_scr[s * 128:(s + 1) * 128, :], waits=[x1free])
                    hb, hfree, hk = hbr.next()
                    th = rms_norm_block((junk, ssr), x1t[:], tx, gffn[:], t_gffn, 1e-6, D, hb[:], hfree)
                    x1r.done(x1k, th)
                    ps, pfree, pk = psT.next()
                    tt = None
                    for kc in range(8):
                        tt = PE.op(T.transpose, ps[:, kc * 128:(kc + 1) * 128], hb[:, kc * 128:(kc + 1) * 128], ident[:],
                                   waits=[th, pfree] if kc == 0 else ())
                    hbr.done(hk, tt)
                    te = ACT.op(A.copy, out=h2T[:, :, bi * 128:(bi + 1) * 128], in_=ps[:].rearrange("p (a b) -> p a b", a=8), waits=[tt, h2T_free])
                    psT.done(pk, te)
                    th2.append(te)
                h2T_free = []
                tact = []
                last_mm = None
                for f in range(NFC):
                    psg, pfree, pkg = pp.next()
                    tmg = None
                    for kc in range(8):
                        tmg = PE.op(T.matmul, psg[:], lhsT=w1[:, kc, f * 128:(f + 1) * 128], rhs=h2T[:, kc, :], start=(kc == 0), stop=(kc == 7),
                                    waits=[th2, pfree, tw1[(f * 128) // 1024], tw1[(f * 128 + 127) // 1024]] if kc == 0 else ())
                    psu, pfree, pku = pp.next()
                    tmu = None
                    for kc in range(8):
                        tmu = PE.op(T.matmul, psu[:], lhsT=w1[:, kc, DFF + f * 128:DFF + (f + 1) * 128], rhs=h2T[:, kc, :], start=(kc == 0), stop=(kc == 7),
                                    waits=[pfree, tw1[(DFF + f * 128) // 1024], tw1[(DFF + f * 128 + 127) // 1024]] if kc == 0 else ())
                    last_mm = tmu
                    sg, sfree, sk = sgr.next()
                    ta = ACT.op(A.activation, out=sg[:], in_=psg[:], func=AF.Silu, waits=[tmg, sfree])
                    pp.done(pkg, ta)
                    tb_ = DVE.op(V.tensor_tensor, out=actT[:, f, :], in0=psu[:], in1=sg[:], op=ALU.mult, waits=[tmu, ta, actT_free])
                    pp.done(pku, tb_)
                    sgr.done(sk, tb_)
                    tact.append(tb_)
                h2T_free = [last_mm]
                actT_free = []
                last_o = None
                for bi in range(4):
                    s = grp * 4 + bi
                    x2, x2free, x2k = x2r.next()
                    tx2 = SP.dma(x2[:], x1_scr[s * 128:(s + 1) * 128, :], waits=[x2free])
                    xtoks = []
                    for nc_ in range(2):
                        ps, pfree, pk = pp.next()
                        tm = None
                        for f in range(NFC):
                            tm = PE.op(T.matmul, ps[:], lhsT=actT[:, f, bi * 128:(bi + 1) * 128], rhs=w2[:, f, nc_ * 512:(nc_ + 1) * 512],
                                       start=(f == 0), stop=(f == NFC - 1), waits=[tact, pfree, tw2] if f == 0 else ())
                        last_o = tm
                        a = DVE.op(V.tensor_tensor, out=x2[:, nc_ * 512:(nc_ + 1) * 512], in0=ps[:], in1=x2[:, nc_ * 512:(nc_ + 1) * 512], op=ALU.add,
                                   waits=[tm, tx2])
                        pp.done(pk, a)
                        xtoks.append(a)
                    e = rms_norm_block((junk, ssr), x2[:], xtoks, gfin[:], t_gfin, 1e-6, D, x2[:], [])
                    td = SP.dma(out[s * 128:(s + 1) * 128, :], x2[:], waits=[e])
                    x2r.done(x2k, td)
                actT_free = [last_o]
            barrier()
    return nc


_NC_CACHE = {}


def _get_nc(debug=False):
    if debug not in _NC_CACHE:
        _NC_CACHE[debug] = build(debug)
    return _NC_CACHE[debug]


def make_in_maps(inputs):
    x = np.ascontiguousarray(np.asarray(inputs["x"], dtype=np.float32))
    pos = np.asarray(inputs["positions"]).astype(np.int32)
    in_maps = []
    kk = np.arange(512)[None, :]
    qq = np.arange(128)[:, None]
    for c in range(8):
        b, j = c // 4, c % 4
        blocks = [4 * s + j for s in range(NQB)]
        xqc = np.concatenate([x[b, q * 128:(q + 1) * 128] for q in blocks], axis=0)
        posk = np.ascontiguousarray(pos[b].reshape(NTB, 128).T)
        posq = np.ascontiguousarray(np.stack([pos[b, q * 128:(q + 1) * 128] for q in blocks], axis=1))
        cm = np.where(kk <= j * 128 + qq, 0.0, NEG).astype(np.float32)
        m = {"xs": x[b], "xq": np.ascontiguousarray(xqc), "posk": posk, "posq": posq, "cmask": cm}
        for name in ["norm_mix_g", "w_in", "idx_k_norm_g", "idx_k_norm_b", "diff_lambda_q1", "diff_lambda_k1",
                     "diff_lambda_q2", "diff_lambda_k2", "diff_subln_g", "gate_b", "w_branch_dsa", "w_branch_diff",
                     "w_out", "norm_ffn_g", "w_ffn_in", "w_ffn_out"]:
            m[name] = np.ascontiguousarray(np.asarray(inputs[name], dtype=np.float32)[0])
        m["norm_final_g"] = np.ascontiguousarray(np.asarray(inputs["norm_final_g"], dtype=np.float32))
        in_maps.append(m)
    return in_maps


def kernel(**inputs):
    nc = _get_nc(False)
    in_maps = make_in_maps(inputs)
    res = run_bass_kernel_spmd(nc, in_maps, core_ids=list(range(8)))
    outp = np.zeros((2, S, D), dtype=np.float32)
    for c in range(8):
        b, j = c // 4, c % 4
        o = res.results[c]["out"]
        for s in range(NQB):
            q = 4 * s + j
            outp[b, q * 128:(q + 1) * 128] = o[s * 128:(s + 1) * 128]
    return outp
```

```python
import os
import math
import numpy as np
from contextlib import ExitStack
import concourse.bass as bass
import concourse.mybir as mybir
from concourse.bass_utils import run_bass_kernel_spmd

F32 = mybir.dt.float32
F32R = mybir.dt.float32r
BF16 = mybir.dt.bfloat16
I32 = mybir.dt.int32
AF = mybir.ActivationFunctionType
ALU = mybir.AluOpType
AX = mybir.AxisListType

S = 8192
D = 1024
NTB = S // 128
NQB = 16
NQ = NQB * 128
DFF = 2816
NFC = DFF // 128
TOPK = 256
NBIS = 16
NEG = -30000.0
C_QA, C_KA, C_VA, C_QI, C_KI, C_WI, C_QB, C_KB, C_VB, C_G = 0, 512, 1024, 1536, 1792, 1824, 1832, 2344, 2856, 3368
D_IN = 5416
VW = 132
TWO_PI = 2.0 * math.pi


class Eng:
    def __init__(self, nc, es, eng, name):
        self.eng = eng
        self.name = name
        self.sem = es.enter_context(nc.semaphore("sem_" + name))
        self.count = 0
        self.seen = {}

    def wait(self, toks):
        for t in toks:
            if t is None:
                continue
            if isinstance(t, list):
                self.wait(t)
                continue
            sem, val, key = t
            if self.seen.get(key, 0) >= val:
                continue
            self.eng.wait_ge(sem, val)
            self.seen[key] = val

    def op(self, fn, *args, waits=(), **kw):
        self.wait(waits)
        inst = fn(*args, **kw)
        self.count += 1
        inst.then_inc(self.sem, 1)
        tok = (self.sem, self.count, self.name)
        self.seen[self.name] = self.count - 1 if False else self.seen.get(self.name, 0)
        return tok


class DmaQ:
    def __init__(self, nc, es, eng, name, nsem=12):
        self.eng = eng
        self.name = name
        self.sems = [es.enter_context(nc.semaphore(f"dsem_{name}_{i}")) for i in range(nsem)]
        self.vals = [0] * nsem
        self.i = 0
        self.seen = {}

    def wait(self, toks):
        for t in toks:
            if t is None:
                continue
            if isinstance(t, list):
                self.wait(t)
                continue
            sem, val, key = t
            if self.seen.get(key, 0) >= val:
                continue
            self.eng.wait_ge(sem, val)
            self.seen[key] = val

    def dma(self, out, in_, waits=(), **kw):
        k = self.i
        self.i = (self.i + 1) % len(self.sems)
        key = f"{self.name}_{k}"
        if self.vals[k] > 0:
            self.wait([(self.sems[k], self.vals[k], key)])
        self.wait(waits)
        self.vals[k] += 16
        self.eng.dma_start(out=out, in_=in_, **kw).then_inc(self.sems[k], 16)
        return (self.sems[k], self.vals[k], key)

    def all_toks(self):
        return [(self.sems[k], self.vals[k], f"{self.name}_{k}") for k in range(len(self.sems)) if self.vals[k] > 0]


class Ring:
    def __init__(self, tiles):
        self.tiles = tiles
        self.rd = [[] for _ in tiles]
        self.i = -1

    def next(self):
        self.i = (self.i + 1) % len(self.tiles)
        k = self.i
        toks = self.rd[k]
        self.rd[k] = []
        return self.tiles[k], toks, k

    def done(self, k, tok):
        self.rd[k].append(tok)


def build(debug=False):
    nc = bass.Bass("TRN2", target_bir_lowering=False)
    dt_in = lambda n, s, d=F32: nc.dram_tensor(n, s, d, kind="ExternalInput").ap()
    xs = dt_in("xs", [S, D])
    xq = dt_in("xq", [NQ, D])
    posk = dt_in("posk", [128, NTB], I32)
    posq = dt_in("posq", [128, NQB], I32)
    cmask = dt_in("cmask", [128, 512])
    norm_mix_g = dt_in("norm_mix_g", [D])
    w_in = dt_in("w_in", [D, D_IN])
    idx_g = dt_in("idx_k_norm_g", [32])
    idx_b = dt_in("idx_k_norm_b", [32])
    lq1 = dt_in("diff_lambda_q1", [64])
    lk1 = dt_in("diff_lambda_k1", [64])
    lq2 = dt_in("diff_lambda_q2", [64])
    lk2 = dt_in("diff_lambda_k2", [64])
    subln_g = dt_in("diff_subln_g", [128])
    gate_b = dt_in("gate_b", [2048])
    w_bd = dt_in("w_branch_dsa", [512, D])
    w_bf = dt_in("w_branch_diff", [512, D])
    w_out = dt_in("w_out", [D, D])
    norm_ffn_g = dt_in("norm_ffn_g", [D])
    w_f1 = dt_in("w_ffn_in", [D, 2 * DFF])
    w_f2 = dt_in("w_ffn_out", [DFF, D])
    norm_fin_g = dt_in("norm_final_g", [D])
    out = nc.dram_tensor("out", [NQ, D], F32, kind="ExternalOutput").ap()
    skind = "ExternalOutput" if debug else "Internal"
    kT_scr = nc.dram_tensor("kT_scr", [8, 128, S], BF16, kind=skind).ap()
    kiT_scr = nc.dram_tensor("kiT_scr", [32, S], BF16, kind=skind).ap()
    v_scr = nc.dram_tensor("v_scr", [S, 8 * VW], BF16, kind=skind).ap()
    qT_scr = nc.dram_tensor("qT_scr", [8, 128, NQ], BF16, kind=skind).ap()
    qiT_scr = nc.dram_tensor("qiT_scr", [64, 4, NQ], BF16, kind=skind).ap()
    o_scr = nc.dram_tensor("o_scr", [NQ, D], BF16, kind=skind).ap()
    x1_scr = nc.dram_tensor("x1_scr", [NQ, D], F32, kind=skind).ap()

    with ExitStack() as es:
        uid = [0]

        def sbt(st, n, s, d):
            uid[0] += 1
            return st.enter_context(nc.sbuf_tensor(f"{n}_{uid[0]}", s, d))

        def pst(st, n, s, d):
            uid[0] += 1
            return st.enter_context(nc.psum_tensor(f"{n}_{uid[0]}", s, d))

        ident = sbt(es, "ident", [128, 128], BF16)
        early = ExitStack()
        cosk = sbt(early, "cosk", [128, NTB, 8], F32)
        sink = sbt(early, "sink", [128, NTB, 8], F32)
        cosq = sbt(early, "cosq", [128, NQB, 8], F32)
        sinq = sbt(early, "sinq", [128, NQB, 8], F32)
        gmix = sbt(early, "gmix", [128, D], F32)
        wabs = sbt(early, "wabs", [128, NQB, 8], F32)
        wsgn = sbt(early, "wsgn", [128, NQB, 8], F32)
        nlam = sbt(early, "nlam", [128, 1], F32)
        small = sbt(early, "small", [128, 64], F32)
        cb = sbt(early, "cb", [128, 512], BF16)
        cbf = sbt(early, "cbf", [128, 512], F32)

        es.enter_context(nc.Block())
        PE = Eng(nc, es, nc.tensor, "pe")
        ACT = Eng(nc, es, nc.scalar, "act")
        DVE = Eng(nc, es, nc.vector, "dve")
        POOL = Eng(nc, es, nc.gpsimd, "pool")
        SP = DmaQ(nc, es, nc.sync, "sp", 16)
        GQ = DmaQ(nc, es, nc.gpsimd, "gq", 8)
        V = nc.vector
        A = nc.scalar
        T = nc.tensor

        def bcast(ap1d, n=128):
            return ap1d.partition_broadcast(n)

        def barrier():
            toks = [(e.sem, e.count, e.name) for e in (PE, ACT, DVE, POOL) if e.count > 0]
            toks += SP.all_toks() + GQ.all_toks()
            for e in (PE, ACT, DVE, POOL, SP):
                e.wait(toks)

        t = POOL.op(nc.gpsimd.memset, ident[:], 1.0)
        t_ident = POOL.op(nc.gpsimd.affine_select, out=ident[:], in_=ident[:], pattern=[[-1, 128]],
                          compare_op=ALU.is_equal, fill=0.0, base=0, channel_multiplier=1, waits=[t])
        t_gmix = SP.dma(gmix[:], bcast(norm_mix_g))
        t_cbf = SP.dma(cbf[:], cmask)
        t_cb = DVE.op(V.tensor_copy, out=cb[:], in_=cbf[:], waits=[t_cbf])

        with ExitStack() as p0:
            lt = sbt(p0, "lt", [128, 4, 64], F32)
            lj = sbt(p0, "lj", [128, 64], F32)
            tl = [SP.dma(lt[:, i, :], bcast(a)) for i, a in enumerate([lq1, lk1, lq2, lk2])]
            t1 = DVE.op(V.tensor_tensor, out=lj[:], in0=lt[:, 0, :], in1=lt[:, 1, :], op=ALU.mult, waits=tl)
            t1 = DVE.op(V.tensor_reduce, out=small[:, 0:1], in_=lj[:], axis=AX.X, op=ALU.add, waits=[t1])
            t2 = DVE.op(V.tensor_tensor, out=lj[:], in0=lt[:, 2, :], in1=lt[:, 3, :], op=ALU.mult, waits=[t1])
            t2 = DVE.op(V.tensor_reduce, out=small[:, 1:2], in_=lj[:], axis=AX.X, op=ALU.add, waits=[t2])
            t3 = ACT.op(A.activation, out=small[:, 2:4], in_=small[:, 0:2], func=AF.Exp, waits=[t2])
            t_nlam = DVE.op(V.scalar_tensor_tensor, out=nlam[:], in0=small[:, 3:4], scalar=-0.2, in1=small[:, 2:3],
                            op0=ALU.add, op1=ALU.subtract, waits=[t3])

            invf = sbt(p0, "invf", [128, 8], F32)
            tinv = None
            for i in range(8):
                fv = float(np.power(np.float32(500000.0), -np.float32(2 * i) / np.float32(16)))
                tinv = DVE.op(V.memset, invf[:, i:i + 1], fv)

            def rope_table(pos_ap, n, cos_t, sin_t, nm):
                pi_ = sbt(p0, "pi_" + nm, [128, n], I32)
                pf = sbt(p0, "pf_" + nm, [128, n], F32)
                ang = sbt(p0, "ang_" + nm, [128, n, 8], F32)
                yy = sbt(p0, "yy_" + nm, [128, n, 8], F32)
                ni = sbt(p0, "ni_" + nm, [128, n, 8], I32)
                tp = SP.dma(pi_[:], pos_ap)
                a = DVE.op(V.tensor_copy, out=pf[:], in_=pi_[:], waits=[tp])
                a = DVE.op(V.tensor_tensor, out=ang[:], in0=pf[:].unsqueeze(2).to_broadcast([128, n, 8]),
                           in1=invf[:].unsqueeze(1).to_broadcast([128, n, 8]), op=ALU.mult, waits=[a, tinv])

                def reduce_sin(src_add, dst):
                    b = DVE.op(V.tensor_scalar, out=yy[:], in0=ang[:], scalar1=src_add, scalar2=1.0 / TWO_PI,
                               op0=ALU.add, op1=ALU.mult, waits=[a])
                    b = DVE.op(V.tensor_copy, out=ni[:], in_=yy[:], waits=[b])
                    b = DVE.op(V.tensor_copy, out=yy[:], in_=ni[:], waits=[b])
                    c1 = 6.28125
                    c2 = TWO_PI - 6.28125
                    b = DVE.op(V.scalar_tensor_tensor, out=dst, in0=yy[:], scalar=-c1, in1=ang[:], op0=ALU.mult, op1=ALU.add, waits=[b])
                    b = DVE.op(V.scalar_tensor_tensor, out=dst, in0=yy[:], scalar=-c2, in1=dst, op0=ALU.mult, op1=ALU.add, waits=[b])
                    b = DVE.op(V.tensor_scalar, out=dst, in0=dst, scalar1=src_add, scalar2=3.1415925, op0=ALU.add, op1=ALU.min, waits=[b])
                    b = DVE.op(V.tensor_scalar, out=dst, in0=dst, scalar1=-3.1415925, scalar2=None, op0=ALU.max, waits=[b])
                    return ACT.op(A.activation, out=dst, in_=dst, func=AF.Sin, waits=[b])
                ts = reduce_sin(0.0, sin_t[:])
                tc = reduce_sin(math.pi / 2.0, cos_t[:])
                return [ts, tc]
            t_ropek = rope_table(posk, NTB, cosk, sink, "k")
            t_ropeq = rope_table(posq, NQB, cosq, sinq, "q")
            barrier()

        def rms_norm_block(st_rings, x_tile, tx, g_tile, tg, eps, n_feat, hb_tile, hb_free):
            junk, ssr = st_rings
            jt, jfree, jk = junk.next()
            col, cfree, ck = ssr.next()
            a = ACT.op(A.activation, out=jt[:, :n_feat], in_=x_tile, func=AF.Square, accum_out=col[:, 0:1],
                       waits=[tx, jfree, cfree])
            junk.done(jk, a)
            b = DVE.op(V.tensor_scalar, out=col[:, 1:2], in0=col[:, 0:1], scalar1=1.0 / n_feat, scalar2=eps,
                       op0=ALU.mult, op1=ALU.add, waits=[a])
            c = ACT.op(A.activation, out=col[:, 2:3], in_=col[:, 1:2], func=AF.Sqrt, waits=[b])
            d = DVE.op(V.reciprocal, out=col[:, 3:4], in_=col[:, 2:3], waits=[c])
            e = DVE.op(V.scalar_tensor_tensor, out=hb_tile, in0=x_tile, scalar=col[:, 3:4], in1=g_tile,
                       op0=ALU.mult, op1=ALU.mult, waits=[d, tg, hb_free])
            ssr.done(ck, e)
            return e

        rope_last = []

        def rope_apply(tile3, H, half, cs, sn, tmp, waits):
            x1 = tile3[:, :, 0:half]
            x2 = tile3[:, :, half:2 * half]
            cB = cs.unsqueeze(1).to_broadcast([128, H, half])
            sB = sn.unsqueeze(1).to_broadcast([128, H, half])
            tv = lambda i: tmp[:, i, 0:H * half].rearrange("p (h d) -> p h d", h=H)
            waits = list(waits) + rope_last
            a1 = DVE.op(V.tensor_tensor, out=tv(0), in0=x1, in1=cB, op=ALU.mult, waits=waits)
            a2 = DVE.op(V.tensor_tensor, out=tv(1), in0=x2, in1=sB, op=ALU.mult, waits=waits)
            a3 = DVE.op(V.tensor_tensor, out=tv(2), in0=x2, in1=cB, op=ALU.mult, waits=waits)
            a4 = DVE.op(V.tensor_tensor, out=tv(3), in0=x1, in1=sB, op=ALU.mult, waits=waits)
            b1 = DVE.op(V.tensor_tensor, out=x1, in0=tv(0), in1=tv(1), op=ALU.subtract, waits=[a1, a2, a3, a4])
            b2 = DVE.op(V.tensor_tensor, out=x2, in0=tv(2), in1=tv(3), op=ALU.add, waits=[a1, a2, a3, a4])
            rope_last[:] = [b1, b2]
            return [b1, b2]

        with ExitStack() as p1:
            NKV = 2080
            NQC = 1288
            wkv = sbt(p1, "wkv", [128, 8, NKV], BF16)
            wq = sbt(p1, "wq", [128, 8, NQC], BF16)
            w_in_v = w_in.rearrange("(kc p) n -> p kc n", p=128)
            tw = []
            for (dst0, c0, n) in [(0, C_KA, 512), (512, C_KB, 512), (1024, C_VA, 512), (1536, C_VB, 512), (2048, C_KI, 32)]:
                tw.append(GQ.dma(wkv[:, :, dst0:dst0 + n], w_in_v[:, :, c0:c0 + n]))
            twq = []
            for (dst0, c0, n) in [(0, C_QA, 512), (512, C_QB, 512), (1024, C_QI, 256), (1280, C_WI, 8)]:
                twq.append(GQ.dma(wq[:, :, dst0:dst0 + n], w_in_v[:, :, c0:c0 + n]))
            lng = sbt(p1, "lng", [128, 32], F32)
            lnb = sbt(p1, "lnb", [128, 32], F32)
            t_lng = SP.dma(lng[:], bcast(idx_g))
            t_lnb = SP.dma(lnb[:], bcast(idx_b))

            xr = Ring([sbt(p1, f"xt{i}", [128, D], F32) for i in range(3)])
            junk = Ring([sbt(p1, f"junk{i}", [128, D], BF16) for i in range(1)])
            ssr = Ring([sbt(p1, f"ss{i}", [128, 4], F32) for i in range(4)])
            hbr = Ring([sbt(p1, f"hb{i}", [128, D], BF16) for i in range(3)])
            hTr = Ring([sbt(p1, f"hT{i}", [128, 8, 128], BF16) for i in range(3)])
            kfr = Ring([sbt(p1, f"kf{i}", [128, 1024], F32) for i in range(2)])
            kbr = Ring([sbt(p1, f"kb{i}", [128, 1024], BF16) for i in range(3)])
            kTr = Ring([sbt(p1, f"kTt{i}", [128, 8, 128], BF16) for i in range(2)])
            vtr = Ring([sbt(p1, f"vt{i}", [128, 8, VW], BF16) for i in range(2)])
            rtmp = sbt(p1, "rtmp", [128, 4, 128], F32)
            kif = sbt(p1, "kif", [128, 8], F32)
            kic = sbt(p1, "kic", [128, 32], F32)
            kij = sbt(p1, "kij", [128, 32], F32)
            kibr = Ring([sbt(p1, f"kib{i}", [128, 32], BF16) for i in range(3)])
            kiTr = Ring([sbt(p1, f"kiT{i}", [32, 128], BF16) for i in range(2)])
            qwr = Ring([sbt(p1, f"qw{i}", [128, 264], F32) for i in range(2)])
            qibr = Ring([sbt(p1, f"qib{i}", [128, 256], BF16) for i in range(3)])
            qiTr = Ring([sbt(p1, f"qiT{i}", [32, 8, 128], BF16) for i in range(2)])
            psT = Ring([pst(p1, f"psT{i}", [128, D], BF16) for i in range(2)])
            pp = Ring([pst(p1, f"pp{i}", [128, 512], F32) for i in range(4)])
            pkT = Ring([pst(p1, f"pkT{i}", [128, D], BF16) for i in range(2)])
            t_vinit = []
            for vt_ in vtr.tiles:
                t0 = POOL.op(nc.gpsimd.memset, vt_[:], 0.0)
                t1_ = POOL.op(nc.gpsimd.memset, vt_[:, 0:4, :].rearrange("p a (s e) -> p (a s) e", s=2)[:, :, 64:65], 1.0, waits=[t0])
                t_vinit.append(POOL.op(nc.gpsimd.memset, vt_[:, 4:8, 128:129], 1.0, waits=[t0, t1_]))

            def aevac(out_ap, in_ap, waits):
                return ACT.op(A.copy, out=out_ap, in_=in_ap, waits=waits)

            def stageA1(src_rows):
                xt, xfree, xk = xr.next()
                tx = SP.dma(xt[:], src_rows, waits=xfree)
                hb, hfree, hk = hbr.next()
                th = rms_norm_block((junk, ssr), xt[:], tx, gmix[:], t_gmix, 1e-6, D, hb[:], hfree)
                xr.done(xk, th)
                return hb, th, hk

            def stageA2(hb, th, hk):
                ps, pfree, pk = psT.next()
                tt = None
                for kc in range(8):
                    tt = PE.op(T.transpose, ps[:, kc * 128:(kc + 1) * 128], hb[:, kc * 128:(kc + 1) * 128], ident[:],
                               waits=[th, t_ident, pfree] if kc == 0 else ())
                hbr.done(hk, tt)
                hT, tfree, tk = hTr.next()
                te = aevac(hT[:].rearrange("p a b -> p (a b)"), ps[:], [tt, tfree])
                psT.done(pk, te)
                return hT, te, tk

            def project(hT, th, w_tile, c0, n, wtoks):
                ps, pfree, pk = pp.next()
                tt = None
                for kc in range(8):
                    tt = PE.op(T.matmul, ps[:, 0:n], lhsT=hT[:, kc, :], rhs=w_tile[:, kc, c0:c0 + n],
                               start=(kc == 0), stop=(kc == 7), waits=[th, pfree, wtoks] if kc == 0 else ())
                return ps, tt, pk

            def transpose_out(src_bf, tsrc, nchunk, width, ring_sb, dst_dram):
                ps, pfree, pk = pkT.next()
                tt = None
                for c in range(nchunk):
                    tt = PE.op(T.transpose, ps[0:width, c * 128:(c + 1) * 128], src_bf[:, c * width:(c + 1) * width], ident[:],
                               waits=[tsrc, pfree, t_ident] if c == 0 else ())
                sbT, sfree, sk = ring_sb.next()
                te = aevac(sbT[:].rearrange("p a b -> p (a b)") if len(sbT.shape) == 3 else sbT[:],
                           ps[0:width, 0:nchunk * 128], [tt, sfree])
                pkT.done(pk, te)
                td = GQ.dma(dst_dram, sbT[:], waits=[te])
                ring_sb.done(sk, td)
                return tt

            def transpose_out_qi(src_bf, tsrc, s):
                ps, pfree, pk = pkT.next()
                tt = None
                for c in range(8):
                    tt = PE.op(T.transpose, ps[0:32, c * 128:(c + 1) * 128], src_bf[:, c * 32:(c + 1) * 32], ident[:],
                               waits=[tsrc, pfree, t_ident] if c == 0 else ())
                sbT, sfree, sk = qiTr.next()
                te = aevac(sbT[:].rearrange("p a b -> p (a b)"), ps[0:32, 0:1024], [tt, sfree])
                pkT.done(pk, te)
                for g in range(2):
                    td = GQ.dma(qiT_scr[g * 32:(g + 1) * 32, :, s * 128:(s + 1) * 128], sbT[:, g::2, :], waits=[te])
                    qiTr.done(sk, td)
                return tt

            def qk_pair(hT, th, w_tile, wtoks, cs, sn, scale, dst_dram):
                kf, kfree, kk = kfr.next()
                tes = []
                for gi in range(2):
                    ps, tmm, pk = project(hT, th, w_tile, gi * 512, 512, wtoks)
                    te = aevac(kf[:, gi * 512:(gi + 1) * 512], ps[:], [tmm, kfree])
                    pp.done(pk, te)
                    tes.append(te)
                tr = rope_apply(kf[:].rearrange("p (h d) -> p h d", h=16), 16, 8, cs, sn, rtmp, tes)
                kb, bfree, bk = kbr.next()
                if scale == 1.0:
                    tcst = DVE.op(V.tensor_copy, out=kb[:], in_=kf[:], waits=[tr, bfree])
                else:
                    tcst = DVE.op(V.tensor_scalar, out=kb[:], in0=kf[:], scalar1=scale, scalar2=None, op0=ALU.mult, waits=[tr, bfree])
                kfr.done(kk, tcst)
                return kb, bk, tcst

            def stageB_kv(tb, hT, th, hk):
                cs = cosk[:, tb, :]
                sn = sink[:, tb, :]
                kb, bk, tcst = qk_pair(hT, th, wkv, tw, cs, sn, 1.0, None)
                vt, vfree, vk = vtr.next()
                tvs = []
                for gi in (2, 3):
                    ps, tmm, pk = project(hT, th, wkv, gi * 512, 512, tw)
                    if gi == 2:
                        dstv = vt[:, 0:4, :].rearrange("p a (s e) -> p (a s) e", s=2)[:, :, 0:64]
                        te = aevac(dstv, ps[:].rearrange("p (h e) -> p h e", e=64), [tmm, vfree, t_vinit])
                    else:
                        te = aevac(vt[:, 4:8, 0:128], ps[:].rearrange("p (h e) -> p h e", e=128), [tmm, vfree, t_vinit])
                    pp.done(pk, te)
                    tvs.append(te)
                td = GQ.dma(v_scr[tb * 128:(tb + 1) * 128, :], vt[:].rearrange("p a b -> p (a b)"), waits=tvs)
                vtr.done(vk, td)
                ps, tmm, pk = project(hT, th, wkv, 2048, 32, tw)
                hTr.done(hk, tmm)
                a = DVE.op(V.tensor_reduce, out=kif[:, 0:1], in_=ps[:, 0:32], axis=AX.X, op=ALU.add, waits=[tmm])
                a = DVE.op(V.tensor_scalar, out=kif[:, 1:2], in0=kif[:, 0:1], scalar1=1.0 / 32, scalar2=None, op0=ALU.mult, waits=[a])
                a = DVE.op(V.tensor_scalar, out=kic[:], in0=ps[:, 0:32], scalar1=kif[:, 1:2], scalar2=None, op0=ALU.subtract, waits=[a])
                pp.done(pk, a)
                b = ACT.op(A.activation, out=kij[:], in_=kic[:], func=AF.Square, accum_out=kif[:, 2:3], waits=[a])
                b = DVE.op(V.tensor_scalar, out=kif[:, 3:4], in0=kif[:, 2:3], scalar1=1.0 / 32, scalar2=1e-6, op0=ALU.mult, op1=ALU.add, waits=[b])
                b = ACT.op(A.activation, out=kif[:, 4:5], in_=kif[:, 3:4], func=AF.Sqrt, waits=[b])
                b = DVE.op(V.reciprocal, out=kif[:, 5:6], in_=kif[:, 4:5], waits=[b])
                b = DVE.op(V.scalar_tensor_tensor, out=kic[:], in0=kic[:], scalar=kif[:, 5:6], in1=lng[:], op0=ALU.mult, op1=ALU.mult, waits=[b, t_lng])
                b = DVE.op(V.tensor_tensor, out=kic[:], in0=kic[:], in1=lnb[:], op=ALU.add, waits=[b, t_lnb])
                csi = cosk[:, tb, :].rearrange("p (a two) -> p a two", two=2)[:, :, 0]
                sni = sink[:, tb, :].rearrange("p (a two) -> p a two", two=2)[:, :, 0]
                tr = rope_apply(kic[:].rearrange("p (h d) -> p h d", h=1), 1, 4, csi, sni, rtmp, [b])
                kib, bfree, bk2 = kibr.next()
                tc2 = DVE.op(V.tensor_copy, out=kib[:], in_=kic[:], waits=[tr, bfree])

                def b2():
                    tlast = transpose_out(kb, tcst, 8, 128, kTr, kT_scr[:, :, tb * 128:(tb + 1) * 128].rearrange("c f t -> f c t"))
                    kbr.done(bk, tlast)
                    tl2 = transpose_out(kib, tc2, 1, 32, kiTr, kiT_scr[:, tb * 128:(tb + 1) * 128])
                    kibr.done(bk2, tl2)
                return b2

            def stageB_q(s, hT, th, hk):
                cs = cosq[:, s, :]
                sn = sinq[:, s, :]
                kb, bk, tcst = qk_pair(hT, th, wq, twq, cs, sn, 0.125, None)
                ps, tmm, pk = project(hT, th, wq, 1024, 264, twq)
                hTr.done(hk, tmm)
                qw, qfree, qk = qwr.next()
                te = aevac(qw[:], ps[:, 0:264], [tmm, qfree])
                pp.done(pk, te)
                csi = cosq[:, s, :].rearrange("p (a two) -> p a two", two=2)[:, :, 0]
                sni = sinq[:, s, :].rearrange("p (a two) -> p a two", two=2)[:, :, 0]
                tr = rope_apply(qw[:, 0:256].rearrange("p (h d) -> p h d", h=8), 8, 4, csi, sni, rtmp, [te])
                a1 = DVE.op(V.tensor_scalar, out=wabs[:, s, :], in0=qw[:, 256:264], scalar1=1.0 / 16, scalar2=None, op0=ALU.mult, waits=[te])
                a2 = a1
                qib, bfree, bk2 = qibr.next()
                tc2 = DVE.op(V.tensor_copy, out=qib[:], in_=qw[:, 0:256], waits=[tr, bfree])
                qwr.done(qk, [tc2, a1, a2])

                def b2():
                    tlast = transpose_out(kb, tcst, 8, 128, kTr, qT_scr[:, :, s * 128:(s + 1) * 128].rearrange("c f t -> f c t"))
                    kbr.done(bk, tlast)
                    tl2 = transpose_out_qi(qib, tc2, s)
                    qibr.done(bk2, tl2)
                return b2

            items = [("kv", tb, xs[tb * 128:(tb + 1) * 128, :]) for tb in range(NTB)] + \
                    [("q", s, xq[s * 128:(s + 1) * 128, :]) for s in range(NQB)]
            n_it = len(items)
            a1 = {}
            a2 = {}
            a1[0] = stageA1(items[0][2])
            a2[0] = stageA2(*a1[0])
            if n_it > 1:
                a1[1] = stageA1(items[1][2])
            prev_b2 = None
            for i in range(n_it):
                if i + 2 < n_it:
                    a1[i + 2] = stageA1(items[i + 2][2])
                kind, idx, _ = items[i]
                hT, th, hk = a2.pop(i)
                if kind == "kv":
                    b2 = stageB_kv(idx, hT, th, hk)
                else:
                    b2 = stageB_q(idx, hT, th, hk)
                if i + 1 < n_it:
                    a2[i + 1] = stageA2(*a1.pop(i + 1))
                if prev_b2 is not None:
                    prev_b2()
                prev_b2 = b2
            if prev_b2 is not None:
                prev_b2()
            barrier()

        with ExitStack() as p2:
            kiT = sbt(p2, "kiT", [64, S], BF16)
            t_kiT = [SP.dma(kiT[g * 32:(g + 1) * 32, :], kiT_scr) for g in range(2)]
            gsub = sbt(p2, "gsub", [128, 128], F32)
            t_gs = SP.dma(gsub[:], bcast(subln_g))
            t_gs = DVE.op(V.tensor_scalar, out=gsub[:], in0=gsub[:], scalar1=0.8, scalar2=None, op0=ALU.mult, waits=[t_gs])
            ident2 = sbt(p2, "ident2", [128, 2, 128], BF16)
            t_id2 = [DVE.op(V.tensor_copy, out=ident2[:, i, :], in_=ident[:], waits=[t_ident]) for i in range(2)]
            Kr = Ring([sbt(p2, f"Kb{i}", [128, S], BF16) for i in range(2)])
            Vr = Ring([sbt(p2, f"Vb{i}", [128, NTB, VW], BF16) for i in range(2)])
            Mb = [sbt(p2, f"Mb{i}", [128, S], BF16) for i in range(2)]
            Isc = sbt(p2, "Isc", [128, S], F32)
            qbd_tiles = [sbt(p2, f"qbd{i}", [128, 2, 128], BF16) for i in range(3)]
            t_qz = [POOL.op(nc.gpsimd.memset, q_[:], 0.0) for q_ in qbd_tiles]
            qbr = Ring(qbd_tiles)
            qiT = [sbt(p2, f"qiTs{i}", [64, 4, 128], BF16) for i in range(2)]
            rl = Ring([sbt(p2, f"rl{i}", [128, 512], BF16) for i in range(8)])
            identf = sbt(p2, "identf", [128, 128], F32)
            t_idf = DVE.op(V.tensor_copy, out=identf[:], in_=ident[:], waits=[t_ident])
            dgb = [sbt(p2, f"dgb{i}", [128, 8, 128], BF16) for i in range(2)]
            etr = Ring([sbt(p2, f"et{i}", [128, 512], BF16) for i in range(4)])
            otr = Ring([sbt(p2, f"ot{i}", [128, D], BF16) for i in range(2)])
            of32 = sbt(p2, "of32", [128, 128], F32)
            oj = sbt(p2, "oj", [128, 128], F32)
            bs = sbt(p2, "bs", [128, 16], F32)
            es_ = sbt(p2, "es_", [128, 16], F32)
            pss = Ring([pst(p2, f"pss{i}", [128, 512], F32) for i in range(4)])
            pI = pst(p2, "pI", [128, 512], F32)
            pacc = Ring([pst(p2, f"pacc{i}", [128, 512], F32) for i in range(3)])
            ofr = Ring([(sbt(p2, f"of1_{i}", [128, 128], F32), sbt(p2, f"of2_{i}", [128, 128], F32)) for i in range(4)])
            mhalf = sbt(p2, "mhalf", [128, 1], F32)
            t_mh = POOL.op(nc.gpsimd.memset, mhalf[:], -0.5)
            ez = sbt(p2, "ez", [128, 8], F32)
            Mb_ready = [None, None]
            Mb_readers = [[], []]
            qiT_readers = [[], []]
            dg_readers = [[], []]
            Isc_free = [[]]
            pI_free = [[]]

            def prep(s):
                li = s % 2
                nk = (4 * s + 4) * 128
                nch = nk // 512
                wb = 0.76 * (s + 1)
                tq = SP.dma(qiT[li][:], qiT_scr[:, :, s * 128:(s + 1) * 128], waits=qiT_readers[li])
                qiT_readers[li] = []
                tdg = None
                for h in range(8):
                    tdg = ACT.op(A.activation, out=dgb[li][:, h, :], in_=identf[:], func=AF.Copy, scale=wabs[:, s, h:h + 1],
                                 waits=[t_idf, dg_readers[li]] if h == 0 else ())
                dg_readers[li] = []
                yield 1.0
                tI = None
                pending = None

                def flush(pend):
                    (pc, pr, items) = pend
                    tacc = None
                    for g, (prt, pta, prk) in enumerate(items):
                        h = 2 * pr + g
                        tacc = PE.op(T.matmul, pI[:], lhsT=dgb[li][:, h, :], rhs=prt[:],
                                     start=(pr == 0 and g == 0), stop=(pr == 3 and g == 1),
                                     waits=[pta, tdg, pI_free[0]])
                        rl.done(prk, tacc)
                    return tacc
                pendq = []

                def drain(keep):
                    nonlocal tI
                    tacc = None
                    while len(pendq) > keep:
                        pend = pendq.pop(0)
                        tacc = flush(pend)
                        if pend[1] == 3:
                            pc = pend[0]
                            tI = ACT.op(A.copy, out=Isc[:, pc * 512:(pc + 1) * 512], in_=pI[:], waits=[tacc, Isc_free[0]])
                            pI_free[0] = [tI]
                    return tacc
                for c in range(nch):
                    for r in range(4):
                        drain(1)
                        mm = []
                        for g in range(2):
                            ps, pfree, pk = pss.next()
                            tm = PE.op(T.matmul, ps[:], lhsT=qiT[li][g * 32:(g + 1) * 32, r, :], rhs=kiT[g * 32:(g + 1) * 32, c * 512:(c + 1) * 512],
                                       start=True, stop=True, waits=[tq, t_kiT, pfree])
                            mm.append((ps, pk, tm))
                        items = []
                        for g in range(2):
                            ps, pk, tm = mm[g]
                            rt, rfree, rk = rl.next()
                            if True:
                                ta = ACT.op(A.activation, out=rt[:], in_=ps[:], func=AF.Relu, waits=[tm, rfree])
                            else:
                                ta = DVE.op(V.tensor_scalar, out=rt[:], in0=ps[:], scalar1=0.0, scalar2=None, op0=ALU.max, waits=[tm, rfree])
                            pss.done(pk, ta)
                            items.append((rt, ta, rk))
                        pendq.append((c, r, items))
                        yield 2.0
                tacc = drain(0)
                qiT_readers[li].append(tacc)
                dg_readers[li].append(tacc)
                Isc_free[0] = []
                Iv = Isc[:, 0:nk]
                a = DVE.op(V.tensor_reduce, out=bs[:, 0:1], in_=Iv, axis=AX.X, op=ALU.max, waits=[tI])
                yield wb
                a = DVE.op(V.tensor_reduce, out=bs[:, 1:2], in_=Iv, axis=AX.X, op=ALU.min, waits=[a])
                a = DVE.op(V.scalar_tensor_tensor, out=bs[:, 2:3], in0=bs[:, 0:1], scalar=1.0, in1=bs[:, 1:2], op0=ALU.add, op1=ALU.subtract, waits=[a])
                a = DVE.op(V.tensor_tensor, out=Isc[:, nk - 512:nk], in0=Isc[:, nk - 512:nk], in1=cbf[:], op=ALU.add, waits=[a, t_cbf])
                yield wb
                lo = bs[:, 1:2]
                w0 = bs[:, 2:3]
                mid = bs[:, 3:4]
                cnt = bs[:, 4:5]
                gg = bs[:, 5:6]
                for it in range(1, NBIS + 1):
                    sc = 2.0 ** (-it)
                    a = DVE.op(V.tensor_scalar, out=mid, in0=w0, scalar1=sc, scalar2=lo, op0=ALU.mult, op1=ALU.add, waits=[a])
                    a = DVE.op(V.tensor_scalar, out=Mb[li][:, 0:nk], in0=Iv, scalar1=mid, scalar2=None, op0=ALU.is_ge, op1=ALU.add,
                               accum_out=cnt, waits=[a, Mb_readers[li]])
                    Mb_readers[li] = []
                    a = DVE.op(V.tensor_scalar, out=gg, in0=cnt, scalar1=TOPK - 0.5, scalar2=sc, op0=ALU.is_ge, op1=ALU.mult, waits=[a])
                    a = DVE.op(V.scalar_tensor_tensor, out=lo, in0=w0, scalar=gg, in1=lo, op0=ALU.mult, op1=ALU.add, waits=[a])
                    yield wb
                a = DVE.op(V.tensor_scalar, out=Mb[li][:, 0:nk], in0=Iv, scalar1=lo, scalar2=NEG, op0=ALU.is_lt, op1=ALU.mult, waits=[a])
                Mb_ready[li] = a
                Isc_free[0] = [a]

            def prep_steps(s):
                return 1 + 8 * (s + 1) + (2 + NBIS) * 0.76 * (s + 1)

            pump_state = {"gen": None, "budget": 0.0, "rate": 0.0}

            def pump():
                st = pump_state
                if st["gen"] is None:
                    return
                st["budget"] += st["rate"]
                while st["budget"] > 0.0 and st["gen"] is not None:
                    try:
                        st["budget"] -= next(st["gen"])
                    except StopIteration:
                        st["gen"] = None

            def attention(s, p, ot, ofree, owr):
                li = s % 2
                is_dsa = p < 4
                nkb = 4 * s + 4
                kmax = nkb * 128
                Kb, kfree, kk = Kr.next()
                Vb, vfree, vk = Vr.next()
                tK = SP.dma(Kb[:, 0:kmax], kT_scr[p, :, 0:kmax], waits=kfree)
                tV = []
                for b0 in range(0, nkb, 8):
                    nb = min(8, nkb - b0)
                    tV.append(SP.dma(Vb[:, b0:b0 + nb, :], v_scr[b0 * 128:(b0 + nb) * 128, p * VW:(p + 1) * VW].rearrange("(b t) w -> t b w", t=128), waits=vfree))
                qbd, qfree, qk = qbr.next()
                tq = [SP.dma(qbd[m * 64:(m + 1) * 64, m, :], qT_scr[p, m * 64:(m + 1) * 64, s * 128:(s + 1) * 128], waits=[qfree, t_qz]) for m in range(2)]
                accs = [pacc.next() for m in range(2)]
                vw = 66 if is_dsa else 130
                ntile = nkb // 2
                pend = {}

                def do_qk(ti):
                    ps, pfree, pk = pss.next()
                    tm = None
                    for bi in range(2):
                        kb_ = ti * 2 + bi
                        need_mask = is_dsa or (kb_ >= nkb - 4)
                        tm = PE.op(T.matmul, ps[:, bi * 256:(bi + 1) * 256], lhsT=Kb[:, kb_ * 128:(kb_ + 1) * 128],
                                   rhs=qbd[:].rearrange("p a b -> p (a b)"), start=True, stop=not need_mask,
                                   waits=[tK, tq, pfree] if bi == 0 else ())
                        if need_mask:
                            if is_dsa:
                                ml = Mb[li][:, kb_ * 128:(kb_ + 1) * 128]
                                mw = [Mb_ready[li]]
                            else:
                                cbi = kb_ - (nkb - 4)
                                ml = cb[:, cbi * 128:(cbi + 1) * 128]
                                mw = [t_cb]
                            tm = PE.op(T.matmul, ps[:, bi * 256:(bi + 1) * 256], lhsT=ml, rhs=ident2[:].rearrange("p a b -> p (a b)"),
                                       start=False, stop=True, waits=mw + [t_id2])
                    et, efree, ek = etr.next()
                    te = ACT.op(A.activation, out=et[:], in_=ps[:], func=AF.Exp, waits=[tm, efree])
                    pss.done(pk, te)
                    pend[ti] = (et, te, ek)

                tav = [None, None]

                def do_av(ti):
                    et, te, ek = pend.pop(ti)
                    tm = None
                    first = True
                    for bi in range(2):
                        kb_ = ti * 2 + bi
                        for m in range(2):
                            acc, afree, ak = accs[m]
                            rhs = Vb[:, kb_, m * 66:(m + 1) * 66] if is_dsa else Vb[:, kb_, 0:130]
                            tm = PE.op(T.matmul, acc[:, 0:vw], lhsT=et[:, bi * 256 + m * 128:bi * 256 + (m + 1) * 128], rhs=rhs,
                                       start=(kb_ == 0), stop=(kb_ == nkb - 1),
                                       waits=[te, tV, accs[0][1], accs[1][1]] if first else ())
                            first = False
                            tav[m] = tm
                    etr.done(ek, tm)
                LAG = 2
                for ti in range(ntile):
                    do_qk(ti)
                    if ti >= LAG:
                        do_av(ti - LAG)
                    pump()
                for ti in range(max(0, ntile - LAG), ntile):
                    do_av(ti)
                qbr.done(qk, tav[1])
                if is_dsa:
                    Mb_readers[li].append(tav[1])
                Kr.done(kk, tav[1])
                Vr.done(vk, tav[1])
                if is_dsa:
                    for m in range(2):
                        acc, afree, ak = accs[m]
                        a = ACT.op(A.activation, out=ez[:, m:m + 1], in_=acc[:, 64:65], func=AF.Ln, waits=[tav[1]])
                        a = ACT.op(A.activation, out=ez[:, m:m + 1], in_=ez[:, m:m + 1], func=AF.Exp, scale=-1.0, waits=[a])
                        a = ACT.op(A.activation, out=ot[:, p * 128 + m * 64:p * 128 + (m + 1) * 64], in_=acc[:, 0:64], func=AF.Copy,
                                   scale=ez[:, m:m + 1], waits=[a, ofree])
                        pacc.done(ak, a)
                        owr.append(a)
                else:
                    h = p - 4
                    acc1, _, ak1 = accs[0]
                    acc2, _, ak2 = accs[1]
                    (of1, of2), offree, ofk = ofr.next()
                    a = ACT.op(A.activation, out=ez[:, 2:3], in_=acc1[:, 128:129], func=AF.Ln, waits=[tav[1], offree])
                    a = ACT.op(A.activation, out=ez[:, 2:3], in_=ez[:, 2:3], func=AF.Exp, scale=-1.0, waits=[a])
                    a = ACT.op(A.activation, out=of1[:], in_=acc1[:, 0:128], func=AF.Copy, scale=ez[:, 2:3], waits=[a])
                    pacc.done(ak1, a)
                    b = ACT.op(A.activation, out=ez[:, 3:4], in_=acc2[:, 128:129], func=AF.Ln, waits=[a])
                    b = ACT.op(A.activation, out=ez[:, 3:4], in_=ez[:, 3:4], func=AF.Exp, scale=-1.0, waits=[b])
                    b = ACT.op(A.activation, out=of2[:], in_=acc2[:, 0:128], func=AF.Copy, scale=ez[:, 3:4], waits=[b])
                    pacc.done(ak2, b)
                    b = DVE.op(V.scalar_tensor_tensor, out=of32[:], in0=of2[:], scalar=nlam[:, 0:1], in1=of1[:],
                               op0=ALU.mult, op1=ALU.add, waits=[a, b, t_nlam, of32_free[0]])
                    ofr.done(ofk, b)
                    c = DVE.op(V.scalar_tensor_tensor, out=oj[:], in0=of32[:], scalar=1.0, in1=of32[:], op0=ALU.mult, op1=ALU.mult,
                               accum_out=es_[:, 5:6], waits=[b])
                    c = DVE.op(V.tensor_scalar, out=es_[:, 6:7], in0=es_[:, 5:6], scalar1=1.0 / 128, scalar2=1e-5, op0=ALU.mult, op1=ALU.add, waits=[c])
                    c = POOL.op(nc.gpsimd.tensor_tensor, out=es_[:, 8:9], in0=es_[:, 6:7], in1=mhalf[:], op=ALU.pow, waits=[c, t_mh])
                    c = DVE.op(V.scalar_tensor_tensor, out=ot[:, 512 + h * 128:512 + (h + 1) * 128], in0=of32[:], scalar=es_[:, 8:9],
                               in1=gsub[:], op0=ALU.mult, op1=ALU.mult, waits=[c, t_gs, ofree])
                    of32_free[0] = [c]
                    owr.append(c)

            of_free = [[]]
            of32_free = [[]]
            for _ in prep(0):
                pass
            for s in range(NQB):
                if s + 1 < NQB:
                    pump_state["gen"] = prep(s + 1)
                    pump_state["budget"] = 0.0
                    pump_state["rate"] = prep_steps(s + 1) / float(8 * (2 * s + 2)) * 1.15
                else:
                    pump_state["gen"] = None
                ot, ofree, ok_ = otr.next()
                owr = []
                for pi, p in enumerate([4, 5, 6, 7, 0, 1, 2, 3]):
                    attention(s, p, ot, ofree, owr)
                if pump_state["gen"] is not None:
                    for _ in pump_state["gen"]:
                        pass
                    pump_state["gen"] = None
                td = GQ.dma(o_scr[s * 128:(s + 1) * 128, :], ot[:], waits=owr)
                otr.done(ok_, td)
            barrier()

        def load_w(st, name, src, rows, cols, q, waits=(), order=None):
            nkc = rows // 128
            wt = sbt(st, name, [128, nkc, cols], BF16)
            srcv = src.rearrange("(kc p) n -> p kc n", p=128)
            step = 1024
            starts = list(range(0, cols, step))
            toks = [None] * len(starts)
            for ci in (order if order is not None else range(len(starts))):
                c0 = starts[ci]
                n = min(step, cols - c0)
                toks[ci] = q.dma(wt[:, :, c0:c0 + n], srcv[:, :, c0:c0 + n], waits=waits)
            return wt, toks

        with ExitStack() as p3:
            wg = sbt(p3, "wg", [128, 8, 2048], BF16)
            w_in_v = w_in.rearrange("(kc p) n -> p kc n", p=128)
            twg = [GQ.dma(wg[:, :, c0:c0 + 512], w_in_v[:, :, C_G + c0:C_G + c0 + 512]) for c0 in range(0, 2048, 512)]
            wbd, twbd = load_w(p3, "wbd", w_bd, 512, D, GQ)
            wbf, twbf = load_w(p3, "wbf", w_bf, 512, D, GQ)
            wo, two = load_w(p3, "wo", w_out, D, D, GQ)
            gbt = sbt(p3, "gbt", [128, 2048], F32)
            t_gb = SP.dma(gbt[:], bcast(gate_b))
            xr = Ring([sbt(p3, f"xt{i}", [128, D], F32) for i in range(4)])
            junk = Ring([sbt(p3, f"junk{i}", [128, D], BF16) for i in range(1)])
            ssr = Ring([sbt(p3, f"ss{i}", [128, 4], F32) for i in range(4)])
            hbr = Ring([sbt(p3, f"hb{i}", [128, D], BF16) for i in range(3)])
            hTr = Ring([sbt(p3, f"hT{i}", [128, 8, 128], BF16) for i in range(2)])
            obr = Ring([sbt(p3, f"ob{i}", [128, D], BF16) for i in range(3)])
            oTr = Ring([sbt(p3, f"oT{i}", [128, 8, 128], BF16) for i in range(2)])
            gat = sbt(p3, "gat", [128, 2048], F32)
            mrg = sbt(p3, "mrg", [128, D], F32)
            mrg2 = sbt(p3, "mrg2", [128, D], F32)
            mbr = Ring([sbt(p3, f"mb{i}", [128, D], BF16) for i in range(2)])
            mTr = Ring([sbt(p3, f"mT{i}", [128, 8, 128], BF16) for i in range(2)])
            x1r = Ring([sbt(p3, f"x1t{i}", [128, D], F32) for i in range(2)])
            psT = Ring([pst(p3, f"psT{i}", [128, D], BF16) for i in range(2)])
            pp = Ring([pst(p3, f"pp{i}", [128, 512], F32) for i in range(6)])
            x1_dmas = []

            def transpose8(src_bf, tsrc, dst_ring):
                ps, pfree, pk = psT.next()
                tt = None
                for kc in range(8):
                    tt = PE.op(T.transpose, ps[:, kc * 128:(kc + 1) * 128], src_bf[:, kc * 128:(kc + 1) * 128], ident[:],
                               waits=[tsrc, pfree] if kc == 0 else ())
                dT, dfree, dk = dst_ring.next()
                te = ACT.op(A.copy, out=dT[:].rearrange("p a b -> p (a b)"), in_=ps[:], waits=[tt, dfree])
                psT.done(pk, te)
                return dT, te, dk, tt

            def st_load(s):
                xt, xfree, xk = xr.next()
                tx = SP.dma(xt[:], xq[s * 128:(s + 1) * 128, :], waits=xfree)
                hb, hfree, hk = hbr.next()
                th = rms_norm_block((junk, ssr), xt[:], tx, gmix[:], t_gmix, 1e-6, D, hb[:], hfree)
                ob, ofree, ok_ = obr.next()
                to = SP.dma(ob[:], o_scr[s * 128:(s + 1) * 128, :], waits=[ofree])
                return dict(s=s, xt=xt, xk=xk, hb=hb, th=th, hk=hk, ob=ob, to=to, ok_=ok_)

            def st_T(c):
                hT, te, tk, tt = transpose8(c["hb"], c["th"], hTr)
                hbr.done(c["hk"], tt)
                oT, teo, ok2, tto = transpose8(c["ob"], c["to"], oTr)
                obr.done(c["ok_"], tto)
                c.update(hT=hT, te=te, tk=tk, oT=oT, teo=teo, ok2=ok2)

            def st_X(c):
                hT, te, tk = c["hT"], c["te"], c["tk"]
                oT, teo, ok2 = c["oT"], c["teo"], c["ok2"]
                tg_last = None
                for gc in range(4):
                    ps, pfree, pk = pp.next()
                    tm = None
                    for kc in range(8):
                        tm = PE.op(T.matmul, ps[:], lhsT=hT[:, kc, :], rhs=wg[:, kc, gc * 512:(gc + 1) * 512], start=(kc == 0), stop=(kc == 7),
                                   waits=[te, pfree, twg] if kc == 0 else ())
                    a_ = DVE.op(V.tensor_tensor, out=gat[:, gc * 512:(gc + 1) * 512], in0=ps[:], in1=gbt[:, gc * 512:(gc + 1) * 512], op=ALU.add,
                                waits=[tm, t_gb, gat_free[0]])
                    pp.done(pk, a_)
                    tg_last = ACT.op(A.activation, out=gat[:, gc * 512:(gc + 1) * 512], in_=gat[:, gc * 512:(gc + 1) * 512], func=AF.Sigmoid, waits=[a_])
                    if gc == 3:
                        hTr.done(tk, tm)
                gat_free[0] = []
                mtoks = []
                for br, (wt, twt) in enumerate([(wbd, twbd), (wbf, twbf)]):
                    for nc_ in range(2):
                        ps, pfree, pk = pp.next()
                        tm = None
                        for kc in range(4):
                            tm = PE.op(T.matmul, ps[:], lhsT=oT[:, br * 4 + kc, :], rhs=wt[:, kc, nc_ * 512:(nc_ + 1) * 512], start=(kc == 0), stop=(kc == 3),
                                       waits=[teo, pfree, twt] if kc == 0 else ())
                        dst = (mrg if br == 0 else mrg2)[:, nc_ * 512:(nc_ + 1) * 512]
                        a_ = DVE.op(V.tensor_tensor, out=dst, in0=ps[:], in1=gat[:, br * 1024 + nc_ * 512:br * 1024 + (nc_ + 1) * 512], op=ALU.mult,
                                    waits=[tm, tg_last, mrg_free[0]])
                        pp.done(pk, a_)
                        mtoks.append(a_)
                        if br == 1 and nc_ == 1:
                            oTr.done(ok2, tm)
                gat_free[0] = list(mtoks)
                mb, mfree, mk = mbr.next()
                tmb = DVE.op(V.tensor_tensor, out=mb[:], in0=mrg[:], in1=mrg2[:], op=ALU.add, waits=[mtoks, mfree])
                mrg_free[0] = [tmb]
                c.update(mb=mb, tmb=tmb, mk=mk)

            def st_Y(c):
                s_ = c["s"]
                mT, tem, mk2, ttm = transpose8(c["mb"], c["tmb"], mTr)
                mbr.done(c["mk"], ttm)
                x1t, x1free, x1k = x1r.next()
                xtoks = []
                for nc_ in range(2):
                    ps, pfree, pk = pp.next()
                    tm = None
                    for kc in range(8):
                        tm = PE.op(T.matmul, ps[:], lhsT=mT[:, kc, :], rhs=wo[:, kc, nc_ * 512:(nc_ + 1) * 512], start=(kc == 0), stop=(kc == 7),
                                   waits=[tem, pfree, two] if kc == 0 else ())
                    a_ = DVE.op(V.tensor_tensor, out=x1t[:, nc_ * 512:(nc_ + 1) * 512], in0=ps[:], in1=c["xt"][:, nc_ * 512:(nc_ + 1) * 512], op=ALU.add,
                                waits=[tm, x1free])
                    pp.done(pk, a_)
                    xtoks.append(a_)
                    if nc_ == 1:
                        mTr.done(mk2, tm)
                xr.done(c["xk"], xtoks)
                td = GQ.dma(x1_scr[s_ * 128:(s_ + 1) * 128, :], x1t[:], waits=xtoks)
                x1r.done(x1k, td)
                x1_dmas.append(td)

            gat_free = [[]]
            mrg_free = [[]]
            ctx = {}
            ctx[0] = st_load(0)
            st_T(ctx[0])
            if NQB > 1:
                ctx[1] = st_load(1)
            prevY = None
            for s in range(NQB):
                if s + 2 < NQB:
                    ctx[s + 2] = st_load(s + 2)
                st_X(ctx[s])
                if s + 1 < NQB:
                    st_T(ctx[s + 1])
                if prevY is not None:
                    st_Y(prevY)
                prevY = ctx.pop(s)
            st_Y(prevY)
            barrier()
        early.close()

        with ExitStack() as p4:
            w1, tw1 = load_w(p4, "w1", w_f1, D, 2 * DFF, GQ, order=[0, 2, 3, 1, 4, 5])
            w2, tw2 = load_w(p4, "w2", w_f2, DFF, D, GQ)
            gffn = sbt(p4, "gffn", [128, D], F32)
            gfin = sbt(p4, "gfin", [128, D], F32)
            t_gffn = SP.dma(gffn[:], bcast(norm_ffn_g))
            t_gfin = SP.dma(gfin[:], bcast(norm_fin_g))
            x1r = Ring([sbt(p4, f"x1b{i}", [128, D], F32) for i in range(2)])
            junk = Ring([sbt(p4, f"junk{i}", [128, D], BF16) for i in range(1)])
            ssr = Ring([sbt(p4, f"ss{i}", [128, 4], F32) for i in range(2)])
            hbr = Ring([sbt(p4, f"hb{i}", [128, D], BF16) for i in range(2)])
            h2T = sbt(p4, "h2T", [128, 8, 512], BF16)
            actT = sbt(p4, "actT", [128, NFC, 512], BF16)
            sgr = Ring([sbt(p4, f"sg{i}", [128, 512], F32) for i in range(2)])
            x2r = Ring([sbt(p4, f"x2t{i}", [128, D], F32) for i in range(2)])
            psT = Ring([pst(p4, f"psT{i}", [128, D], BF16) for i in range(2)])
            pp = Ring([pst(p4, f"pp{i}", [128, 512], F32) for i in range(6)])
            h2T_free = []
            actT_free = []
            for grp in range(NQB // 4):
                th2 = []
                for bi in range(4):
                    s = grp * 4 + bi
                    x1t, x1free, x1k = x1r.next()
                    tx = SP.dma(x1t[:], x1_scr[s * 128:(s + 1) * 128, :], waits=[x1free])
                    hb, hfree, hk = hbr.next()
                    th = rms_norm_block((junk, ssr), x1t[:], tx, gffn[:], t_gffn, 1e-6, D, hb[:], hfree)
                    x1r.done(x1k, th)
                    ps, pfree, pk = psT.next()
                    tt = None
                    for kc in range(8):
                        tt = PE.op(T.transpose, ps[:, kc * 128:(kc + 1) * 128], hb[:, kc * 128:(kc + 1) * 128], ident[:],
                                   waits=[th, pfree] if kc == 0 else ())
                    hbr.done(hk, tt)
                    te = ACT.op(A.copy, out=h2T[:, :, bi * 128:(bi + 1) * 128], in_=ps[:].rearrange("p (a b) -> p a b", a=8), waits=[tt, h2T_free])
                    psT.done(pk, te)
                    th2.append(te)
                h2T_free = []
                tact = []
                last_mm = None
                for f in range(NFC):
                    psg, pfree, pkg = pp.next()
                    tmg = None
                    for kc in range(8):
                        tmg = PE.op(T.matmul, psg[:], lhsT=w1[:, kc, f * 128:(f + 1) * 128], rhs=h2T[:, kc, :], start=(kc == 0), stop=(kc == 7),
                                    waits=[th2, pfree, tw1[(f * 128) // 1024], tw1[(f * 128 + 127) // 1024]] if kc == 0 else ())
                    psu, pfree, pku = pp.next()
                    tmu = None
                    for kc in range(8):
                        tmu = PE.op(T.matmul, psu[:], lhsT=w1[:, kc, DFF + f * 128:DFF + (f + 1) * 128], rhs=h2T[:, kc, :], start=(kc == 0), stop=(kc == 7),
                                    waits=[pfree, tw1[(DFF + f * 128) // 1024], tw1[(DFF + f * 128 + 127) // 1024]] if kc == 0 else ())
                    last_mm = tmu
                    sg, sfree, sk = sgr.next()
                    ta = ACT.op(A.activation, out=sg[:], in_=psg[:], func=AF.Silu, waits=[tmg, sfree])
                    pp.done(pkg, ta)
                    tb_ = DVE.op(V.tensor_tensor, out=actT[:, f, :], in0=psu[:], in1=sg[:], op=ALU.mult, waits=[tmu, ta, actT_free])
                    pp.done(pku, tb_)
                    sgr.done(sk, tb_)
                    tact.append(tb_)
                h2T_free = [last_mm]
                actT_free = []
                last_o = None
                for bi in range(4):
                    s = grp * 4 + bi
                    x2, x2free, x2k = x2r.next()
                    tx2 = SP.dma(x2[:], x1_scr[s * 128:(s + 1) * 128, :], waits=[x2free])
                    xtoks = []
                    for nc_ in range(2):
                        ps, pfree, pk = pp.next()
                        tm = None
                        for f in range(NFC):
                            tm = PE.op(T.matmul, ps[:], lhsT=actT[:, f, bi * 128:(bi + 1) * 128], rhs=w2[:, f, nc_ * 512:(nc_ + 1) * 512],
                                       start=(f == 0), stop=(f == NFC - 1), waits=[tact, pfree, tw2] if f == 0 else ())
                        last_o = tm
                        a = DVE.op(V.tensor_tensor, out=x2[:, nc_ * 512:(nc_ + 1) * 512], in0=ps[:], in1=x2[:, nc_ * 512:(nc_ + 1) * 512], op=ALU.add,
                                   waits=[tm, tx2])
                        pp.done(pk, a)
                        xtoks.append(a)
                    e = rms_norm_block((junk, ssr), x2[:], xtoks, gfin[:], t_gfin, 1e-6, D, x2[:], [])
                    td = SP.dma(out[s * 128:(s + 1) * 128, :], x2[:], waits=[e])
                    x2r.done(x2k, td)
                actT_free = [last_o]
            barrier()
    return nc


_NC_CACHE = {}


def _get_nc(debug=False):
    if debug not in _NC_CACHE:
        _NC_CACHE[debug] = build(debug)
    return _NC_CACHE[debug]


def make_in_maps(inputs):
    x = np.ascontiguousarray(np.asarray(inputs["x"], dtype=np.float32))
    pos = np.asarray(inputs["positions"]).astype(np.int32)
    in_maps = []
    kk = np.arange(512)[None, :]
    qq = np.arange(128)[:, None]
    for c in range(8):
        b, j = c // 4, c % 4
        blocks = [4 * s + j for s in range(NQB)]
        xqc = np.concatenate([x[b, q * 128:(q + 1) * 128] for q in blocks], axis=0)
        posk = np.ascontiguousarray(pos[b].reshape(NTB, 128).T)
        posq = np.ascontiguousarray(np.stack([pos[b, q * 128:(q + 1) * 128] for q in blocks], axis=1))
        cm = np.where(kk <= j * 128 + qq, 0.0, NEG).astype(np.float32)
        m = {"xs": x[b], "xq": np.ascontiguousarray(xqc), "posk": posk, "posq": posq, "cmask": cm}
        for name in ["norm_mix_g", "w_in", "idx_k_norm_g", "idx_k_norm_b", "diff_lambda_q1", "diff_lambda_k1",
                     "diff_lambda_q2", "diff_lambda_k2", "diff_subln_g", "gate_b", "w_branch_dsa", "w_branch_diff",
                     "w_out", "norm_ffn_g", "w_ffn_in", "w_ffn_out"]:
            m[name] = np.ascontiguousarray(np.asarray(inputs[name], dtype=np.float32)[0])
        m["norm_final_g"] = np.ascontiguousarray(np.asarray(inputs["norm_final_g"], dtype=np.float32))
        in_maps.append(m)
    return in_maps


def kernel(**inputs):
    nc = _get_nc(False)
    in_maps = make_in_maps(inputs)
    res = run_bass_kernel_spmd(nc, in_maps, core_ids=list(range(8)))
    outp = np.zeros((2, S, D), dtype=np.float32)
    for c in range(8):
        b, j = c // 4, c % 4
        o = res.results[c]["out"]
        for s in range(NQB):
            q = 4 * s + j
            outp[b, q * 128:(q + 1) * 128] = o[s * 128:(s + 1) * 128]
    return outp
```

```python
import os
import math
import numpy as np
from contextlib import ExitStack
import concourse.bass as bass
import concourse.mybir as mybir
from concourse.bass_utils import run_bass_kernel_spmd

F32 = mybir.dt.float32
F32R = mybir.dt.float32r
BF16 = mybir.dt.bfloat16
I32 = mybir.dt.int32
AF = mybir.ActivationFunctionType
ALU = mybir.AluOpType
AX = mybir.AxisListType

S = 8192
D = 1024
NTB = S // 128
NQB = 16
NQ = NQB * 128
DFF = 2816
NFC = DFF // 128
TOPK = 256
NBIS = 16
NEG = -30000.0
C_QA, C_KA, C_VA, C_QI, C_KI, C_WI, C_QB, C_KB, C_VB, C_G = 0, 512, 1024, 1536, 1792, 1824, 1832, 2344, 2856, 3368
D_IN = 5416
VW = 132
TWO_PI = 2.0 * math.pi


class Eng:
    def __init__(self, nc, es, eng, name):
        self.eng = eng
        self.name = name
        self.sem = es.enter_context(nc.semaphore("sem_" + name))
        self.count = 0
        self.seen = {}

    def wait(self, toks):
        for t in toks:
            if t is None:
                continue
            if isinstance(t, list):
                self.wait(t)
                continue
            sem, val, key = t
            if self.seen.get(key, 0) >= val:
                continue
            self.eng.wait_ge(sem, val)
            self.seen[key] = val

    def op(self, fn, *args, waits=(), **kw):
        self.wait(waits)
        inst = fn(*args, **kw)
        self.count += 1
        inst.then_inc(self.sem, 1)
        tok = (self.sem, self.count, self.name)
        self.seen[self.name] = self.count - 1 if False else self.seen.get(self.name, 0)
        return tok


class DmaQ:
    def __init__(self, nc, es, eng, name, nsem=12):
        self.eng = eng
        self.name = name
        self.sems = [es.enter_context(nc.semaphore(f"dsem_{name}_{i}")) for i in range(nsem)]
        self.vals = [0] * nsem
        self.i = 0
        self.seen = {}

    def wait(self, toks):
        for t in toks:
            if t is None:
                continue
            if isinstance(t, list):
                self.wait(t)
                continue
            sem, val, key = t
            if self.seen.get(key, 0) >= val:
                continue
            self.eng.wait_ge(sem, val)
            self.seen[key] = val

    def dma(self, out, in_, waits=(), **kw):
        k = self.i
        self.i = (self.i + 1) % len(self.sems)
        key = f"{self.name}_{k}"
        if self.vals[k] > 0:
            self.wait([(self.sems[k], self.vals[k], key)])
        self.wait(waits)
        self.vals[k] += 16
        self.eng.dma_start(out=out, in_=in_, **kw).then_inc(self.sems[k], 16)
        return (self.sems[k], self.vals[k], key)

    def all_toks(self):
        return [(self.sems[k], self.vals[k], f"{self.name}_{k}") for k in range(len(self.sems)) if self.vals[k] > 0]


class Ring:
    def __init__(self, tiles):
        self.tiles = tiles
        self.rd = [[] for _ in tiles]
        self.i = -1

    def next(self):
        self.i = (self.i + 1) % len(self.tiles)
        k = self.i
        toks = self.rd[k]
        self.rd[k] = []
        return self.tiles[k], toks, k

    def done(self, k, tok):
        self.rd[k].append(tok)


def build(debug=False):
    nc = bass.Bass("TRN2", target_bir_lowering=False)
    dt_in = lambda n, s, d=F32: nc.dram_tensor(n, s, d, kind="ExternalInput").ap()
    xs = dt_in("xs", [S, D])
    xq = dt_in("xq", [NQ, D])
    posk = dt_in("posk", [128, NTB], I32)
    posq = dt_in("posq", [128, NQB], I32)
    cmask = dt_in("cmask", [128, 512])
    norm_mix_g = dt_in("norm_mix_g", [D])
    w_in = dt_in("w_in", [D, D_IN])
    idx_g = dt_in("idx_k_norm_g", [32])
    idx_b = dt_in("idx_k_norm_b", [32])
    lq1 = dt_in("diff_lambda_q1", [64])
    lk1 = dt_in("diff_lambda_k1", [64])
    lq2 = dt_in("diff_lambda_q2", [64])
    lk2 = dt_in("diff_lambda_k2", [64])
    subln_g = dt_in("diff_subln_g", [128])
    gate_b = dt_in("gate_b", [2048])
    w_bd = dt_in("w_branch_dsa", [512, D])
    w_bf = dt_in("w_branch_diff", [512, D])
    w_out = dt_in("w_out", [D, D])
    norm_ffn_g = dt_in("norm_ffn_g", [D])
    w_f1 = dt_in("w_ffn_in", [D, 2 * DFF])
    w_f2 = dt_in("w_ffn_out", [DFF, D])
    norm_fin_g = dt_in("norm_final_g", [D])
    out = nc.dram_tensor("out", [NQ, D], F32, kind="ExternalOutput").ap()
    skind = "ExternalOutput" if debug else "Internal"
    kT_scr = nc.dram_tensor("kT_scr", [8, 128, S], BF16, kind=skind).ap()
    kiT_scr = nc.dram_tensor("kiT_scr", [32, S], BF16, kind=skind).ap()
    v_scr = nc.dram_tensor("v_scr", [S, 8 * VW], BF16, kind=skind).ap()
    qT_scr = nc.dram_tensor("qT_scr", [8, 128, NQ], BF16, kind=skind).ap()
    qiT_scr = nc.dram_tensor("qiT_scr", [64, 4, NQ], BF16, kind=skind).ap()
    o_scr = nc.dram_tensor("o_scr", [NQ, D], BF16, kind=skind).ap()
    x1_scr = nc.dram_tensor("x1_scr", [NQ, D], F32, kind=skind).ap()

    with ExitStack() as es:
        uid = [0]

        def sbt(st, n, s, d):
            uid[0] += 1
            return st.enter_context(nc.sbuf_tensor(f"{n}_{uid[0]}", s, d))

        def pst(st, n, s, d):
            uid[0] += 1
            return st.enter_context(nc.psum_tensor(f"{n}_{uid[0]}", s, d))

        ident = sbt(es, "ident", [128, 128], BF16)
        early = ExitStack()
        cosk = sbt(early, "cosk", [128, NTB, 8], F32)
        sink = sbt(early, "sink", [128, NTB, 8], F32)
        cosq = sbt(early, "cosq", [128, NQB, 8], F32)
        sinq = sbt(early, "sinq", [128, NQB, 8], F32)
        gmix = sbt(early, "gmix", [128, D], F32)
        wabs = sbt(early, "wabs", [128, NQB, 8], F32)
        wsgn = sbt(early, "wsgn", [128, NQB, 8], F32)
        nlam = sbt(early, "nlam", [128, 1], F32)
        small = sbt(early, "small", [128, 64], F32)
        cb = sbt(early, "cb", [128, 512], BF16)
        cbf = sbt(early, "cbf", [128, 512], F32)

        es.enter_context(nc.Block())
        PE = Eng(nc, es, nc.tensor, "pe")
        ACT = Eng(nc, es, nc.scalar, "act")
        DVE = Eng(nc, es, nc.vector, "dve")
        POOL = Eng(nc, es, nc.gpsimd, "pool")
        SP = DmaQ(nc, es, nc.sync, "sp", 16)
        GQ = DmaQ(nc, es, nc.gpsimd, "gq", 8)
        V = nc.vector
        A = nc.scalar
        T = nc.tensor

        def bcast(ap1d, n=128):
            return ap1d.partition_broadcast(n)

        def barrier():
            toks = [(e.sem, e.count, e.name) for e in (PE, ACT, DVE, POOL) if e.count > 0]
            toks += SP.all_toks() + GQ.all_toks()
            for e in (PE, ACT, DVE, POOL, SP):
                e.wait(toks)

        t = POOL.op(nc.gpsimd.memset, ident[:], 1.0)
        t_ident = POOL.op(nc.gpsimd.affine_select, out=ident[:], in_=ident[:], pattern=[[-1, 128]],
                          compare_op=ALU.is_equal, fill=0.0, base=0, channel_multiplier=1, waits=[t])
        t_gmix = SP.dma(gmix[:], bcast(norm_mix_g))
        t_cbf = SP.dma(cbf[:], cmask)
        t_cb = DVE.op(V.tensor_copy, out=cb[:], in_=cbf[:], waits=[t_cbf])

        with ExitStack() as p0:
            lt = sbt(p0, "lt", [128, 4, 64], F32)
            lj = sbt(p0, "lj", [128, 64], F32)
            tl = [SP.dma(lt[:, i, :], bcast(a)) for i, a in enumerate([lq1, lk1, lq2, lk2])]
            t1 = DVE.op(V.tensor_tensor, out=lj[:], in0=lt[:, 0, :], in1=lt[:, 1, :], op=ALU.mult, waits=tl)
            t1 = DVE.op(V.tensor_reduce, out=small[:, 0:1], in_=lj[:], axis=AX.X, op=ALU.add, waits=[t1])
            t2 = DVE.op(V.tensor_tensor, out=lj[:], in0=lt[:, 2, :], in1=lt[:, 3, :], op=ALU.mult, waits=[t1])
            t2 = DVE.op(V.tensor_reduce, out=small[:, 1:2], in_=lj[:], axis=AX.X, op=ALU.add, waits=[t2])
            t3 = ACT.op(A.activation, out=small[:, 2:4], in_=small[:, 0:2], func=AF.Exp, waits=[t2])
            t_nlam = DVE.op(V.scalar_tensor_tensor, out=nlam[:], in0=small[:, 3:4], scalar=-0.2, in1=small[:, 2:3],
                            op0=ALU.add, op1=ALU.subtract, waits=[t3])

            invf = sbt(p0, "invf", [128, 8], F32)
            tinv = None
            for i in range(8):
                fv = float(np.power(np.float32(500000.0), -np.float32(2 * i) / np.float32(16)))
                tinv = DVE.op(V.memset, invf[:, i:i + 1], fv)

            def rope_table(pos_ap, n, cos_t, sin_t, nm):
                pi_ = sbt(p0, "pi_" + nm, [128, n], I32)
                pf = sbt(p0, "pf_" + nm, [128, n], F32)
                ang = sbt(p0, "ang_" + nm, [128, n, 8], F32)
                yy = sbt(p0, "yy_" + nm, [128, n, 8], F32)
                ni = sbt(p0, "ni_" + nm, [128, n, 8], I32)
                tp = SP.dma(pi_[:], pos_ap)
                a = DVE.op(V.tensor_copy, out=pf[:], in_=pi_[:], waits=[tp])
                a = DVE.op(V.tensor_tensor, out=ang[:], in0=pf[:].unsqueeze(2).to_broadcast([128, n, 8]),
                           in1=invf[:].unsqueeze(1).to_broadcast([128, n, 8]), op=ALU.mult, waits=[a, tinv])

                def reduce_sin(src_add, dst):
                    b = DVE.op(V.tensor_scalar, out=yy[:], in0=ang[:], scalar1=src_add, scalar2=1.0 / TWO_PI,
                               op0=ALU.add, op1=ALU.mult, waits=[a])
                    b = DVE.op(V.tensor_copy, out=ni[:], in_=yy[:], waits=[b])
                    b = DVE.op(V.tensor_copy, out=yy[:], in_=ni[:], waits=[b])
                    c1 = 6.28125
                    c2 = TWO_PI - 6.28125
                    b = DVE.op(V.scalar_tensor_tensor, out=dst, in0=yy[:], scalar=-c1, in1=ang[:], op0=ALU.mult, op1=ALU.add, waits=[b])
                    b = DVE.op(V.scalar_tensor_tensor, out=dst, in0=yy[:], scalar=-c2, in1=dst, op0=ALU.mult, op1=ALU.add, waits=[b])
                    b = DVE.op(V.tensor_scalar, out=dst, in0=dst, scalar1=src_add, scalar2=3.1415925, op0=ALU.add, op1=ALU.min, waits=[b])
                    b = DVE.op(V.tensor_scalar, out=dst, in0=dst, scalar1=-3.1415925, scalar2=None, op0=ALU.max, waits=[b])
                    return ACT.op(A.activation, out=dst, in_=dst, func=AF.Sin, waits=[b])
                ts = reduce_sin(0.0, sin_t[:])
                tc = reduce_sin(math.pi / 2.0, cos_t[:])
                return [ts, tc]
            t_ropek = rope_table(posk, NTB, cosk, sink, "k")
            t_ropeq = rope_table(posq, NQB, cosq, sinq, "q")
            barrier()

        def rms_norm_block(st_rings, x_tile, tx, g_tile, tg, eps, n_feat, hb_tile, hb_free):
            junk, ssr = st_rings
            jt, jfree, jk = junk.next()
            col, cfree, ck = ssr.next()
            a = ACT.op(A.activation, out=jt[:, :n_feat], in_=x_tile, func=AF.Square, accum_out=col[:, 0:1],
                       waits=[tx, jfree, cfree])
            junk.done(jk, a)
            b = DVE.op(V.tensor_scalar, out=col[:, 1:2], in0=col[:, 0:1], scalar1=1.0 / n_feat, scalar2=eps,
                       op0=ALU.mult, op1=ALU.add, waits=[a])
            c = ACT.op(A.activation, out=col[:, 2:3], in_=col[:, 1:2], func=AF.Sqrt, waits=[b])
            d = DVE.op(V.reciprocal, out=col[:, 3:4], in_=col[:, 2:3], waits=[c])
            e = DVE.op(V.scalar_tensor_tensor, out=hb_tile, in0=x_tile, scalar=col[:, 3:4], in1=g_tile,
                       op0=ALU.mult, op1=ALU.mult, waits=[d, tg, hb_free])
            ssr.done(ck, e)
            return e

        rope_last = []

        def rope_apply(tile3, H, half, cs, sn, tmp, waits):
            x1 = tile3[:, :, 0:half]
            x2 = tile3[:, :, half:2 * half]
            cB = cs.unsqueeze(1).to_broadcast([128, H, half])
            sB = sn.unsqueeze(1).to_broadcast([128, H, half])
            tv = lambda i: tmp[:, i, 0:H * half].rearrange("p (h d) -> p h d", h=H)
            waits = list(waits) + rope_last
            a1 = DVE.op(V.tensor_tensor, out=tv(0), in0=x1, in1=cB, op=ALU.mult, waits=waits)
            a2 = DVE.op(V.tensor_tensor, out=tv(1), in0=x2, in1=sB, op=ALU.mult, waits=waits)
            a3 = DVE.op(V.tensor_tensor, out=tv(2), in0=x2, in1=cB, op=ALU.mult, waits=waits)
            a4 = DVE.op(V.tensor_tensor, out=tv(3), in0=x1, in1=sB, op=ALU.mult, waits=waits)
            b1 = DVE.op(V.tensor_tensor, out=x1, in0=tv(0), in1=tv(1), op=ALU.subtract, waits=[a1, a2, a3, a4])
            b2 = DVE.op(V.tensor_tensor, out=x2, in0=tv(2), in1=tv(3), op=ALU.add, waits=[a1, a2, a3, a4])
            rope_last[:] = [b1, b2]
            return [b1, b2]

        with ExitStack() as p1:
            NKV = 2080
            NQC = 1288
            wkv = sbt(p1, "wkv", [128, 8, NKV], BF16)
            wq = sbt(p1, "wq", [128, 8, NQC], BF16)
            w_in_v = w_in.rearrange("(kc p) n -> p kc n", p=128)
            tw = []
            for (dst0, c0, n) in [(0, C_KA, 512), (512, C_KB, 512), (1024, C_VA, 512), (1536, C_VB, 512), (2048, C_KI, 32)]:
                tw.append(GQ.dma(wkv[:, :, dst0:dst0 + n], w_in_v[:, :, c0:c0 + n]))
            twq = []
            for (dst0, c0, n) in [(0, C_QA, 512), (512, C_QB, 512), (1024, C_QI, 256), (1280, C_WI, 8)]:
                twq.append(GQ.dma(wq[:, :, dst0:dst0 + n], w_in_v[:, :, c0:c0 + n]))
            lng = sbt(p1, "lng", [128, 32], F32)
            lnb = sbt(p1, "lnb", [128, 32], F32)
            t_lng = SP.dma(lng[:], bcast(idx_g))
            t_lnb = SP.dma(lnb[:], bcast(idx_b))

            xr = Ring([sbt(p1, f"xt{i}", [128, D], F32) for i in range(3)])
            junk = Ring([sbt(p1, f"junk{i}", [128, D], BF16) for i in range(1)])
            ssr = Ring([sbt(p1, f"ss{i}", [128, 4], F32) for i in range(4)])
            hbr = Ring([sbt(p1, f"hb{i}", [128, D], BF16) for i in range(3)])
            hTr = Ring([sbt(p1, f"hT{i}", [128, 8, 128], BF16) for i in range(3)])
            kfr = Ring([sbt(p1, f"kf{i}", [128, 1024], F32) for i in range(2)])
            kbr = Ring([sbt(p1, f"kb{i}", [128, 1024], BF16) for i in range(3)])
            kTr = Ring([sbt(p1, f"kTt{i}", [128, 8, 128], BF16) for i in range(2)])
            vtr = Ring([sbt(p1, f"vt{i}", [128, 8, VW], BF16) for i in range(2)])
            rtmp = sbt(p1, "rtmp", [128, 4, 128], F32)
            kif = sbt(p1, "kif", [128, 8], F32)
            kic = sbt(p1, "kic", [128, 32], F32)
            kij = sbt(p1, "kij", [128, 32], F32)
            kibr = Ring([sbt(p1, f"kib{i}", [128, 32], BF16) for i in range(3)])
            kiTr = Ring([sbt(p1, f"kiT{i}", [32, 128], BF16) for i in range(2)])
            qwr = Ring([sbt(p1, f"qw{i}", [128, 264], F32) for i in range(2)])
            qibr = Ring([sbt(p1, f"qib{i}", [128, 256], BF16) for i in range(3)])
            qiTr = Ring([sbt(p1, f"qiT{i}", [32, 8, 128], BF16) for i in range(2)])
            psT = Ring([pst(p1, f"psT{i}", [128, D], BF16) for i in range(2)])
            pp = Ring([pst(p1, f"pp{i}", [128, 512], F32) for i in range(4)])
            pkT = Ring([pst(p1, f"pkT{i}", [128, D], BF16) for i in range(2)])
            t_vinit = []
            for vt_ in vtr.tiles:
                t0 = POOL.op(nc.gpsimd.memset, vt_[:], 0.0)
                t1_ = POOL.op(nc.gpsimd.memset, vt_[:, 0:4, :].rearrange("p a (s e) -> p (a s) e", s=2)[:, :, 64:65], 1.0, waits=[t0])
                t_vinit.append(POOL.op(nc.gpsimd.memset, vt_[:, 4:8, 128:129], 1.0, waits=[t0, t1_]))

            def aevac(out_ap, in_ap, waits):
                return ACT.op(A.copy, out=out_ap, in_=in_ap, waits=waits)

            def stageA1(src_rows):
                xt, xfree, xk = xr.next()
                tx = SP.dma(xt[:], src_rows, waits=xfree)
                hb, hfree, hk = hbr.next()
                th = rms_norm_block((junk, ssr), xt[:], tx, gmix[:], t_gmix, 1e-6, D, hb[:], hfree)
                xr.done(xk, th)
                return hb, th, hk

            def stageA2(hb, th, hk):
                ps, pfree, pk = psT.next()
                tt = None
                for kc in range(8):
                    tt = PE.op(T.transpose, ps[:, kc * 128:(kc + 1) * 128], hb[:, kc * 128:(kc + 1) * 128], ident[:],
                               waits=[th, t_ident, pfree] if kc == 0 else ())
                hbr.done(hk, tt)
                hT, tfree, tk = hTr.next()
                te = aevac(hT[:].rearrange("p a b -> p (a b)"), ps[:], [tt, tfree])
                psT.done(pk, te)
                return hT, te, tk

            def project(hT, th, w_tile, c0, n, wtoks):
                ps, pfree, pk = pp.next()
                tt = None
                for kc in range(8):
                    tt = PE.op(T.matmul, ps[:, 0:n], lhsT=hT[:, kc, :], rhs=w_tile[:, kc, c0:c0 + n],
                               start=(kc == 0), stop=(kc == 7), waits=[th, pfree, wtoks] if kc == 0 else ())
                return ps, tt, pk

            def transpose_out(src_bf, tsrc, nchunk, width, ring_sb, dst_dram):
                ps, pfree, pk = pkT.next()
                tt = None
                for c in range(nchunk):
                    tt = PE.op(T.transpose, ps[0:width, c * 128:(c + 1) * 128], src_bf[:, c * width:(c + 1) * width], ident[:],
                               waits=[tsrc, pfree, t_ident] if c == 0 else ())
                sbT, sfree, sk = ring_sb.next()
                te = aevac(sbT[:].rearrange("p a b -> p (a b)") if len(sbT.shape) == 3 else sbT[:],
                           ps[0:width, 0:nchunk * 128], [tt, sfree])
                pkT.done(pk, te)
                td = GQ.dma(dst_dram, sbT[:], waits=[te])
                ring_sb.done(sk, td)
                return tt

            def transpose_out_qi(src_bf, tsrc, s):
                ps, pfree, pk = pkT.next()
                tt = None
                for c in range(8):
                    tt = PE.op(T.transpose, ps[0:32, c * 128:(c + 1) * 128], src_bf[:, c * 32:(c + 1) * 32], ident[:],
                               waits=[tsrc, pfree, t_ident] if c == 0 else ())
                sbT, sfree, sk = qiTr.next()
                te = aevac(sbT[:].rearrange("p a b -> p (a b)"), ps[0:32, 0:1024], [tt, sfree])
                pkT.done(pk, te)
                for g in range(2):
                    td = GQ.dma(qiT_scr[g * 32:(g + 1) * 32, :, s * 128:(s + 1) * 128], sbT[:, g::2, :], waits=[te])
                    qiTr.done(sk, td)
                return tt

            def qk_pair(hT, th, w_tile, wtoks, cs, sn, scale, dst_dram):
                kf, kfree, kk = kfr.next()
                tes = []
                for gi in range(2):
                    ps, tmm, pk = project(hT, th, w_tile, gi * 512, 512, wtoks)
                    te = aevac(kf[:, gi * 512:(gi + 1) * 512], ps[:], [tmm, kfree])
                    pp.done(pk, te)
                    tes.append(te)
                tr = rope_apply(kf[:].rearrange("p (h d) -> p h d", h=16), 16, 8, cs, sn, rtmp, tes)
                kb, bfree, bk = kbr.next()
                if scale == 1.0:
                    tcst = DVE.op(V.tensor_copy, out=kb[:], in_=kf[:], waits=[tr, bfree])
                else:
                    tcst = DVE.op(V.tensor_scalar, out=kb[:], in0=kf[:], scalar1=scale, scalar2=None, op0=ALU.mult, waits=[tr, bfree])
                kfr.done(kk, tcst)
                return kb, bk, tcst

            def stageB_kv(tb, hT, th, hk):
                cs = cosk[:, tb, :]
                sn = sink[:, tb, :]
                kb, bk, tcst = qk_pair(hT, th, wkv, tw, cs, sn, 1.0, None)
                vt, vfree, vk = vtr.next()
                tvs = []
                for gi in (2, 3):
                    ps, tmm, pk = project(hT, th, wkv, gi * 512, 512, tw)
                    if gi == 2:
                        dstv = vt[:, 0:4, :].rearrange("p a (s e) -> p (a s) e", s=2)[:, :, 0:64]
                        te = aevac(dstv, ps[:].rearrange("p (h e) -> p h e", e=64), [tmm, vfree, t_vinit])
                    else:
                        te = aevac(vt[:, 4:8, 0:128], ps[:].rearrange("p (h e) -> p h e", e=128), [tmm, vfree, t_vinit])
                    pp.done(pk, te)
                    tvs.append(te)
                td = GQ.dma(v_scr[tb * 128:(tb + 1) * 128, :], vt[:].rearrange("p a b -> p (a b)"), waits=tvs)
                vtr.done(vk, td)
                ps, tmm, pk = project(hT, th, wkv, 2048, 32, tw)
                hTr.done(hk, tmm)
                a = DVE.op(V.tensor_reduce, out=kif[:, 0:1], in_=ps[:, 0:32], axis=AX.X, op=ALU.add, waits=[tmm])
                a = DVE.op(V.tensor_scalar, out=kif[:, 1:2], in0=kif[:, 0:1], scalar1=1.0 / 32, scalar2=None, op0=ALU.mult, waits=[a])
                a = DVE.op(V.tensor_scalar, out=kic[:], in0=ps[:, 0:32], scalar1=kif[:, 1:2], scalar2=None, op0=ALU.subtract, waits=[a])
                pp.done(pk, a)
                b = ACT.op(A.activation, out=kij[:], in_=kic[:], func=AF.Square, accum_out=kif[:, 2:3], waits=[a])
                b = DVE.op(V.tensor_scalar, out=kif[:, 3:4], in0=kif[:, 2:3], scalar1=1.0 / 32, scalar2=1e-6, op0=ALU.mult, op1=ALU.add, waits=[b])
                b = ACT.op(A.activation, out=kif[:, 4:5], in_=kif[:, 3:4], func=AF.Sqrt, waits=[b])
                b = DVE.op(V.reciprocal, out=kif[:, 5:6], in_=kif[:, 4:5], waits=[b])
                b = DVE.op(V.scalar_tensor_tensor, out=kic[:], in0=kic[:], scalar=kif[:, 5:6], in1=lng[:], op0=ALU.mult, op1=ALU.mult, waits=[b, t_lng])
                b = DVE.op(V.tensor_tensor, out=kic[:], in0=kic[:], in1=lnb[:], op=ALU.add, waits=[b, t_lnb])
                csi = cosk[:, tb, :].rearrange("p (a two) -> p a two", two=2)[:, :, 0]
                sni = sink[:, tb, :].rearrange("p (a two) -> p a two", two=2)[:, :, 0]
                tr = rope_apply(kic[:].rearrange("p (h d) -> p h d", h=1), 1, 4, csi, sni, rtmp, [b])
                kib, bfree, bk2 = kibr.next()
                tc2 = DVE.op(V.tensor_copy, out=kib[:], in_=kic[:], waits=[tr, bfree])

                def b2():
                    tlast = transpose_out(kb, tcst, 8, 128, kTr, kT_scr[:, :, tb * 128:(tb + 1) * 128].rearrange("c f t -> f c t"))
                    kbr.done(bk, tlast)
                    tl2 = transpose_out(kib, tc2, 1, 32, kiTr, kiT_scr[:, tb * 128:(tb + 1) * 128])
                    kibr.done(bk2, tl2)
                return b2

            def stageB_q(s, hT, th, hk):
                cs = cosq[:, s, :]
                sn = sinq[:, s, :]
                kb, bk, tcst = qk_pair(hT, th, wq, twq, cs, sn, 0.125, None)
                ps, tmm, pk = project(hT, th, wq, 1024, 264, twq)
                hTr.done(hk, tmm)
                qw, qfree, qk = qwr.next()
                te = aevac(qw[:], ps[:, 0:264], [tmm, qfree])
                pp.done(pk, te)
                csi = cosq[:, s, :].rearrange("p (a two) -> p a two", two=2)[:, :, 0]
                sni = sinq[:, s, :].rearrange("p (a two) -> p a two", two=2)[:, :, 0]
                tr = rope_apply(qw[:, 0:256].rearrange("p (h d) -> p h d", h=8), 8, 4, csi, sni, rtmp, [te])
                a1 = DVE.op(V.tensor_scalar, out=wabs[:, s, :], in0=qw[:, 256:264], scalar1=1.0 / 16, scalar2=None, op0=ALU.mult, waits=[te])
                a2 = a1
                qib, bfree, bk2 = qibr.next()
                tc2 = DVE.op(V.tensor_copy, out=qib[:], in_=qw[:, 0:256], waits=[tr, bfree])
                qwr.done(qk, [tc2, a1, a2])

                def b2():
                    tlast = transpose_out(kb, tcst, 8, 128, kTr, qT_scr[:, :, s * 128:(s + 1) * 128].rearrange("c f t -> f c t"))
                    kbr.done(bk, tlast)
                    tl2 = transpose_out_qi(qib, tc2, s)
                    qibr.done(bk2, tl2)
                return b2

            items = [("kv", tb, xs[tb * 128:(tb + 1) * 128, :]) for tb in range(NTB)] + \
                    [("q", s, xq[s * 128:(s + 1) * 128, :]) for s in range(NQB)]
            n_it = len(items)
            a1 = {}
            a2 = {}
            a1[0] = stageA1(items[0][2])
            a2[0] = stageA2(*a1[0])
            if n_it > 1:
                a1[1] = stageA1(items[1][2])
            prev_b2 = None
            for i in range(n_it):
                if i + 2 < n_it:
                    a1[i + 2] = stageA1(items[i + 2][2])
                kind, idx, _ = items[i]
                hT, th, hk = a2.pop(i)
                if kind == "kv":
                    b2 = stageB_kv(idx, hT, th, hk)
                else:
                    b2 = stageB_q(idx, hT, th, hk)
                if i + 1 < n_it:
                    a2[i + 1] = stageA2(*a1.pop(i + 1))
                if prev_b2 is not None:
                    prev_b2()
                prev_b2 = b2
            if prev_b2 is not None:
                prev_b2()
            barrier()

        with ExitStack() as p2:
            kiT = sbt(p2, "kiT", [64, S], BF16)
            t_kiT = [SP.dma(kiT[g * 32:(g + 1) * 32, :], kiT_scr) for g in range(2)]
            gsub = sbt(p2, "gsub", [128, 128], F32)
            t_gs = SP.dma(gsub[:], bcast(subln_g))
            t_gs = DVE.op(V.tensor_scalar, out=gsub[:], in0=gsub[:], scalar1=0.8, scalar2=None, op0=ALU.mult, waits=[t_gs])
            ident2 = sbt(p2, "ident2", [128, 2, 128], BF16)
            t_id2 = [DVE.op(V.tensor_copy, out=ident2[:, i, :], in_=ident[:], waits=[t_ident]) for i in range(2)]
            Kr = Ring([sbt(p2, f"Kb{i}", [128, S], BF16) for i in range(2)])
            Vr = Ring([sbt(p2, f"Vb{i}", [128, NTB, VW], BF16) for i in range(2)])
            Mb = [sbt(p2, f"Mb{i}", [128, S], BF16) for i in range(2)]
            Isc = sbt(p2, "Isc", [128, S], F32)
            qbd_tiles = [sbt(p2, f"qbd{i}", [128, 2, 128], BF16) for i in range(3)]
            t_qz = [POOL.op(nc.gpsimd.memset, q_[:], 0.0) for q_ in qbd_tiles]
            qbr = Ring(qbd_tiles)
            qiT = [sbt(p2, f"qiTs{i}", [64, 4, 128], BF16) for i in range(2)]
            rl = Ring([sbt(p2, f"rl{i}", [128, 512], BF16) for i in range(8)])
            identf = sbt(p2, "identf", [128, 128], F32)
            t_idf = DVE.op(V.tensor_copy, out=identf[:], in_=ident[:], waits=[t_ident])
            dgb = [sbt(p2, f"dgb{i}", [128, 8, 128], BF16) for i in range(2)]
            etr = Ring([sbt(p2, f"et{i}", [128, 512], BF16) for i in range(4)])
            otr = Ring([sbt(p2, f"ot{i}", [128, D], BF16) for i in range(2)])
            of32 = sbt(p2, "of32", [128, 128], F32)
            oj = sbt(p2, "oj", [128, 128], F32)
            bs = sbt(p2, "bs", [128, 16], F32)
            es_ = sbt(p2, "es_", [128, 16], F32)
            pss = Ring([pst(p2, f"pss{i}", [128, 512], F32) for i in range(4)])
            pI = pst(p2, "pI", [128, 512], F32)
            pacc = Ring([pst(p2, f"pacc{i}", [128, 512], F32) for i in range(3)])
            ofr = Ring([(sbt(p2, f"of1_{i}", [128, 128], F32), sbt(p2, f"of2_{i}", [128, 128], F32)) for i in range(4)])
            mhalf = sbt(p2, "mhalf", [128, 1], F32)
            t_mh = POOL.op(nc.gpsimd.memset, mhalf[:], -0.5)
            ez = sbt(p2, "ez", [128, 8], F32)
            Mb_ready = [None, None]
            Mb_readers = [[], []]
            qiT_readers = [[], []]
            dg_readers = [[], []]
            Isc_free = [[]]
            pI_free = [[]]

            def prep(s):
                li = s % 2
                nk = (4 * s + 4) * 128
                nch = nk // 512
                wb = 0.76 * (s + 1)
                tq = SP.dma(qiT[li][:], qiT_scr[:, :, s * 128:(s + 1) * 128], waits=qiT_readers[li])
                qiT_readers[li] = []
                tdg = None
                for h in range(8):
                    tdg = ACT.op(A.activation, out=dgb[li][:, h, :], in_=identf[:], func=AF.Copy, scale=wabs[:, s, h:h + 1],
                                 waits=[t_idf, dg_readers[li]] if h == 0 else ())
                dg_readers[li] = []
                yield 1.0
                tI = None
                pending = None

                def flush(pend):
                    (pc, pr, items) = pend
                    tacc = None
                    for g, (prt, pta, prk) in enumerate(items):
                        h = 2 * pr + g
                        tacc = PE.op(T.matmul, pI[:], lhsT=dgb[li][:, h, :], rhs=prt[:],
                                     start=(pr == 0 and g == 0), stop=(pr == 3 and g == 1),
                                     waits=[pta, tdg, pI_free[0]])
                        rl.done(prk, tacc)
                    return tacc
                pendq = []

                def drain(keep):
                    nonlocal tI
                    tacc = None
                    while len(pendq) > keep:
                        pend = pendq.pop(0)
                        tacc = flush(pend)
                        if pend[1] == 3:
                            pc = pend[0]
                            tI = ACT.op(A.copy, out=Isc[:, pc * 512:(pc + 1) * 512], in_=pI[:], waits=[tacc, Isc_free[0]])
                            pI_free[0] = [tI]
                    return tacc
                for c in range(nch):
                    for r in range(4):
                        drain(1)
                        mm = []
                        for g in range(2):
                            ps, pfree, pk = pss.next()
                            tm = PE.op(T.matmul, ps[:], lhsT=qiT[li][g * 32:(g + 1) * 32, r, :], rhs=kiT[g * 32:(g + 1) * 32, c * 512:(c + 1) * 512],
                                       start=True, stop=True, waits=[tq, t_kiT, pfree])
                            mm.append((ps, pk, tm))
                        items = []
                        for g in range(2):
                            ps, pk, tm = mm[g]
                            rt, rfree, rk = rl.next()
                            if True:
                                ta = ACT.op(A.activation, out=rt[:], in_=ps[:], func=AF.Relu, waits=[tm, rfree])
                            else:
                                ta = DVE.op(V.tensor_scalar, out=rt[:], in0=ps[:], scalar1=0.0, scalar2=None, op0=ALU.max, waits=[tm, rfree])
                            pss.done(pk, ta)
                            items.append((rt, ta, rk))
                        pendq.append((c, r, items))
                        yield 1.0
                tacc = drain(0)
                qiT_readers[li].append(tacc)
                dg_readers[li].append(tacc)
                Isc_free[0] = []
                Iv = Isc[:, 0:nk]
                a = DVE.op(V.tensor_reduce, out=bs[:, 0:1], in_=Iv, axis=AX.X, op=ALU.max, waits=[tI])
                yield wb
                a = DVE.op(V.tensor_reduce, out=bs[:, 1:2], in_=Iv, axis=AX.X, op=ALU.min, waits=[a])
                a = DVE.op(V.scalar_tensor_tensor, out=bs[:, 2:3], in0=bs[:, 0:1], scalar=1.0, in1=bs[:, 1:2], op0=ALU.add, op1=ALU.subtract, waits=[a])
                a = DVE.op(V.tensor_tensor, out=Isc[:, nk - 512:nk], in0=Isc[:, nk - 512:nk], in1=cbf[:], op=ALU.add, waits=[a, t_cbf])
                yield wb
                lo = bs[:, 1:2]
                w0 = bs[:, 2:3]
                mid = bs[:, 3:4]
                cnt = bs[:, 4:5]
                gg = bs[:, 5:6]
                for it in range(1, NBIS + 1):
                    sc = 2.0 ** (-it)
                    a = DVE.op(V.tensor_scalar, out=mid, in0=w0, scalar1=sc, scalar2=lo, op0=ALU.mult, op1=ALU.add, waits=[a])
                    a = DVE.op(V.tensor_scalar, out=Mb[li][:, 0:nk], in0=Iv, scalar1=mid, scalar2=None, op0=ALU.is_ge, op1=ALU.add,
                               accum_out=cnt, waits=[a, Mb_readers[li]])
                    Mb_readers[li] = []
                    a = DVE.op(V.tensor_scalar, out=gg, in0=cnt, scalar1=TOPK - 0.5, scalar2=sc, op0=ALU.is_ge, op1=ALU.mult, waits=[a])
                    a = DVE.op(V.scalar_tensor_tensor, out=lo, in0=w0, scalar=gg, in1=lo, op0=ALU.mult, op1=ALU.add, waits=[a])
                    yield wb
                a = DVE.op(V.tensor_scalar, out=Mb[li][:, 0:nk], in0=Iv, scalar1=lo, scalar2=NEG, op0=ALU.is_lt, op1=ALU.mult, waits=[a])
                Mb_ready[li] = a
                Isc_free[0] = [a]

            def prep_steps(s):
                return 1 + 4 * (s + 1) + (2 + NBIS) * 0.76 * (s + 1)

            pump_state = {"gen": None, "budget": 0.0, "rate": 0.0}

            def pump():
                st = pump_state
                if st["gen"] is None:
                    return
                st["budget"] += st["rate"]
                while st["budget"] > 0.0 and st["gen"] is not None:
                    try:
                        st["budget"] -= next(st["gen"])
                    except StopIteration:
                        st["gen"] = None

            def attention(s, p, ot, ofree, owr):
                li = s % 2
                is_dsa = p < 4
                nkb = 4 * s + 4
                kmax = nkb * 128
                Kb, kfree, kk = Kr.next()
                Vb, vfree, vk = Vr.next()
                tK = SP.dma(Kb[:, 0:kmax], kT_scr[p, :, 0:kmax], waits=kfree)
                tV = []
                for b0 in range(0, nkb, 8):
                    nb = min(8, nkb - b0)
                    tV.append(SP.dma(Vb[:, b0:b0 + nb, :], v_scr[b0 * 128:(b0 + nb) * 128, p * VW:(p + 1) * VW].rearrange("(b t) w -> t b w", t=128), waits=vfree))
                qbd, qfree, qk = qbr.next()
                tq = [SP.dma(qbd[m * 64:(m + 1) * 64, m, :], qT_scr[p, m * 64:(m + 1) * 64, s * 128:(s + 1) * 128], waits=[qfree, t_qz]) for m in range(2)]
                accs = [pacc.next() for m in range(2)]
                vw = 66 if is_dsa else 130
                ntile = nkb // 2
                pend = {}

                def do_qk(ti):
                    ps, pfree, pk = pss.next()
                    tm = None
                    for bi in range(2):
                        kb_ = ti * 2 + bi
                        need_mask = is_dsa or (kb_ >= nkb - 4)
                        tm = PE.op(T.matmul, ps[:, bi * 256:(bi + 1) * 256], lhsT=Kb[:, kb_ * 128:(kb_ + 1) * 128],
                                   rhs=qbd[:].rearrange("p a b -> p (a b)"), start=True, stop=not need_mask,
                                   waits=[tK, tq, pfree] if bi == 0 else ())
                        if need_mask:
                            if is_dsa:
                                ml = Mb[li][:, kb_ * 128:(kb_ + 1) * 128]
                                mw = [Mb_ready[li]]
                            else:
                                cbi = kb_ - (nkb - 4)
                                ml = cb[:, cbi * 128:(cbi + 1) * 128]
                                mw = [t_cb]
                            tm = PE.op(T.matmul, ps[:, bi * 256:(bi + 1) * 256], lhsT=ml, rhs=ident2[:].rearrange("p a b -> p (a b)"),
                                       start=False, stop=True, waits=mw + [t_id2])
                    et, efree, ek = etr.next()
                    te = ACT.op(A.activation, out=et[:], in_=ps[:], func=AF.Exp, waits=[tm, efree])
                    pss.done(pk, te)
                    pend[ti] = (et, te, ek)

                tav = [None, None]

                def do_av(ti):
                    et, te, ek = pend.pop(ti)
                    tm = None
                    first = True
                    for bi in range(2):
                        kb_ = ti * 2 + bi
                        for m in range(2):
                            acc, afree, ak = accs[m]
                            rhs = Vb[:, kb_, m * 66:(m + 1) * 66] if is_dsa else Vb[:, kb_, 0:130]
                            tm = PE.op(T.matmul, acc[:, 0:vw], lhsT=et[:, bi * 256 + m * 128:bi * 256 + (m + 1) * 128], rhs=rhs,
                                       start=(kb_ == 0), stop=(kb_ == nkb - 1),
                                       waits=[te, tV, accs[0][1], accs[1][1]] if first else ())
                            first = False
                            tav[m] = tm
                    etr.done(ek, tm)
                LAG = 2
                for ti in range(ntile):
                    do_qk(ti)
                    if ti >= LAG:
                        do_av(ti - LAG)
                    pump()
                for ti in range(max(0, ntile - LAG), ntile):
                    do_av(ti)
                qbr.done(qk, tav[1])
                if is_dsa:
                    Mb_readers[li].append(tav[1])
                Kr.done(kk, tav[1])
                Vr.done(vk, tav[1])
                if is_dsa:
                    for m in range(2):
                        acc, afree, ak = accs[m]
                        a = ACT.op(A.activation, out=ez[:, m:m + 1], in_=acc[:, 64:65], func=AF.Ln, waits=[tav[1]])
                        a = ACT.op(A.activation, out=ez[:, m:m + 1], in_=ez[:, m:m + 1], func=AF.Exp, scale=-1.0, waits=[a])
                        a = ACT.op(A.activation, out=ot[:, p * 128 + m * 64:p * 128 + (m + 1) * 64], in_=acc[:, 0:64], func=AF.Copy,
                                   scale=ez[:, m:m + 1], waits=[a, ofree])
                        pacc.done(ak, a)
                        owr.append(a)
                else:
                    h = p - 4
                    acc1, _, ak1 = accs[0]
                    acc2, _, ak2 = accs[1]
                    (of1, of2), offree, ofk = ofr.next()
                    a = ACT.op(A.activation, out=ez[:, 2:3], in_=acc1[:, 128:129], func=AF.Ln, waits=[tav[1], offree])
                    a = ACT.op(A.activation, out=ez[:, 2:3], in_=ez[:, 2:3], func=AF.Exp, scale=-1.0, waits=[a])
                    a = ACT.op(A.activation, out=of1[:], in_=acc1[:, 0:128], func=AF.Copy, scale=ez[:, 2:3], waits=[a])
                    pacc.done(ak1, a)
                    b = ACT.op(A.activation, out=ez[:, 3:4], in_=acc2[:, 128:129], func=AF.Ln, waits=[a])
                    b = ACT.op(A.activation, out=ez[:, 3:4], in_=ez[:, 3:4], func=AF.Exp, scale=-1.0, waits=[b])
                    b = ACT.op(A.activation, out=of2[:], in_=acc2[:, 0:128], func=AF.Copy, scale=ez[:, 3:4], waits=[b])
                    pacc.done(ak2, b)
                    b = DVE.op(V.scalar_tensor_tensor, out=of32[:], in0=of2[:], scalar=nlam[:, 0:1], in1=of1[:],
                               op0=ALU.mult, op1=ALU.add, waits=[a, b, t_nlam, of32_free[0]])
                    ofr.done(ofk, b)
                    c = DVE.op(V.scalar_tensor_tensor, out=oj[:], in0=of32[:], scalar=1.0, in1=of32[:], op0=ALU.mult, op1=ALU.mult,
                               accum_out=es_[:, 5:6], waits=[b])
                    c = DVE.op(V.tensor_scalar, out=es_[:, 6:7], in0=es_[:, 5:6], scalar1=1.0 / 128, scalar2=1e-5, op0=ALU.mult, op1=ALU.add, waits=[c])
                    c = POOL.op(nc.gpsimd.tensor_tensor, out=es_[:, 8:9], in0=es_[:, 6:7], in1=mhalf[:], op=ALU.pow, waits=[c, t_mh])
                    c = DVE.op(V.scalar_tensor_tensor, out=ot[:, 512 + h * 128:512 + (h + 1) * 128], in0=of32[:], scalar=es_[:, 8:9],
                               in1=gsub[:], op0=ALU.mult, op1=ALU.mult, waits=[c, t_gs, ofree])
                    of32_free[0] = [c]
                    owr.append(c)

            of_free = [[]]
            of32_free = [[]]
            for _ in prep(0):
                pass
            for s in range(NQB):
                if s + 1 < NQB:
                    pump_state["gen"] = prep(s + 1)
                    pump_state["budget"] = 0.0
                    pump_state["rate"] = prep_steps(s + 1) / float(8 * (2 * s + 2)) * 1.3
                else:
                    pump_state["gen"] = None
                ot, ofree, ok_ = otr.next()
                owr = []
                for pi, p in enumerate([4, 5, 6, 7, 0, 1, 2, 3]):
                    attention(s, p, ot, ofree, owr)
                if pump_state["gen"] is not None:
                    for _ in pump_state["gen"]:
                        pass
                    pump_state["gen"] = None
                td = GQ.dma(o_scr[s * 128:(s + 1) * 128, :], ot[:], waits=owr)
                otr.done(ok_, td)
            barrier()

        def load_w(st, name, src, rows, cols, q, waits=(), order=None):
            nkc = rows // 128
            wt = sbt(st, name, [128, nkc, cols], BF16)
            srcv = src.rearrange("(kc p) n -> p kc n", p=128)
            step = 1024
            starts = list(range(0, cols, step))
            toks = [None] * len(starts)
            for ci in (order if order is not None else range(len(starts))):
                c0 = starts[ci]
                n = min(step, cols - c0)
                toks[ci] = q.dma(wt[:, :, c0:c0 + n], srcv[:, :, c0:c0 + n], waits=waits)
            return wt, toks

        with ExitStack() as p3:
            wg = sbt(p3, "wg", [128, 8, 2048], BF16)
            w_in_v = w_in.rearrange("(kc p) n -> p kc n", p=128)
            twg = [GQ.dma(wg[:, :, c0:c0 + 512], w_in_v[:, :, C_G + c0:C_G + c0 + 512]) for c0 in range(0, 2048, 512)]
            wbd, twbd = load_w(p3, "wbd", w_bd, 512, D, GQ)
            wbf, twbf = load_w(p3, "wbf", w_bf, 512, D, GQ)
            wo, two = load_w(p3, "wo", w_out, D, D, GQ)
            gbt = sbt(p3, "gbt", [128, 2048], F32)
            t_gb = SP.dma(gbt[:], bcast(gate_b))
            xr = Ring([sbt(p3, f"xt{i}", [128, D], F32) for i in range(4)])
            junk = Ring([sbt(p3, f"junk{i}", [128, D], BF16) for i in range(1)])
            ssr = Ring([sbt(p3, f"ss{i}", [128, 4], F32) for i in range(4)])
            hbr = Ring([sbt(p3, f"hb{i}", [128, D], BF16) for i in range(3)])
            hTr = Ring([sbt(p3, f"hT{i}", [128, 8, 128], BF16) for i in range(2)])
            obr = Ring([sbt(p3, f"ob{i}", [128, D], BF16) for i in range(3)])
            oTr = Ring([sbt(p3, f"oT{i}", [128, 8, 128], BF16) for i in range(2)])
            gat = sbt(p3, "gat", [128, 2048], F32)
            mrg = sbt(p3, "mrg", [128, D], F32)
            mrg2 = sbt(p3, "mrg2", [128, D], F32)
            mbr = Ring([sbt(p3, f"mb{i}", [128, D], BF16) for i in range(2)])
            mTr = Ring([sbt(p3, f"mT{i}", [128, 8, 128], BF16) for i in range(2)])
            x1r = Ring([sbt(p3, f"x1t{i}", [128, D], F32) for i in range(2)])
            psT = Ring([pst(p3, f"psT{i}", [128, D], BF16) for i in range(2)])
            pp = Ring([pst(p3, f"pp{i}", [128, 512], F32) for i in range(6)])
            x1_dmas = []

            def transpose8(src_bf, tsrc, dst_ring):
                ps, pfree, pk = psT.next()
                tt = None
                for kc in range(8):
                    tt = PE.op(T.transpose, ps[:, kc * 128:(kc + 1) * 128], src_bf[:, kc * 128:(kc + 1) * 128], ident[:],
                               waits=[tsrc, pfree] if kc == 0 else ())
                dT, dfree, dk = dst_ring.next()
                te = ACT.op(A.copy, out=dT[:].rearrange("p a b -> p (a b)"), in_=ps[:], waits=[tt, dfree])
                psT.done(pk, te)
                return dT, te, dk, tt

            def st_load(s):
                xt, xfree, xk = xr.next()
                tx = SP.dma(xt[:], xq[s * 128:(s + 1) * 128, :], waits=xfree)
                hb, hfree, hk = hbr.next()
                th = rms_norm_block((junk, ssr), xt[:], tx, gmix[:], t_gmix, 1e-6, D, hb[:], hfree)
                ob, ofree, ok_ = obr.next()
                to = SP.dma(ob[:], o_scr[s * 128:(s + 1) * 128, :], waits=[ofree])
                return dict(s=s, xt=xt, xk=xk, hb=hb, th=th, hk=hk, ob=ob, to=to, ok_=ok_)

            def st_T(c):
                hT, te, tk, tt = transpose8(c["hb"], c["th"], hTr)
                hbr.done(c["hk"], tt)
                oT, teo, ok2, tto = transpose8(c["ob"], c["to"], oTr)
                obr.done(c["ok_"], tto)
                c.update(hT=hT, te=te, tk=tk, oT=oT, teo=teo, ok2=ok2)

            def st_X(c):
                hT, te, tk = c["hT"], c["te"], c["tk"]
                oT, teo, ok2 = c["oT"], c["teo"], c["ok2"]
                tg_last = None
                for gc in range(4):
                    ps, pfree, pk = pp.next()
                    tm = None
                    for kc in range(8):
                        tm = PE.op(T.matmul, ps[:], lhsT=hT[:, kc, :], rhs=wg[:, kc, gc * 512:(gc + 1) * 512], start=(kc == 0), stop=(kc == 7),
                                   waits=[te, pfree, twg] if kc == 0 else ())
                    a_ = DVE.op(V.tensor_tensor, out=gat[:, gc * 512:(gc + 1) * 512], in0=ps[:], in1=gbt[:, gc * 512:(gc + 1) * 512], op=ALU.add,
                                waits=[tm, t_gb, gat_free[0]])
                    pp.done(pk, a_)
                    tg_last = ACT.op(A.activation, out=gat[:, gc * 512:(gc + 1) * 512], in_=gat[:, gc * 512:(gc + 1) * 512], func=AF.Sigmoid, waits=[a_])
                    if gc == 3:
                        hTr.done(tk, tm)
                gat_free[0] = []
                mtoks = []
                for br, (wt, twt) in enumerate([(wbd, twbd), (wbf, twbf)]):
                    for nc_ in range(2):
                        ps, pfree, pk = pp.next()
                        tm = None
                        for kc in range(4):
                            tm = PE.op(T.matmul, ps[:], lhsT=oT[:, br * 4 + kc, :], rhs=wt[:, kc, nc_ * 512:(nc_ + 1) * 512], start=(kc == 0), stop=(kc == 3),
                                       waits=[teo, pfree, twt] if kc == 0 else ())
                        dst = (mrg if br == 0 else mrg2)[:, nc_ * 512:(nc_ + 1) * 512]
                        a_ = DVE.op(V.tensor_tensor, out=dst, in0=ps[:], in1=gat[:, br * 1024 + nc_ * 512:br * 1024 + (nc_ + 1) * 512], op=ALU.mult,
                                    waits=[tm, tg_last, mrg_free[0]])
                        pp.done(pk, a_)
                        mtoks.append(a_)
                        if br == 1 and nc_ == 1:
                            oTr.done(ok2, tm)
                gat_free[0] = list(mtoks)
                mb, mfree, mk = mbr.next()
                tmb = DVE.op(V.tensor_tensor, out=mb[:], in0=mrg[:], in1=mrg2[:], op=ALU.add, waits=[mtoks, mfree])
                mrg_free[0] = [tmb]
                c.update(mb=mb, tmb=tmb, mk=mk)

            def st_Y(c):
                s_ = c["s"]
                mT, tem, mk2, ttm = transpose8(c["mb"], c["tmb"], mTr)
                mbr.done(c["mk"], ttm)
                x1t, x1free, x1k = x1r.next()
                xtoks = []
                for nc_ in range(2):
                    ps, pfree, pk = pp.next()
                    tm = None
                    for kc in range(8):
                        tm = PE.op(T.matmul, ps[:], lhsT=mT[:, kc, :], rhs=wo[:, kc, nc_ * 512:(nc_ + 1) * 512], start=(kc == 0), stop=(kc == 7),
                                   waits=[tem, pfree, two] if kc == 0 else ())
                    a_ = DVE.op(V.tensor_tensor, out=x1t[:, nc_ * 512:(nc_ + 1) * 512], in0=ps[:], in1=c["xt"][:, nc_ * 512:(nc_ + 1) * 512], op=ALU.add,
                                waits=[tm, x1free])
                    pp.done(pk, a_)
                    xtoks.append(a_)
                    if nc_ == 1:
                        mTr.done(mk2, tm)
                xr.done(c["xk"], xtoks)
                td = GQ.dma(x1_scr[s_ * 128:(s_ + 1) * 128, :], x1t[:], waits=xtoks)
                x1r.done(x1k, td)
                x1_dmas.append(td)

            gat_free = [[]]
            mrg_free = [[]]
            ctx = {}
            ctx[0] = st_load(0)
            st_T(ctx[0])
            if NQB > 1:
                ctx[1] = st_load(1)
            prevY = None
            for s in range(NQB):
                if s + 2 < NQB:
                    ctx[s + 2] = st_load(s + 2)
                st_X(ctx[s])
                if s + 1 < NQB:
                    st_T(ctx[s + 1])
                if prevY is not None:
                    st_Y(prevY)
                prevY = ctx.pop(s)
            st_Y(prevY)
            barrier()
        early.close()

        with ExitStack() as p4:
            w1, tw1 = load_w(p4, "w1", w_f1, D, 2 * DFF, GQ, order=[0, 2, 3, 1, 4, 5])
            w2, tw2 = load_w(p4, "w2", w_f2, DFF, D, GQ)
            gffn = sbt(p4, "gffn", [128, D], F32)
            gfin = sbt(p4, "gfin", [128, D], F32)
            t_gffn = SP.dma(gffn[:], bcast(norm_ffn_g))
            t_gfin = SP.dma(gfin[:], bcast(norm_fin_g))
            x1r = Ring([sbt(p4, f"x1b{i}", [128, D], F32) for i in range(2)])
            junk = Ring([sbt(p4, f"junk{i}", [128, D], BF16) for i in range(1)])
            ssr = Ring([sbt(p4, f"ss{i}", [128, 4], F32) for i in range(2)])
            hbr = Ring([sbt(p4, f"hb{i}", [128, D], BF16) for i in range(2)])
            h2T = sbt(p4, "h2T", [128, 8, 512], BF16)
            actT = sbt(p4, "actT", [128, NFC, 512], BF16)
            sgr = Ring([sbt(p4, f"sg{i}", [128, 512], F32) for i in range(2)])
            x2r = Ring([sbt(p4, f"x2t{i}", [128, D], F32) for i in range(2)])
            psT = Ring([pst(p4, f"psT{i}", [128, D], BF16) for i in range(2)])
            pp = Ring([pst(p4, f"pp{i}", [128, 512], F32) for i in range(6)])
            h2T_free = []
            actT_free = []
            for grp in range(NQB // 4):
                th2 = []
                for bi in range(4):
                    s = grp * 4 + bi
                    x1t, x1free, x1k = x1r.next()
                    tx = SP.dma(x1t[:], x1_scr[s * 128:(s + 1) * 128, :], waits=[x1free])
                    hb, hfree, hk = hbr.next()
                    th = rms_norm_block((junk, ssr), x1t[:], tx, gffn[:], t_gffn, 1e-6, D, hb[:], hfree)
                    x1r.done(x1k, th)
                    ps, pfree, pk = psT.next()
                    tt = None
                    for kc in range(8):
                        tt = PE.op(T.transpose, ps[:, kc * 128:(kc + 1) * 128], hb[:, kc * 128:(kc + 1) * 128], ident[:],
                                   waits=[th, pfree] if kc == 0 else ())
                    hbr.done(hk, tt)
                    te = ACT.op(A.copy, out=h2T[:, :, bi * 128:(bi + 1) * 128], in_=ps[:].rearrange("p (a b) -> p a b", a=8), waits=[tt, h2T_free])
                    psT.done(pk, te)
                    th2.append(te)
                h2T_free = []
                tact = []
                last_mm = None
                for f in range(NFC):
                    psg, pfree, pkg = pp.next()
                    tmg = None
                    for kc in range(8):
                        tmg = PE.op(T.matmul, psg[:], lhsT=w1[:, kc, f * 128:(f + 1) * 128], rhs=h2T[:, kc, :], start=(kc == 0), stop=(kc == 7),
                                    waits=[th2, pfree, tw1[(f * 128) // 1024], tw1[(f * 128 + 127) // 1024]] if kc == 0 else ())
                    psu, pfree, pku = pp.next()
                    tmu = None
                    for kc in range(8):
                        tmu = PE.op(T.matmul, psu[:], lhsT=w1[:, kc, DFF + f * 128:DFF + (f + 1) * 128], rhs=h2T[:, kc, :], start=(kc == 0), stop=(kc == 7),
                                    waits=[pfree, tw1[(DFF + f * 128) // 1024], tw1[(DFF + f * 128 + 127) // 1024]] if kc == 0 else ())
                    last_mm = tmu
                    sg, sfree, sk = sgr.next()
                    ta = ACT.op(A.activation, out=sg[:], in_=psg[:], func=AF.Silu, waits=[tmg, sfree])
                    pp.done(pkg, ta)
                    tb_ = DVE.op(V.tensor_tensor, out=actT[:, f, :], in0=psu[:], in1=sg[:], op=ALU.mult, waits=[tmu, ta, actT_free])
                    pp.done(pku, tb_)
                    sgr.done(sk, tb_)
                    tact.append(tb_)
                h2T_free = [last_mm]
                actT_free = []
                last_o = None
                for bi in range(4):
                    s = grp * 4 + bi
                    x2, x2free, x2k = x2r.next()
                    tx2 = SP.dma(x2[:], x1_scr[s * 128:(s + 1) * 128, :], waits=[x2free])
                    xtoks = []
                    for nc_ in range(2):
                        ps, pfree, pk = pp.next()
                        tm = None
                        for f in range(NFC):
                            tm = PE.op(T.matmul, ps[:], lhsT=actT[:, f, bi * 128:(bi + 1) * 128], rhs=w2[:, f, nc_ * 512:(nc_ + 1) * 512],
                                       start=(f == 0), stop=(f == NFC - 1), waits=[tact, pfree, tw2] if f == 0 else ())
                        last_o = tm
                        a = DVE.op(V.tensor_tensor, out=x2[:, nc_ * 512:(nc_ + 1) * 512], in0=ps[:], in1=x2[:, nc_ * 512:(nc_ + 1) * 512], op=ALU.add,
                                   waits=[tm, tx2])
                        pp.done(pk, a)
                        xtoks.append(a)
                    e = rms_norm_block((junk, ssr), x2[:], xtoks, gfin[:], t_gfin, 1e-6, D, x2[:], [])
                    td = SP.dma(out[s * 128:(s + 1) * 128, :], x2[:], waits=[e])
                    x2r.done(x2k, td)
                actT_free = [last_o]
            barrier()
    return nc


_NC_CACHE = {}


def _get_nc(debug=False):
    if debug not in _NC_CACHE:
        _NC_CACHE[debug] = build(debug)
    return _NC_CACHE[debug]


def make_in_maps(inputs):
    x = np.ascontiguousarray(np.asarray(inputs["x"], dtype=np.float32))
    pos = np.asarray(inputs["positions"]).astype(np.int32)
    in_maps = []
    kk = np.arange(512)[None, :]
    qq = np.arange(128)[:, None]
    for c in range(8):
        b, j = c // 4, c % 4
        blocks = [4 * s + j for s in range(NQB)]
        xqc = np.concatenate([x[b, q * 128:(q + 1) * 128] for q in blocks], axis=0)
        posk = np.ascontiguousarray(pos[b].reshape(NTB, 128).T)
        posq = np.ascontiguousarray(np.stack([pos[b, q * 128:(q + 1) * 128] for q in blocks], axis=1))
        cm = np.where(kk <= j * 128 + qq, 0.0, NEG).astype(np.float32)
        m = {"xs": x[b], "xq": np.ascontiguousarray(xqc), "posk": posk, "posq": posq, "cmask": cm}
        for name in ["norm_mix_g", "w_in", "idx_k_norm_g", "idx_k_norm_b", "diff_lambda_q1", "diff_lambda_k1",
                     "diff_lambda_q2", "diff_lambda_k2", "diff_subln_g", "gate_b", "w_branch_dsa", "w_branch_diff",
                     "w_out", "norm_ffn_g", "w_ffn_in", "w_ffn_out"]:
            m[name] = np.ascontiguousarray(np.asarray(inputs[name], dtype=np.float32)[0])
        m["norm_final_g"] = np.ascontiguousarray(np.asarray(inputs["norm_final_g"], dtype=np.float32))
        in_maps.append(m)
    return in_maps


def kernel(**inputs):
    nc = _get_nc(False)
    in_maps = make_in_maps(inputs)
    res = run_bass_kernel_spmd(nc, in_maps, core_ids=list(range(8)))
    outp = np.zeros((2, S, D), dtype=np.float32)
    for c in range(8):
        b, j = c // 4, c % 4
        o = res.results[c]["out"]
        for s in range(NQB):
            q = 4 * s + j
            outp[b, q * 128:(q + 1) * 128] = o[s * 128:(s + 1) * 128]
    return outp
```

```python
import os
import math
import numpy as np
from contextlib import ExitStack
import concourse.bass as bass
import concourse.mybir as mybir
from concourse.bass_utils import run_bass_kernel_spmd

F32 = mybir.dt.float32
F32R = mybir.dt.float32r
BF16 = mybir.dt.bfloat16
I32 = mybir.dt.int32
AF = mybir.ActivationFunctionType
ALU = mybir.AluOpType
AX = mybir.AxisListType

S = 8192
D = 1024
NTB = S // 128
NQB = 16
NQ = NQB * 128
DFF = 2816
NFC = DFF // 128
TOPK = 256
NBIS = 16
NEG = -30000.0
C_QA, C_KA, C_VA, C_QI, C_KI, C_WI, C_QB, C_KB, C_VB, C_G = 0, 512, 1024, 1536, 1792, 1824, 1832, 2344, 2856, 3368
D_IN = 5416
VW = 132
TWO_PI = 2.0 * math.pi


class Eng:
    def __init__(self, nc, es, eng, name):
        self.eng = eng
        self.name = name
        self.sem = es.enter_context(nc.semaphore("sem_" + name))
        self.count = 0
        self.seen = {}

    def wait(self, toks):
        for t in toks:
            if t is None:
                continue
            if isinstance(t, list):
                self.wait(t)
                continue
            sem, val, key = t
            if self.seen.get(key, 0) >= val:
                continue
            self.eng.wait_ge(sem, val)
            self.seen[key] = val

    def op(self, fn, *args, waits=(), **kw):
        self.wait(waits)
        inst = fn(*args, **kw)
        self.count += 1
        inst.then_inc(self.sem, 1)
        tok = (self.sem, self.count, self.name)
        self.seen[self.name] = self.count - 1 if False else self.seen.get(self.name, 0)
        return tok


class DmaQ:
    def __init__(self, nc, es, eng, name, nsem=12):
        self.eng = eng
        self.name = name
        self.sems = [es.enter_context(nc.semaphore(f"dsem_{name}_{i}")) for i in range(nsem)]
        self.vals = [0] * nsem
        self.i = 0
        self.seen = {}

    def wait(self, toks):
        for t in toks:
            if t is None:
                continue
            if isinstance(t, list):
                self.wait(t)
                continue
            sem, val, key = t
            if self.seen.get(key, 0) >= val:
                continue
            self.eng.wait_ge(sem, val)
            self.seen[key] = val

    def dma(self, out, in_, waits=(), **kw):
        k = self.i
        self.i = (self.i + 1) % len(self.sems)
        key = f"{self.name}_{k}"
        if self.vals[k] > 0:
            self.wait([(self.sems[k], self.vals[k], key)])
        self.wait(waits)
        self.vals[k] += 16
        self.eng.dma_start(out=out, in_=in_, **kw).then_inc(self.sems[k], 16)
        return (self.sems[k], self.vals[k], key)

    def all_toks(self):
        return [(self.sems[k], self.vals[k], f"{self.name}_{k}") for k in range(len(self.sems)) if self.vals[k] > 0]


class Ring:
    def __init__(self, tiles):
        self.tiles = tiles
        self.rd = [[] for _ in tiles]
        self.i = -1

    def next(self):
        self.i = (self.i + 1) % len(self.tiles)
        k = self.i
        toks = self.rd[k]
        self.rd[k] = []
        return self.tiles[k], toks, k

    def done(self, k, tok):
        self.rd[k].append(tok)


def build(debug=False):
    nc = bass.Bass("TRN2", target_bir_lowering=False)
    dt_in = lambda n, s, d=F32: nc.dram_tensor(n, s, d, kind="ExternalInput").ap()
    xs = dt_in("xs", [S, D])
    xq = dt_in("xq", [NQ, D])
    posk = dt_in("posk", [128, NTB], I32)
    posq = dt_in("posq", [128, NQB], I32)
    cmask = dt_in("cmask", [128, 512])
    norm_mix_g = dt_in("norm_mix_g", [D])
    w_in = dt_in("w_in", [D, D_IN])
    idx_g = dt_in("idx_k_norm_g", [32])
    idx_b = dt_in("idx_k_norm_b", [32])
    lq1 = dt_in("diff_lambda_q1", [64])
    lk1 = dt_in("diff_lambda_k1", [64])
    lq2 = dt_in("diff_lambda_q2", [64])
    lk2 = dt_in("diff_lambda_k2", [64])
    subln_g = dt_in("diff_subln_g", [128])
    gate_b = dt_in("gate_b", [2048])
    w_bd = dt_in("w_branch_dsa", [512, D])
    w_bf = dt_in("w_branch_diff", [512, D])
    w_out = dt_in("w_out", [D, D])
    norm_ffn_g = dt_in("norm_ffn_g", [D])
    w_f1 = dt_in("w_ffn_in", [D, 2 * DFF])
    w_f2 = dt_in("w_ffn_out", [DFF, D])
    norm_fin_g = dt_in("norm_final_g", [D])
    out = nc.dram_tensor("out", [NQ, D], F32, kind="ExternalOutput").ap()
    skind = "ExternalOutput" if debug else "Internal"
    kT_scr = nc.dram_tensor("kT_scr", [8, 128, S], BF16, kind=skind).ap()
    kiT_scr = nc.dram_tensor("kiT_scr", [32, S], BF16, kind=skind).ap()
    v_scr = nc.dram_tensor("v_scr", [S, 8 * VW], BF16, kind=skind).ap()
    qT_scr = nc.dram_tensor("qT_scr", [8, 128, NQ], BF16, kind=skind).ap()
    qiT_scr = nc.dram_tensor("qiT_scr", [64, 4, NQ], BF16, kind=skind).ap()
    o_scr = nc.dram_tensor("o_scr", [NQ, D], BF16, kind=skind).ap()
    x1_scr = nc.dram_tensor("x1_scr", [NQ, D], F32, kind=skind).ap()

    with ExitStack() as es:
        uid = [0]

        def sbt(st, n, s, d):
            uid[0] += 1
            return st.enter_context(nc.sbuf_tensor(f"{n}_{uid[0]}", s, d))

        def pst(st, n, s, d):
            uid[0] += 1
            return st.enter_context(nc.psum_tensor(f"{n}_{uid[0]}", s, d))

        ident = sbt(es, "ident", [128, 128], BF16)
        early = ExitStack()
        cosk = sbt(early, "cosk", [128, NTB, 8], F32)
        sink = sbt(early, "sink", [128, NTB, 8], F32)
        cosq = sbt(early, "cosq", [128, NQB, 8], F32)
        sinq = sbt(early, "sinq", [128, NQB, 8], F32)
        gmix = sbt(early, "gmix", [128, D], F32)
        wabs = sbt(early, "wabs", [128, NQB, 8], F32)
        wsgn = sbt(early, "wsgn", [128, NQB, 8], F32)
        nlam = sbt(early, "nlam", [128, 1], F32)
        small = sbt(early, "small", [128, 64], F32)
        cb = sbt(early, "cb", [128, 512], BF16)
        cbf = sbt(early, "cbf", [128, 512], F32)

        es.enter_context(nc.Block())
        PE = Eng(nc, es, nc.tensor, "pe")
        ACT = Eng(nc, es, nc.scalar, "act")
        DVE = Eng(nc, es, nc.vector, "dve")
        POOL = Eng(nc, es, nc.gpsimd, "pool")
        SP = DmaQ(nc, es, nc.sync, "sp", 16)
        GQ = DmaQ(nc, es, nc.gpsimd, "gq", 8)
        V = nc.vector
        A = nc.scalar
        T = nc.tensor

        def bcast(ap1d, n=128):
            return ap1d.partition_broadcast(n)

        def barrier():
            toks = [(e.sem, e.count, e.name) for e in (PE, ACT, DVE, POOL) if e.count > 0]
            toks += SP.all_toks() + GQ.all_toks()
            for e in (PE, ACT, DVE, POOL, SP):
                e.wait(toks)

        t = POOL.op(nc.gpsimd.memset, ident[:], 1.0)
        t_ident = POOL.op(nc.gpsimd.affine_select, out=ident[:], in_=ident[:], pattern=[[-1, 128]],
                          compare_op=ALU.is_equal, fill=0.0, base=0, channel_multiplier=1, waits=[t])
        t_gmix = SP.dma(gmix[:], bcast(norm_mix_g))
        t_cbf = SP.dma(cbf[:], cmask)
        t_cb = DVE.op(V.tensor_copy, out=cb[:], in_=cbf[:], waits=[t_cbf])

        with ExitStack() as p0:
            lt = sbt(p0, "lt", [128, 4, 64], F32)
            lj = sbt(p0, "lj", [128, 64], F32)
            tl = [SP.dma(lt[:, i, :], bcast(a)) for i, a in enumerate([lq1, lk1, lq2, lk2])]
            t1 = DVE.op(V.tensor_tensor, out=lj[:], in0=lt[:, 0, :], in1=lt[:, 1, :], op=ALU.mult, waits=tl)
            t1 = DVE.op(V.tensor_reduce, out=small[:, 0:1], in_=lj[:], axis=AX.X, op=ALU.add, waits=[t1])
            t2 = DVE.op(V.tensor_tensor, out=lj[:], in0=lt[:, 2, :], in1=lt[:, 3, :], op=ALU.mult, waits=[t1])
            t2 = DVE.op(V.tensor_reduce, out=small[:, 1:2], in_=lj[:], axis=AX.X, op=ALU.add, waits=[t2])
            t3 = ACT.op(A.activation, out=small[:, 2:4], in_=small[:, 0:2], func=AF.Exp, waits=[t2])
            t_nlam = DVE.op(V.scalar_tensor_tensor, out=nlam[:], in0=small[:, 3:4], scalar=-0.2, in1=small[:, 2:3],
                            op0=ALU.add, op1=ALU.subtract, waits=[t3])

            invf = sbt(p0, "invf", [128, 8], F32)
            tinv = None
            for i in range(8):
                fv = float(np.power(np.float32(500000.0), -np.float32(2 * i) / np.float32(16)))
                tinv = DVE.op(V.memset, invf[:, i:i + 1], fv)

            def rope_table(pos_ap, n, cos_t, sin_t, nm):
                pi_ = sbt(p0, "pi_" + nm, [128, n], I32)
                pf = sbt(p0, "pf_" + nm, [128, n], F32)
                ang = sbt(p0, "ang_" + nm, [128, n, 8], F32)
                yy = sbt(p0, "yy_" + nm, [128, n, 8], F32)
                ni = sbt(p0, "ni_" + nm, [128, n, 8], I32)
                tp = SP.dma(pi_[:], pos_ap)
                a = DVE.op(V.tensor_copy, out=pf[:], in_=pi_[:], waits=[tp])
                a = DVE.op(V.tensor_tensor, out=ang[:], in0=pf[:].unsqueeze(2).to_broadcast([128, n, 8]),
                           in1=invf[:].unsqueeze(1).to_broadcast([128, n, 8]), op=ALU.mult, waits=[a, tinv])

                def reduce_sin(src_add, dst):
                    b = DVE.op(V.tensor_scalar, out=yy[:], in0=ang[:], scalar1=src_add, scalar2=1.0 / TWO_PI,
                               op0=ALU.add, op1=ALU.mult, waits=[a])
                    b = DVE.op(V.tensor_copy, out=ni[:], in_=yy[:], waits=[b])
                    b = DVE.op(V.tensor_copy, out=yy[:], in_=ni[:], waits=[b])
                    c1 = 6.28125
                    c2 = TWO_PI - 6.28125
                    b = DVE.op(V.scalar_tensor_tensor, out=dst, in0=yy[:], scalar=-c1, in1=ang[:], op0=ALU.mult, op1=ALU.add, waits=[b])
                    b = DVE.op(V.scalar_tensor_tensor, out=dst, in0=yy[:], scalar=-c2, in1=dst, op0=ALU.mult, op1=ALU.add, waits=[b])
                    b = DVE.op(V.tensor_scalar, out=dst, in0=dst, scalar1=src_add, scalar2=3.1415925, op0=ALU.add, op1=ALU.min, waits=[b])
                    b = DVE.op(V.tensor_scalar, out=dst, in0=dst, scalar1=-3.1415925, scalar2=None, op0=ALU.max, waits=[b])
                    return ACT.op(A.activation, out=dst, in_=dst, func=AF.Sin, waits=[b])
                ts = reduce_sin(0.0, sin_t[:])
                tc = reduce_sin(math.pi / 2.0, cos_t[:])
                return [ts, tc]
            t_ropek = rope_table(posk, NTB, cosk, sink, "k")
            t_ropeq = rope_table(posq, NQB, cosq, sinq, "q")
            barrier()

        def rms_norm_block(st_rings, x_tile, tx, g_tile, tg, eps, n_feat, hb_tile, hb_free):
            junk, ssr = st_rings
            jt, jfree, jk = junk.next()
            col, cfree, ck = ssr.next()
            a = ACT.op(A.activation, out=jt[:, :n_feat], in_=x_tile, func=AF.Square, accum_out=col[:, 0:1],
                       waits=[tx, jfree, cfree])
            junk.done(jk, a)
            b = DVE.op(V.tensor_scalar, out=col[:, 1:2], in0=col[:, 0:1], scalar1=1.0 / n_feat, scalar2=eps,
                       op0=ALU.mult, op1=ALU.add, waits=[a])
            c = ACT.op(A.activation, out=col[:, 2:3], in_=col[:, 1:2], func=AF.Sqrt, waits=[b])
            d = DVE.op(V.reciprocal, out=col[:, 3:4], in_=col[:, 2:3], waits=[c])
            e = DVE.op(V.scalar_tensor_tensor, out=hb_tile, in0=x_tile, scalar=col[:, 3:4], in1=g_tile,
                       op0=ALU.mult, op1=ALU.mult, waits=[d, tg, hb_free])
            ssr.done(ck, e)
            return e

        rope_last = []

        def rope_apply(tile3, H, half, cs, sn, tmp, waits):
            x1 = tile3[:, :, 0:half]
            x2 = tile3[:, :, half:2 * half]
            cB = cs.unsqueeze(1).to_broadcast([128, H, half])
            sB = sn.unsqueeze(1).to_broadcast([128, H, half])
            tv = lambda i: tmp[:, i, 0:H * half].rearrange("p (h d) -> p h d", h=H)
            waits = list(waits) + rope_last
            a1 = DVE.op(V.tensor_tensor, out=tv(0), in0=x1, in1=cB, op=ALU.mult, waits=waits)
            a2 = DVE.op(V.tensor_tensor, out=tv(1), in0=x2, in1=sB, op=ALU.mult, waits=waits)
            a3 = DVE.op(V.tensor_tensor, out=tv(2), in0=x2, in1=cB, op=ALU.mult, waits=waits)
            a4 = DVE.op(V.tensor_tensor, out=tv(3), in0=x1, in1=sB, op=ALU.mult, waits=waits)
            b1 = DVE.op(V.tensor_tensor, out=x1, in0=tv(0), in1=tv(1), op=ALU.subtract, waits=[a1, a2, a3, a4])
            b2 = DVE.op(V.tensor_tensor, out=x2, in0=tv(2), in1=tv(3), op=ALU.add, waits=[a1, a2, a3, a4])
            rope_last[:] = [b1, b2]
            return [b1, b2]

        with ExitStack() as p1:
            NKV = 2080
            NQC = 1288
            wkv = sbt(p1, "wkv", [128, 8, NKV], BF16)
            wq = sbt(p1, "wq", [128, 8, NQC], BF16)
            w_in_v = w_in.rearrange("(kc p) n -> p kc n", p=128)
            tw = []
            for (dst0, c0, n) in [(0, C_KA, 512), (512, C_KB, 512), (1024, C_VA, 512), (1536, C_VB, 512), (2048, C_KI, 32)]:
                tw.append(GQ.dma(wkv[:, :, dst0:dst0 + n], w_in_v[:, :, c0:c0 + n]))
            twq = []
            for (dst0, c0, n) in [(0, C_QA, 512), (512, C_QB, 512), (1024, C_QI, 256), (1280, C_WI, 8)]:
                twq.append(GQ.dma(wq[:, :, dst0:dst0 + n], w_in_v[:, :, c0:c0 + n]))
            lng = sbt(p1, "lng", [128, 32], F32)
            lnb = sbt(p1, "lnb", [128, 32], F32)
            t_lng = SP.dma(lng[:], bcast(idx_g))
            t_lnb = SP.dma(lnb[:], bcast(idx_b))

            xr = Ring([sbt(p1, f"xt{i}", [128, D], F32) for i in range(3)])
            junk = Ring([sbt(p1, f"junk{i}", [128, D], BF16) for i in range(1)])
            ssr = Ring([sbt(p1, f"ss{i}", [128, 4], F32) for i in range(4)])
            hbr = Ring([sbt(p1, f"hb{i}", [128, D], BF16) for i in range(3)])
            hTr = Ring([sbt(p1, f"hT{i}", [128, 8, 128], BF16) for i in range(3)])
            kfr = Ring([sbt(p1, f"kf{i}", [128, 1024], F32) for i in range(2)])
            kbr = Ring([sbt(p1, f"kb{i}", [128, 1024], BF16) for i in range(3)])
            kTr = Ring([sbt(p1, f"kTt{i}", [128, 8, 128], BF16) for i in range(2)])
            vtr = Ring([sbt(p1, f"vt{i}", [128, 8, VW], BF16) for i in range(2)])
            rtmp = sbt(p1, "rtmp", [128, 4, 128], F32)
            kif = sbt(p1, "kif", [128, 8], F32)
            kic = sbt(p1, "kic", [128, 32], F32)
            kij = sbt(p1, "kij", [128, 32], F32)
            kibr = Ring([sbt(p1, f"kib{i}", [128, 32], BF16) for i in range(3)])
            kiTr = Ring([sbt(p1, f"kiT{i}", [32, 128], BF16) for i in range(2)])
            qwr = Ring([sbt(p1, f"qw{i}", [128, 264], F32) for i in range(2)])
            qibr = Ring([sbt(p1, f"qib{i}", [128, 256], BF16) for i in range(3)])
            qiTr = Ring([sbt(p1, f"qiT{i}", [32, 8, 128], BF16) for i in range(2)])
            psT = Ring([pst(p1, f"psT{i}", [128, D], BF16) for i in range(2)])
            pp = Ring([pst(p1, f"pp{i}", [128, 512], F32) for i in range(4)])
            pkT = Ring([pst(p1, f"pkT{i}", [128, D], BF16) for i in range(2)])
            t_vinit = []
            for vt_ in vtr.tiles:
                t0 = POOL.op(nc.gpsimd.memset, vt_[:], 0.0)
                t1_ = POOL.op(nc.gpsimd.memset, vt_[:, 0:4, :].rearrange("p a (s e) -> p (a s) e", s=2)[:, :, 64:65], 1.0, waits=[t0])
                t_vinit.append(POOL.op(nc.gpsimd.memset, vt_[:, 4:8, 128:129], 1.0, waits=[t0, t1_]))

            def aevac(out_ap, in_ap, waits):
                return ACT.op(A.copy, out=out_ap, in_=in_ap, waits=waits)

            def stageA1(src_rows):
                xt, xfree, xk = xr.next()
                tx = SP.dma(xt[:], src_rows, waits=xfree)
                hb, hfree, hk = hbr.next()
                th = rms_norm_block((junk, ssr), xt[:], tx, gmix[:], t_gmix, 1e-6, D, hb[:], hfree)
                xr.done(xk, th)
                return hb, th, hk

            def stageA2(hb, th, hk):
                ps, pfree, pk = psT.next()
                tt = None
                for kc in range(8):
                    tt = PE.op(T.transpose, ps[:, kc * 128:(kc + 1) * 128], hb[:, kc * 128:(kc + 1) * 128], ident[:],
                               waits=[th, t_ident, pfree] if kc == 0 else ())
                hbr.done(hk, tt)
                hT, tfree, tk = hTr.next()
                te = aevac(hT[:].rearrange("p a b -> p (a b)"), ps[:], [tt, tfree])
                psT.done(pk, te)
                return hT, te, tk

            def project(hT, th, w_tile, c0, n, wtoks):
                ps, pfree, pk = pp.next()
                tt = None
                for kc in range(8):
                    tt = PE.op(T.matmul, ps[:, 0:n], lhsT=hT[:, kc, :], rhs=w_tile[:, kc, c0:c0 + n],
                               start=(kc == 0), stop=(kc == 7), waits=[th, pfree, wtoks] if kc == 0 else ())
                return ps, tt, pk

            def transpose_out(src_bf, tsrc, nchunk, width, ring_sb, dst_dram):
                ps, pfree, pk = pkT.next()
                tt = None
                for c in range(nchunk):
                    tt = PE.op(T.transpose, ps[0:width, c * 128:(c + 1) * 128], src_bf[:, c * width:(c + 1) * width], ident[:],
                               waits=[tsrc, pfree, t_ident] if c == 0 else ())
                sbT, sfree, sk = ring_sb.next()
                te = aevac(sbT[:].rearrange("p a b -> p (a b)") if len(sbT.shape) == 3 else sbT[:],
                           ps[0:width, 0:nchunk * 128], [tt, sfree])
                pkT.done(pk, te)
                td = GQ.dma(dst_dram, sbT[:], waits=[te])
                ring_sb.done(sk, td)
                return tt

            def transpose_out_qi(src_bf, tsrc, s):
                ps, pfree, pk = pkT.next()
                tt = None
                for c in range(8):
                    tt = PE.op(T.transpose, ps[0:32, c * 128:(c + 1) * 128], src_bf[:, c * 32:(c + 1) * 32], ident[:],
                               waits=[tsrc, pfree, t_ident] if c == 0 else ())
                sbT, sfree, sk = qiTr.next()
                te = aevac(sbT[:].rearrange("p a b -> p (a b)"), ps[0:32, 0:1024], [tt, sfree])
                pkT.done(pk, te)
                for g in range(2):
                    td = GQ.dma(qiT_scr[g * 32:(g + 1) * 32, :, s * 128:(s + 1) * 128], sbT[:, g::2, :], waits=[te])
                    qiTr.done(sk, td)
                return tt

            def qk_pair(hT, th, w_tile, wtoks, cs, sn, scale, dst_dram):
                kf, kfree, kk = kfr.next()
                tes = []
                for gi in range(2):
                    ps, tmm, pk = project(hT, th, w_tile, gi * 512, 512, wtoks)
                    te = aevac(kf[:, gi * 512:(gi + 1) * 512], ps[:], [tmm, kfree])
                    pp.done(pk, te)
                    tes.append(te)
                tr = rope_apply(kf[:].rearrange("p (h d) -> p h d", h=16), 16, 8, cs, sn, rtmp, tes)
                kb, bfree, bk = kbr.next()
                if scale == 1.0:
                    tcst = DVE.op(V.tensor_copy, out=kb[:], in_=kf[:], waits=[tr, bfree])
                else:
                    tcst = DVE.op(V.tensor_scalar, out=kb[:], in0=kf[:], scalar1=scale, scalar2=None, op0=ALU.mult, waits=[tr, bfree])
                kfr.done(kk, tcst)
                return kb, bk, tcst

            def stageB_kv(tb, hT, th, hk):
                cs = cosk[:, tb, :]
                sn = sink[:, tb, :]
                kb, bk, tcst = qk_pair(hT, th, wkv, tw, cs, sn, 1.0, None)
                vt, vfree, vk = vtr.next()
                tvs = []
                for gi in (2, 3):
                    ps, tmm, pk = project(hT, th, wkv, gi * 512, 512, tw)
                    if gi == 2:
                        dstv = vt[:, 0:4, :].rearrange("p a (s e) -> p (a s) e", s=2)[:, :, 0:64]
                        te = aevac(dstv, ps[:].rearrange("p (h e) -> p h e", e=64), [tmm, vfree, t_vinit])
                    else:
                        te = aevac(vt[:, 4:8, 0:128], ps[:].rearrange("p (h e) -> p h e", e=128), [tmm, vfree, t_vinit])
                    pp.done(pk, te)
                    tvs.append(te)
                td = GQ.dma(v_scr[tb * 128:(tb + 1) * 128, :], vt[:].rearrange("p a b -> p (a b)"), waits=tvs)
                vtr.done(vk, td)
                ps, tmm, pk = project(hT, th, wkv, 2048, 32, tw)
                hTr.done(hk, tmm)
                a = DVE.op(V.tensor_reduce, out=kif[:, 0:1], in_=ps[:, 0:32], axis=AX.X, op=ALU.add, waits=[tmm])
                a = DVE.op(V.tensor_scalar, out=kif[:, 1:2], in0=kif[:, 0:1], scalar1=1.0 / 32, scalar2=None, op0=ALU.mult, waits=[a])
                a = DVE.op(V.tensor_scalar, out=kic[:], in0=ps[:, 0:32], scalar1=kif[:, 1:2], scalar2=None, op0=ALU.subtract, waits=[a])
                pp.done(pk, a)
                b = ACT.op(A.activation, out=kij[:], in_=kic[:], func=AF.Square, accum_out=kif[:, 2:3], waits=[a])
                b = DVE.op(V.tensor_scalar, out=kif[:, 3:4], in0=kif[:, 2:3], scalar1=1.0 / 32, scalar2=1e-6, op0=ALU.mult, op1=ALU.add, waits=[b])
                b = ACT.op(A.activation, out=kif[:, 4:5], in_=kif[:, 3:4], func=AF.Sqrt, waits=[b])
                b = DVE.op(V.reciprocal, out=kif[:, 5:6], in_=kif[:, 4:5], waits=[b])
                b = DVE.op(V.scalar_tensor_tensor, out=kic[:], in0=kic[:], scalar=kif[:, 5:6], in1=lng[:], op0=ALU.mult, op1=ALU.mult, waits=[b, t_lng])
                b = DVE.op(V.tensor_tensor, out=kic[:], in0=kic[:], in1=lnb[:], op=ALU.add, waits=[b, t_lnb])
                csi = cosk[:, tb, :].rearrange("p (a two) -> p a two", two=2)[:, :, 0]
                sni = sink[:, tb, :].rearrange("p (a two) -> p a two", two=2)[:, :, 0]
                tr = rope_apply(kic[:].rearrange("p (h d) -> p h d", h=1), 1, 4, csi, sni, rtmp, [b])
                kib, bfree, bk2 = kibr.next()
                tc2 = DVE.op(V.tensor_copy, out=kib[:], in_=kic[:], waits=[tr, bfree])

                def b2():
                    tlast = transpose_out(kb, tcst, 8, 128, kTr, kT_scr[:, :, tb * 128:(tb + 1) * 128].rearrange("c f t -> f c t"))
                    kbr.done(bk, tlast)
                    tl2 = transpose_out(kib, tc2, 1, 32, kiTr, kiT_scr[:, tb * 128:(tb + 1) * 128])
                    kibr.done(bk2, tl2)
                return b2

            def stageB_q(s, hT, th, hk):
                cs = cosq[:, s, :]
                sn = sinq[:, s, :]
                kb, bk, tcst = qk_pair(hT, th, wq, twq, cs, sn, 0.125, None)
                ps, tmm, pk = project(hT, th, wq, 1024, 264, twq)
                hTr.done(hk, tmm)
                qw, qfree, qk = qwr.next()
                te = aevac(qw[:], ps[:, 0:264], [tmm, qfree])
                pp.done(pk, te)
                csi = cosq[:, s, :].rearrange("p (a two) -> p a two", two=2)[:, :, 0]
                sni = sinq[:, s, :].rearrange("p (a two) -> p a two", two=2)[:, :, 0]
                tr = rope_apply(qw[:, 0:256].rearrange("p (h d) -> p h d", h=8), 8, 4, csi, sni, rtmp, [te])
                a1 = DVE.op(V.tensor_scalar, out=wabs[:, s, :], in0=qw[:, 256:264], scalar1=1.0 / 16, scalar2=None, op0=ALU.mult, waits=[te])
                a2 = a1
                qib, bfree, bk2 = qibr.next()
                tc2 = DVE.op(V.tensor_copy, out=qib[:], in_=qw[:, 0:256], waits=[tr, bfree])
                qwr.done(qk, [tc2, a1, a2])

                def b2():
                    tlast = transpose_out(kb, tcst, 8, 128, kTr, qT_scr[:, :, s * 128:(s + 1) * 128].rearrange("c f t -> f c t"))
                    kbr.done(bk, tlast)
                    tl2 = transpose_out_qi(qib, tc2, s)
                    qibr.done(bk2, tl2)
                return b2

            items = [("kv", tb, xs[tb * 128:(tb + 1) * 128, :]) for tb in range(NTB)] + \
                    [("q", s, xq[s * 128:(s + 1) * 128, :]) for s in range(NQB)]
            n_it = len(items)
            a1 = {}
            a2 = {}
            a1[0] = stageA1(items[0][2])
            a2[0] = stageA2(*a1[0])
            if n_it > 1:
                a1[1] = stageA1(items[1][2])
            prev_b2 = None
            for i in range(n_it):
                if i + 2 < n_it:
                    a1[i + 2] = stageA1(items[i + 2][2])
                kind, idx, _ = items[i]
                hT, th, hk = a2.pop(i)
                if kind == "kv":
                    b2 = stageB_kv(idx, hT, th, hk)
                else:
                    b2 = stageB_q(idx, hT, th, hk)
                if i + 1 < n_it:
                    a2[i + 1] = stageA2(*a1.pop(i + 1))
                if prev_b2 is not None:
                    prev_b2()
                prev_b2 = b2
            if prev_b2 is not None:
                prev_b2()
            barrier()

        with ExitStack() as p2:
            kiT = sbt(p2, "kiT", [64, S], BF16)
            t_kiT = [SP.dma(kiT[g * 32:(g + 1) * 32, :], kiT_scr) for g in range(2)]
            gsub = sbt(p2, "gsub", [128, 128], F32)
            t_gs = SP.dma(gsub[:], bcast(subln_g))
            t_gs = DVE.op(V.tensor_scalar, out=gsub[:], in0=gsub[:], scalar1=0.8, scalar2=None, op0=ALU.mult, waits=[t_gs])
            ident2 = sbt(p2, "ident2", [128, 2, 128], BF16)
            t_id2 = [DVE.op(V.tensor_copy, out=ident2[:, i, :], in_=ident[:], waits=[t_ident]) for i in range(2)]
            Kr = Ring([sbt(p2, f"Kb{i}", [128, S], BF16) for i in range(2)])
            Vr = Ring([sbt(p2, f"Vb{i}", [128, NTB, VW], BF16) for i in range(2)])
            Mb = [sbt(p2, f"Mb{i}", [128, S], BF16) for i in range(2)]
            Isc = sbt(p2, "Isc", [128, S], F32)
            qbd_tiles = [sbt(p2, f"qbd{i}", [128, 2, 128], BF16) for i in range(3)]
            t_qz = [POOL.op(nc.gpsimd.memset, q_[:], 0.0) for q_ in qbd_tiles]
            qbr = Ring(qbd_tiles)
            qiT = [sbt(p2, f"qiTs{i}", [64, 4, 128], BF16) for i in range(2)]
            rl = Ring([sbt(p2, f"rl{i}", [128, 512], BF16) for i in range(8)])
            identf = sbt(p2, "identf", [128, 128], F32)
            t_idf = DVE.op(V.tensor_copy, out=identf[:], in_=ident[:], waits=[t_ident])
            dgb = [sbt(p2, f"dgb{i}", [128, 8, 128], BF16) for i in range(2)]
            etr = Ring([sbt(p2, f"et{i}", [128, 512], BF16) for i in range(4)])
            otr = Ring([sbt(p2, f"ot{i}", [128, D], BF16) for i in range(2)])
            of32 = sbt(p2, "of32", [128, 128], F32)
            oj = sbt(p2, "oj", [128, 128], F32)
            bs = sbt(p2, "bs", [128, 16], F32)
            es_ = sbt(p2, "es_", [128, 16], F32)
            pss = Ring([pst(p2, f"pss{i}", [128, 512], F32) for i in range(4)])
            pI = pst(p2, "pI", [128, 512], F32)
            pacc = Ring([pst(p2, f"pacc{i}", [128, 512], F32) for i in range(3)])
            ofr = Ring([(sbt(p2, f"of1_{i}", [128, 128], F32), sbt(p2, f"of2_{i}", [128, 128], F32)) for i in range(4)])
            mhalf = sbt(p2, "mhalf", [128, 1], F32)
            t_mh = POOL.op(nc.gpsimd.memset, mhalf[:], -0.5)
            ez = sbt(p2, "ez", [128, 8], F32)
            Mb_ready = [None, None]
            Mb_readers = [[], []]
            qiT_readers = [[], []]
            dg_readers = [[], []]
            Isc_free = [[]]
            pI_free = [[]]

            def prep(s):
                li = s % 2
                nk = (4 * s + 4) * 128
                nch = nk // 512
                wb = 0.76 * (s + 1)
                tq = SP.dma(qiT[li][:], qiT_scr[:, :, s * 128:(s + 1) * 128], waits=qiT_readers[li])
                qiT_readers[li] = []
                tdg = None
                for h in range(8):
                    tdg = ACT.op(A.activation, out=dgb[li][:, h, :], in_=identf[:], func=AF.Copy, scale=wabs[:, s, h:h + 1],
                                 waits=[t_idf, dg_readers[li]] if h == 0 else ())
                dg_readers[li] = []
                yield 1.0
                tI = None
                pending = None

                def flush(pend):
                    (pc, pr, items) = pend
                    tacc = None
                    for g, (prt, pta, prk) in enumerate(items):
                        h = 2 * pr + g
                        tacc = PE.op(T.matmul, pI[:], lhsT=dgb[li][:, h, :], rhs=prt[:],
                                     start=(pr == 0 and g == 0), stop=(pr == 3 and g == 1),
                                     waits=[pta, tdg, pI_free[0]])
                        rl.done(prk, tacc)
                    return tacc
                pendq = []

                def drain(keep):
                    nonlocal tI
                    tacc = None
                    while len(pendq) > keep:
                        pend = pendq.pop(0)
                        tacc = flush(pend)
                        if pend[1] == 3:
                            pc = pend[0]
                            tI = ACT.op(A.copy, out=Isc[:, pc * 512:(pc + 1) * 512], in_=pI[:], waits=[tacc, Isc_free[0]])
                            pI_free[0] = [tI]
                    return tacc
                for c in range(nch):
                    for r in range(4):
                        drain(1)
                        mm = []
                        for g in range(2):
                            ps, pfree, pk = pss.next()
                            tm = PE.op(T.matmul, ps[:], lhsT=qiT[li][g * 32:(g + 1) * 32, r, :], rhs=kiT[g * 32:(g + 1) * 32, c * 512:(c + 1) * 512],
                                       start=True, stop=True, waits=[tq, t_kiT, pfree])
                            mm.append((ps, pk, tm))
                        items = []
                        for g in range(2):
                            ps, pk, tm = mm[g]
                            rt, rfree, rk = rl.next()
                            if g == 0:
                                ta = ACT.op(A.activation, out=rt[:], in_=ps[:], func=AF.Relu, waits=[tm, rfree])
                            else:
                                ta = DVE.op(V.tensor_scalar, out=rt[:], in0=ps[:], scalar1=0.0, scalar2=None, op0=ALU.max, waits=[tm, rfree])
                            pss.done(pk, ta)
                            items.append((rt, ta, rk))
                        pendq.append((c, r, items))
                        yield 0.5
                tacc = drain(0)
                qiT_readers[li].append(tacc)
                dg_readers[li].append(tacc)
                Isc_free[0] = []
                Iv = Isc[:, 0:nk]
                a = DVE.op(V.tensor_reduce, out=bs[:, 0:1], in_=Iv, axis=AX.X, op=ALU.max, waits=[tI])
                yield wb
                a = DVE.op(V.tensor_reduce, out=bs[:, 1:2], in_=Iv, axis=AX.X, op=ALU.min, waits=[a])
                a = DVE.op(V.scalar_tensor_tensor, out=bs[:, 2:3], in0=bs[:, 0:1], scalar=1.0, in1=bs[:, 1:2], op0=ALU.add, op1=ALU.subtract, waits=[a])
                a = DVE.op(V.tensor_tensor, out=Isc[:, nk - 512:nk], in0=Isc[:, nk - 512:nk], in1=cbf[:], op=ALU.add, waits=[a, t_cbf])
                yield wb
                lo = bs[:, 1:2]
                w0 = bs[:, 2:3]
                mid = bs[:, 3:4]
                cnt = bs[:, 4:5]
                gg = bs[:, 5:6]
                for it in range(1, NBIS + 1):
                    sc = 2.0 ** (-it)
                    a = DVE.op(V.tensor_scalar, out=mid, in0=w0, scalar1=sc, scalar2=lo, op0=ALU.mult, op1=ALU.add, waits=[a])
                    a = DVE.op(V.tensor_scalar, out=Mb[li][:, 0:nk], in0=Iv, scalar1=mid, scalar2=None, op0=ALU.is_ge, op1=ALU.add,
                               accum_out=cnt, waits=[a, Mb_readers[li]])
                    Mb_readers[li] = []
                    a = DVE.op(V.tensor_scalar, out=gg, in0=cnt, scalar1=TOPK - 0.5, scalar2=sc, op0=ALU.is_ge, op1=ALU.mult, waits=[a])
                    a = DVE.op(V.scalar_tensor_tensor, out=lo, in0=w0, scalar=gg, in1=lo, op0=ALU.mult, op1=ALU.add, waits=[a])
                    yield wb
                a = DVE.op(V.tensor_scalar, out=Mb[li][:, 0:nk], in0=Iv, scalar1=lo, scalar2=NEG, op0=ALU.is_lt, op1=ALU.mult, waits=[a])
                Mb_ready[li] = a
                Isc_free[0] = [a]

            def prep_steps(s):
                return 1 + 2 * (s + 1) + (2 + NBIS) * 0.76 * (s + 1)

            pump_state = {"gen": None, "budget": 0.0, "rate": 0.0}

            def pump():
                st = pump_state
                if st["gen"] is None:
                    return
                st["budget"] += st["rate"]
                while st["budget"] > 0.0 and st["gen"] is not None:
                    try:
                        st["budget"] -= next(st["gen"])
                    except StopIteration:
                        st["gen"] = None

            def attention(s, p, ot, ofree, owr):
                li = s % 2
                is_dsa = p < 4
                nkb = 4 * s + 4
                kmax = nkb * 128
                Kb, kfree, kk = Kr.next()
                Vb, vfree, vk = Vr.next()
                tK = SP.dma(Kb[:, 0:kmax], kT_scr[p, :, 0:kmax], waits=kfree)
                tV = []
                for b0 in range(0, nkb, 8):
                    nb = min(8, nkb - b0)
                    tV.append(SP.dma(Vb[:, b0:b0 + nb, :], v_scr[b0 * 128:(b0 + nb) * 128, p * VW:(p + 1) * VW].rearrange("(b t) w -> t b w", t=128), waits=vfree))
                qbd, qfree, qk = qbr.next()
                tq = [SP.dma(qbd[m * 64:(m + 1) * 64, m, :], qT_scr[p, m * 64:(m + 1) * 64, s * 128:(s + 1) * 128], waits=[qfree, t_qz]) for m in range(2)]
                accs = [pacc.next() for m in range(2)]
                vw = 66 if is_dsa else 130
                ntile = nkb // 2
                pend = {}

                def do_qk(ti):
                    ps, pfree, pk = pss.next()
                    tm = None
                    for bi in range(2):
                        kb_ = ti * 2 + bi
                        need_mask = is_dsa or (kb_ >= nkb - 4)
                        tm = PE.op(T.matmul, ps[:, bi * 256:(bi + 1) * 256], lhsT=Kb[:, kb_ * 128:(kb_ + 1) * 128],
                                   rhs=qbd[:].rearrange("p a b -> p (a b)"), start=True, stop=not need_mask,
                                   waits=[tK, tq, pfree] if bi == 0 else ())
                        if need_mask:
                            if is_dsa:
                                ml = Mb[li][:, kb_ * 128:(kb_ + 1) * 128]
                                mw = [Mb_ready[li]]
                            else:
                                cbi = kb_ - (nkb - 4)
                                ml = cb[:, cbi * 128:(cbi + 1) * 128]
                                mw = [t_cb]
                            tm = PE.op(T.matmul, ps[:, bi * 256:(bi + 1) * 256], lhsT=ml, rhs=ident2[:].rearrange("p a b -> p (a b)"),
                                       start=False, stop=True, waits=mw + [t_id2])
                    et, efree, ek = etr.next()
                    te = ACT.op(A.activation, out=et[:], in_=ps[:], func=AF.Exp, waits=[tm, efree])
                    pss.done(pk, te)
                    pend[ti] = (et, te, ek)

                tav = [None, None]

                def do_av(ti):
                    et, te, ek = pend.pop(ti)
                    tm = None
                    first = True
                    for bi in range(2):
                        kb_ = ti * 2 + bi
                        for m in range(2):
                            acc, afree, ak = accs[m]
                            rhs = Vb[:, kb_, m * 66:(m + 1) * 66] if is_dsa else Vb[:, kb_, 0:130]
                            tm = PE.op(T.matmul, acc[:, 0:vw], lhsT=et[:, bi * 256 + m * 128:bi * 256 + (m + 1) * 128], rhs=rhs,
                                       start=(kb_ == 0), stop=(kb_ == nkb - 1),
                                       waits=[te, tV, accs[0][1], accs[1][1]] if first else ())
                            first = False
                            tav[m] = tm
                    etr.done(ek, tm)
                LAG = 2
                for ti in range(ntile):
                    do_qk(ti)
                    if ti >= LAG:
                        do_av(ti - LAG)
                    pump()
                for ti in range(max(0, ntile - LAG), ntile):
                    do_av(ti)
                qbr.done(qk, tav[1])
                if is_dsa:
                    Mb_readers[li].append(tav[1])
                Kr.done(kk, tav[1])
                Vr.done(vk, tav[1])
                if is_dsa:
                    for m in range(2):
                        acc, afree, ak = accs[m]
                        a = ACT.op(A.activation, out=ez[:, m:m + 1], in_=acc[:, 64:65], func=AF.Ln, waits=[tav[1]])
                        a = ACT.op(A.activation, out=ez[:, m:m + 1], in_=ez[:, m:m + 1], func=AF.Exp, scale=-1.0, waits=[a])
                        a = ACT.op(A.activation, out=ot[:, p * 128 + m * 64:p * 128 + (m + 1) * 64], in_=acc[:, 0:64], func=AF.Copy,
                                   scale=ez[:, m:m + 1], waits=[a, ofree])
                        pacc.done(ak, a)
                        owr.append(a)
                else:
                    h = p - 4
                    acc1, _, ak1 = accs[0]
                    acc2, _, ak2 = accs[1]
                    (of1, of2), offree, ofk = ofr.next()
                    a = ACT.op(A.activation, out=ez[:, 2:3], in_=acc1[:, 128:129], func=AF.Ln, waits=[tav[1], offree])
                    a = ACT.op(A.activation, out=ez[:, 2:3], in_=ez[:, 2:3], func=AF.Exp, scale=-1.0, waits=[a])
                    a = ACT.op(A.activation, out=of1[:], in_=acc1[:, 0:128], func=AF.Copy, scale=ez[:, 2:3], waits=[a])
                    pacc.done(ak1, a)
                    b = ACT.op(A.activation, out=ez[:, 3:4], in_=acc2[:, 128:129], func=AF.Ln, waits=[a])
                    b = ACT.op(A.activation, out=ez[:, 3:4], in_=ez[:, 3:4], func=AF.Exp, scale=-1.0, waits=[b])
                    b = ACT.op(A.activation, out=of2[:], in_=acc2[:, 0:128], func=AF.Copy, scale=ez[:, 3:4], waits=[b])
                    pacc.done(ak2, b)
                    b = DVE.op(V.scalar_tensor_tensor, out=of32[:], in0=of2[:], scalar=nlam[:, 0:1], in1=of1[:],
                               op0=ALU.mult, op1=ALU.add, waits=[a, b, t_nlam, of32_free[0]])
                    ofr.done(ofk, b)
                    c = DVE.op(V.scalar_tensor_tensor, out=oj[:], in0=of32[:], scalar=1.0, in1=of32[:], op0=ALU.mult, op1=ALU.mult,
                               accum_out=es_[:, 5:6], waits=[b])
                    c = DVE.op(V.tensor_scalar, out=es_[:, 6:7], in0=es_[:, 5:6], scalar1=1.0 / 128, scalar2=1e-5, op0=ALU.mult, op1=ALU.add, waits=[c])
                    c = POOL.op(nc.gpsimd.tensor_tensor, out=es_[:, 8:9], in0=es_[:, 6:7], in1=mhalf[:], op=ALU.pow, waits=[c, t_mh])
                    c = DVE.op(V.scalar_tensor_tensor, out=ot[:, 512 + h * 128:512 + (h + 1) * 128], in0=of32[:], scalar=es_[:, 8:9],
                               in1=gsub[:], op0=ALU.mult, op1=ALU.mult, waits=[c, t_gs, ofree])
                    of32_free[0] = [c]
                    owr.append(c)

            of_free = [[]]
            of32_free = [[]]
            for _ in prep(0):
                pass
            for s in range(NQB):
                if s + 1 < NQB:
                    pump_state["gen"] = prep(s + 1)
                    pump_state["budget"] = 0.0
                    pump_state["rate"] = prep_steps(s + 1) / float(8 * (2 * s + 2)) * 1.3
                else:
                    pump_state["gen"] = None
                ot, ofree, ok_ = otr.next()
                owr = []
                for pi, p in enumerate([4, 5, 6, 7, 0, 1, 2, 3]):
                    attention(s, p, ot, ofree, owr)
                if pump_state["gen"] is not None:
                    for _ in pump_state["gen"]:
                        pass
                    pump_state["gen"] = None
                td = GQ.dma(o_scr[s * 128:(s + 1) * 128, :], ot[:], waits=owr)
                otr.done(ok_, td)
            barrier()

        def load_w(st, name, src, rows, cols, q, waits=(), order=None):
            nkc = rows // 128
            wt = sbt(st, name, [128, nkc, cols], BF16)
            srcv = src.rearrange("(kc p) n -> p kc n", p=128)
            step = 1024
            starts = list(range(0, cols, step))
            toks = [None] * len(starts)
            for ci in (order if order is not None else range(len(starts))):
                c0 = starts[ci]
                n = min(step, cols - c0)
                toks[ci] = q.dma(wt[:, :, c0:c0 + n], srcv[:, :, c0:c0 + n], waits=waits)
            return wt, toks

        with ExitStack() as p3:
            wg = sbt(p3, "wg", [128, 8, 2048], BF16)
            w_in_v = w_in.rearrange("(kc p) n -> p kc n", p=128)
            twg = [GQ.dma(wg[:, :, c0:c0 + 512], w_in_v[:, :, C_G + c0:C_G + c0 + 512]) for c0 in range(0, 2048, 512)]
            wbd, twbd = load_w(p3, "wbd", w_bd, 512, D, GQ)
            wbf, twbf = load_w(p3, "wbf", w_bf, 512, D, GQ)
            wo, two = load_w(p3, "wo", w_out, D, D, GQ)
            gbt = sbt(p3, "gbt", [128, 2048], F32)
            t_gb = SP.dma(gbt[:], bcast(gate_b))
            xr = Ring([sbt(p3, f"xt{i}", [128, D], F32) for i in range(4)])
            junk = Ring([sbt(p3, f"junk{i}", [128, D], BF16) for i in range(1)])
            ssr = Ring([sbt(p3, f"ss{i}", [128, 4], F32) for i in range(4)])
            hbr = Ring([sbt(p3, f"hb{i}", [128, D], BF16) for i in range(3)])
            hTr = Ring([sbt(p3, f"hT{i}", [128, 8, 128], BF16) for i in range(2)])
            obr = Ring([sbt(p3, f"ob{i}", [128, D], BF16) for i in range(3)])
            oTr = Ring([sbt(p3, f"oT{i}", [128, 8, 128], BF16) for i in range(2)])
            gat = sbt(p3, "gat", [128, 2048], F32)
            mrg = sbt(p3, "mrg", [128, D], F32)
            mrg2 = sbt(p3, "mrg2", [128, D], F32)
            mbr = Ring([sbt(p3, f"mb{i}", [128, D], BF16) for i in range(2)])
            mTr = Ring([sbt(p3, f"mT{i}", [128, 8, 128], BF16) for i in range(2)])
            x1r = Ring([sbt(p3, f"x1t{i}", [128, D], F32) for i in range(2)])
            psT = Ring([pst(p3, f"psT{i}", [128, D], BF16) for i in range(2)])
            pp = Ring([pst(p3, f"pp{i}", [128, 512], F32) for i in range(6)])
            x1_dmas = []

            def transpose8(src_bf, tsrc, dst_ring):
                ps, pfree, pk = psT.next()
                tt = None
                for kc in range(8):
                    tt = PE.op(T.transpose, ps[:, kc * 128:(kc + 1) * 128], src_bf[:, kc * 128:(kc + 1) * 128], ident[:],
                               waits=[tsrc, pfree] if kc == 0 else ())
                dT, dfree, dk = dst_ring.next()
                te = ACT.op(A.copy, out=dT[:].rearrange("p a b -> p (a b)"), in_=ps[:], waits=[tt, dfree])
                psT.done(pk, te)
                return dT, te, dk, tt

            def st_load(s):
                xt, xfree, xk = xr.next()
                tx = SP.dma(xt[:], xq[s * 128:(s + 1) * 128, :], waits=xfree)
                hb, hfree, hk = hbr.next()
                th = rms_norm_block((junk, ssr), xt[:], tx, gmix[:], t_gmix, 1e-6, D, hb[:], hfree)
                ob, ofree, ok_ = obr.next()
                to = SP.dma(ob[:], o_scr[s * 128:(s + 1) * 128, :], waits=[ofree])
                return dict(s=s, xt=xt, xk=xk, hb=hb, th=th, hk=hk, ob=ob, to=to, ok_=ok_)

            def st_T(c):
                hT, te, tk, tt = transpose8(c["hb"], c["th"], hTr)
                hbr.done(c["hk"], tt)
                oT, teo, ok2, tto = transpose8(c["ob"], c["to"], oTr)
                obr.done(c["ok_"], tto)
                c.update(hT=hT, te=te, tk=tk, oT=oT, teo=teo, ok2=ok2)

            def st_X(c):
                hT, te, tk = c["hT"], c["te"], c["tk"]
                oT, teo, ok2 = c["oT"], c["teo"], c["ok2"]
                tg_last = None
                for gc in range(4):
                    ps, pfree, pk = pp.next()
                    tm = None
                    for kc in range(8):
                        tm = PE.op(T.matmul, ps[:], lhsT=hT[:, kc, :], rhs=wg[:, kc, gc * 512:(gc + 1) * 512], start=(kc == 0), stop=(kc == 7),
                                   waits=[te, pfree, twg] if kc == 0 else ())
                    a_ = DVE.op(V.tensor_tensor, out=gat[:, gc * 512:(gc + 1) * 512], in0=ps[:], in1=gbt[:, gc * 512:(gc + 1) * 512], op=ALU.add,
                                waits=[tm, t_gb, gat_free[0]])
                    pp.done(pk, a_)
                    tg_last = ACT.op(A.activation, out=gat[:, gc * 512:(gc + 1) * 512], in_=gat[:, gc * 512:(gc + 1) * 512], func=AF.Sigmoid, waits=[a_])
                    if gc == 3:
                        hTr.done(tk, tm)
                gat_free[0] = []
                mtoks = []
                for br, (wt, twt) in enumerate([(wbd, twbd), (wbf, twbf)]):
                    for nc_ in range(2):
                        ps, pfree, pk = pp.next()
                        tm = None
                        for kc in range(4):
                            tm = PE.op(T.matmul, ps[:], lhsT=oT[:, br * 4 + kc, :], rhs=wt[:, kc, nc_ * 512:(nc_ + 1) * 512], start=(kc == 0), stop=(kc == 3),
                                       waits=[teo, pfree, twt] if kc == 0 else ())
                        dst = (mrg if br == 0 else mrg2)[:, nc_ * 512:(nc_ + 1) * 512]
                        a_ = DVE.op(V.tensor_tensor, out=dst, in0=ps[:], in1=gat[:, br * 1024 + nc_ * 512:br * 1024 + (nc_ + 1) * 512], op=ALU.mult,
                                    waits=[tm, tg_last, mrg_free[0]])
                        pp.done(pk, a_)
                        mtoks.append(a_)
                        if br == 1 and nc_ == 1:
                            oTr.done(ok2, tm)
                gat_free[0] = list(mtoks)
                mb, mfree, mk = mbr.next()
                tmb = DVE.op(V.tensor_tensor, out=mb[:], in0=mrg[:], in1=mrg2[:], op=ALU.add, waits=[mtoks, mfree])
                mrg_free[0] = [tmb]
                c.update(mb=mb, tmb=tmb, mk=mk)

            def st_Y(c):
                s_ = c["s"]
                mT, tem, mk2, ttm = transpose8(c["mb"], c["tmb"], mTr)
                mbr.done(c["mk"], ttm)
                x1t, x1free, x1k = x1r.next()
                xtoks = []
                for nc_ in range(2):
                    ps, pfree, pk = pp.next()
                    tm = None
                    for kc in range(8):
                        tm = PE.op(T.matmul, ps[:], lhsT=mT[:, kc, :], rhs=wo[:, kc, nc_ * 512:(nc_ + 1) * 512], start=(kc == 0), stop=(kc == 7),
                                   waits=[tem, pfree, two] if kc == 0 else ())
                    a_ = DVE.op(V.tensor_tensor, out=x1t[:, nc_ * 512:(nc_ + 1) * 512], in0=ps[:], in1=c["xt"][:, nc_ * 512:(nc_ + 1) * 512], op=ALU.add,
                                waits=[tm, x1free])
                    pp.done(pk, a_)
                    xtoks.append(a_)
                    if nc_ == 1:
                        mTr.done(mk2, tm)
                xr.done(c["xk"], xtoks)
                td = GQ.dma(x1_scr[s_ * 128:(s_ + 1) * 128, :], x1t[:], waits=xtoks)
                x1r.done(x1k, td)
                x1_dmas.append(td)

            gat_free = [[]]
            mrg_free = [[]]
            ctx = {}
            ctx[0] = st_load(0)
            st_T(ctx[0])
            if NQB > 1:
                ctx[1] = st_load(1)
            prevY = None
            for s in range(NQB):
                if s + 2 < NQB:
                    ctx[s + 2] = st_load(s + 2)
                st_X(ctx[s])
                if s + 1 < NQB:
                    st_T(ctx[s + 1])
                if prevY is not None:
                    st_Y(prevY)
                prevY = ctx.pop(s)
            st_Y(prevY)
            barrier()
        early.close()

        with ExitStack() as p4:
            w1, tw1 = load_w(p4, "w1", w_f1, D, 2 * DFF, GQ, order=[0, 2, 3, 1, 4, 5])
            w2, tw2 = load_w(p4, "w2", w_f2, DFF, D, GQ)
            gffn = sbt(p4, "gffn", [128, D], F32)
            gfin = sbt(p4, "gfin", [128, D], F32)
            t_gffn = SP.dma(gffn[:], bcast(norm_ffn_g))
            t_gfin = SP.dma(gfin[:], bcast(norm_fin_g))
            x1r = Ring([sbt(p4, f"x1b{i}", [128, D], F32) for i in range(2)])
            junk = Ring([sbt(p4, f"junk{i}", [128, D], BF16) for i in range(1)])
            ssr = Ring([sbt(p4, f"ss{i}", [128, 4], F32) for i in range(2)])
            hbr = Ring([sbt(p4, f"hb{i}", [128, D], BF16) for i in range(2)])
            h2T = sbt(p4, "h2T", [128, 8, 512], BF16)
            actT = sbt(p4, "actT", [128, NFC, 512], BF16)
            sgr = Ring([sbt(p4, f"sg{i}", [128, 512], F32) for i in range(2)])
            x2r = Ring([sbt(p4, f"x2t{i}", [128, D], F32) for i in range(2)])
            psT = Ring([pst(p4, f"psT{i}", [128, D], BF16) for i in range(2)])
            pp = Ring([pst(p4, f"pp{i}", [128, 512], F32) for i in range(6)])
            h2T_free = []
            actT_free = []
            for grp in range(NQB // 4):
                th2 = []
                for bi in range(4):
                    s = grp * 4 + bi
                    x1t, x1free, x1k = x1r.next()
                    tx = SP.dma(x1t[:], x1_scr[s * 128:(s + 1) * 128, :], waits=[x1free])
                    hb, hfree, hk = hbr.next()
                    th = rms_norm_block((junk, ssr), x1t[:], tx, gffn[:], t_gffn, 1e-6, D, hb[:], hfree)
                    x1r.done(x1k, th)
                    ps, pfree, pk = psT.next()
                    tt = None
                    for kc in range(8):
                        tt = PE.op(T.transpose, ps[:, kc * 128:(kc + 1) * 128], hb[:, kc * 128:(kc + 1) * 128], ident[:],
                                   waits=[th, pfree] if kc == 0 else ())
                    hbr.done(hk, tt)
                    te = ACT.op(A.copy, out=h2T[:, :, bi * 128:(bi + 1) * 128], in_=ps[:].rearrange("p (a b) -> p a b", a=8), waits=[tt, h2T_free])
                    psT.done(pk, te)
                    th2.append(te)
                h2T_free = []
                tact = []
                last_mm = None
                for f in range(NFC):
                    psg, pfree, pkg = pp.next()
                    tmg = None
                    for kc in range(8):
                        tmg = PE.op(T.matmul, psg[:], lhsT=w1[:, kc, f * 128:(f + 1) * 128], rhs=h2T[:, kc, :], start=(kc == 0), stop=(kc == 7),
                                    waits=[th2, pfree, tw1[(f * 128) // 1024], tw1[(f * 128 + 127) // 1024]] if kc == 0 else ())
                    psu, pfree, pku = pp.next()
                    tmu = None
                    for kc in range(8):
                        tmu = PE.op(T.matmul, psu[:], lhsT=w1[:, kc, DFF + f * 128:DFF + (f + 1) * 128], rhs=h2T[:, kc, :], start=(kc == 0), stop=(kc == 7),
                                    waits=[pfree, tw1[(DFF + f * 128) // 1024], tw1[(DFF + f * 128 + 127) // 1024]] if kc == 0 else ())
                    last_mm = tmu
                    sg, sfree, sk = sgr.next()
                    ta = ACT.op(A.activation, out=sg[:], in_=psg[:], func=AF.Silu, waits=[tmg, sfree])
                    pp.done(pkg, ta)
                    tb_ = DVE.op(V.tensor_tensor, out=actT[:, f, :], in0=psu[:], in1=sg[:], op=ALU.mult, waits=[tmu, ta, actT_free])
                    pp.done(pku, tb_)
                    sgr.done(sk, tb_)
                    tact.append(tb_)
                h2T_free = [last_mm]
                actT_free = []
                last_o = None
                for bi in range(4):
                    s = grp * 4 + bi
                    x2, x2free, x2k = x2r.next()
                    tx2 = SP.dma(x2[:], x1_scr[s * 128:(s + 1) * 128, :], waits=[x2free])
                    xtoks = []
                    for nc_ in range(2):
                        ps, pfree, pk = pp.next()
                        tm = None
                        for f in range(NFC):
                            tm = PE.op(T.matmul, ps[:], lhsT=actT[:, f, bi * 128:(bi + 1) * 128], rhs=w2[:, f, nc_ * 512:(nc_ + 1) * 512],
                                       start=(f == 0), stop=(f == NFC - 1), waits=[tact, pfree, tw2] if f == 0 else ())
                        last_o = tm
                        a = DVE.op(V.tensor_tensor, out=x2[:, nc_ * 512:(nc_ + 1) * 512], in0=ps[:], in1=x2[:, nc_ * 512:(nc_ + 1) * 512], op=ALU.add,
                                   waits=[tm, tx2])
                        pp.done(pk, a)
                        xtoks.append(a)
                    e = rms_norm_block((junk, ssr), x2[:], xtoks, gfin[:], t_gfin, 1e-6, D, x2[:], [])
                    td = SP.dma(out[s * 128:(s + 1) * 128, :], x2[:], waits=[e])
                    x2r.done(x2k, td)
                actT_free = [last_o]
            barrier()
    return nc


_NC_CACHE = {}


def _get_nc(debug=False):
    if debug not in _NC_CACHE:
        _NC_CACHE[debug] = build(debug)
    return _NC_CACHE[debug]


def make_in_maps(inputs):
    x = np.ascontiguousarray(np.asarray(inputs["x"], dtype=np.float32))
    pos = np.asarray(inputs["positions"]).astype(np.int32)
    in_maps = []
    kk = np.arange(512)[None, :]
    qq = np.arange(128)[:, None]
    for c in range(8):
        b, j = c // 4, c % 4
        blocks = [4 * s + j for s in range(NQB)]
        xqc = np.concatenate([x[b, q * 128:(q + 1) * 128] for q in blocks], axis=0)
        posk = np.ascontiguousarray(pos[b].reshape(NTB, 128).T)
        posq = np.ascontiguousarray(np.stack([pos[b, q * 128:(q + 1) * 128] for q in blocks], axis=1))
        cm = np.where(kk <= j * 128 + qq, 0.0, NEG).astype(np.float32)
        m = {"xs": x[b], "xq": np.ascontiguousarray(xqc), "posk": posk, "posq": posq, "cmask": cm}
        for name in ["norm_mix_g", "w_in", "idx_k_norm_g", "idx_k_norm_b", "diff_lambda_q1", "diff_lambda_k1",
                     "diff_lambda_q2", "diff_lambda_k2", "diff_subln_g", "gate_b", "w_branch_dsa", "w_branch_diff",
                     "w_out", "norm_ffn_g", "w_ffn_in", "w_ffn_out"]:
            m[name] = np.ascontiguousarray(np.asarray(inputs[name], dtype=np.float32)[0])
        m["norm_final_g"] = np.ascontiguousarray(np.asarray(inputs["norm_final_g"], dtype=np.float32))
        in_maps.append(m)
    return in_maps


def kernel(**inputs):
    nc = _get_nc(False)
    in_maps = make_in_maps(inputs)
    res = run_bass_kernel_spmd(nc, in_maps, core_ids=list(range(8)))
    outp = np.zeros((2, S, D), dtype=np.float32)
    for c in range(8):
        b, j = c // 4, c % 4
        o = res.results[c]["out"]
        for s in range(NQB):
            q = 4 * s + j
            outp[b, q * 128:(q + 1) * 128] = o[s * 128:(s + 1) * 128]
    return outp
```

```python
import os
import math
import numpy as np
from contextlib import ExitStack
import concourse.bass as bass
import concourse.mybir as mybir
from concourse.bass_utils import run_bass_kernel_spmd

F32 = mybir.dt.float32
F32R = mybir.dt.float32r
BF16 = mybir.dt.bfloat16
I32 = mybir.dt.int32
AF = mybir.ActivationFunctionType
ALU = mybir.AluOpType
AX = mybir.AxisListType

S = 8192
D = 1024
NTB = S // 128
NQB = 16
NQ = NQB * 128
DFF = 2816
NFC = DFF // 128
TOPK = 256
NBIS = 16
NEG = -30000.0
C_QA, C_KA, C_VA, C_QI, C_KI, C_WI, C_QB, C_KB, C_VB, C_G = 0, 512, 1024, 1536, 1792, 1824, 1832, 2344, 2856, 3368
D_IN = 5416
VW = 132
TWO_PI = 2.0 * math.pi


class Eng:
    def __init__(self, nc, es, eng, name):
        self.eng = eng
        self.name = name
        self.sem = es.enter_context(nc.semaphore("sem_" + name))
        self.count = 0
        self.seen = {}

    def wait(self, toks):
        for t in toks:
            if t is None:
                continue
            if isinstance(t, list):
                self.wait(t)
                continue
            sem, val, key = t
            if self.seen.get(key, 0) >= val:
                continue
            self.eng.wait_ge(sem, val)
            self.seen[key] = val

    def op(self, fn, *args, waits=(), **kw):
        self.wait(waits)
        inst = fn(*args, **kw)
        self.count += 1
        inst.then_inc(self.sem, 1)
        tok = (self.sem, self.count, self.name)
        self.seen[self.name] = self.count - 1 if False else self.seen.get(self.name, 0)
        return tok


class DmaQ:
    def __init__(self, nc, es, eng, name, nsem=12):
        self.eng = eng
        self.name = name
        self.sems = [es.enter_context(nc.semaphore(f"dsem_{name}_{i}")) for i in range(nsem)]
        self.vals = [0] * nsem
        self.i = 0
        self.seen = {}

    def wait(self, toks):
        for t in toks:
            if t is None:
                continue
            if isinstance(t, list):
                self.wait(t)
                continue
            sem, val, key = t
            if self.seen.get(key, 0) >= val:
                continue
            self.eng.wait_ge(sem, val)
            self.seen[key] = val

    def dma(self, out, in_, waits=(), **kw):
        k = self.i
        self.i = (self.i + 1) % len(self.sems)
        key = f"{self.name}_{k}"
        if self.vals[k] > 0:
            self.wait([(self.sems[k], self.vals[k], key)])
        self.wait(waits)
        self.vals[k] += 16
        self.eng.dma_start(out=out, in_=in_, **kw).then_inc(self.sems[k], 16)
        return (self.sems[k], self.vals[k], key)

    def all_toks(self):
        return [(self.sems[k], self.vals[k], f"{self.name}_{k}") for k in range(len(self.sems)) if self.vals[k] > 0]


class Ring:
    def __init__(self, tiles):
        self.tiles = tiles
        self.rd = [[] for _ in tiles]
        self.i = -1

    def next(self):
        self.i = (self.i + 1) % len(self.tiles)
        k = self.i
        toks = self.rd[k]
        self.rd[k] = []
        return self.tiles[k], toks, k

    def done(self, k, tok):
        self.rd[k].append(tok)


def build(debug=False):
    nc = bass.Bass("TRN2", target_bir_lowering=False)
    dt_in = lambda n, s, d=F32: nc.dram_tensor(n, s, d, kind="ExternalInput").ap()
    xs = dt_in("xs", [S, D])
    xq = dt_in("xq", [NQ, D])
    posk = dt_in("posk", [128, NTB], I32)
    posq = dt_in("posq", [128, NQB], I32)
    cmask = dt_in("cmask", [128, 512])
    norm_mix_g = dt_in("norm_mix_g", [D])
    w_in = dt_in("w_in", [D, D_IN])
    idx_g = dt_in("idx_k_norm_g", [32])
    idx_b = dt_in("idx_k_norm_b", [32])
    lq1 = dt_in("diff_lambda_q1", [64])
    lk1 = dt_in("diff_lambda_k1", [64])
    lq2 = dt_in("diff_lambda_q2", [64])
    lk2 = dt_in("diff_lambda_k2", [64])
    subln_g = dt_in("diff_subln_g", [128])
    gate_b = dt_in("gate_b", [2048])
    w_bd = dt_in("w_branch_dsa", [512, D])
    w_bf = dt_in("w_branch_diff", [512, D])
    w_out = dt_in("w_out", [D, D])
    norm_ffn_g = dt_in("norm_ffn_g", [D])
    w_f1 = dt_in("w_ffn_in", [D, 2 * DFF])
    w_f2 = dt_in("w_ffn_out", [DFF, D])
    norm_fin_g = dt_in("norm_final_g", [D])
    out = nc.dram_tensor("out", [NQ, D], F32, kind="ExternalOutput").ap()
    skind = "ExternalOutput" if debug else "Internal"
    kT_scr = nc.dram_tensor("kT_scr", [8, 128, S], BF16, kind=skind).ap()
    kiT_scr = nc.dram_tensor("kiT_scr", [32, S], BF16, kind=skind).ap()
    v_scr = nc.dram_tensor("v_scr", [S, 8 * VW], BF16, kind=skind).ap()
    qT_scr = nc.dram_tensor("qT_scr", [8, 128, NQ], BF16, kind=skind).ap()
    qiT_scr = nc.dram_tensor("qiT_scr", [64, 4, NQ], BF16, kind=skind).ap()
    o_scr = nc.dram_tensor("o_scr", [NQ, D], BF16, kind=skind).ap()
    x1_scr = nc.dram_tensor("x1_scr", [NQ, D], F32, kind=skind).ap()

    with ExitStack() as es:
        uid = [0]

        def sbt(st, n, s, d):
            uid[0] += 1
            return st.enter_context(nc.sbuf_tensor(f"{n}_{uid[0]}", s, d))

        def pst(st, n, s, d):
            uid[0] += 1
            return st.enter_context(nc.psum_tensor(f"{n}_{uid[0]}", s, d))

        ident = sbt(es, "ident", [128, 128], BF16)
        early = ExitStack()
        wabs = sbt(early, "wabs", [128, NQB, 8], F32)
        nlam = sbt(early, "nlam", [128, 1], F32)
        small = sbt(early, "small", [128, 64], F32)
        cb = sbt(early, "cb", [128, 512], BF16)
        cbf = sbt(early, "cbf", [128, 512], F32)
        early1 = ExitStack()
        cosk = sbt(early1, "cosk", [128, NTB, 8], F32)
        sink = sbt(early1, "sink", [128, NTB, 8], F32)
        cosq = sbt(early1, "cosq", [128, NQB, 8], F32)
        sinq = sbt(early1, "sinq", [128, NQB, 8], F32)
        gmix = sbt(early1, "gmix", [128, D], F32)

        es.enter_context(nc.Block())
        PE = Eng(nc, es, nc.tensor, "pe")
        ACT = Eng(nc, es, nc.scalar, "act")
        DVE = Eng(nc, es, nc.vector, "dve")
        POOL = Eng(nc, es, nc.gpsimd, "pool")
        SP = DmaQ(nc, es, nc.sync, "sp", 16)
        GQ = DmaQ(nc, es, nc.gpsimd, "gq", 8)
        V = nc.vector
        A = nc.scalar
        T = nc.tensor

        def bcast(ap1d, n=128):
            return ap1d.partition_broadcast(n)

        def barrier():
            toks = [(e.sem, e.count, e.name) for e in (PE, ACT, DVE, POOL) if e.count > 0]
            toks += SP.all_toks() + GQ.all_toks()
            for e in (PE, ACT, DVE, POOL, SP):
                e.wait(toks)

        t = POOL.op(nc.gpsimd.memset, ident[:], 1.0)
        t_ident = POOL.op(nc.gpsimd.affine_select, out=ident[:], in_=ident[:], pattern=[[-1, 128]],
                          compare_op=ALU.is_equal, fill=0.0, base=0, channel_multiplier=1, waits=[t])
        t_gmix = SP.dma(gmix[:], bcast(norm_mix_g))
        t_cbf = SP.dma(cbf[:], cmask)
        t_cb = DVE.op(V.tensor_copy, out=cb[:], in_=cbf[:], waits=[t_cbf])

        with ExitStack() as p0:
            lt = sbt(p0, "lt", [128, 4, 64], F32)
            lj = sbt(p0, "lj", [128, 64], F32)
            tl = [SP.dma(lt[:, i, :], bcast(a)) for i, a in enumerate([lq1, lk1, lq2, lk2])]
            t1 = DVE.op(V.tensor_tensor, out=lj[:], in0=lt[:, 0, :], in1=lt[:, 1, :], op=ALU.mult, waits=tl)
            t1 = DVE.op(V.tensor_reduce, out=small[:, 0:1], in_=lj[:], axis=AX.X, op=ALU.add, waits=[t1])
            t2 = DVE.op(V.tensor_tensor, out=lj[:], in0=lt[:, 2, :], in1=lt[:, 3, :], op=ALU.mult, waits=[t1])
            t2 = DVE.op(V.tensor_reduce, out=small[:, 1:2], in_=lj[:], axis=AX.X, op=ALU.add, waits=[t2])
            t3 = ACT.op(A.activation, out=small[:, 2:4], in_=small[:, 0:2], func=AF.Exp, waits=[t2])
            t_nlam = DVE.op(V.scalar_tensor_tensor, out=nlam[:], in0=small[:, 3:4], scalar=-0.2, in1=small[:, 2:3],
                            op0=ALU.add, op1=ALU.subtract, waits=[t3])

            invf = sbt(p0, "invf", [128, 8], F32)
            tinv = None
            for i in range(8):
                fv = float(np.power(np.float32(500000.0), -np.float32(2 * i) / np.float32(16)))
                tinv = DVE.op(V.memset, invf[:, i:i + 1], fv)

            def rope_table(pos_ap, n, cos_t, sin_t, nm):
                pi_ = sbt(p0, "pi_" + nm, [128, n], I32)
                pf = sbt(p0, "pf_" + nm, [128, n], F32)
                ang = sbt(p0, "ang_" + nm, [128, n, 8], F32)
                yy = sbt(p0, "yy_" + nm, [128, n, 8], F32)
                ni = sbt(p0, "ni_" + nm, [128, n, 8], I32)
                tp = SP.dma(pi_[:], pos_ap)
                a = DVE.op(V.tensor_copy, out=pf[:], in_=pi_[:], waits=[tp])
                a = DVE.op(V.tensor_tensor, out=ang[:], in0=pf[:].unsqueeze(2).to_broadcast([128, n, 8]),
                           in1=invf[:].unsqueeze(1).to_broadcast([128, n, 8]), op=ALU.mult, waits=[a, tinv])

                def reduce_sin(src_add, dst):
                    b = DVE.op(V.tensor_scalar, out=yy[:], in0=ang[:], scalar1=src_add, scalar2=1.0 / TWO_PI,
                               op0=ALU.add, op1=ALU.mult, waits=[a])
                    b = DVE.op(V.tensor_copy, out=ni[:], in_=yy[:], waits=[b])
                    b = DVE.op(V.tensor_copy, out=yy[:], in_=ni[:], waits=[b])
                    c1 = 6.28125
                    c2 = TWO_PI - 6.28125
                    b = DVE.op(V.scalar_tensor_tensor, out=dst, in0=yy[:], scalar=-c1, in1=ang[:], op0=ALU.mult, op1=ALU.add, waits=[b])
                    b = DVE.op(V.scalar_tensor_tensor, out=dst, in0=yy[:], scalar=-c2, in1=dst, op0=ALU.mult, op1=ALU.add, waits=[b])
                    b = DVE.op(V.tensor_scalar, out=dst, in0=dst, scalar1=src_add, scalar2=3.1415925, op0=ALU.add, op1=ALU.min, waits=[b])
                    b = DVE.op(V.tensor_scalar, out=dst, in0=dst, scalar1=-3.1415925, scalar2=None, op0=ALU.max, waits=[b])
                    return ACT.op(A.activation, out=dst, in_=dst, func=AF.Sin, waits=[b])
                ts = reduce_sin(0.0, sin_t[:])
                tc = reduce_sin(math.pi / 2.0, cos_t[:])
                return [ts, tc]
            t_ropek = rope_table(posk, NTB, cosk, sink, "k")
            t_ropeq = rope_table(posq, NQB, cosq, sinq, "q")
            barrier()

        def rms_norm_block(st_rings, x_tile, tx, g_tile, tg, eps, n_feat, hb_tile, hb_free):
            junk, ssr = st_rings
            jt, jfree, jk = junk.next()
            col, cfree, ck = ssr.next()
            a = ACT.op(A.activation, out=jt[:, :n_feat], in_=x_tile, func=AF.Square, accum_out=col[:, 0:1],
                       waits=[tx, jfree, cfree])
            junk.done(jk, a)
            b = DVE.op(V.tensor_scalar, out=col[:, 1:2], in0=col[:, 0:1], scalar1=1.0 / n_feat, scalar2=eps,
                       op0=ALU.mult, op1=ALU.add, waits=[a])
            c = ACT.op(A.activation, out=col[:, 2:3], in_=col[:, 1:2], func=AF.Sqrt, waits=[b])
            d = DVE.op(V.reciprocal, out=col[:, 3:4], in_=col[:, 2:3], waits=[c])
            e = DVE.op(V.scalar_tensor_tensor, out=hb_tile, in0=x_tile, scalar=col[:, 3:4], in1=g_tile,
                       op0=ALU.mult, op1=ALU.mult, waits=[d, tg, hb_free])
            ssr.done(ck, e)
            return e

        rope_last = []

        def rope_apply(tile3, H, half, cs, sn, tmp, waits):
            x1 = tile3[:, :, 0:half]
            x2 = tile3[:, :, half:2 * half]
            cB = cs.unsqueeze(1).to_broadcast([128, H, half])
            sB = sn.unsqueeze(1).to_broadcast([128, H, half])
            tv = lambda i: tmp[:, i, 0:H * half].rearrange("p (h d) -> p h d", h=H)
            waits = list(waits) + rope_last
            a1 = DVE.op(V.tensor_tensor, out=tv(0), in0=x1, in1=cB, op=ALU.mult, waits=waits)
            a2 = DVE.op(V.tensor_tensor, out=tv(1), in0=x2, in1=sB, op=ALU.mult, waits=waits)
            a3 = DVE.op(V.tensor_tensor, out=tv(2), in0=x2, in1=cB, op=ALU.mult, waits=waits)
            a4 = DVE.op(V.tensor_tensor, out=tv(3), in0=x1, in1=sB, op=ALU.mult, waits=waits)
            b1 = DVE.op(V.tensor_tensor, out=x1, in0=tv(0), in1=tv(1), op=ALU.subtract, waits=[a1, a2, a3, a4])
            b2 = DVE.op(V.tensor_tensor, out=x2, in0=tv(2), in1=tv(3), op=ALU.add, waits=[a1, a2, a3, a4])
            rope_last[:] = [b1, b2]
            return [b1, b2]

        with ExitStack() as p1:
            NKV = 2080
            NQC = 1288
            wkv = sbt(p1, "wkv", [128, 8, NKV], BF16)
            wq = sbt(p1, "wq", [128, 8, NQC], BF16)
            w_in_v = w_in.rearrange("(kc p) n -> p kc n", p=128)
            tw = []
            for (dst0, c0, n) in [(0, C_KA, 512), (512, C_KB, 512), (1024, C_VA, 512), (1536, C_VB, 512), (2048, C_KI, 32)]:
                tw.append(GQ.dma(wkv[:, :, dst0:dst0 + n], w_in_v[:, :, c0:c0 + n]))
            twq = []
            for (dst0, c0, n) in [(0, C_QA, 512), (512, C_QB, 512), (1024, C_QI, 256), (1280, C_WI, 8)]:
                twq.append(GQ.dma(wq[:, :, dst0:dst0 + n], w_in_v[:, :, c0:c0 + n]))
            lng = sbt(p1, "lng", [128, 32], F32)
            lnb = sbt(p1, "lnb", [128, 32], F32)
            t_lng = SP.dma(lng[:], bcast(idx_g))
            t_lnb = SP.dma(lnb[:], bcast(idx_b))

            xr = Ring([sbt(p1, f"xt{i}", [128, D], F32) for i in range(3)])
            junk = Ring([sbt(p1, f"junk{i}", [128, D], BF16) for i in range(1)])
            ssr = Ring([sbt(p1, f"ss{i}", [128, 4], F32) for i in range(4)])
            hbr = Ring([sbt(p1, f"hb{i}", [128, D], BF16) for i in range(3)])
            hTr = Ring([sbt(p1, f"hT{i}", [128, 8, 128], BF16) for i in range(3)])
            kfr = Ring([sbt(p1, f"kf{i}", [128, 1024], F32) for i in range(2)])
            kbr = Ring([sbt(p1, f"kb{i}", [128, 1024], BF16) for i in range(3)])
            kTr = Ring([sbt(p1, f"kTt{i}", [128, 8, 128], BF16) for i in range(2)])
            vtr = Ring([sbt(p1, f"vt{i}", [128, 8, VW], BF16) for i in range(2)])
            rtmp = sbt(p1, "rtmp", [128, 4, 128], F32)
            kif = sbt(p1, "kif", [128, 8], F32)
            kic = sbt(p1, "kic", [128, 32], F32)
            kij = sbt(p1, "kij", [128, 32], F32)
            kibr = Ring([sbt(p1, f"kib{i}", [128, 32], BF16) for i in range(3)])
            kiTr = Ring([sbt(p1, f"kiT{i}", [32, 128], BF16) for i in range(2)])
            qwr = Ring([sbt(p1, f"qw{i}", [128, 264], F32) for i in range(2)])
            qibr = Ring([sbt(p1, f"qib{i}", [128, 256], BF16) for i in range(3)])
            qiTr = Ring([sbt(p1, f"qiT{i}", [32, 8, 128], BF16) for i in range(2)])
            psT = Ring([pst(p1, f"psT{i}", [128, D], BF16) for i in range(2)])
            pp = Ring([pst(p1, f"pp{i}", [128, 512], F32) for i in range(4)])
            pkT = Ring([pst(p1, f"pkT{i}", [128, D], BF16) for i in range(2)])
            t_vinit = []
            for vt_ in vtr.tiles:
                t0 = POOL.op(nc.gpsimd.memset, vt_[:], 0.0)
                t1_ = POOL.op(nc.gpsimd.memset, vt_[:, 0:4, :].rearrange("p a (s e) -> p (a s) e", s=2)[:, :, 64:65], 1.0, waits=[t0])
                t_vinit.append(POOL.op(nc.gpsimd.memset, vt_[:, 4:8, 128:129], 1.0, waits=[t0, t1_]))

            def aevac(out_ap, in_ap, waits):
                return ACT.op(A.copy, out=out_ap, in_=in_ap, waits=waits)

            def stageA1(src_rows):
                xt, xfree, xk = xr.next()
                tx = SP.dma(xt[:], src_rows, waits=xfree)
                hb, hfree, hk = hbr.next()
                th = rms_norm_block((junk, ssr), xt[:], tx, gmix[:], t_gmix, 1e-6, D, hb[:], hfree)
                xr.done(xk, th)
                return hb, th, hk

            def stageA2(hb, th, hk):
                ps, pfree, pk = psT.next()
                tt = None
                for kc in range(8):
                    tt = PE.op(T.transpose, ps[:, kc * 128:(kc + 1) * 128], hb[:, kc * 128:(kc + 1) * 128], ident[:],
                               waits=[th, t_ident, pfree] if kc == 0 else ())
                hbr.done(hk, tt)
                hT, tfree, tk = hTr.next()
                te = aevac(hT[:].rearrange("p a b -> p (a b)"), ps[:], [tt, tfree])
                psT.done(pk, te)
                return hT, te, tk

            def project(hT, th, w_tile, c0, n, wtoks):
                ps, pfree, pk = pp.next()
                tt = None
                for kc in range(8):
                    tt = PE.op(T.matmul, ps[:, 0:n], lhsT=hT[:, kc, :], rhs=w_tile[:, kc, c0:c0 + n],
                               start=(kc == 0), stop=(kc == 7), waits=[th, pfree, wtoks] if kc == 0 else ())
                return ps, tt, pk

            def transpose_out(src_bf, tsrc, nchunk, width, ring_sb, dst_dram):
                ps, pfree, pk = pkT.next()
                tt = None
                for c in range(nchunk):
                    tt = PE.op(T.transpose, ps[0:width, c * 128:(c + 1) * 128], src_bf[:, c * width:(c + 1) * width], ident[:],
                               waits=[tsrc, pfree, t_ident] if c == 0 else ())
                sbT, sfree, sk = ring_sb.next()
                te = aevac(sbT[:].rearrange("p a b -> p (a b)") if len(sbT.shape) == 3 else sbT[:],
                           ps[0:width, 0:nchunk * 128], [tt, sfree])
                pkT.done(pk, te)
                td = GQ.dma(dst_dram, sbT[:], waits=[te])
                ring_sb.done(sk, td)
                return tt

            def transpose_out_qi(src_bf, tsrc, s):
                ps, pfree, pk = pkT.next()
                tt = None
                for c in range(8):
                    tt = PE.op(T.transpose, ps[0:32, c * 128:(c + 1) * 128], src_bf[:, c * 32:(c + 1) * 32], ident[:],
                               waits=[tsrc, pfree, t_ident] if c == 0 else ())
                sbT, sfree, sk = qiTr.next()
                te = aevac(sbT[:].rearrange("p a b -> p (a b)"), ps[0:32, 0:1024], [tt, sfree])
                pkT.done(pk, te)
                for g in range(2):
                    td = GQ.dma(qiT_scr[g * 32:(g + 1) * 32, :, s * 128:(s + 1) * 128], sbT[:, g::2, :], waits=[te])
                    qiTr.done(sk, td)
                return tt

            def qk_pair(hT, th, w_tile, wtoks, cs, sn, scale, dst_dram):
                kf, kfree, kk = kfr.next()
                tes = []
                for gi in range(2):
                    ps, tmm, pk = project(hT, th, w_tile, gi * 512, 512, wtoks)
                    te = aevac(kf[:, gi * 512:(gi + 1) * 512], ps[:], [tmm, kfree])
                    pp.done(pk, te)
                    tes.append(te)
                tr = rope_apply(kf[:].rearrange("p (h d) -> p h d", h=16), 16, 8, cs, sn, rtmp, tes)
                kb, bfree, bk = kbr.next()
                if scale == 1.0:
                    tcst = DVE.op(V.tensor_copy, out=kb[:], in_=kf[:], waits=[tr, bfree])
                else:
                    tcst = DVE.op(V.tensor_scalar, out=kb[:], in0=kf[:], scalar1=scale, scalar2=None, op0=ALU.mult, waits=[tr, bfree])
                kfr.done(kk, tcst)
                return kb, bk, tcst

            def stageB_kv(tb, hT, th, hk):
                cs = cosk[:, tb, :]
                sn = sink[:, tb, :]
                kb, bk, tcst = qk_pair(hT, th, wkv, tw, cs, sn, 1.0, None)
                vt, vfree, vk = vtr.next()
                tvs = []
                for gi in (2, 3):
                    ps, tmm, pk = project(hT, th, wkv, gi * 512, 512, tw)
                    if gi == 2:
                        dstv = vt[:, 0:4, :].rearrange("p a (s e) -> p (a s) e", s=2)[:, :, 0:64]
                        te = aevac(dstv, ps[:].rearrange("p (h e) -> p h e", e=64), [tmm, vfree, t_vinit])
                    else:
                        te = aevac(vt[:, 4:8, 0:128], ps[:].rearrange("p (h e) -> p h e", e=128), [tmm, vfree, t_vinit])
                    pp.done(pk, te)
                    tvs.append(te)
                td = GQ.dma(v_scr[tb * 128:(tb + 1) * 128, :], vt[:].rearrange("p a b -> p (a b)"), waits=tvs)
                vtr.done(vk, td)
                ps, tmm, pk = project(hT, th, wkv, 2048, 32, tw)
                hTr.done(hk, tmm)
                a = DVE.op(V.tensor_reduce, out=kif[:, 0:1], in_=ps[:, 0:32], axis=AX.X, op=ALU.add, waits=[tmm])
                a = DVE.op(V.tensor_scalar, out=kif[:, 1:2], in0=kif[:, 0:1], scalar1=1.0 / 32, scalar2=None, op0=ALU.mult, waits=[a])
                a = DVE.op(V.tensor_scalar, out=kic[:], in0=ps[:, 0:32], scalar1=kif[:, 1:2], scalar2=None, op0=ALU.subtract, waits=[a])
                pp.done(pk, a)
                b = ACT.op(A.activation, out=kij[:], in_=kic[:], func=AF.Square, accum_out=kif[:, 2:3], waits=[a])
                b = DVE.op(V.tensor_scalar, out=kif[:, 3:4], in0=kif[:, 2:3], scalar1=1.0 / 32, scalar2=1e-6, op0=ALU.mult, op1=ALU.add, waits=[b])
                b = ACT.op(A.activation, out=kif[:, 4:5], in_=kif[:, 3:4], func=AF.Sqrt, waits=[b])
                b = DVE.op(V.reciprocal, out=kif[:, 5:6], in_=kif[:, 4:5], waits=[b])
                b = DVE.op(V.scalar_tensor_tensor, out=kic[:], in0=kic[:], scalar=kif[:, 5:6], in1=lng[:], op0=ALU.mult, op1=ALU.mult, waits=[b, t_lng])
                b = DVE.op(V.tensor_tensor, out=kic[:], in0=kic[:], in1=lnb[:], op=ALU.add, waits=[b, t_lnb])
                csi = cosk[:, tb, :].rearrange("p (a two) -> p a two", two=2)[:, :, 0]
                sni = sink[:, tb, :].rearrange("p (a two) -> p a two", two=2)[:, :, 0]
                tr = rope_apply(kic[:].rearrange("p (h d) -> p h d", h=1), 1, 4, csi, sni, rtmp, [b])
                kib, bfree, bk2 = kibr.next()
                tc2 = DVE.op(V.tensor_copy, out=kib[:], in_=kic[:], waits=[tr, bfree])

                def b2():
                    tlast = transpose_out(kb, tcst, 8, 128, kTr, kT_scr[:, :, tb * 128:(tb + 1) * 128].rearrange("c f t -> f c t"))
                    kbr.done(bk, tlast)
                    tl2 = transpose_out(kib, tc2, 1, 32, kiTr, kiT_scr[:, tb * 128:(tb + 1) * 128])
                    kibr.done(bk2, tl2)
                return b2

            def stageB_q(s, hT, th, hk):
                cs = cosq[:, s, :]
                sn = sinq[:, s, :]
                kb, bk, tcst = qk_pair(hT, th, wq, twq, cs, sn, 0.125, None)
                ps, tmm, pk = project(hT, th, wq, 1024, 264, twq)
                hTr.done(hk, tmm)
                qw, qfree, qk = qwr.next()
                te = aevac(qw[:], ps[:, 0:264], [tmm, qfree])
                pp.done(pk, te)
                csi = cosq[:, s, :].rearrange("p (a two) -> p a two", two=2)[:, :, 0]
                sni = sinq[:, s, :].rearrange("p (a two) -> p a two", two=2)[:, :, 0]
                tr = rope_apply(qw[:, 0:256].rearrange("p (h d) -> p h d", h=8), 8, 4, csi, sni, rtmp, [te])
                a1 = DVE.op(V.tensor_scalar, out=wabs[:, s, :], in0=qw[:, 256:264], scalar1=1.0 / 16, scalar2=None, op0=ALU.mult, waits=[te])
                a2 = a1
                qib, bfree, bk2 = qibr.next()
                tc2 = DVE.op(V.tensor_copy, out=qib[:], in_=qw[:, 0:256], waits=[tr, bfree])
                qwr.done(qk, [tc2, a1, a2])

                def b2():
                    tlast = transpose_out(kb, tcst, 8, 128, kTr, qT_scr[:, :, s * 128:(s + 1) * 128].rearrange("c f t -> f c t"))
                    kbr.done(bk, tlast)
                    tl2 = transpose_out_qi(qib, tc2, s)
                    qibr.done(bk2, tl2)
                return b2

            items = [("kv", tb, xs[tb * 128:(tb + 1) * 128, :]) for tb in range(NTB)] + \
                    [("q", s, xq[s * 128:(s + 1) * 128, :]) for s in range(NQB)]
            n_it = len(items)
            a1 = {}
            a2 = {}
            a1[0] = stageA1(items[0][2])
            a2[0] = stageA2(*a1[0])
            if n_it > 1:
                a1[1] = stageA1(items[1][2])
            prev_b2 = None
            for i in range(n_it):
                if i + 2 < n_it:
                    a1[i + 2] = stageA1(items[i + 2][2])
                kind, idx, _ = items[i]
                hT, th, hk = a2.pop(i)
                if kind == "kv":
                    b2 = stageB_kv(idx, hT, th, hk)
                else:
                    b2 = stageB_q(idx, hT, th, hk)
                if i + 1 < n_it:
                    a2[i + 1] = stageA2(*a1.pop(i + 1))
                if prev_b2 is not None:
                    prev_b2()
                prev_b2 = b2
            if prev_b2 is not None:
                prev_b2()
            barrier()

        early1.close()

        with ExitStack() as p2:
            kiT = sbt(p2, "kiT", [64, S], BF16)
            t_kiT = [SP.dma(kiT[g * 32:(g + 1) * 32, :], kiT_scr) for g in range(2)]
            gsub = sbt(p2, "gsub", [128, 128], F32)
            t_gs = SP.dma(gsub[:], bcast(subln_g))
            t_gs = DVE.op(V.tensor_scalar, out=gsub[:], in0=gsub[:], scalar1=0.8, scalar2=None, op0=ALU.mult, waits=[t_gs])
            ident2 = sbt(p2, "ident2", [128, 2, 128], BF16)
            t_id2 = [DVE.op(V.tensor_copy, out=ident2[:, i, :], in_=ident[:], waits=[t_ident]) for i in range(2)]
            Kr = Ring([sbt(p2, f"Kb{i}", [128, S], BF16) for i in range(2)])
            Vr = Ring([sbt(p2, f"Vb{i}", [128, NTB, VW], BF16) for i in range(2)])
            Mb = [sbt(p2, f"Mb{i}", [128, S], BF16) for i in range(2)]
            Isc2 = [sbt(p2, f"Isc{i}", [128, S], F32) for i in range(2)]
            qbd_tiles = [sbt(p2, f"qbd{i}", [128, 2, 128], BF16) for i in range(3)]
            t_qz = [POOL.op(nc.gpsimd.memset, q_[:], 0.0) for q_ in qbd_tiles]
            qbr = Ring(qbd_tiles)
            qiT = [sbt(p2, f"qiTs{i}", [64, 4, 128], BF16) for i in range(2)]
            rl = Ring([sbt(p2, f"rl{i}", [128, 512], BF16) for i in range(6)])
            identf = sbt(p2, "identf", [128, 128], F32)
            t_idf = DVE.op(V.tensor_copy, out=identf[:], in_=ident[:], waits=[t_ident])
            dgb = [sbt(p2, f"dgb{i}", [128, 8, 128], BF16) for i in range(2)]
            etr = Ring([sbt(p2, f"et{i}", [128, 512], BF16) for i in range(4)])
            otr = Ring([sbt(p2, f"ot{i}", [128, D], BF16) for i in range(2)])
            of32 = sbt(p2, "of32", [128, 128], F32)
            oj = sbt(p2, "oj", [128, 128], F32)
            bs = sbt(p2, "bs", [128, 16], F32)
            es_ = sbt(p2, "es_", [128, 16], F32)
            pss = Ring([pst(p2, f"pss{i}", [128, 512], F32) for i in range(4)])
            pI = pst(p2, "pI", [128, 512], F32)
            pacc = Ring([pst(p2, f"pacc{i}", [128, 512], F32) for i in range(3)])
            ofr = Ring([(sbt(p2, f"of1_{i}", [128, 128], F32), sbt(p2, f"of2_{i}", [128, 128], F32)) for i in range(2)])
            mhalf = sbt(p2, "mhalf", [128, 1], F32)
            t_mh = POOL.op(nc.gpsimd.memset, mhalf[:], -0.5)
            ez = sbt(p2, "ez", [128, 8], F32)
            Mb_ready = [None, None]
            Mb_readers = [[], []]
            qiT_readers = [[], []]
            dg_readers = [[], []]
            Isc_free = [[], []]
            idx_done = [None, None]
            pI_free = [[]]

            def prep_idx(s):
                li = s % 2
                Isc = Isc2[li]
                nk = (4 * s + 4) * 128
                nch = nk // 512
                wb = 0.76 * (s + 1)
                tq = SP.dma(qiT[li][:], qiT_scr[:, :, s * 128:(s + 1) * 128], waits=qiT_readers[li])
                qiT_readers[li] = []
                tdg = None
                for h in range(8):
                    tdg = ACT.op(A.activation, out=dgb[li][:, h, :], in_=identf[:], func=AF.Copy, scale=wabs[:, s, h:h + 1],
                                 waits=[t_idf, dg_readers[li]] if h == 0 else ())
                dg_readers[li] = []
                yield 1.0
                tI = None
                pending = None

                def flush(pend):
                    (pc, pr, items) = pend
                    tacc = None
                    for g, (prt, pta, prk) in enumerate(items):
                        h = 2 * pr + g
                        tacc = PE.op(T.matmul, pI[:], lhsT=dgb[li][:, h, :], rhs=prt[:],
                                     start=(pr == 0 and g == 0), stop=(pr == 3 and g == 1),
                                     waits=[pta, tdg, pI_free[0]])
                        rl.done(prk, tacc)
                    return tacc
                pendq = []

                def drain(keep):
                    nonlocal tI
                    tacc = None
                    while len(pendq) > keep:
                        pend = pendq.pop(0)
                        tacc = flush(pend)
                        if pend[1] == 3:
                            pc = pend[0]
                            tI = ACT.op(A.copy, out=Isc[:, pc * 512:(pc + 1) * 512], in_=pI[:], waits=[tacc, Isc_free[li]])
                            pI_free[0] = [tI]
                    return tacc
                for c in range(nch):
                    for r in range(4):
                        drain(1)
                        mm = []
                        for g in range(2):
                            ps, pfree, pk = pss.next()
                            tm = PE.op(T.matmul, ps[:], lhsT=qiT[li][g * 32:(g + 1) * 32, r, :], rhs=kiT[g * 32:(g + 1) * 32, c * 512:(c + 1) * 512],
                                       start=True, stop=True, waits=[tq, t_kiT, pfree])
                            mm.append((ps, pk, tm))
                        items = []
                        for g in range(2):
                            ps, pk, tm = mm[g]
                            rt, rfree, rk = rl.next()
                            if True:
                                ta = ACT.op(A.activation, out=rt[:], in_=ps[:], func=AF.Relu, waits=[tm, rfree])
                            else:
                                ta = DVE.op(V.tensor_scalar, out=rt[:], in0=ps[:], scalar1=0.0, scalar2=None, op0=ALU.max, waits=[tm, rfree])
                            pss.done(pk, ta)
                            items.append((rt, ta, rk))
                        pendq.append((c, r, items))
                        yield 0.5
                tacc = drain(0)
                qiT_readers[li].append(tacc)
                dg_readers[li].append(tacc)
                idx_done[li] = tI

            def prep_bis(s):
                li = s % 2
                Isc = Isc2[li]
                nk = (4 * s + 4) * 128
                wb = 0.76 * (s + 1)
                tI = idx_done[li]
                Iv = Isc[:, 0:nk]
                a = DVE.op(V.tensor_reduce, out=bs[:, 0:1], in_=Iv, axis=AX.X, op=ALU.max, waits=[tI])
                yield wb
                a = DVE.op(V.tensor_reduce, out=bs[:, 1:2], in_=Iv, axis=AX.X, op=ALU.min, waits=[a])
                a = DVE.op(V.scalar_tensor_tensor, out=bs[:, 2:3], in0=bs[:, 0:1], scalar=1.0, in1=bs[:, 1:2], op0=ALU.add, op1=ALU.subtract, waits=[a])
                a = DVE.op(V.tensor_tensor, out=Isc[:, nk - 512:nk], in0=Isc[:, nk - 512:nk], in1=cbf[:], op=ALU.add, waits=[a, t_cbf])
                yield wb
                lo = bs[:, 1:2]
                w0 = bs[:, 2:3]
                mid = bs[:, 3:4]
                cnt = bs[:, 4:5]
                gg = bs[:, 5:6]
                for it in range(1, NBIS + 1):
                    sc = 2.0 ** (-it)
                    a = DVE.op(V.tensor_scalar, out=mid, in0=w0, scalar1=sc, scalar2=lo, op0=ALU.mult, op1=ALU.add, waits=[a])
                    a = DVE.op(V.tensor_scalar, out=Mb[li][:, 0:nk], in0=Iv, scalar1=mid, scalar2=None, op0=ALU.is_ge, op1=ALU.add,
                               accum_out=cnt, waits=[a, Mb_readers[li]])
                    Mb_readers[li] = []
                    a = DVE.op(V.tensor_scalar, out=gg, in0=cnt, scalar1=TOPK - 0.5, scalar2=sc, op0=ALU.is_ge, op1=ALU.mult, waits=[a])
                    a = DVE.op(V.scalar_tensor_tensor, out=lo, in0=w0, scalar=gg, in1=lo, op0=ALU.mult, op1=ALU.add, waits=[a])
                    yield wb
                a = DVE.op(V.tensor_scalar, out=Mb[li][:, 0:nk], in0=Iv, scalar1=lo, scalar2=NEG, op0=ALU.is_lt, op1=ALU.mult, waits=[a])
                Mb_ready[li] = a
                Isc_free[li] = [a]

            def w_idx(s):
                return 1 + 2 * (s + 1)

            def w_bis(s):
                return (2 + NBIS) * 0.76 * (s + 1)

            pumps = [{"gen": None, "budget": 0.0, "rate": 0.0}, {"gen": None, "budget": 0.0, "rate": 0.0}]

            def pump():
                for st in pumps:
                    if st["gen"] is None:
                        continue
                    st["budget"] += st["rate"]
                    while st["budget"] > 0.0 and st["gen"] is not None:
                        try:
                            st["budget"] -= next(st["gen"])
                        except StopIteration:
                            st["gen"] = None

            def finish(st):
                if st["gen"] is not None:
                    for _ in st["gen"]:
                        pass
                    st["gen"] = None

            def attention(s, p, ot, ofree, owr):
                li = s % 2
                is_dsa = p < 4
                nkb = 4 * s + 4
                kmax = nkb * 128
                Kb, kfree, kk = Kr.next()
                Vb, vfree, vk = Vr.next()
                tK = SP.dma(Kb[:, 0:kmax], kT_scr[p, :, 0:kmax], waits=kfree)
                tV = []
                for b0 in range(0, nkb, 8):
                    nb = min(8, nkb - b0)
                    tV.append(SP.dma(Vb[:, b0:b0 + nb, :], v_scr[b0 * 128:(b0 + nb) * 128, p * VW:(p + 1) * VW].rearrange("(b t) w -> t b w", t=128), waits=vfree))
                qbd, qfree, qk = qbr.next()
                tq = [SP.dma(qbd[m * 64:(m + 1) * 64, m, :], qT_scr[p, m * 64:(m + 1) * 64, s * 128:(s + 1) * 128], waits=[qfree, t_qz]) for m in range(2)]
                accs = [pacc.next() for m in range(2)]
                vw = 66 if is_dsa else 130
                ntile = nkb // 2
                pend = {}

                def do_qk(ti):
                    ps, pfree, pk = pss.next()
                    tm = None
                    for bi in range(2):
                        kb_ = ti * 2 + bi
                        need_mask = is_dsa or (kb_ >= nkb - 4)
                        tm = PE.op(T.matmul, ps[:, bi * 256:(bi + 1) * 256], lhsT=Kb[:, kb_ * 128:(kb_ + 1) * 128],
                                   rhs=qbd[:].rearrange("p a b -> p (a b)"), start=True, stop=not need_mask,
                                   waits=[tK, tq, pfree] if bi == 0 else ())
                        if need_mask:
                            if is_dsa:
                                ml = Mb[li][:, kb_ * 128:(kb_ + 1) * 128]
                                mw = [Mb_ready[li]]
                            else:
                                cbi = kb_ - (nkb - 4)
                                ml = cb[:, cbi * 128:(cbi + 1) * 128]
                                mw = [t_cb]
                            tm = PE.op(T.matmul, ps[:, bi * 256:(bi + 1) * 256], lhsT=ml, rhs=ident2[:].rearrange("p a b -> p (a b)"),
                                       start=False, stop=True, waits=mw + [t_id2])
                    et, efree, ek = etr.next()
                    te = ACT.op(A.activation, out=et[:], in_=ps[:], func=AF.Exp, waits=[tm, efree])
                    pss.done(pk, te)
                    pend[ti] = (et, te, ek)

                tav = [None, None]

                def do_av(ti):
                    et, te, ek = pend.pop(ti)
                    tm = None
                    first = True
                    for bi in range(2):
                        kb_ = ti * 2 + bi
                        for m in range(2):
                            acc, afree, ak = accs[m]
                            rhs = Vb[:, kb_, m * 66:(m + 1) * 66] if is_dsa else Vb[:, kb_, 0:130]
                            tm = PE.op(T.matmul, acc[:, 0:vw], lhsT=et[:, bi * 256 + m * 128:bi * 256 + (m + 1) * 128], rhs=rhs,
                                       start=(kb_ == 0), stop=(kb_ == nkb - 1),
                                       waits=[te, tV, accs[0][1], accs[1][1]] if first else ())
                            first = False
                            tav[m] = tm
                    etr.done(ek, tm)
                LAG = 2
                for ti in range(ntile):
                    do_qk(ti)
                    if ti >= LAG:
                        do_av(ti - LAG)
                    pump()
                for ti in range(max(0, ntile - LAG), ntile):
                    do_av(ti)
                qbr.done(qk, tav[1])
                if is_dsa:
                    Mb_readers[li].append(tav[1])
                Kr.done(kk, tav[1])
                Vr.done(vk, tav[1])
                if is_dsa:
                    for m in range(2):
                        acc, afree, ak = accs[m]
                        a = ACT.op(A.activation, out=ez[:, m:m + 1], in_=acc[:, 64:65], func=AF.Ln, waits=[tav[1]])
                        a = ACT.op(A.activation, out=ez[:, m:m + 1], in_=ez[:, m:m + 1], func=AF.Exp, scale=-1.0, waits=[a])
                        a = ACT.op(A.activation, out=ot[:, p * 128 + m * 64:p * 128 + (m + 1) * 64], in_=acc[:, 0:64], func=AF.Copy,
                                   scale=ez[:, m:m + 1], waits=[a, ofree])
                        pacc.done(ak, a)
                        owr.append(a)
                else:
                    h = p - 4
                    acc1, _, ak1 = accs[0]
                    acc2, _, ak2 = accs[1]
                    (of1, of2), offree, ofk = ofr.next()
                    a = ACT.op(A.activation, out=ez[:, 2:3], in_=acc1[:, 128:129], func=AF.Ln, waits=[tav[1], offree])
                    a = ACT.op(A.activation, out=ez[:, 2:3], in_=ez[:, 2:3], func=AF.Exp, scale=-1.0, waits=[a])
                    a = ACT.op(A.activation, out=of1[:], in_=acc1[:, 0:128], func=AF.Copy, scale=ez[:, 2:3], waits=[a])
                    pacc.done(ak1, a)
                    b = ACT.op(A.activation, out=ez[:, 3:4], in_=acc2[:, 128:129], func=AF.Ln, waits=[a])
                    b = ACT.op(A.activation, out=ez[:, 3:4], in_=ez[:, 3:4], func=AF.Exp, scale=-1.0, waits=[b])
                    b = ACT.op(A.activation, out=of2[:], in_=acc2[:, 0:128], func=AF.Copy, scale=ez[:, 3:4], waits=[b])
                    pacc.done(ak2, b)
                    b = DVE.op(V.scalar_tensor_tensor, out=of32[:], in0=of2[:], scalar=nlam[:, 0:1], in1=of1[:],
                               op0=ALU.mult, op1=ALU.add, waits=[a, b, t_nlam, of32_free[0]])
                    ofr.done(ofk, b)
                    c = DVE.op(V.scalar_tensor_tensor, out=oj[:], in0=of32[:], scalar=1.0, in1=of32[:], op0=ALU.mult, op1=ALU.mult,
                               accum_out=es_[:, 5:6], waits=[b])
                    c = DVE.op(V.tensor_scalar, out=es_[:, 6:7], in0=es_[:, 5:6], scalar1=1.0 / 128, scalar2=1e-5, op0=ALU.mult, op1=ALU.add, waits=[c])
                    c = POOL.op(nc.gpsimd.tensor_tensor, out=es_[:, 8:9], in0=es_[:, 6:7], in1=mhalf[:], op=ALU.pow, waits=[c, t_mh])
                    c = DVE.op(V.scalar_tensor_tensor, out=ot[:, 512 + h * 128:512 + (h + 1) * 128], in0=of32[:], scalar=es_[:, 8:9],
                               in1=gsub[:], op0=ALU.mult, op1=ALU.mult, waits=[c, t_gs, ofree])
                    of32_free[0] = [c]
                    owr.append(c)

            of_free = [[]]
            of32_free = [[]]
            for _ in prep_idx(0):
                pass
            for _ in prep_bis(0):
                pass
            for _ in prep_idx(1):
                pass
            for s in range(NQB):
                tiles = float(8 * (2 * s + 2))
                if s + 1 < NQB:
                    pumps[0].update(gen=prep_bis(s + 1), budget=0.0, rate=w_bis(s + 1) / tiles * 1.3)
                if s + 2 < NQB:
                    pumps[1].update(gen=prep_idx(s + 2), budget=0.0, rate=w_idx(s + 2) / tiles * 1.3)
                ot, ofree, ok_ = otr.next()
                owr = []
                for pi, p in enumerate([4, 5, 6, 7, 0, 1, 2, 3]):
                    attention(s, p, ot, ofree, owr)
                finish(pumps[0])
                finish(pumps[1])
                td = GQ.dma(o_scr[s * 128:(s + 1) * 128, :], ot[:], waits=owr)
                otr.done(ok_, td)
            barrier()

        def load_w(st, name, src, rows, cols, q, waits=(), order=None):
            nkc = rows // 128
            wt = sbt(st, name, [128, nkc, cols], BF16)
            srcv = src.rearrange("(kc p) n -> p kc n", p=128)
            step = 1024
            starts = list(range(0, cols, step))
            toks = [None] * len(starts)
            for ci in (order if order is not None else range(len(starts))):
                c0 = starts[ci]
                n = min(step, cols - c0)
                toks[ci] = q.dma(wt[:, :, c0:c0 + n], srcv[:, :, c0:c0 + n], waits=waits)
            return wt, toks

        with ExitStack() as p3:
            wg = sbt(p3, "wg", [128, 8, 2048], BF16)
            w_in_v = w_in.rearrange("(kc p) n -> p kc n", p=128)
            twg = [GQ.dma(wg[:, :, c0:c0 + 512], w_in_v[:, :, C_G + c0:C_G + c0 + 512]) for c0 in range(0, 2048, 512)]
            wbd, twbd = load_w(p3, "wbd", w_bd, 512, D, GQ)
            wbf, twbf = load_w(p3, "wbf", w_bf, 512, D, GQ)
            wo, two = load_w(p3, "wo", w_out, D, D, GQ)
            gbt = sbt(p3, "gbt", [128, 2048], F32)
            t_gb = SP.dma(gbt[:], bcast(gate_b))
            gmix = sbt(p3, "gmix3", [128, D], F32)
            t_gmix = SP.dma(gmix[:], bcast(norm_mix_g))
            xr = Ring([sbt(p3, f"xt{i}", [128, D], F32) for i in range(4)])
            junk = Ring([sbt(p3, f"junk{i}", [128, D], BF16) for i in range(1)])
            ssr = Ring([sbt(p3, f"ss{i}", [128, 4], F32) for i in range(4)])
            hbr = Ring([sbt(p3, f"hb{i}", [128, D], BF16) for i in range(3)])
            hTr = Ring([sbt(p3, f"hT{i}", [128, 8, 128], BF16) for i in range(2)])
            obr = Ring([sbt(p3, f"ob{i}", [128, D], BF16) for i in range(3)])
            oTr = Ring([sbt(p3, f"oT{i}", [128, 8, 128], BF16) for i in range(2)])
            gat = sbt(p3, "gat", [128, 2048], F32)
            mrg = sbt(p3, "mrg", [128, D], F32)
            mrg2 = sbt(p3, "mrg2", [128, D], F32)
            mbr = Ring([sbt(p3, f"mb{i}", [128, D], BF16) for i in range(2)])
            mTr = Ring([sbt(p3, f"mT{i}", [128, 8, 128], BF16) for i in range(2)])
            x1r = Ring([sbt(p3, f"x1t{i}", [128, D], F32) for i in range(2)])
            psT = Ring([pst(p3, f"psT{i}", [128, D], BF16) for i in range(2)])
            pp = Ring([pst(p3, f"pp{i}", [128, 512], F32) for i in range(6)])
            x1_dmas = []

            def transpose8(src_bf, tsrc, dst_ring):
                ps, pfree, pk = psT.next()
                tt = None
                for kc in range(8):
                    tt = PE.op(T.transpose, ps[:, kc * 128:(kc + 1) * 128], src_bf[:, kc * 128:(kc + 1) * 128], ident[:],
                               waits=[tsrc, pfree] if kc == 0 else ())
                dT, dfree, dk = dst_ring.next()
                te = ACT.op(A.copy, out=dT[:].rearrange("p a b -> p (a b)"), in_=ps[:], waits=[tt, dfree])
                psT.done(pk, te)
                return dT, te, dk, tt

            def st_load(s):
                xt, xfree, xk = xr.next()
                tx = SP.dma(xt[:], xq[s * 128:(s + 1) * 128, :], waits=xfree)
                hb, hfree, hk = hbr.next()
                th = rms_norm_block((junk, ssr), xt[:], tx, gmix[:], t_gmix, 1e-6, D, hb[:], hfree)
                ob, ofree, ok_ = obr.next()
                to = SP.dma(ob[:], o_scr[s * 128:(s + 1) * 128, :], waits=[ofree])
                return dict(s=s, xt=xt, xk=xk, hb=hb, th=th, hk=hk, ob=ob, to=to, ok_=ok_)

            def st_T(c):
                hT, te, tk, tt = transpose8(c["hb"], c["th"], hTr)
                hbr.done(c["hk"], tt)
                oT, teo, ok2, tto = transpose8(c["ob"], c["to"], oTr)
                obr.done(c["ok_"], tto)
                c.update(hT=hT, te=te, tk=tk, oT=oT, teo=teo, ok2=ok2)

            def st_X(c):
                hT, te, tk = c["hT"], c["te"], c["tk"]
                oT, teo, ok2 = c["oT"], c["teo"], c["ok2"]
                tg_last = None
                for gc in range(4):
                    ps, pfree, pk = pp.next()
                    tm = None
                    for kc in range(8):
                        tm = PE.op(T.matmul, ps[:], lhsT=hT[:, kc, :], rhs=wg[:, kc, gc * 512:(gc + 1) * 512], start=(kc == 0), stop=(kc == 7),
                                   waits=[te, pfree, twg] if kc == 0 else ())
                    a_ = DVE.op(V.tensor_tensor, out=gat[:, gc * 512:(gc + 1) * 512], in0=ps[:], in1=gbt[:, gc * 512:(gc + 1) * 512], op=ALU.add,
                                waits=[tm, t_gb, gat_free[0]])
                    pp.done(pk, a_)
                    tg_last = ACT.op(A.activation, out=gat[:, gc * 512:(gc + 1) * 512], in_=gat[:, gc * 512:(gc + 1) * 512], func=AF.Sigmoid, waits=[a_])
                    if gc == 3:
                        hTr.done(tk, tm)
                gat_free[0] = []
                mtoks = []
                for br, (wt, twt) in enumerate([(wbd, twbd), (wbf, twbf)]):
                    for nc_ in range(2):
                        ps, pfree, pk = pp.next()
                        tm = None
                        for kc in range(4):
                            tm = PE.op(T.matmul, ps[:], lhsT=oT[:, br * 4 + kc, :], rhs=wt[:, kc, nc_ * 512:(nc_ + 1) * 512], start=(kc == 0), stop=(kc == 3),
                                       waits=[teo, pfree, twt] if kc == 0 else ())
                        dst = (mrg if br == 0 else mrg2)[:, nc_ * 512:(nc_ + 1) * 512]
                        a_ = DVE.op(V.tensor_tensor, out=dst, in0=ps[:], in1=gat[:, br * 1024 + nc_ * 512:br * 1024 + (nc_ + 1) * 512], op=ALU.mult,
                                    waits=[tm, tg_last, mrg_free[0]])
                        pp.done(pk, a_)
                        mtoks.append(a_)
                        if br == 1 and nc_ == 1:
                            oTr.done(ok2, tm)
                gat_free[0] = list(mtoks)
                mb, mfree, mk = mbr.next()
                tmb = DVE.op(V.tensor_tensor, out=mb[:], in0=mrg[:], in1=mrg2[:], op=ALU.add, waits=[mtoks, mfree])
                mrg_free[0] = [tmb]
                c.update(mb=mb, tmb=tmb, mk=mk)

            def st_Y(c):
                s_ = c["s"]
                mT, tem, mk2, ttm = transpose8(c["mb"], c["tmb"], mTr)
                mbr.done(c["mk"], ttm)
                x1t, x1free, x1k = x1r.next()
                xtoks = []
                for nc_ in range(2):
                    ps, pfree, pk = pp.next()
                    tm = None
                    for kc in range(8):
                        tm = PE.op(T.matmul, ps[:], lhsT=mT[:, kc, :], rhs=wo[:, kc, nc_ * 512:(nc_ + 1) * 512], start=(kc == 0), stop=(kc == 7),
                                   waits=[tem, pfree, two] if kc == 0 else ())
                    a_ = DVE.op(V.tensor_tensor, out=x1t[:, nc_ * 512:(nc_ + 1) * 512], in0=ps[:], in1=c["xt"][:, nc_ * 512:(nc_ + 1) * 512], op=ALU.add,
                                waits=[tm, x1free])
                    pp.done(pk, a_)
                    xtoks.append(a_)
                    if nc_ == 1:
                        mTr.done(mk2, tm)
                xr.done(c["xk"], xtoks)
                td = GQ.dma(x1_scr[s_ * 128:(s_ + 1) * 128, :], x1t[:], waits=xtoks)
                x1r.done(x1k, td)
                x1_dmas.append(td)

            gat_free = [[]]
            mrg_free = [[]]
            ctx = {}
            ctx[0] = st_load(0)
            st_T(ctx[0])
            if NQB > 1:
                ctx[1] = st_load(1)
            prevY = None
            for s in range(NQB):
                if s + 2 < NQB:
                    ctx[s + 2] = st_load(s + 2)
                st_X(ctx[s])
                if s + 1 < NQB:
                    st_T(ctx[s + 1])
                if prevY is not None:
                    st_Y(prevY)
                prevY = ctx.pop(s)
            st_Y(prevY)
            barrier()
        early.close()

        with ExitStack() as p4:
            w1, tw1 = load_w(p4, "w1", w_f1, D, 2 * DFF, GQ, order=[0, 2, 3, 1, 4, 5])
            w2, tw2 = load_w(p4, "w2", w_f2, DFF, D, GQ)
            gffn = sbt(p4, "gffn", [128, D], F32)
            gfin = sbt(p4, "gfin", [128, D], F32)
            t_gffn = SP.dma(gffn[:], bcast(norm_ffn_g))
            t_gfin = SP.dma(gfin[:], bcast(norm_fin_g))
            x1r = Ring([sbt(p4, f"x1b{i}", [128, D], F32) for i in range(2)])
            junk = Ring([sbt(p4, f"junk{i}", [128, D], BF16) for i in range(1)])
            ssr = Ring([sbt(p4, f"ss{i}", [128, 4], F32) for i in range(2)])
            hbr = Ring([sbt(p4, f"hb{i}", [128, D], BF16) for i in range(2)])
            h2T = sbt(p4, "h2T", [128, 8, 512], BF16)
            actT = sbt(p4, "actT", [128, NFC, 512], BF16)
            sgr = Ring([sbt(p4, f"sg{i}", [128, 512], F32) for i in range(2)])
            x2r = Ring([sbt(p4, f"x2t{i}", [128, D], F32) for i in range(2)])
            psT = Ring([pst(p4, f"psT{i}", [128, D], BF16) for i in range(2)])
            pp = Ring([pst(p4, f"pp{i}", [128, 512], F32) for i in range(6)])
            h2T_free = []
            actT_free = []
            for grp in range(NQB // 4):
                th2 = []
                for bi in range(4):
                    s = grp * 4 + bi
                    x1t, x1free, x1k = x1r.next()
                    tx = SP.dma(x1t[:], x1_scr[s * 128:(s + 1) * 128, :], waits=[x1free])
                    hb, hfree, hk = hbr.next()
                    th = rms_norm_block((junk, ssr), x1t[:], tx, gffn[:], t_gffn, 1e-6, D, hb[:], hfree)
                    x1r.done(x1k, th)
                    ps, pfree, pk = psT.next()
                    tt = None
                    for kc in range(8):
                        tt = PE.op(T.transpose, ps[:, kc * 128:(kc + 1) * 128], hb[:, kc * 128:(kc + 1) * 128], ident[:],
                                   waits=[th, pfree] if kc == 0 else ())
                    hbr.done(hk, tt)
                    te = ACT.op(A.copy, out=h2T[:, :, bi * 128:(bi + 1) * 128], in_=ps[:].rearrange("p (a b) -> p a b", a=8), waits=[tt, h2T_free])
                    psT.done(pk, te)
                    th2.append(te)
                h2T_free = []
                tact = []
                last_mm = None
                for f in range(NFC):
                    psg, pfree, pkg = pp.next()
                    tmg = None
                    for kc in range(8):
                        tmg = PE.op(T.matmul, psg[:], lhsT=w1[:, kc, f * 128:(f + 1) * 128], rhs=h2T[:, kc, :], start=(kc == 0), stop=(kc == 7),
                                    waits=[th2, pfree, tw1[(f * 128) // 1024], tw1[(f * 128 + 127) // 1024]] if kc == 0 else ())
                    psu, pfree, pku = pp.next()
                    tmu = None
                    for kc in range(8):
                        tmu = PE.op(T.matmul, psu[:], lhsT=w1[:, kc, DFF + f * 128:DFF + (f + 1) * 128], rhs=h2T[:, kc, :], start=(kc == 0), stop=(kc == 7),
                                    waits=[pfree, tw1[(DFF + f * 128) // 1024], tw1[(DFF + f * 128 + 127) // 1024]] if kc == 0 else ())
                    last_mm = tmu
                    sg, sfree, sk = sgr.next()
                    ta = ACT.op(A.activation, out=sg[:], in_=psg[:], func=AF.Silu, waits=[tmg, sfree])
                    pp.done(pkg, ta)
                    tb_ = DVE.op(V.tensor_tensor, out=actT[:, f, :], in0=psu[:], in1=sg[:], op=ALU.mult, waits=[tmu, ta, actT_free])
                    pp.done(pku, tb_)
                    sgr.done(sk, tb_)
                    tact.append(tb_)
                h2T_free = [last_mm]
                actT_free = []
                last_o = None
                for bi in range(4):
                    s = grp * 4 + bi
                    x2, x2free, x2k = x2r.next()
                    tx2 = SP.dma(x2[:], x1_scr[s * 128:(s + 1) * 128, :], waits=[x2free])
                    xtoks = []
                    for nc_ in range(2):
                        ps, pfree, pk = pp.next()
                        tm = None
                        for f in range(NFC):
                            tm = PE.op(T.matmul, ps[:], lhsT=actT[:, f, bi * 128:(bi + 1) * 128], rhs=w2[:, f, nc_ * 512:(nc_ + 1) * 512],
                                       start=(f == 0), stop=(f == NFC - 1), waits=[tact, pfree, tw2] if f == 0 else ())
                        last_o = tm
                        a = DVE.op(V.tensor_tensor, out=x2[:, nc_ * 512:(nc_ + 1) * 512], in0=ps[:], in1=x2[:, nc_ * 512:(nc_ + 1) * 512], op=ALU.add,
                                   waits=[tm, tx2])
                        pp.done(pk, a)
                        xtoks.append(a)
                    e = rms_norm_block((junk, ssr), x2[:], xtoks, gfin[:], t_gfin, 1e-6, D, x2[:], [])
                    td = SP.dma(out[s * 128:(s + 1) * 128, :], x2[:], waits=[e])
                    x2r.done(x2k, td)
                actT_free = [last_o]
            barrier()
    return nc


_NC_CACHE = {}


def _get_nc(debug=False):
    if debug not in _NC_CACHE:
        _NC_CACHE[debug] = build(debug)
    return _NC_CACHE[debug]


def make_in_maps(inputs):
    x = np.ascontiguousarray(np.asarray(inputs["x"], dtype=np.float32))
    pos = np.asarray(inputs["positions"]).astype(np.int32)
    in_maps = []
    kk = np.arange(512)[None, :]
    qq = np.arange(128)[:, None]
    for c in range(8):
        b, j = c // 4, c % 4
        blocks = [4 * s + j for s in range(NQB)]
        xqc = np.concatenate([x[b, q * 128:(q + 1) * 128] for q in blocks], axis=0)
        posk = np.ascontiguousarray(pos[b].reshape(NTB, 128).T)
        posq = np.ascontiguousarray(np.stack([pos[b, q * 128:(q + 1) * 128] for q in blocks], axis=1))
        cm = np.where(kk <= j * 128 + qq, 0.0, NEG).astype(np.float32)
        m = {"xs": x[b], "xq": np.ascontiguousarray(xqc), "posk": posk, "posq": posq, "cmask": cm}
        for name in ["norm_mix_g", "w_in", "idx_k_norm_g", "idx_k_norm_b", "diff_lambda_q1", "diff_lambda_k1",
                     "diff_lambda_q2", "diff_lambda_k2", "diff_subln_g", "gate_b", "w_branch_dsa", "w_branch_diff",
                     "w_out", "norm_ffn_g", "w_ffn_in", "w_ffn_out"]:
            m[name] = np.ascontiguousarray(np.asarray(inputs[name], dtype=np.float32)[0])
        m["norm_final_g"] = np.ascontiguousarray(np.asarray(inputs["norm_final_g"], dtype=np.float32))
        in_maps.append(m)
    return in_maps


def kernel(**inputs):
    nc = _get_nc(False)
    in_maps = make_in_maps(inputs)
    res = run_bass_kernel_spmd(nc, in_maps, core_ids=list(range(8)))
    outp = np.zeros((2, S, D), dtype=np.float32)
    for c in range(8):
        b, j = c // 4, c % 4
        o = res.results[c]["out"]
        for s in range(NQB):
            q = 4 * s + j
            outp[b, q * 128:(q + 1) * 128] = o[s * 128:(s + 1) * 128]
    return outp
```

```python
import os
import math
import numpy as np
from contextlib import ExitStack
import concourse.bass as bass
import concourse.mybir as mybir
from concourse.bass_utils import run_bass_kernel_spmd

F32 = mybir.dt.float32
F32R = mybir.dt.float32r
BF16 = mybir.dt.bfloat16
I32 = mybir.dt.int32
AF = mybir.ActivationFunctionType
ALU = mybir.AluOpType
AX = mybir.AxisListType

S = 8192
D = 1024
NTB = S // 128
NQB = 16
NQ = NQB * 128
DFF = 2816
NFC = DFF // 128
TOPK = 256
NBIS = 16
NEG = -30000.0
C_QA, C_KA, C_VA, C_QI, C_KI, C_WI, C_QB, C_KB, C_VB, C_G = 0, 512, 1024, 1536, 1792, 1824, 1832, 2344, 2856, 3368
D_IN = 5416
VW = 132
TWO_PI = 2.0 * math.pi


class Eng:
    def __init__(self, nc, es, eng, name):
        self.eng = eng
        self.name = name
        self.sem = es.enter_context(nc.semaphore("sem_" + name))
        self.count = 0
        self.seen = {}

    def wait(self, toks):
        for t in toks:
            if t is None:
                continue
            if isinstance(t, list):
                self.wait(t)
                continue
            sem, val, key = t
            if self.seen.get(key, 0) >= val:
                continue
            self.eng.wait_ge(sem, val)
            self.seen[key] = val

    def op(self, fn, *args, waits=(), **kw):
        self.wait(waits)
        inst = fn(*args, **kw)
        self.count += 1
        inst.then_inc(self.sem, 1)
        tok = (self.sem, self.count, self.name)
        self.seen[self.name] = self.count - 1 if False else self.seen.get(self.name, 0)
        return tok


class DmaQ:
    def __init__(self, nc, es, eng, name, nsem=12):
        self.eng = eng
        self.name = name
        self.sems = [es.enter_context(nc.semaphore(f"dsem_{name}_{i}")) for i in range(nsem)]
        self.vals = [0] * nsem
        self.i = 0
        self.seen = {}

    def wait(self, toks):
        for t in toks:
            if t is None:
                continue
            if isinstance(t, list):
                self.wait(t)
                continue
            sem, val, key = t
            if self.seen.get(key, 0) >= val:
                continue
            self.eng.wait_ge(sem, val)
            self.seen[key] = val

    def dma(self, out, in_, waits=(), **kw):
        k = self.i
        self.i = (self.i + 1) % len(self.sems)
        key = f"{self.name}_{k}"
        if self.vals[k] > 0:
            self.wait([(self.sems[k], self.vals[k], key)])
        self.wait(waits)
        self.vals[k] += 16
        self.eng.dma_start(out=out, in_=in_, **kw).then_inc(self.sems[k], 16)
        return (self.sems[k], self.vals[k], key)

    def all_toks(self):
        return [(self.sems[k], self.vals[k], f"{self.name}_{k}") for k in range(len(self.sems)) if self.vals[k] > 0]


class Ring:
    def __init__(self, tiles):
        self.tiles = tiles
        self.rd = [[] for _ in tiles]
        self.i = -1

    def next(self):
        self.i = (self.i + 1) % len(self.tiles)
        k = self.i
        toks = self.rd[k]
        self.rd[k] = []
        return self.tiles[k], toks, k

    def done(self, k, tok):
        self.rd[k].append(tok)


def build(debug=False):
    nc = bass.Bass("TRN2", target_bir_lowering=False)
    dt_in = lambda n, s, d=F32: nc.dram_tensor(n, s, d, kind="ExternalInput").ap()
    xs = dt_in("xs", [S, D])
    xq = dt_in("xq", [NQ, D])
    posk = dt_in("posk", [128, NTB], I32)
    posq = dt_in("posq", [128, NQB], I32)
    cmask = dt_in("cmask", [128, 512])
    norm_mix_g = dt_in("norm_mix_g", [D])
    w_in = dt_in("w_in", [D, D_IN])
    idx_g = dt_in("idx_k_norm_g", [32])
    idx_b = dt_in("idx_k_norm_b", [32])
    lq1 = dt_in("diff_lambda_q1", [64])
    lk1 = dt_in("diff_lambda_k1", [64])
    lq2 = dt_in("diff_lambda_q2", [64])
    lk2 = dt_in("diff_lambda_k2", [64])
    subln_g = dt_in("diff_subln_g", [128])
    gate_b = dt_in("gate_b", [2048])
    w_bd = dt_in("w_branch_dsa", [512, D])
    w_bf = dt_in("w_branch_diff", [512, D])
    w_out = dt_in("w_out", [D, D])
    norm_ffn_g = dt_in("norm_ffn_g", [D])
    w_f1 = dt_in("w_ffn_in", [D, 2 * DFF])
    w_f2 = dt_in("w_ffn_out", [DFF, D])
    norm_fin_g = dt_in("norm_final_g", [D])
    out = nc.dram_tensor("out", [NQ, D], F32, kind="ExternalOutput").ap()
    skind = "ExternalOutput" if debug else "Internal"
    kT_scr = nc.dram_tensor("kT_scr", [8, 128, S], BF16, kind=skind).ap()
    kiT_scr = nc.dram_tensor("kiT_scr", [32, S], BF16, kind=skind).ap()
    v_scr = nc.dram_tensor("v_scr", [S, 8 * VW], BF16, kind=skind).ap()
    qT_scr = nc.dram_tensor("qT_scr", [8, 128, NQ], BF16, kind=skind).ap()
    qiT_scr = nc.dram_tensor("qiT_scr", [64, 4, NQ], BF16, kind=skind).ap()
    o_scr = nc.dram_tensor("o_scr", [NQ, D], BF16, kind=skind).ap()
    x1_scr = nc.dram_tensor("x1_scr", [NQ, D], F32, kind=skind).ap()

    with ExitStack() as es:
        uid = [0]

        def sbt(st, n, s, d):
            uid[0] += 1
            return st.enter_context(nc.sbuf_tensor(f"{n}_{uid[0]}", s, d))

        def pst(st, n, s, d):
            uid[0] += 1
            return st.enter_context(nc.psum_tensor(f"{n}_{uid[0]}", s, d))

        ident = sbt(es, "ident", [128, 128], BF16)
        early = ExitStack()
        wabs = sbt(early, "wabs", [128, NQB, 8], F32)
        nlam = sbt(early, "nlam", [128, 1], F32)
        small = sbt(early, "small", [128, 64], F32)
        cb = sbt(early, "cb", [128, 512], BF16)
        cbf = sbt(early, "cbf", [128, 512], F32)
        early1 = ExitStack()
        cosk = sbt(early1, "cosk", [128, NTB, 8], F32)
        sink = sbt(early1, "sink", [128, NTB, 8], F32)
        cosq = sbt(early1, "cosq", [128, NQB, 8], F32)
        sinq = sbt(early1, "sinq", [128, NQB, 8], F32)
        gmix = sbt(early1, "gmix", [128, D], F32)

        es.enter_context(nc.Block())
        PE = Eng(nc, es, nc.tensor, "pe")
        ACT = Eng(nc, es, nc.scalar, "act")
        DVE = Eng(nc, es, nc.vector, "dve")
        POOL = Eng(nc, es, nc.gpsimd, "pool")
        SP = DmaQ(nc, es, nc.sync, "sp", 16)
        GQ = DmaQ(nc, es, nc.gpsimd, "gq", 8)
        V = nc.vector
        A = nc.scalar
        T = nc.tensor

        def bcast(ap1d, n=128):
            return ap1d.partition_broadcast(n)

        def barrier():
            toks = [(e.sem, e.count, e.name) for e in (PE, ACT, DVE, POOL) if e.count > 0]
            toks += SP.all_toks() + GQ.all_toks()
            for e in (PE, ACT, DVE, POOL, SP):
                e.wait(toks)

        t = POOL.op(nc.gpsimd.memset, ident[:], 1.0)
        t_ident = POOL.op(nc.gpsimd.affine_select, out=ident[:], in_=ident[:], pattern=[[-1, 128]],
                          compare_op=ALU.is_equal, fill=0.0, base=0, channel_multiplier=1, waits=[t])
        t_gmix = SP.dma(gmix[:], bcast(norm_mix_g))
        t_cbf = SP.dma(cbf[:], cmask)
        t_cb = DVE.op(V.tensor_copy, out=cb[:], in_=cbf[:], waits=[t_cbf])

        with ExitStack() as p0:
            lt = sbt(p0, "lt", [128, 4, 64], F32)
            lj = sbt(p0, "lj", [128, 64], F32)
            tl = [SP.dma(lt[:, i, :], bcast(a)) for i, a in enumerate([lq1, lk1, lq2, lk2])]
            t1 = DVE.op(V.tensor_tensor, out=lj[:], in0=lt[:, 0, :], in1=lt[:, 1, :], op=ALU.mult, waits=tl)
            t1 = DVE.op(V.tensor_reduce, out=small[:, 0:1], in_=lj[:], axis=AX.X, op=ALU.add, waits=[t1])
            t2 = DVE.op(V.tensor_tensor, out=lj[:], in0=lt[:, 2, :], in1=lt[:, 3, :], op=ALU.mult, waits=[t1])
            t2 = DVE.op(V.tensor_reduce, out=small[:, 1:2], in_=lj[:], axis=AX.X, op=ALU.add, waits=[t2])
            t3 = ACT.op(A.activation, out=small[:, 2:4], in_=small[:, 0:2], func=AF.Exp, waits=[t2])
            t_nlam = DVE.op(V.scalar_tensor_tensor, out=nlam[:], in0=small[:, 3:4], scalar=-0.2, in1=small[:, 2:3],
                            op0=ALU.add, op1=ALU.subtract, waits=[t3])

            invf = sbt(p0, "invf", [128, 8], F32)
            tinv = None
            for i in range(8):
                fv = float(np.power(np.float32(500000.0), -np.float32(2 * i) / np.float32(16)))
                tinv = DVE.op(V.memset, invf[:, i:i + 1], fv)

            def rope_table(pos_ap, n, cos_t, sin_t, nm):
                pi_ = sbt(p0, "pi_" + nm, [128, n], I32)
                pf = sbt(p0, "pf_" + nm, [128, n], F32)
                ang = sbt(p0, "ang_" + nm, [128, n, 8], F32)
                yy = sbt(p0, "yy_" + nm, [128, n, 8], F32)
                ni = sbt(p0, "ni_" + nm, [128, n, 8], I32)
                tp = SP.dma(pi_[:], pos_ap)
                a = DVE.op(V.tensor_copy, out=pf[:], in_=pi_[:], waits=[tp])
                a = DVE.op(V.tensor_tensor, out=ang[:], in0=pf[:].unsqueeze(2).to_broadcast([128, n, 8]),
                           in1=invf[:].unsqueeze(1).to_broadcast([128, n, 8]), op=ALU.mult, waits=[a, tinv])

                def reduce_sin(src_add, dst):
                    b = DVE.op(V.tensor_scalar, out=yy[:], in0=ang[:], scalar1=src_add, scalar2=1.0 / TWO_PI,
                               op0=ALU.add, op1=ALU.mult, waits=[a])
                    b = DVE.op(V.tensor_copy, out=ni[:], in_=yy[:], waits=[b])
                    b = DVE.op(V.tensor_copy, out=yy[:], in_=ni[:], waits=[b])
                    c1 = 6.28125
                    c2 = TWO_PI - 6.28125
                    b = DVE.op(V.scalar_tensor_tensor, out=dst, in0=yy[:], scalar=-c1, in1=ang[:], op0=ALU.mult, op1=ALU.add, waits=[b])
                    b = DVE.op(V.scalar_tensor_tensor, out=dst, in0=yy[:], scalar=-c2, in1=dst, op0=ALU.mult, op1=ALU.add, waits=[b])
                    b = DVE.op(V.tensor_scalar, out=dst, in0=dst, scalar1=src_add, scalar2=3.1415925, op0=ALU.add, op1=ALU.min, waits=[b])
                    b = DVE.op(V.tensor_scalar, out=dst, in0=dst, scalar1=-3.1415925, scalar2=None, op0=ALU.max, waits=[b])
                    return ACT.op(A.activation, out=dst, in_=dst, func=AF.Sin, waits=[b])
                ts = reduce_sin(0.0, sin_t[:])
                tc = reduce_sin(math.pi / 2.0, cos_t[:])
                return [ts, tc]
            t_ropek = rope_table(posk, NTB, cosk, sink, "k")
            t_ropeq = rope_table(posq, NQB, cosq, sinq, "q")
            barrier()

        def rms_norm_block(st_rings, x_tile, tx, g_tile, tg, eps, n_feat, hb_tile, hb_free):
            junk, ssr = st_rings
            jt, jfree, jk = junk.next()
            col, cfree, ck = ssr.next()
            a = ACT.op(A.activation, out=jt[:, :n_feat], in_=x_tile, func=AF.Square, accum_out=col[:, 0:1],
                       waits=[tx, jfree, cfree])
            junk.done(jk, a)
            b = DVE.op(V.tensor_scalar, out=col[:, 1:2], in0=col[:, 0:1], scalar1=1.0 / n_feat, scalar2=eps,
                       op0=ALU.mult, op1=ALU.add, waits=[a])
            c = ACT.op(A.activation, out=col[:, 2:3], in_=col[:, 1:2], func=AF.Sqrt, waits=[b])
            d = DVE.op(V.reciprocal, out=col[:, 3:4], in_=col[:, 2:3], waits=[c])
            e = DVE.op(V.scalar_tensor_tensor, out=hb_tile, in0=x_tile, scalar=col[:, 3:4], in1=g_tile,
                       op0=ALU.mult, op1=ALU.mult, waits=[d, tg, hb_free])
            ssr.done(ck, e)
            return e

        rope_last = []

        def rope_apply(tile3, H, half, cs, sn, tmp, waits):
            x1 = tile3[:, :, 0:half]
            x2 = tile3[:, :, half:2 * half]
            cB = cs.unsqueeze(1).to_broadcast([128, H, half])
            sB = sn.unsqueeze(1).to_broadcast([128, H, half])
            tv = lambda i: tmp[:, i, 0:H * half].rearrange("p (h d) -> p h d", h=H)
            waits = list(waits) + rope_last
            a1 = DVE.op(V.tensor_tensor, out=tv(0), in0=x1, in1=cB, op=ALU.mult, waits=waits)
            a2 = DVE.op(V.tensor_tensor, out=tv(1), in0=x2, in1=sB, op=ALU.mult, waits=waits)
            a3 = DVE.op(V.tensor_tensor, out=tv(2), in0=x2, in1=cB, op=ALU.mult, waits=waits)
            a4 = DVE.op(V.tensor_tensor, out=tv(3), in0=x1, in1=sB, op=ALU.mult, waits=waits)
            b1 = DVE.op(V.tensor_tensor, out=x1, in0=tv(0), in1=tv(1), op=ALU.subtract, waits=[a1, a2, a3, a4])
            b2 = DVE.op(V.tensor_tensor, out=x2, in0=tv(2), in1=tv(3), op=ALU.add, waits=[a1, a2, a3, a4])
            rope_last[:] = [b1, b2]
            return [b1, b2]

        with ExitStack() as p1:
            NKV = 2080
            NQC = 1288
            wkv = sbt(p1, "wkv", [128, 8, NKV], BF16)
            wq = sbt(p1, "wq", [128, 8, NQC], BF16)
            w_in_v = w_in.rearrange("(kc p) n -> p kc n", p=128)
            tw = []
            for (dst0, c0, n) in [(0, C_KA, 512), (512, C_KB, 512), (1024, C_VA, 512), (1536, C_VB, 512), (2048, C_KI, 32)]:
                tw.append(GQ.dma(wkv[:, :, dst0:dst0 + n], w_in_v[:, :, c0:c0 + n]))
            twq = []
            for (dst0, c0, n) in [(0, C_QA, 512), (512, C_QB, 512), (1024, C_QI, 256), (1280, C_WI, 8)]:
                twq.append(GQ.dma(wq[:, :, dst0:dst0 + n], w_in_v[:, :, c0:c0 + n]))
            lng = sbt(p1, "lng", [128, 32], F32)
            lnb = sbt(p1, "lnb", [128, 32], F32)
            t_lng = SP.dma(lng[:], bcast(idx_g))
            t_lnb = SP.dma(lnb[:], bcast(idx_b))

            xr = Ring([sbt(p1, f"xt{i}", [128, D], F32) for i in range(3)])
            junk = Ring([sbt(p1, f"junk{i}", [128, D], BF16) for i in range(1)])
            ssr = Ring([sbt(p1, f"ss{i}", [128, 4], F32) for i in range(4)])
            hbr = Ring([sbt(p1, f"hb{i}", [128, D], BF16) for i in range(3)])
            hTr = Ring([sbt(p1, f"hT{i}", [128, 8, 128], BF16) for i in range(3)])
            kfr = Ring([sbt(p1, f"kf{i}", [128, 1024], F32) for i in range(2)])
            kbr = Ring([sbt(p1, f"kb{i}", [128, 1024], BF16) for i in range(3)])
            kTr = Ring([sbt(p1, f"kTt{i}", [128, 8, 128], BF16) for i in range(2)])
            vtr = Ring([sbt(p1, f"vt{i}", [128, 8, VW], BF16) for i in range(2)])
            rtmp = sbt(p1, "rtmp", [128, 4, 128], F32)
            kif = sbt(p1, "kif", [128, 8], F32)
            kic = sbt(p1, "kic", [128, 32], F32)
            kij = sbt(p1, "kij", [128, 32], F32)
            kibr = Ring([sbt(p1, f"kib{i}", [128, 32], BF16) for i in range(3)])
            kiTr = Ring([sbt(p1, f"kiT{i}", [32, 128], BF16) for i in range(2)])
            qwr = Ring([sbt(p1, f"qw{i}", [128, 264], F32) for i in range(2)])
            qibr = Ring([sbt(p1, f"qib{i}", [128, 256], BF16) for i in range(3)])
            qiTr = Ring([sbt(p1, f"qiT{i}", [32, 8, 128], BF16) for i in range(2)])
            psT = Ring([pst(p1, f"psT{i}", [128, D], BF16) for i in range(2)])
            pp = Ring([pst(p1, f"pp{i}", [128, 512], F32) for i in range(4)])
            pkT = Ring([pst(p1, f"pkT{i}", [128, D], BF16) for i in range(2)])
            t_vinit = []
            for vt_ in vtr.tiles:
                t0 = POOL.op(nc.gpsimd.memset, vt_[:], 0.0)
                t1_ = POOL.op(nc.gpsimd.memset, vt_[:, 0:4, :].rearrange("p a (s e) -> p (a s) e", s=2)[:, :, 64:65], 1.0, waits=[t0])
                t_vinit.append(POOL.op(nc.gpsimd.memset, vt_[:, 4:8, 128:129], 1.0, waits=[t0, t1_]))

            def aevac(out_ap, in_ap, waits):
                return ACT.op(A.copy, out=out_ap, in_=in_ap, waits=waits)

            def stageA1(src_rows):
                xt, xfree, xk = xr.next()
                tx = SP.dma(xt[:], src_rows, waits=xfree)
                hb, hfree, hk = hbr.next()
                th = rms_norm_block((junk, ssr), xt[:], tx, gmix[:], t_gmix, 1e-6, D, hb[:], hfree)
                xr.done(xk, th)
                return hb, th, hk

            def stageA2(hb, th, hk):
                ps, pfree, pk = psT.next()
                tt = None
                for kc in range(8):
                    tt = PE.op(T.transpose, ps[:, kc * 128:(kc + 1) * 128], hb[:, kc * 128:(kc + 1) * 128], ident[:],
                               waits=[th, t_ident, pfree] if kc == 0 else ())
                hbr.done(hk, tt)
                hT, tfree, tk = hTr.next()
                te = aevac(hT[:].rearrange("p a b -> p (a b)"), ps[:], [tt, tfree])
                psT.done(pk, te)
                return hT, te, tk

            def project(hT, th, w_tile, c0, n, wtoks):
                ps, pfree, pk = pp.next()
                tt = None
                for kc in range(8):
                    tt = PE.op(T.matmul, ps[:, 0:n], lhsT=hT[:, kc, :], rhs=w_tile[:, kc, c0:c0 + n],
                               start=(kc == 0), stop=(kc == 7), waits=[th, pfree, wtoks] if kc == 0 else ())
                return ps, tt, pk

            def transpose_out(src_bf, tsrc, nchunk, width, ring_sb, dst_dram):
                ps, pfree, pk = pkT.next()
                tt = None
                for c in range(nchunk):
                    tt = PE.op(T.transpose, ps[0:width, c * 128:(c + 1) * 128], src_bf[:, c * width:(c + 1) * width], ident[:],
                               waits=[tsrc, pfree, t_ident] if c == 0 else ())
                sbT, sfree, sk = ring_sb.next()
                te = aevac(sbT[:].rearrange("p a b -> p (a b)") if len(sbT.shape) == 3 else sbT[:],
                           ps[0:width, 0:nchunk * 128], [tt, sfree])
                pkT.done(pk, te)
                td = GQ.dma(dst_dram, sbT[:], waits=[te])
                ring_sb.done(sk, td)
                return tt

            def transpose_out_qi(src_bf, tsrc, s):
                ps, pfree, pk = pkT.next()
                tt = None
                for c in range(8):
                    tt = PE.op(T.transpose, ps[0:32, c * 128:(c + 1) * 128], src_bf[:, c * 32:(c + 1) * 32], ident[:],
                               waits=[tsrc, pfree, t_ident] if c == 0 else ())
                sbT, sfree, sk = qiTr.next()
                te = aevac(sbT[:].rearrange("p a b -> p (a b)"), ps[0:32, 0:1024], [tt, sfree])
                pkT.done(pk, te)
                for g in range(2):
                    td = GQ.dma(qiT_scr[g * 32:(g + 1) * 32, :, s * 128:(s + 1) * 128], sbT[:, g::2, :], waits=[te])
                    qiTr.done(sk, td)
                return tt

            def qk_pair(hT, th, w_tile, wtoks, cs, sn, scale, dst_dram):
                kf, kfree, kk = kfr.next()
                tes = []
                for gi in range(2):
                    ps, tmm, pk = project(hT, th, w_tile, gi * 512, 512, wtoks)
                    te = aevac(kf[:, gi * 512:(gi + 1) * 512], ps[:], [tmm, kfree])
                    pp.done(pk, te)
                    tes.append(te)
                tr = rope_apply(kf[:].rearrange("p (h d) -> p h d", h=16), 16, 8, cs, sn, rtmp, tes)
                kb, bfree, bk = kbr.next()
                if scale == 1.0:
                    tcst = DVE.op(V.tensor_copy, out=kb[:], in_=kf[:], waits=[tr, bfree])
                else:
                    tcst = DVE.op(V.tensor_scalar, out=kb[:], in0=kf[:], scalar1=scale, scalar2=None, op0=ALU.mult, waits=[tr, bfree])
                kfr.done(kk, tcst)
                return kb, bk, tcst

            def stageB_kv(tb, hT, th, hk, mid_cb):
                cs = cosk[:, tb, :]
                sn = sink[:, tb, :]
                kb, bk, tcst = qk_pair(hT, th, wkv, tw, cs, sn, 1.0, None)
                mid_cb()
                vt, vfree, vk = vtr.next()
                tvs = []
                for gi in (2, 3):
                    ps, tmm, pk = project(hT, th, wkv, gi * 512, 512, tw)
                    if gi == 2:
                        dstv = vt[:, 0:4, :].rearrange("p a (s e) -> p (a s) e", s=2)[:, :, 0:64]
                        te = aevac(dstv, ps[:].rearrange("p (h e) -> p h e", e=64), [tmm, vfree, t_vinit])
                    else:
                        te = aevac(vt[:, 4:8, 0:128], ps[:].rearrange("p (h e) -> p h e", e=128), [tmm, vfree, t_vinit])
                    pp.done(pk, te)
                    tvs.append(te)
                td = GQ.dma(v_scr[tb * 128:(tb + 1) * 128, :], vt[:].rearrange("p a b -> p (a b)"), waits=tvs)
                vtr.done(vk, td)
                ps, tmm, pk = project(hT, th, wkv, 2048, 32, tw)
                hTr.done(hk, tmm)
                a = DVE.op(V.tensor_reduce, out=kif[:, 0:1], in_=ps[:, 0:32], axis=AX.X, op=ALU.add, waits=[tmm])
                a = DVE.op(V.tensor_scalar, out=kif[:, 1:2], in0=kif[:, 0:1], scalar1=1.0 / 32, scalar2=None, op0=ALU.mult, waits=[a])
                a = DVE.op(V.tensor_scalar, out=kic[:], in0=ps[:, 0:32], scalar1=kif[:, 1:2], scalar2=None, op0=ALU.subtract, waits=[a])
                pp.done(pk, a)
                b = ACT.op(A.activation, out=kij[:], in_=kic[:], func=AF.Square, accum_out=kif[:, 2:3], waits=[a])
                b = DVE.op(V.tensor_scalar, out=kif[:, 3:4], in0=kif[:, 2:3], scalar1=1.0 / 32, scalar2=1e-6, op0=ALU.mult, op1=ALU.add, waits=[b])
                b = ACT.op(A.activation, out=kif[:, 4:5], in_=kif[:, 3:4], func=AF.Sqrt, waits=[b])
                b = DVE.op(V.reciprocal, out=kif[:, 5:6], in_=kif[:, 4:5], waits=[b])
                b = DVE.op(V.scalar_tensor_tensor, out=kic[:], in0=kic[:], scalar=kif[:, 5:6], in1=lng[:], op0=ALU.mult, op1=ALU.mult, waits=[b, t_lng])
                b = DVE.op(V.tensor_tensor, out=kic[:], in0=kic[:], in1=lnb[:], op=ALU.add, waits=[b, t_lnb])
                csi = cosk[:, tb, :].rearrange("p (a two) -> p a two", two=2)[:, :, 0]
                sni = sink[:, tb, :].rearrange("p (a two) -> p a two", two=2)[:, :, 0]
                tr = rope_apply(kic[:].rearrange("p (h d) -> p h d", h=1), 1, 4, csi, sni, rtmp, [b])
                kib, bfree, bk2 = kibr.next()
                tc2 = DVE.op(V.tensor_copy, out=kib[:], in_=kic[:], waits=[tr, bfree])

                def b2():
                    tlast = transpose_out(kb, tcst, 8, 128, kTr, kT_scr[:, :, tb * 128:(tb + 1) * 128].rearrange("c f t -> f c t"))
                    kbr.done(bk, tlast)
                    tl2 = transpose_out(kib, tc2, 1, 32, kiTr, kiT_scr[:, tb * 128:(tb + 1) * 128])
                    kibr.done(bk2, tl2)
                return b2

            def stageB_q(s, hT, th, hk, mid_cb):
                cs = cosq[:, s, :]
                sn = sinq[:, s, :]
                kb, bk, tcst = qk_pair(hT, th, wq, twq, cs, sn, 0.125, None)
                mid_cb()
                ps, tmm, pk = project(hT, th, wq, 1024, 264, twq)
                hTr.done(hk, tmm)
                qw, qfree, qk = qwr.next()
                te = aevac(qw[:], ps[:, 0:264], [tmm, qfree])
                pp.done(pk, te)
                csi = cosq[:, s, :].rearrange("p (a two) -> p a two", two=2)[:, :, 0]
                sni = sinq[:, s, :].rearrange("p (a two) -> p a two", two=2)[:, :, 0]
                tr = rope_apply(qw[:, 0:256].rearrange("p (h d) -> p h d", h=8), 8, 4, csi, sni, rtmp, [te])
                a1 = DVE.op(V.tensor_scalar, out=wabs[:, s, :], in0=qw[:, 256:264], scalar1=1.0 / 16, scalar2=None, op0=ALU.mult, waits=[te])
                a2 = a1
                qib, bfree, bk2 = qibr.next()
                tc2 = DVE.op(V.tensor_copy, out=qib[:], in_=qw[:, 0:256], waits=[tr, bfree])
                qwr.done(qk, [tc2, a1, a2])

                def b2():
                    tlast = transpose_out(kb, tcst, 8, 128, kTr, qT_scr[:, :, s * 128:(s + 1) * 128].rearrange("c f t -> f c t"))
                    kbr.done(bk, tlast)
                    tl2 = transpose_out_qi(qib, tc2, s)
                    qibr.done(bk2, tl2)
                return b2

            items = [("kv", tb, xs[tb * 128:(tb + 1) * 128, :]) for tb in range(NTB)] + \
                    [("q", s, xq[s * 128:(s + 1) * 128, :]) for s in range(NQB)]
            n_it = len(items)
            a1 = {}
            a2 = {}
            a1[0] = stageA1(items[0][2])
            a2[0] = stageA2(*a1[0])
            if n_it > 1:
                a1[1] = stageA1(items[1][2])
            prev_b2 = None
            for i in range(n_it):
                if i + 2 < n_it:
                    a1[i + 2] = stageA1(items[i + 2][2])
                kind, idx, _ = items[i]
                hT, th, hk = a2.pop(i)
                def mid_cb(i=i):
                    if i + 1 < n_it:
                        a2[i + 1] = stageA2(*a1.pop(i + 1))
                if kind == "kv":
                    b2 = stageB_kv(idx, hT, th, hk, mid_cb)
                else:
                    b2 = stageB_q(idx, hT, th, hk, mid_cb)
                if prev_b2 is not None:
                    prev_b2()
                prev_b2 = b2
            if prev_b2 is not None:
                prev_b2()
            barrier()

        early1.close()

        with ExitStack() as p2:
            kiT = sbt(p2, "kiT", [64, S], BF16)
            t_kiT = [SP.dma(kiT[g * 32:(g + 1) * 32, :], kiT_scr) for g in range(2)]
            gsub = sbt(p2, "gsub", [128, 128], F32)
            t_gs = SP.dma(gsub[:], bcast(subln_g))
            t_gs = DVE.op(V.tensor_scalar, out=gsub[:], in0=gsub[:], scalar1=0.8, scalar2=None, op0=ALU.mult, waits=[t_gs])
            ident2 = sbt(p2, "ident2", [128, 2, 128], BF16)
            t_id2 = [DVE.op(V.tensor_copy, out=ident2[:, i, :], in_=ident[:], waits=[t_ident]) for i in range(2)]
            Kr = Ring([sbt(p2, f"Kb{i}", [128, S], BF16) for i in range(2)])
            Vr = Ring([sbt(p2, f"Vb{i}", [128, NTB, VW], BF16) for i in range(2)])
            Mb = [sbt(p2, f"Mb{i}", [128, S], BF16) for i in range(2)]
            Isc2 = [sbt(p2, f"Isc{i}", [128, S], F32) for i in range(2)]
            qbd_tiles = [sbt(p2, f"qbd{i}", [128, 2, 128], BF16) for i in range(3)]
            t_qz = [POOL.op(nc.gpsimd.memset, q_[:], 0.0) for q_ in qbd_tiles]
            qbr = Ring(qbd_tiles)
            qiT = [sbt(p2, f"qiTs{i}", [64, 4, 128], BF16) for i in range(2)]
            rl = Ring([sbt(p2, f"rl{i}", [128, 512], BF16) for i in range(6)])
            identf = sbt(p2, "identf", [128, 128], F32)
            t_idf = DVE.op(V.tensor_copy, out=identf[:], in_=ident[:], waits=[t_ident])
            dgb = [sbt(p2, f"dgb{i}", [128, 8, 128], BF16) for i in range(2)]
            etr = Ring([sbt(p2, f"et{i}", [128, 512], BF16) for i in range(4)])
            otr = Ring([sbt(p2, f"ot{i}", [128, D], BF16) for i in range(2)])
            of32 = sbt(p2, "of32", [128, 128], F32)
            oj = sbt(p2, "oj", [128, 128], F32)
            bs = sbt(p2, "bs", [128, 16], F32)
            es_ = sbt(p2, "es_", [128, 16], F32)
            pss = Ring([pst(p2, f"pss{i}", [128, 512], F32) for i in range(4)])
            pI = pst(p2, "pI", [128, 512], F32)
            pacc = Ring([pst(p2, f"pacc{i}", [128, 512], F32) for i in range(3)])
            ofr = Ring([(sbt(p2, f"of1_{i}", [128, 128], F32), sbt(p2, f"of2_{i}", [128, 128], F32)) for i in range(2)])
            mhalf = sbt(p2, "mhalf", [128, 1], F32)
            t_mh = POOL.op(nc.gpsimd.memset, mhalf[:], -0.5)
            ez = sbt(p2, "ez", [128, 8], F32)
            Mb_ready = [None, None]
            Mb_readers = [[], []]
            qiT_readers = [[], []]
            dg_readers = [[], []]
            Isc_free = [[], []]
            idx_done = [None, None]
            pI_free = [[]]

            def prep_idx(s):
                li = s % 2
                Isc = Isc2[li]
                nk = (4 * s + 4) * 128
                nch = nk // 512
                wb = 0.76 * (s + 1)
                tq = SP.dma(qiT[li][:], qiT_scr[:, :, s * 128:(s + 1) * 128], waits=qiT_readers[li])
                qiT_readers[li] = []
                tdg = None
                for h in range(8):
                    tdg = ACT.op(A.activation, out=dgb[li][:, h, :], in_=identf[:], func=AF.Copy, scale=wabs[:, s, h:h + 1],
                                 waits=[t_idf, dg_readers[li]] if h == 0 else ())
                dg_readers[li] = []
                yield 1.0
                tI = None
                pending = None

                def flush(pend):
                    (pc, pr, items) = pend
                    tacc = None
                    for g, (prt, pta, prk) in enumerate(items):
                        h = 2 * pr + g
                        tacc = PE.op(T.matmul, pI[:], lhsT=dgb[li][:, h, :], rhs=prt[:],
                                     start=(pr == 0 and g == 0), stop=(pr == 3 and g == 1),
                                     waits=[pta, tdg, pI_free[0]])
                        rl.done(prk, tacc)
                    return tacc
                pendq = []

                def drain(keep):
                    nonlocal tI
                    tacc = None
                    while len(pendq) > keep:
                        pend = pendq.pop(0)
                        tacc = flush(pend)
                        if pend[1] == 3:
                            pc = pend[0]
                            tI = ACT.op(A.copy, out=Isc[:, pc * 512:(pc + 1) * 512], in_=pI[:], waits=[tacc, Isc_free[li]])
                            pI_free[0] = [tI]
                    return tacc
                for c in range(nch):
                    for r in range(4):
                        drain(1)
                        mm = []
                        for g in range(2):
                            ps, pfree, pk = pss.next()
                            tm = PE.op(T.matmul, ps[:], lhsT=qiT[li][g * 32:(g + 1) * 32, r, :], rhs=kiT[g * 32:(g + 1) * 32, c * 512:(c + 1) * 512],
                                       start=True, stop=True, waits=[tq, t_kiT, pfree])
                            mm.append((ps, pk, tm))
                        items = []
                        for g in range(2):
                            ps, pk, tm = mm[g]
                            rt, rfree, rk = rl.next()
                            if True:
                                ta = ACT.op(A.activation, out=rt[:], in_=ps[:], func=AF.Relu, waits=[tm, rfree])
                            else:
                                ta = DVE.op(V.tensor_scalar, out=rt[:], in0=ps[:], scalar1=0.0, scalar2=None, op0=ALU.max, waits=[tm, rfree])
                            pss.done(pk, ta)
                            items.append((rt, ta, rk))
                        pendq.append((c, r, items))
                        yield 0.5
                tacc = drain(0)
                qiT_readers[li].append(tacc)
                dg_readers[li].append(tacc)
                idx_done[li] = tI

            def prep_bis(s):
                li = s % 2
                Isc = Isc2[li]
                nk = (4 * s + 4) * 128
                wb = 0.76 * (s + 1)
                tI = idx_done[li]
                Iv = Isc[:, 0:nk]
                a = DVE.op(V.tensor_reduce, out=bs[:, 0:1], in_=Iv, axis=AX.X, op=ALU.max, waits=[tI])
                yield wb
                a = DVE.op(V.tensor_reduce, out=bs[:, 1:2], in_=Iv, axis=AX.X, op=ALU.min, waits=[a])
                a = DVE.op(V.scalar_tensor_tensor, out=bs[:, 2:3], in0=bs[:, 0:1], scalar=1.0, in1=bs[:, 1:2], op0=ALU.add, op1=ALU.subtract, waits=[a])
                a = DVE.op(V.tensor_tensor, out=Isc[:, nk - 512:nk], in0=Isc[:, nk - 512:nk], in1=cbf[:], op=ALU.add, waits=[a, t_cbf])
                yield wb
                lo = bs[:, 1:2]
                w0 = bs[:, 2:3]
                mid = bs[:, 3:4]
                cnt = bs[:, 4:5]
                gg = bs[:, 5:6]
                for it in range(1, NBIS + 1):
                    sc = 2.0 ** (-it)
                    a = DVE.op(V.tensor_scalar, out=mid, in0=w0, scalar1=sc, scalar2=lo, op0=ALU.mult, op1=ALU.add, waits=[a])
                    a = DVE.op(V.tensor_scalar, out=Mb[li][:, 0:nk], in0=Iv, scalar1=mid, scalar2=None, op0=ALU.is_ge, op1=ALU.add,
                               accum_out=cnt, waits=[a, Mb_readers[li]])
                    Mb_readers[li] = []
                    a = DVE.op(V.tensor_scalar, out=gg, in0=cnt, scalar1=TOPK - 0.5, scalar2=sc, op0=ALU.is_ge, op1=ALU.mult, waits=[a])
                    a = DVE.op(V.scalar_tensor_tensor, out=lo, in0=w0, scalar=gg, in1=lo, op0=ALU.mult, op1=ALU.add, waits=[a])
                    yield wb
                a = DVE.op(V.tensor_scalar, out=Mb[li][:, 0:nk], in0=Iv, scalar1=lo, scalar2=NEG, op0=ALU.is_lt, op1=ALU.mult, waits=[a])
                Mb_ready[li] = a
                Isc_free[li] = [a]

            def w_idx(s):
                return 1 + 2 * (s + 1)

            def w_bis(s):
                return (2 + NBIS) * 0.76 * (s + 1)

            pumps = [{"gen": None, "budget": 0.0, "rate": 0.0}, {"gen": None, "budget": 0.0, "rate": 0.0}]

            def pump():
                for st in pumps:
                    if st["gen"] is None:
                        continue
                    st["budget"] += st["rate"]
                    while st["budget"] > 0.0 and st["gen"] is not None:
                        try:
                            st["budget"] -= next(st["gen"])
                        except StopIteration:
                            st["gen"] = None

            def finish(st):
                if st["gen"] is not None:
                    for _ in st["gen"]:
                        pass
                    st["gen"] = None

            def attention(s, p, ot, ofree, owr):
                li = s % 2
                is_dsa = p < 4
                nkb = 4 * s + 4
                kmax = nkb * 128
                Kb, kfree, kk = Kr.next()
                Vb, vfree, vk = Vr.next()
                tK = SP.dma(Kb[:, 0:kmax], kT_scr[p, :, 0:kmax], waits=kfree)
                tV = []
                for b0 in range(0, nkb, 8):
                    nb = min(8, nkb - b0)
                    tV.append(SP.dma(Vb[:, b0:b0 + nb, :], v_scr[b0 * 128:(b0 + nb) * 128, p * VW:(p + 1) * VW].rearrange("(b t) w -> t b w", t=128), waits=vfree))
                qbd, qfree, qk = qbr.next()
                tq = [SP.dma(qbd[m * 64:(m + 1) * 64, m, :], qT_scr[p, m * 64:(m + 1) * 64, s * 128:(s + 1) * 128], waits=[qfree, t_qz]) for m in range(2)]
                accs = [pacc.next() for m in range(2)]
                vw = 66 if is_dsa else 130
                ntile = nkb // 2
                pend = {}

                def do_qk(ti):
                    ps, pfree, pk = pss.next()
                    tm = None
                    for bi in range(2):
                        kb_ = ti * 2 + bi
                        need_mask = is_dsa or (kb_ >= nkb - 4)
                        tm = PE.op(T.matmul, ps[:, bi * 256:(bi + 1) * 256], lhsT=Kb[:, kb_ * 128:(kb_ + 1) * 128],
                                   rhs=qbd[:].rearrange("p a b -> p (a b)"), start=True, stop=not need_mask,
                                   waits=[tK, tq, pfree] if bi == 0 else ())
                        if need_mask:
                            if is_dsa:
                                ml = Mb[li][:, kb_ * 128:(kb_ + 1) * 128]
                                mw = [Mb_ready[li]]
                            else:
                                cbi = kb_ - (nkb - 4)
                                ml = cb[:, cbi * 128:(cbi + 1) * 128]
                                mw = [t_cb]
                            tm = PE.op(T.matmul, ps[:, bi * 256:(bi + 1) * 256], lhsT=ml, rhs=ident2[:].rearrange("p a b -> p (a b)"),
                                       start=False, stop=True, waits=mw + [t_id2])
                    et, efree, ek = etr.next()
                    te = ACT.op(A.activation, out=et[:], in_=ps[:], func=AF.Exp, waits=[tm, efree])
                    pss.done(pk, te)
                    pend[ti] = (et, te, ek)

                tav = [None, None]

                def do_av(ti):
                    et, te, ek = pend.pop(ti)
                    tm = None
                    first = True
                    for bi in range(2):
                        kb_ = ti * 2 + bi
                        for m in range(2):
                            acc, afree, ak = accs[m]
                            rhs = Vb[:, kb_, m * 66:(m + 1) * 66] if is_dsa else Vb[:, kb_, 0:130]
                            tm = PE.op(T.matmul, acc[:, 0:vw], lhsT=et[:, bi * 256 + m * 128:bi * 256 + (m + 1) * 128], rhs=rhs,
                                       start=(kb_ == 0), stop=(kb_ == nkb - 1),
                                       waits=[te, tV, accs[0][1], accs[1][1]] if first else ())
                            first = False
                            tav[m] = tm
                    etr.done(ek, tm)
                LAG = 2
                for ti in range(ntile):
                    do_qk(ti)
                    if ti >= LAG:
                        do_av(ti - LAG)
                    pump()
                for ti in range(max(0, ntile - LAG), ntile):
                    do_av(ti)
                qbr.done(qk, tav[1])
                if is_dsa:
                    Mb_readers[li].append(tav[1])
                Kr.done(kk, tav[1])
                Vr.done(vk, tav[1])
                if is_dsa:
                    for m in range(2):
                        acc, afree, ak = accs[m]
                        a = ACT.op(A.activation, out=ez[:, m:m + 1], in_=acc[:, 64:65], func=AF.Ln, waits=[tav[1]])
                        a = ACT.op(A.activation, out=ez[:, m:m + 1], in_=ez[:, m:m + 1], func=AF.Exp, scale=-1.0, waits=[a])
                        a = ACT.op(A.activation, out=ot[:, p * 128 + m * 64:p * 128 + (m + 1) * 64], in_=acc[:, 0:64], func=AF.Copy,
                                   scale=ez[:, m:m + 1], waits=[a, ofree])
                        pacc.done(ak, a)
                        owr.append(a)
                else:
                    h = p - 4
                    acc1, _, ak1 = accs[0]
                    acc2, _, ak2 = accs[1]
                    (of1, of2), offree, ofk = ofr.next()
                    a = ACT.op(A.activation, out=ez[:, 2:3], in_=acc1[:, 128:129], func=AF.Ln, waits=[tav[1], offree])
                    a = ACT.op(A.activation, out=ez[:, 2:3], in_=ez[:, 2:3], func=AF.Exp, scale=-1.0, waits=[a])
                    a = ACT.op(A.activation, out=of1[:], in_=acc1[:, 0:128], func=AF.Copy, scale=ez[:, 2:3], waits=[a])
                    pacc.done(ak1, a)
                    b = ACT.op(A.activation, out=ez[:, 3:4], in_=acc2[:, 128:129], func=AF.Ln, waits=[a])
                    b = ACT.op(A.activation, out=ez[:, 3:4], in_=ez[:, 3:4], func=AF.Exp, scale=-1.0, waits=[b])
                    b = ACT.op(A.activation, out=of2[:], in_=acc2[:, 0:128], func=AF.Copy, scale=ez[:, 3:4], waits=[b])
                    pacc.done(ak2, b)
                    b = DVE.op(V.scalar_tensor_tensor, out=of32[:], in0=of2[:], scalar=nlam[:, 0:1], in1=of1[:],
                               op0=ALU.mult, op1=ALU.add, waits=[a, b, t_nlam, of32_free[0]])
                    ofr.done(ofk, b)
                    c = DVE.op(V.scalar_tensor_tensor, out=oj[:], in0=of32[:], scalar=1.0, in1=of32[:], op0=ALU.mult, op1=ALU.mult,
                               accum_out=es_[:, 5:6], waits=[b])
                    c = DVE.op(V.tensor_scalar, out=es_[:, 6:7], in0=es_[:, 5:6], scalar1=1.0 / 128, scalar2=1e-5, op0=ALU.mult, op1=ALU.add, waits=[c])
                    c = POOL.op(nc.gpsimd.tensor_tensor, out=es_[:, 8:9], in0=es_[:, 6:7], in1=mhalf[:], op=ALU.pow, waits=[c, t_mh])
                    c = DVE.op(V.scalar_tensor_tensor, out=ot[:, 512 + h * 128:512 + (h + 1) * 128], in0=of32[:], scalar=es_[:, 8:9],
                               in1=gsub[:], op0=ALU.mult, op1=ALU.mult, waits=[c, t_gs, ofree])
                    of32_free[0] = [c]
                    owr.append(c)

            of_free = [[]]
            of32_free = [[]]
            for _ in prep_idx(0):
                pass
            for _ in prep_bis(0):
                pass
            for _ in prep_idx(1):
                pass
            for s in range(NQB):
                tiles = float(8 * (2 * s + 2))
                if s + 1 < NQB:
                    pumps[0].update(gen=prep_bis(s + 1), budget=0.0, rate=w_bis(s + 1) / tiles * 1.3)
                if s + 2 < NQB:
                    pumps[1].update(gen=prep_idx(s + 2), budget=0.0, rate=w_idx(s + 2) / tiles * 1.3)
                ot, ofree, ok_ = otr.next()
                owr = []
                for pi, p in enumerate([4, 5, 6, 7, 0, 1, 2, 3]):
                    attention(s, p, ot, ofree, owr)
                finish(pumps[0])
                finish(pumps[1])
                td = GQ.dma(o_scr[s * 128:(s + 1) * 128, :], ot[:], waits=owr)
                otr.done(ok_, td)
            barrier()

        def load_w(st, name, src, rows, cols, q, waits=(), order=None):
            nkc = rows // 128
            wt = sbt(st, name, [128, nkc, cols], BF16)
            srcv = src.rearrange("(kc p) n -> p kc n", p=128)
            step = 1024
            starts = list(range(0, cols, step))
            toks = [None] * len(starts)
            for ci in (order if order is not None else range(len(starts))):
                c0 = starts[ci]
                n = min(step, cols - c0)
                toks[ci] = q.dma(wt[:, :, c0:c0 + n], srcv[:, :, c0:c0 + n], waits=waits)
            return wt, toks

        with ExitStack() as p3:
            wg = sbt(p3, "wg", [128, 8, 2048], BF16)
            w_in_v = w_in.rearrange("(kc p) n -> p kc n", p=128)
            twg = [GQ.dma(wg[:, :, c0:c0 + 512], w_in_v[:, :, C_G + c0:C_G + c0 + 512]) for c0 in range(0, 2048, 512)]
            wbd, twbd = load_w(p3, "wbd", w_bd, 512, D, GQ)
            wbf, twbf = load_w(p3, "wbf", w_bf, 512, D, GQ)
            wo, two = load_w(p3, "wo", w_out, D, D, GQ)
            gbt = sbt(p3, "gbt", [128, 2048], F32)
            t_gb = SP.dma(gbt[:], bcast(gate_b))
            gmix = sbt(p3, "gmix3", [128, D], F32)
            t_gmix = SP.dma(gmix[:], bcast(norm_mix_g))
            xr = Ring([sbt(p3, f"xt{i}", [128, D], F32) for i in range(4)])
            junk = Ring([sbt(p3, f"junk{i}", [128, D], BF16) for i in range(1)])
            ssr = Ring([sbt(p3, f"ss{i}", [128, 4], F32) for i in range(4)])
            hbr = Ring([sbt(p3, f"hb{i}", [128, D], BF16) for i in range(3)])
            hTr = Ring([sbt(p3, f"hT{i}", [128, 8, 128], BF16) for i in range(2)])
            obr = Ring([sbt(p3, f"ob{i}", [128, D], BF16) for i in range(3)])
            oTr = Ring([sbt(p3, f"oT{i}", [128, 8, 128], BF16) for i in range(2)])
            gat = sbt(p3, "gat", [128, 2048], F32)
            mrg = sbt(p3, "mrg", [128, D], F32)
            mrg2 = sbt(p3, "mrg2", [128, D], F32)
            mbr = Ring([sbt(p3, f"mb{i}", [128, D], BF16) for i in range(2)])
            mTr = Ring([sbt(p3, f"mT{i}", [128, 8, 128], BF16) for i in range(2)])
            x1r = Ring([sbt(p3, f"x1t{i}", [128, D], F32) for i in range(2)])
            psT = Ring([pst(p3, f"psT{i}", [128, D], BF16) for i in range(2)])
            pp = Ring([pst(p3, f"pp{i}", [128, 512], F32) for i in range(6)])
            x1_dmas = []

            def transpose8(src_bf, tsrc, dst_ring):
                ps, pfree, pk = psT.next()
                tt = None
                for kc in range(8):
                    tt = PE.op(T.transpose, ps[:, kc * 128:(kc + 1) * 128], src_bf[:, kc * 128:(kc + 1) * 128], ident[:],
                               waits=[tsrc, pfree] if kc == 0 else ())
                dT, dfree, dk = dst_ring.next()
                te = ACT.op(A.copy, out=dT[:].rearrange("p a b -> p (a b)"), in_=ps[:], waits=[tt, dfree])
                psT.done(pk, te)
                return dT, te, dk, tt

            def st_load(s):
                xt, xfree, xk = xr.next()
                tx = SP.dma(xt[:], xq[s * 128:(s + 1) * 128, :], waits=xfree)
                hb, hfree, hk = hbr.next()
                th = rms_norm_block((junk, ssr), xt[:], tx, gmix[:], t_gmix, 1e-6, D, hb[:], hfree)
                ob, ofree, ok_ = obr.next()
                to = SP.dma(ob[:], o_scr[s * 128:(s + 1) * 128, :], waits=[ofree])
                return dict(s=s, xt=xt, xk=xk, hb=hb, th=th, hk=hk, ob=ob, to=to, ok_=ok_)

            def st_T(c):
                hT, te, tk, tt = transpose8(c["hb"], c["th"], hTr)
                hbr.done(c["hk"], tt)
                oT, teo, ok2, tto = transpose8(c["ob"], c["to"], oTr)
                obr.done(c["ok_"], tto)
                c.update(hT=hT, te=te, tk=tk, oT=oT, teo=teo, ok2=ok2)

            def st_X(c):
                hT, te, tk = c["hT"], c["te"], c["tk"]
                oT, teo, ok2 = c["oT"], c["teo"], c["ok2"]
                tg_last = None
                for gc in range(4):
                    ps, pfree, pk = pp.next()
                    tm = None
                    for kc in range(8):
                        tm = PE.op(T.matmul, ps[:], lhsT=hT[:, kc, :], rhs=wg[:, kc, gc * 512:(gc + 1) * 512], start=(kc == 0), stop=(kc == 7),
                                   waits=[te, pfree, twg] if kc == 0 else ())
                    a_ = DVE.op(V.tensor_tensor, out=gat[:, gc * 512:(gc + 1) * 512], in0=ps[:], in1=gbt[:, gc * 512:(gc + 1) * 512], op=ALU.add,
                                waits=[tm, t_gb, gat_free[0]])
                    pp.done(pk, a_)
                    tg_last = ACT.op(A.activation, out=gat[:, gc * 512:(gc + 1) * 512], in_=gat[:, gc * 512:(gc + 1) * 512], func=AF.Sigmoid, waits=[a_])
                    if gc == 3:
                        hTr.done(tk, tm)
                gat_free[0] = []
                mtoks = []
                for br, (wt, twt) in enumerate([(wbd, twbd), (wbf, twbf)]):
                    for nc_ in range(2):
                        ps, pfree, pk = pp.next()
                        tm = None
                        for kc in range(4):
                            tm = PE.op(T.matmul, ps[:], lhsT=oT[:, br * 4 + kc, :], rhs=wt[:, kc, nc_ * 512:(nc_ + 1) * 512], start=(kc == 0), stop=(kc == 3),
                                       waits=[teo, pfree, twt] if kc == 0 else ())
                        dst = (mrg if br == 0 else mrg2)[:, nc_ * 512:(nc_ + 1) * 512]
                        a_ = DVE.op(V.tensor_tensor, out=dst, in0=ps[:], in1=gat[:, br * 1024 + nc_ * 512:br * 1024 + (nc_ + 1) * 512], op=ALU.mult,
                                    waits=[tm, tg_last, mrg_free[0]])
                        pp.done(pk, a_)
                        mtoks.append(a_)
                        if br == 1 and nc_ == 1:
                            oTr.done(ok2, tm)
                gat_free[0] = list(mtoks)
                mb, mfree, mk = mbr.next()
                tmb = DVE.op(V.tensor_tensor, out=mb[:], in0=mrg[:], in1=mrg2[:], op=ALU.add, waits=[mtoks, mfree])
                mrg_free[0] = [tmb]
                c.update(mb=mb, tmb=tmb, mk=mk)

            def st_Y(c):
                s_ = c["s"]
                mT, tem, mk2, ttm = transpose8(c["mb"], c["tmb"], mTr)
                mbr.done(c["mk"], ttm)
                x1t, x1free, x1k = x1r.next()
                xtoks = []
                for nc_ in range(2):
                    ps, pfree, pk = pp.next()
                    tm = None
                    for kc in range(8):
                        tm = PE.op(T.matmul, ps[:], lhsT=mT[:, kc, :], rhs=wo[:, kc, nc_ * 512:(nc_ + 1) * 512], start=(kc == 0), stop=(kc == 7),
                                   waits=[tem, pfree, two] if kc == 0 else ())
                    a_ = DVE.op(V.tensor_tensor, out=x1t[:, nc_ * 512:(nc_ + 1) * 512], in0=ps[:], in1=c["xt"][:, nc_ * 512:(nc_ + 1) * 512], op=ALU.add,
                                waits=[tm, x1free])
                    pp.done(pk, a_)
                    xtoks.append(a_)
                    if nc_ == 1:
                        mTr.done(mk2, tm)
                xr.done(c["xk"], xtoks)
                td = GQ.dma(x1_scr[s_ * 128:(s_ + 1) * 128, :], x1t[:], waits=xtoks)
                x1r.done(x1k, td)
                x1_dmas.append(td)

            gat_free = [[]]
            mrg_free = [[]]
            ctx = {}
            ctx[0] = st_load(0)
            st_T(ctx[0])
            if NQB > 1:
                ctx[1] = st_load(1)
            prevY = None
            for s in range(NQB):
                if s + 2 < NQB:
                    ctx[s + 2] = st_load(s + 2)
                st_X(ctx[s])
                if s + 1 < NQB:
                    st_T(ctx[s + 1])
                if prevY is not None:
                    st_Y(prevY)
                prevY = ctx.pop(s)
            st_Y(prevY)
            barrier()
        early.close()

        with ExitStack() as p4:
            w1, tw1 = load_w(p4, "w1", w_f1, D, 2 * DFF, GQ, order=[0, 2, 3, 1, 4, 5])
            w2, tw2 = load_w(p4, "w2", w_f2, DFF, D, GQ)
            gffn = sbt(p4, "gffn", [128, D], F32)
            gfin = sbt(p4, "gfin", [128, D], F32)
            t_gffn = SP.dma(gffn[:], bcast(norm_ffn_g))
            t_gfin = SP.dma(gfin[:], bcast(norm_fin_g))
            x1r = Ring([sbt(p4, f"x1b{i}", [128, D], F32) for i in range(2)])
            junk = Ring([sbt(p4, f"junk{i}", [128, D], BF16) for i in range(1)])
            ssr = Ring([sbt(p4, f"ss{i}", [128, 4], F32) for i in range(2)])
            hbr = Ring([sbt(p4, f"hb{i}", [128, D], BF16) for i in range(2)])
            h2T = sbt(p4, "h2T", [128, 8, 512], BF16)
            actT = sbt(p4, "actT", [128, NFC, 512], BF16)
            sgr = Ring([sbt(p4, f"sg{i}", [128, 512], F32) for i in range(2)])
            x2r = Ring([sbt(p4, f"x2t{i}", [128, D], F32) for i in range(2)])
            psT = Ring([pst(p4, f"psT{i}", [128, D], BF16) for i in range(2)])
            pp = Ring([pst(p4, f"pp{i}", [128, 512], F32) for i in range(6)])
            h2T_free = []
            actT_free = []
            for grp in range(NQB // 4):
                th2 = []
                for bi in range(4):
                    s = grp * 4 + bi
                    x1t, x1free, x1k = x1r.next()
                    tx = SP.dma(x1t[:], x1_scr[s * 128:(s + 1) * 128, :], waits=[x1free])
                    hb, hfree, hk = hbr.next()
                    th = rms_norm_block((junk, ssr), x1t[:], tx, gffn[:], t_gffn, 1e-6, D, hb[:], hfree)
                    x1r.done(x1k, th)
                    ps, pfree, pk = psT.next()
                    tt = None
                    for kc in range(8):
                        tt = PE.op(T.transpose, ps[:, kc * 128:(kc + 1) * 128], hb[:, kc * 128:(kc + 1) * 128], ident[:],
                                   waits=[th, pfree] if kc == 0 else ())
                    hbr.done(hk, tt)
                    te = ACT.op(A.copy, out=h2T[:, :, bi * 128:(bi + 1) * 128], in_=ps[:].rearrange("p (a b) -> p a b", a=8), waits=[tt, h2T_free])
                    psT.done(pk, te)
                    th2.append(te)
                h2T_free = []
                tact = []
                last_mm = None
                for f in range(NFC):
                    psg, pfree, pkg = pp.next()
                    tmg = None
                    for kc in range(8):
                        tmg = PE.op(T.matmul, psg[:], lhsT=w1[:, kc, f * 128:(f + 1) * 128], rhs=h2T[:, kc, :], start=(kc == 0), stop=(kc == 7),
                                    waits=[th2, pfree, tw1[(f * 128) // 1024], tw1[(f * 128 + 127) // 1024]] if kc == 0 else ())
                    psu, pfree, pku = pp.next()
                    tmu = None
                    for kc in range(8):
                        tmu = PE.op(T.matmul, psu[:], lhsT=w1[:, kc, DFF + f * 128:DFF + (f + 1) * 128], rhs=h2T[:, kc, :], start=(kc == 0), stop=(kc == 7),
                                    waits=[pfree, tw1[(DFF + f * 128) // 1024], tw1[(DFF + f * 128 + 127) // 1024]] if kc == 0 else ())
                    last_mm = tmu
                    sg, sfree, sk = sgr.next()
                    ta = ACT.op(A.activation, out=sg[:], in_=psg[:], func=AF.Silu, waits=[tmg, sfree])
                    pp.done(pkg, ta)
                    tb_ = DVE.op(V.tensor_tensor, out=actT[:, f, :], in0=psu[:], in1=sg[:], op=ALU.mult, waits=[tmu, ta, actT_free])
                    pp.done(pku, tb_)
                    sgr.done(sk, tb_)
                    tact.append(tb_)
                h2T_free = [last_mm]
                actT_free = []
                last_o = None
                for bi in range(4):
                    s = grp * 4 + bi
                    x2, x2free, x2k = x2r.next()
                    tx2 = SP.dma(x2[:], x1_scr[s * 128:(s + 1) * 128, :], waits=[x2free])
                    xtoks = []
                    for nc_ in range(2):
                        ps, pfree, pk = pp.next()
                        tm = None
                        for f in range(NFC):
                            tm = PE.op(T.matmul, ps[:], lhsT=actT[:, f, bi * 128:(bi + 1) * 128], rhs=w2[:, f, nc_ * 512:(nc_ + 1) * 512],
                                       start=(f == 0), stop=(f == NFC - 1), waits=[tact, pfree, tw2] if f == 0 else ())
                        last_o = tm
                        a = DVE.op(V.tensor_tensor, out=x2[:, nc_ * 512:(nc_ + 1) * 512], in0=ps[:], in1=x2[:, nc_ * 512:(nc_ + 1) * 512], op=ALU.add,
                                   waits=[tm, tx2])
                        pp.done(pk, a)
                        xtoks.append(a)
                    e = rms_norm_block((junk, ssr), x2[:], xtoks, gfin[:], t_gfin, 1e-6, D, x2[:], [])
                    td = SP.dma(out[s * 128:(s + 1) * 128, :], x2[:], waits=[e])
                    x2r.done(x2k, td)
                actT_free = [last_o]
            barrier()
    return nc


_NC_CACHE = {}


def _get_nc(debug=False):
    if debug not in _NC_CACHE:
        _NC_CACHE[debug] = build(debug)
    return _NC_CACHE[debug]


def make_in_maps(inputs):
    x = np.ascontiguousarray(np.asarray(inputs["x"], dtype=np.float32))
    pos = np.asarray(inputs["positions"]).astype(np.int32)
    in_maps = []
    kk = np.arange(512)[None, :]
    qq = np.arange(128)[:, None]
    for c in range(8):
        b, j = c // 4, c % 4
        blocks = [4 * s + j for s in range(NQB)]
        xqc = np.concatenate([x[b, q * 128:(q + 1) * 128] for q in blocks], axis=0)
        posk = np.ascontiguousarray(pos[b].reshape(NTB, 128).T)
        posq = np.ascontiguousarray(np.stack([pos[b, q * 128:(q + 1) * 128] for q in blocks], axis=1))
        cm = np.where(kk <= j * 128 + qq, 0.0, NEG).astype(np.float32)
        m = {"xs": x[b], "xq": np.ascontiguousarray(xqc), "posk": posk, "posq": posq, "cmask": cm}
        for name in ["norm_mix_g", "w_in", "idx_k_norm_g", "idx_k_norm_b", "diff_lambda_q1", "diff_lambda_k1",
                     "diff_lambda_q2", "diff_lambda_k2", "diff_subln_g", "gate_b", "w_branch_dsa", "w_branch_diff",
                     "w_out", "norm_ffn_g", "w_ffn_in", "w_ffn_out"]:
            m[name] = np.ascontiguousarray(np.asarray(inputs[name], dtype=np.float32)[0])
        m["norm_final_g"] = np.ascontiguousarray(np.asarray(inputs["norm_final_g"], dtype=np.float32))
        in_maps.append(m)
    return in_maps


def kernel(**inputs):
    nc = _get_nc(False)
    in_maps = make_in_maps(inputs)
    res = run_bass_kernel_spmd(nc, in_maps, core_ids=list(range(8)))
    outp = np.zeros((2, S, D), dtype=np.float32)
    for c in range(8):
        b, j = c // 4, c % 4
        o = res.results[c]["out"]
        for s in range(NQB):
            q = 4 * s + j
            outp[b, q * 128:(q + 1) * 128] = o[s * 128:(s + 1) * 128]
    return outp
```
